# Optimizing a Trainium2 kernel written in Bass

```python
import jax, jax.numpy as jnp
from jax import lax
import numpy as np

D_MODEL = 2048
BATCH = 2
SEQ = 4096
DEPTH = 4

CHUNK = 64
EXPAND = 2
D_INNER = EXPAND * D_MODEL
HEAD_DIM = 128
N_HEADS = D_INNER // HEAD_DIM
N_LEFT_CHUNKS = 8
BAND = (N_LEFT_CHUNKS + 1) * CHUNK
REL_CLIP = 256
N_REL = 2 * REL_CLIP + 1
Q_BLOCK = 128
N_A_LAYERS = DEPTH // 2
N_B_LAYERS = DEPTH - N_A_LAYERS
EPS = 1e-6
NEG_INF = -1e30

kernel_name = "yoco_chunked_relbias_forgetting_attn"


def rms_norm(x, g):
    xf = x.astype(jnp.float32)
    y = xf * lax.rsqrt(jnp.mean(xf * xf, axis=-1, keepdims=True) + EPS)
    return (y * g.astype(jnp.float32)).astype(x.dtype)


def split_heads(t):
    return t.reshape(t.shape[0], t.shape[1], N_HEADS, HEAD_DIM)


def chunked_relbias_attention(q, k, v, rel_bias):
    b, s = q.shape[0], q.shape[1]
    n_chunks = s // CHUNK
    pad = N_LEFT_CHUNKS * CHUNK
    k_pad = jnp.pad(k, ((0, 0), (pad, 0), (0, 0), (0, 0)))
    v_pad = jnp.pad(v, ((0, 0), (pad, 0), (0, 0), (0, 0)))
    qi = jnp.arange(CHUNK)[:, None]
    kj = jnp.arange(BAND)[None, :]
    rel_idx = jnp.clip(pad + qi - kj, -REL_CLIP, REL_CLIP) + REL_CLIP
    bias = rel_bias.astype(jnp.float32)[:, rel_idx]
    scale = HEAD_DIM ** -0.5
    band_pos = jnp.arange(BAND)

    def one_chunk(c):
        start = c * CHUNK
        q_c = lax.dynamic_slice_in_dim(q, start, CHUNK, axis=1)
        k_c = lax.dynamic_slice_in_dim(k_pad, start, BAND, axis=1)
        v_c = lax.dynamic_slice_in_dim(v_pad, start, BAND, axis=1)
        sc = jnp.einsum('bqhd,bkhd->bhqk', q_c, k_c).astype(jnp.float32) * scale + bias
        valid = (start + band_pos) >= pad
        sc = jnp.where(valid[None, None, None, :], sc, NEG_INF)
        p = jax.nn.softmax(sc, axis=-1).astype(v.dtype)
        return jnp.einsum('bhqk,bkhd->bqhd', p, v_c)

    out = lax.map(one_chunk, jnp.arange(n_chunks))
    return jnp.moveaxis(out, 0, 1).reshape(b, s, N_HEADS, HEAD_DIM)


def forgetting_attention(q, k, v, cum_logf_h):
    s = q.shape[1]
    scale = HEAD_DIM ** -0.5
    outs = []
    for i in range(s // Q_BLOCK):
        q0 = i * Q_BLOCK
        end = q0 + Q_BLOCK
        q_b = q[:, q0:end]
        k_b = k[:, :end]
        v_b = v[:, :end]
        sc = jnp.einsum('bqhd,bkhd->bhqk', q_b, k_b).astype(jnp.float32) * scale
        decay = cum_logf_h[:, :, q0:end, None] - cum_logf_h[:, :, None, :end]
        causal = (q0 + jnp.arange(Q_BLOCK))[:, None] >= jnp.arange(end)[None, :]
        sc = jnp.where(causal[None, None], sc + decay, NEG_INF)
        p = jax.nn.softmax(sc, axis=-1).astype(v.dtype)
        outs.append(jnp.einsum('bhqk,bkhd->bqhd', p, v_b))
    return jnp.concatenate(outs, axis=1)


def setup_inputs(seed: int = 0) -> dict:
    key = jax.random.key(seed)
    ks = jax.random.split(key, 16)
    f32 = jnp.float32
    d_scale = D_MODEL ** -0.5
    out_scale = 0.5 * D_INNER ** -0.5
    x = jax.random.normal(ks[0], (BATCH, SEQ, D_MODEL), f32)
    a_norm = 1.0 + 0.02 * jax.random.normal(ks[1], (N_A_LAYERS, D_MODEL), f32)
    a_w_in = d_scale * jax.random.normal(ks[2], (N_A_LAYERS, D_MODEL, 4 * D_INNER), f32)
    a_rel_bias = 0.1 * jax.random.normal(ks[3], (N_A_LAYERS, N_HEADS, N_REL), f32)
    a_w_out = out_scale * jax.random.normal(ks[4], (N_A_LAYERS, D_INNER, D_MODEL), f32)
    kv_norm = 1.0 + 0.02 * jax.random.normal(ks[5], (D_MODEL,), f32)
    kv_w = d_scale * jax.random.normal(ks[6], (D_MODEL, 2 * D_INNER), f32)
    f_w = d_scale * jax.random.normal(ks[7], (D_MODEL, N_HEADS), f32)
    f_b = jnp.linspace(1.0, 5.0, N_HEADS, dtype=f32) + 0.1 * jax.random.normal(ks[8], (N_HEADS,), f32)
    b_norm = 1.0 + 0.02 * jax.random.normal(ks[9], (N_B_LAYERS, D_MODEL), f32)
    b_w_in = d_scale * jax.random.normal(ks[10], (N_B_LAYERS, D_MODEL, 2 * D_INNER), f32)
    b_w_out = out_scale * jax.random.normal(ks[11], (N_B_LAYERS, D_INNER, D_MODEL), f32)
    final_norm = 1.0 + 0.02 * jax.random.normal(ks[12], (D_MODEL,), f32)
    return {"x": x, "a_norm": a_norm, "a_w_in": a_w_in, "a_rel_bias": a_rel_bias,
            "a_w_out": a_w_out, "kv_norm": kv_norm, "kv_w": kv_w, "f_w": f_w,
            "f_b": f_b, "b_norm": b_norm, "b_w_in": b_w_in, "b_w_out": b_w_out,
            "final_norm": final_norm}


def reference(x, a_norm, a_w_in, a_rel_bias, a_w_out, kv_norm, kv_w, f_w, f_b,
              b_norm, b_w_in, b_w_out, final_norm):
    b, s, _ = x.shape
    k_sh = v_sh = cum_logf_h = None
    for layer in range(DEPTH):
        if layer < N_A_LAYERS:
            h = rms_norm(x, a_norm[layer])
            q, k, v, g = jnp.split(h @ a_w_in[layer], 4, axis=-1)
            o = chunked_relbias_attention(split_heads(q), split_heads(k), split_heads(v),
                                          a_rel_bias[layer]).reshape(b, s, D_INNER)
            x = x + (o * jax.nn.silu(g)) @ a_w_out[layer]
        else:
            if layer == N_A_LAYERS:
                h_kv = rms_norm(x, kv_norm)
                k_s, v_s = jnp.split(h_kv @ kv_w, 2, axis=-1)
                k_sh, v_sh = split_heads(k_s), split_heads(v_s)
                logf = jax.nn.log_sigmoid((h_kv @ f_w + f_b).astype(jnp.float32))
                cum_logf_h = jnp.transpose(jnp.cumsum(logf, axis=1), (0, 2, 1))
            lb = layer - N_A_LAYERS
            h = rms_norm(x, b_norm[lb])
            q, g = jnp.split(h @ b_w_in[lb], 2, axis=-1)
            o = forgetting_attention(split_heads(q), k_sh, v_sh, cum_logf_h).reshape(b, s, D_INNER)
            x = x + (o * jax.nn.silu(g)) @ b_w_out[lb]
    return rms_norm(x, final_norm)
```

```python
import numpy as np
from contextlib import ExitStack
import ml_dtypes
import concourse.bass as bass
import concourse.mybir as mybir
from concourse.bass_utils import run_bass_kernel_spmd

F32 = mybir.dt.float32
BF16 = mybir.dt.bfloat16
AF = mybir.ActivationFunctionType
ALU = mybir.AluOpType
NPBF = ml_dtypes.bfloat16

D_MODEL = 2048
NCH = 16
D_INNER = 4096
NH = 32
DH = 128
SEQ = 4096
BATCH = 2
T = 1024
SEG = 512
NSEG = 2
EPS = 1e-6
NEG = -30000.0
SCALE = DH ** -0.5
HG = 4
UMAX = (16, 32)


class Res:
    __slots__ = ("name", "w", "rs")

    def __init__(self, name=""):
        self.name = name
        self.w = None
        self.rs = {}


class Op:
    __slots__ = ("eng", "fn", "deps", "dma", "sig", "need", "n", "cc")

    def __init__(self, eng, fn, dma, cc=None):
        self.eng = eng
        self.fn = fn
        self.dma = dma
        self.deps = []
        self.sig = None
        self.need = dma
        self.n = 0
        self.cc = cc


class Prog:
    ENGS = ("pe", "act", "dve", "pool", "sp")
    NDSEM = 24
    EPOCH = 30000

    def __init__(self, nc):
        self.nc = nc
        self.ops = {e: [] for e in self.ENGS}
        self.ndma = {e: 0 for e in self.ENGS}
        self.count = 0
        self.ccs = {}

    def _track(self, o, reads, writes):
        deps = {}
        for r in reads:
            if r.w is not None:
                deps[id(r.w)] = r.w
        for r in writes:
            if r.w is not None:
                deps[id(r.w)] = r.w
            for x in r.rs.values():
                deps[id(x)] = x
        for d in deps.values():
            if d is o:
                continue
            if (not d.dma) and (not o.dma) and d.eng == "pe" and o.eng == "pe":
                continue
            d.need = True
            o.deps.append(d)
        for r in reads:
            key = ("dma", self.count) if o.dma else o.eng
            r.rs[key] = o
        for r in writes:
            r.w = o
            r.rs = {}
        self.count += 1

    def op(self, eng, fn, reads=(), writes=()):
        o = Op(eng, fn, False)
        self._track(o, reads, writes)
        self.ops[eng].append(o)
        return o

    def dma(self, q, fn, reads=(), writes=()):
        o = Op(q, fn, True)
        self._track(o, reads, writes)
        self.ops[q].append(o)
        return o

    def coll(self, key, fn, reads=(), writes=()):
        o = Op("pool", fn, True, cc=key)
        self._track(o, reads, writes)
        self.ops["pool"].append(o)
        return o

    def emit(self, es, final_deps):
        nc = self.nc
        fin = Op("sp", None, False)
        for d in final_deps:
            d.need = True
            fin.deps.append(d)
        self.ops["sp"].append(fin)
        sems = {}
        for e in self.ENGS:
            cnt = 0
            nd = 0
            esems = []
            dsems = []
            for o in self.ops[e]:
                if o.cc is not None:
                    if o.cc not in self.ccs:
                        self.ccs[o.cc] = [es.enter_context(nc.semaphore("cc_%s" % str(o.cc))), 0]
                    self.ccs[o.cc][1] += 1
                    o.sig = (self.ccs[o.cc][0], self.ccs[o.cc][1])
                elif o.dma:
                    k = nd % self.NDSEM
                    if k >= len(dsems):
                        dsems.append(es.enter_context(nc.semaphore("d_%s_%d" % (e, k))))
                    o.sig = (dsems[k], 16 * (nd // self.NDSEM + 1))
                    o.n = nd
                    nd += 1
                elif o.need:
                    ep = cnt // self.EPOCH
                    if ep >= len(esems):
                        esems.append(es.enter_context(nc.semaphore("c_%s_%d" % (e, ep))))
                    o.sig = (esems[ep], cnt % self.EPOCH + 1)
                    cnt += 1
            sems[e] = (esems, dsems)
        engobj = {"pe": nc.tensor, "act": nc.scalar, "dve": nc.vector, "pool": nc.gpsimd, "sp": nc.sync}
        block = es.enter_context(nc.Block())

        def make(e):
            def body(eng):
                waited = {}
                pre = getattr(self, "pre", {}).get(e)
                if pre is not None:
                    pre(eng)
                for o in self.ops[e]:
                    ws = []
                    for d in o.deps:
                        ws.append(d.sig)
                    if o.dma and o.cc is None and o.n >= self.NDSEM:
                        ws.append((o.sig[0], o.sig[1] - 16))
                    for (s, v) in ws:
                        if waited.get(id(s), 0) < v:
                            waited[id(s)] = v
                            eng.wait_ge(s, v)
                    if o.fn is None:
                        continue
                    ins = o.fn(eng)
                    if o.cc is not None:
                        ins.then_inc(o.sig[0], 1)
                    elif o.dma:
                        ins.then_inc(o.sig[0], 16)
                    elif o.sig is not None:
                        ins.then_inc(o.sig[0], 1)
            return body

        block.tensor(make("pe"))
        block.scalar(make("act"))
        block.vector(make("dve"))
        block.gpsimd(make("pool"))
        block.sync(make("sp"))


class Tl:
    __slots__ = ("t", "r")

    def __init__(self, t, name):
        self.t = t
        self.r = Res(name)


class Ctx:
    def __init__(self):
        self.nc = bass.Bass("TRN2", target_bir_lowering=False)
        self.P = Prog(self.nc)
        self.es = ExitStack()
        self.outs = []
        self.nps = 0

    def dram(self, name, shape, dt, kind):
        return self.nc.dram_tensor(name, list(shape), dt, kind=kind).ap()

    def sb(self, name, shape, dt):
        return Tl(self.es.enter_context(self.nc.sbuf_tensor("s_" + name, list(shape), dt)), name)

    def ps(self, name, dt=F32):
        n = 512 if dt == F32 else 1024
        return Tl(self.es.enter_context(self.nc.psum_tensor("p_" + name, [128, n], dt)), name)

    def finish(self):
        self.P.emit(self.es, self.outs)
        self.es.close()
        return self.nc


def phase_consts(C):
    P = C.P
    C.ones_f = C.sb("ones_f", [128, 128], F32)
    C.ones_b = C.sb("ones_b", [128, 128], BF16)
    P.op("pool", lambda e: e.memset(C.ones_f.t[:], 1.0), writes=[C.ones_f.r])
    P.op("pool", lambda e: e.memset(C.ones_b.t[:], 1.0), writes=[C.ones_b.r])


def phase_load_x(C, xT_d):
    P = C.P
    C.xT = C.sb("xT", [128, NCH, T], F32)
    C.xr = [Res("x%d" % c) for c in range(NCH)]
    for c0 in range(0, NCH, 4):
        P.dma("sp", lambda e, c0=c0: e.dma_start(out=C.xT.t[:, c0:c0 + 4, :], in_=xT_d[:, c0:c0 + 4, :]),
              writes=C.xr[c0:c0 + 4])


def phase_norm(C, gn_d, tag, psA, psB, want_tok_rstd=False):
    P = C.P
    if not hasattr(C, "hT"):
        C.hT = C.sb("hT", [128, NCH, T], BF16)
        C.hr = [Res("h%d" % c) for c in range(NCH)]
        C.xsq = [C.sb("xsq%d" % i, [128, T], F32) for i in range(2)]
        C.rstd = C.sb("rstd", [128, T], F32)
    gn = C.sb("gn_" + tag, [128, NCH], F32)
    P.dma("sp", lambda e: e.dma_start(out=gn.t[:], in_=gn_d), writes=[gn.r])
    pss = (psA, psB)
    for c in range(NCH):
        sq = C.xsq[c % 2]
        P.op("act", lambda e, c=c, sq=sq: e.activation(out=sq.t[:], in_=C.xT.t[:, c, :], func=AF.Square),
             reads=[C.xr[c]], writes=[sq.r])
        for hf in range(2):
            P.op("pe", lambda e, c=c, sq=sq, hf=hf: e.matmul(
                pss[hf].t[:], lhsT=C.ones_f.t[:], rhs=sq.t[:, hf * 512:(hf + 1) * 512],
                start=(c == 0), stop=(c == NCH - 1)),
                reads=[sq.r, C.ones_f.r], writes=[pss[hf].r])
    for hf in range(2):
        sl = slice(hf * 512, (hf + 1) * 512)
        P.op("act", lambda e, hf=hf, sl=sl: e.activation(
            out=C.rstd.t[:, sl], in_=pss[hf].t[:], func=AF.Sqrt, bias=EPS, scale=1.0 / D_MODEL),
            reads=[pss[hf].r], writes=[C.rstd.r])
    P.op("dve", lambda e: e.reciprocal(out=C.rstd.t[:], in_=C.rstd.t[:]), reads=[C.rstd.r], writes=[C.rstd.r])
    for c in range(NCH):
        P.op("dve", lambda e, c=c: e.scalar_tensor_tensor(
            out=C.hT.t[:, c, :], in0=C.xT.t[:, c, :], scalar=gn.t[:, c:c + 1], in1=C.rstd.t[:],
            op0=ALU.mult, op1=ALU.mult), reads=[C.xr[c], gn.r, C.rstd.r], writes=[C.hr[c]])
    return gn


def proj_fm(C, wtile, ps, hf, extra_reads=()):
    P = C.P
    for c in range(NCH):
        P.op("pe", lambda e, c=c: e.matmul(ps.t[:], lhsT=wtile.t[:, c, :], rhs=C.hT.t[:, c, hf * 512:(hf + 1) * 512],
                                           start=(c == 0), stop=(c == NCH - 1)),
             reads=[wtile.r, C.hr[c]] + list(extra_reads), writes=[ps.r])


def build_kv():
    C = Ctx()
    P = C.P
    xT_d = C.dram("xT", [128, NCH, T], F32, "ExternalInput")
    gn_d = C.dram("gn", [128, NCH], F32, "ExternalInput")
    wk_d = C.dram("wk", [NH, 128, NCH, 128], F32, "ExternalInput")
    wv_d = C.dram("wv", [8, 128, NCH, 512], F32, "ExternalInput")
    fw_d = C.dram("fw", [128, NCH, NH], F32, "ExternalInput")
    fb_d = C.dram("fb", [128, NH], F32, "ExternalInput")
    kT_o = C.dram("kT", [NH, 128, T], BF16, "ExternalOutput")
    V_o = C.dram("V", [T // 128, 128, D_INNER], BF16, "ExternalOutput")
    nlf_o = C.dram("nlf", [T // 128, 128, NH], F32, "ExternalOutput")

    ps = [C.ps("ps%d" % i) for i in range(8)]
    phase_consts(C)
    phase_load_x(C, xT_d)
    gn = phase_norm(C, gn_d, "kv", ps[0], ps[1])

    fw = C.sb("fw", [128, NCH, NH], F32)
    fb = C.sb("fb", [128, NH], F32)
    P.dma("sp", lambda e: e.dma_start(out=fw.t[:], in_=fw_d), writes=[fw.r])
    P.dma("sp", lambda e: e.dma_start(out=fb.t[:], in_=fb_d), writes=[fb.r])
    gfw = C.sb("gfw", [128, NCH, NH], F32)
    for c in range(NCH):
        P.op("pool", lambda e, c=c: e.tensor_scalar(out=gfw.t[:, c, :], in0=fw.t[:, c, :], scalar1=gn.t[:, c:c + 1],
                                                    scalar2=None, op0=ALU.mult),
             reads=[fw.r, gn.r], writes=[gfw.r])
    ones_col = C.sb("ones_col", [128, 1], F32)
    P.op("pool", lambda e: e.memset(ones_col.t[:], 1.0), writes=[ones_col.r])
    xsqc = [C.sb("xsqc%d" % i, [128, 128], F32) for i in range(2)]
    zs = [C.sb("zs%d" % i, [128, NH], F32) for i in range(2)]
    rt = [C.sb("rt%d" % i, [128, 1], F32) for i in range(2)]
    NT = T // 128
    k = 0
    for tt in range(NT):
        pz = ps[2 + (tt % 2) * 2]
        pq = ps[3 + (tt % 2) * 2]
        tsl = slice(tt * 128, (tt + 1) * 128)
        for c in range(NCH):
            sq = xsqc[k % 2]
            k += 1
            P.op("act", lambda e, c=c, sq=sq, tsl=tsl: e.activation(out=sq.t[:], in_=C.xT.t[:, c, tsl], func=AF.Square),
                 reads=[C.xr[c]], writes=[sq.r])
            P.op("pe", lambda e, c=c, sq=sq, pq=pq: e.matmul(pq.t[:, 0:1], lhsT=sq.t[:], rhs=ones_col.t[:],
                                                             start=(c == 0), stop=(c == NCH - 1)),
                 reads=[sq.r, ones_col.r], writes=[pq.r])
            P.op("pe", lambda e, c=c, tsl=tsl, pz=pz: e.matmul(pz.t[:, 0:NH], lhsT=C.xT.t[:, c, tsl], rhs=gfw.t[:, c, :],
                                                               start=(c == 0), stop=(c == NCH - 1)),
                 reads=[C.xr[c], gfw.r], writes=[pz.r])
        r1 = rt[tt % 2]
        z = zs[tt % 2]
        P.op("act", lambda e, r1=r1, pq=pq: e.activation(out=r1.t[:], in_=pq.t[:, 0:1], func=AF.Sqrt, bias=EPS,
                                                         scale=1.0 / D_MODEL), reads=[pq.r], writes=[r1.r])
        P.op("dve", lambda e, r1=r1: e.reciprocal(out=r1.t[:], in_=r1.t[:]), reads=[r1.r], writes=[r1.r])
        P.op("dve", lambda e, r1=r1, z=z, pz=pz: e.scalar_tensor_tensor(
            out=z.t[:], in0=pz.t[:, 0:NH], scalar=r1.t[:, 0:1], in1=fb.t[:], op0=ALU.mult, op1=ALU.add),
            reads=[pz.r, r1.r, fb.r], writes=[z.r])
        P.op("act", lambda e, z=z: e.activation(out=z.t[:], in_=z.t[:], func=AF.Exp, scale=-1.0),
             reads=[z.r], writes=[z.r])
        P.op("act", lambda e, z=z: e.activation(out=z.t[:], in_=z.t[:], func=AF.Ln, bias=1.0, scale=1.0),
             reads=[z.r], writes=[z.r])
        C.outs.append(P.dma("sp", lambda e, z=z, tt=tt: e.dma_start(out=nlf_o[tt], in_=z.t[:]), reads=[z.r]))

    wkb = [C.sb("wkb%d" % i, [128, NCH, 128], BF16) for i in range(2)]
    ko = [C.sb("ko%d" % i, [128, T], BF16) for i in range(2)]
    for h in range(NH):
        w = wkb[h % 2]
        o = ko[h % 2]
        P.dma("pool", lambda e, w=w, h=h: e.dma_start(out=w.t[:], in_=wk_d[h]), writes=[w.r])
        for hf in range(2):
            pp = ps[(2 * h + hf) % 4]
            proj_fm(C, w, pp, hf)
            if hf == 0:
                P.op("act", lambda e, o=o, pp=pp: e.activation(out=o.t[:, 0:512], in_=pp.t[:], func=AF.Copy),
                     reads=[pp.r], writes=[o.r])
            else:
                P.op("dve", lambda e, o=o, pp=pp: e.tensor_copy(out=o.t[:, 512:1024], in_=pp.t[:]),
                     reads=[pp.r], writes=[o.r])
        C.outs.append(P.dma("sp", lambda e, o=o, h=h: e.dma_start(out=kT_o[h], in_=o.t[:]), reads=[o.r]))

    wvb = [C.sb("wvb%d" % i, [128, NCH, 512], BF16) for i in range(2)]
    vo = [C.sb("vo%d" % i, [128, 512], BF16) for i in range(4)]
    k = 0
    for b in range(8):
        w = wvb[b % 2]
        P.dma("pool", lambda e, w=w, b=b: e.dma_start(out=w.t[:], in_=wv_d[b]), writes=[w.r])
        for tt in range(NT):
            pp = ps[4 + k % 4]
            o = vo[k % 4]
            for c in range(NCH):
                P.op("pe", lambda e, c=c, tt=tt, w=w, pp=pp: e.matmul(
                    pp.t[:], lhsT=C.hT.t[:, c, tt * 128:(tt + 1) * 128], rhs=w.t[:, c, :],
                    start=(c == 0), stop=(c == NCH - 1)), reads=[w.r, C.hr[c]], writes=[pp.r])
            if k % 2 == 0:
                P.op("act", lambda e, o=o, pp=pp: e.activation(out=o.t[:], in_=pp.t[:], func=AF.Copy),
                     reads=[pp.r], writes=[o.r])
            else:
                P.op("dve", lambda e, o=o, pp=pp: e.tensor_copy(out=o.t[:], in_=pp.t[:]), reads=[pp.r], writes=[o.r])
            C.outs.append(P.dma("sp", lambda e, o=o, tt=tt, b=b: e.dma_start(
                out=V_o[tt, :, b * 512:(b + 1) * 512], in_=o.t[:]), reads=[o.r]))
            k += 1
    return C.finish()


def core_segments(j):
    jj = j % 4
    return (jj, jj + 4)


def core_tokens(j):
    s0, s1 = core_segments(j)
    return np.concatenate([np.arange(s0 * SEG, (s0 + 1) * SEG), np.arange(s1 * SEG, (s1 + 1) * SEG)])


def tile_w_cols(W, col0, ncols, blk):
    Wc = W[:, col0:col0 + ncols]
    nb = ncols // blk
    return np.ascontiguousarray(Wc.reshape(NCH, 128, nb, blk).transpose(2, 1, 0, 3))


def tile_vec(g):
    return np.ascontiguousarray(g.reshape(NCH, 128).T)


def to_xT(xtok):
    t = xtok.shape[0]
    return np.ascontiguousarray(xtok.T.reshape(NCH, 128, t).transpose(1, 0, 2))


def from_xT(xT):
    t = xT.shape[2]
    return np.ascontiguousarray(xT.transpose(1, 0, 2).reshape(D_MODEL, t).T)


_NC_CACHE = {}


def get_nc(kind):
    if kind not in _NC_CACHE:
        _NC_CACHE[kind] = build_kv() if kind == "kv" else build_mix(kind)
    return _NC_CACHE[kind]


def run_kv(xT_cores, gn, W, koff, voff, f_w, f_b):
    nc = get_nc("kv")
    wk = tile_w_cols(W, koff, D_INNER, 128)
    wv = tile_w_cols(W, voff, D_INNER, 512)
    fw = np.ascontiguousarray(f_w.reshape(NCH, 128, NH).transpose(1, 0, 2))
    fb = np.ascontiguousarray(np.broadcast_to(f_b[None, :], (128, NH)))
    g = tile_vec(gn)
    in_maps = [{"xT": xT_cores[j], "gn": g, "wk": wk, "wv": wv, "fw": fw, "fb": fb} for j in range(8)]
    res = run_bass_kernel_spmd(nc, in_maps, core_ids=list(range(8)))
    return res.results


def attn_unit(C, kt_ap, v_ap, q_ap, ncol, sc_ap, adds, acc_o, acc_s, c0, first, kres, vres, qres, ares, psS, Sb, PT):
    P = C.P
    P.op("pe", lambda e: e.matmul(psS.t[:, 0:ncol], lhsT=kt_ap, rhs=q_ap, start=True, stop=True),
         reads=[kres, qres], writes=[psS.r])
    for (o, n, ap, r) in adds:
        P.op("dve", lambda e, o=o, n=n, ap=ap: e.scalar_tensor_tensor(
            out=Sb.t[:, o:o + n], in0=psS.t[:, o:o + n], scalar=sc_ap, in1=ap, op0=ALU.add, op1=ALU.add),
            reads=[psS.r, r] + list(ares), writes=[Sb.r])
    P.op("act", lambda e: e.activation(out=PT.t[:, 0:ncol], in_=Sb.t[:, 0:ncol], func=AF.Exp),
         reads=[Sb.r], writes=[PT.r])
    P.op("pe", lambda e: e.matmul(acc_o.t[:, c0:c0 + ncol], lhsT=v_ap, rhs=PT.t[:, 0:ncol], start=first, stop=False,
                                  skip_group_check=True),
         reads=[vres, PT.r], writes=[acc_o.r])
    P.op("pe", lambda e: e.matmul(acc_s.t[:, c0:c0 + ncol], lhsT=C.ones_b.t[:], rhs=PT.t[:, 0:ncol], start=first,
                                  stop=False, skip_group_check=True),
         reads=[C.ones_b.r, PT.r], writes=[acc_s.r])


def build_mix(kind):
    C = Ctx()
    P = C.P
    isb = kind == "b"
    xT_d = C.dram("xT", [128, NCH, T], F32, "ExternalInput")
    gn_d = C.dram("gn", [128, NCH], F32, "ExternalInput")
    wq_d = C.dram("wq", [NH, 128, NCH, 128], F32, "ExternalInput")
    wg_d = C.dram("wg", [NH, 128, NCH, 128], F32, "ExternalInput")
    wo_d = C.dram("wo", [NH // HG, 128, HG, D_MODEL], F32, "ExternalInput")
    xo_d = C.dram("xo", [128, NCH, T], F32, "ExternalOutput")
    if not isb:
        NU = (8, 8)
        kT_d = [C.dram("kTs%d" % s, [NH, 128, 8 * 128], BF16, "ExternalInput") for s in range(2)]
        V_d = [C.dram("Vs%d" % s, [8, 128, D_INNER], BF16, "ExternalInput") for s in range(2)]
        kmask_d = C.dram("kmask", [128, 2, 8], F32, "ExternalInput")
        rbx_d = C.dram("rbx", [NH, 767], F32, "ExternalInput")
        cmask_d = C.dram("cmask", [128, 640], F32, "ExternalInput")
    else:
        NU = UMAX
        kT_d = [C.dram("kTs%d" % s, [NH, 128, NU[s] * 128], BF16, "ExternalInput") for s in range(2)]
        V_d = [C.dram("Vs%d" % s, [NU[s], 128, D_INNER], BF16, "ExternalInput") for s in range(2)]
        nlf_d = [C.dram("nlfs%d" % s, [128, NU[s], NH], F32, "ExternalInput") for s in range(2)]
        kval_d = [C.dram("kval%d" % s, [128, NU[s]], F32, "ExternalInput") for s in range(2)]
        gfin_d = C.dram("gfin", [128, NCH], F32, "ExternalInput")
        ident_d = C.dram("ident", [128, 128], F32, "ExternalInput")
        tri_d = C.dram("tri", [128, 128], F32, "ExternalInput")
        trim_d = C.dram("trim", [128, 128], F32, "ExternalInput")
        yo_d = C.dram("yo", [128, NCH, T], F32, "ExternalOutput")

    ps = [C.ps("ps%d" % i) for i in range(8)]
    psS = ps[0:2]
    psO = ps[2]
    psSm = ps[3]
    psG = ps[4:8]
    phase_consts(C)
    phase_load_x(C, xT_d)
    sg = [C.sb("sg%d" % i, [128, T], F32) for i in range(2)]
    C.xsq = sg
    C.hT = C.sb("hT", [128, NCH, T], BF16)
    C.hr = [Res("h%d" % c) for c in range(NCH)]
    C.rstd = C.sb("rstd", [128, T], F32)
    phase_norm(C, gn_d, "n1", psG[0], psG[1])

    wqb = [C.sb("wqb%d" % i, [128, NCH, 128], BF16) for i in range(2)]
    wgb = [C.sb("wgb%d" % i, [128, NCH, 128], BF16) for i in range(2)]
    wob = C.sb("wob", [128, HG, D_MODEL], BF16)
    qT = [C.sb("qT%d" % i, [128, T], BF16) for i in range(2)]
    og = C.sb("og", [128, HG, T], BF16)
    kxc = [C.sb("kxc%d" % i, [128, 8 * 128], BF16) for i in range(2)]
    vxc = [C.sb("vxc%d" % i, [128, 8, 128], BF16) for i in range(2)]
    Sb = [C.sb("Sb%d" % i, [128, 512], F32) for i in range(2)]
    PT = [C.sb("PT%d" % i, [128, 512], BF16) for i in range(2)]
    rs = C.sb("rs", [128, 512], F32)
    wgt = C.sb("wgt", [128, 512], F32)

    if not isb:
        kmask = C.sb("kmask", [128, 2, 8], F32)
        P.dma("sp", lambda e: e.dma_start(out=kmask.t[:], in_=kmask_d), writes=[kmask.r])
        cmask = C.sb("cmask", [128, 640], F32)
        P.dma("sp", lambda e: e.dma_start(out=cmask.t[:], in_=cmask_d), writes=[cmask.r])
        BT = [C.sb("BT%d" % i, [128, 640], F32) for i in range(2)]
    else:
        ident = C.sb("ident", [128, 128], F32)
        tri = C.sb("tri", [128, 128], F32)
        trim = C.sb("trim", [128, 128], F32)
        for (t_, d_) in ((ident, ident_d), (tri, tri_d), (trim, trim_d)):
            P.dma("sp", lambda e, t_=t_, d_=d_: e.dma_start(out=t_.t[:], in_=d_), writes=[t_.r])
        nlf_t = C.sb("nlf_t", [128, 32, NH], F32)
        tot_t = C.sb("tot_t", [128, 32, NH], F32)
        ND = [C.sb("ND%d" % s, [128, NU[s], NH], F32) for s in range(2)]
        SC = [C.sb("SC%d" % s, [128, NU[s], NH], F32) for s in range(2)]
        kval = [C.sb("kval%d" % s, [128, NU[s]], F32) for s in range(2)]
        NDQ = C.sb("NDQ", [128, 512], F32)
        NDQd = C.sb("NDQd", [128, 512], F32)
        dexp = [C.sb("dexp%d" % i, [128, 4, 128], BF16) for i in range(2)]
        for s in range(2):
            U = NU[s]
            P.dma("sp", lambda e, s=s, U=U: e.dma_start(out=nlf_t.t[:, 0:U, :], in_=nlf_d[s]), writes=[nlf_t.r])
            P.dma("sp", lambda e, s=s: e.dma_start(out=kval[s].t[:], in_=kval_d[s]), writes=[kval[s].r])
            nflat = nlf_t.t[:, 0:U, :].rearrange("p u h -> p (u h)")
            ndflat = ND[s].t[:].rearrange("p u h -> p (u h)")
            totflat = tot_t.t[:, 0:U, :].rearrange("p u h -> p (u h)")
            for j in range(U * NH // 512):
                sl = slice(j * 512, (j + 1) * 512)
                pa = psG[(2 * j) % 4]
                pb = psG[(2 * j + 1) % 4]
                P.op("pe", lambda e, pa=pa, sl=sl, nflat=nflat: e.matmul(pa.t[:], lhsT=tri.t[:], rhs=nflat[:, sl],
                                                                         start=True, stop=True),
                     reads=[tri.r, nlf_t.r], writes=[pa.r])
                P.op("pe", lambda e, pb=pb, sl=sl, nflat=nflat: e.matmul(pb.t[:], lhsT=C.ones_f.t[:], rhs=nflat[:, sl],
                                                                         start=True, stop=True),
                     reads=[C.ones_f.r, nlf_t.r], writes=[pb.r])
                P.op("act", lambda e, pa=pa, sl=sl, ndflat=ndflat: e.activation(out=ndflat[:, sl], in_=pa.t[:], func=AF.Copy),
                     reads=[pa.r], writes=[ND[s].r])
                P.op("dve", lambda e, pb=pb, sl=sl, totflat=totflat: e.tensor_copy(out=totflat[:, sl], in_=pb.t[:]),
                     reads=[pb.r], writes=[tot_t.r])
            for u in range(1, U - 1):
                P.op("dve", lambda e, u=u: e.tensor_tensor(out=tot_t.t[:, u, :], in0=tot_t.t[:, u, :],
                                                           in1=tot_t.t[:, u - 1, :], op=ALU.add),
                     reads=[tot_t.r], writes=[tot_t.r])
            P.op("dve", lambda e, s=s, U=U: e.tensor_tensor(out=ND[s].t[:, 1:U, :], in0=ND[s].t[:, 1:U, :],
                                                            in1=tot_t.t[:, 0:U - 1, :], op=ALU.add),
                 reads=[tot_t.r, ND[s].r], writes=[ND[s].r])
            P.op("dve", lambda e, s=s, U=U: e.tensor_tensor(
                out=SC[s].t[:], in0=kval[s].t[:].unsqueeze(2).to_broadcast([128, U, NH]), in1=ND[s].t[:],
                op=ALU.subtract), reads=[kval[s].r, ND[s].r], writes=[SC[s].r])

    nK = 0
    nUnit = 0
    for grp in range(NH // HG):
        P.dma("pool", lambda e, grp=grp: e.dma_start(out=wob.t[:], in_=wo_d[grp]), writes=[wob.r])
        for hh in range(HG):
            h = grp * HG + hh
            wq_, wg_ = wqb[h % 2], wgb[h % 2]
            P.dma("pool", lambda e, wq_=wq_, h=h: e.dma_start(out=wq_.t[:], in_=wq_d[h]), writes=[wq_.r])
            P.dma("pool", lambda e, wg_=wg_, h=h: e.dma_start(out=wg_.t[:], in_=wg_d[h]), writes=[wg_.r])
            q_ = qT[h % 2]
            g_ = sg[h % 2]
            for hf in range(2):
                pp = psG[hf]
                proj_fm(C, wq_, pp, hf)
                P.op("dve", lambda e, q_=q_, pp=pp, hf=hf: e.tensor_scalar(
                    out=q_.t[:, hf * 512:(hf + 1) * 512], in0=pp.t[:], scalar1=SCALE, scalar2=None, op0=ALU.mult),
                    reads=[pp.r], writes=[q_.r])
            for hf in range(2):
                pp = psG[2 + hf]
                proj_fm(C, wg_, pp, hf)
                P.op("act", lambda e, g_=g_, pp=pp, hf=hf: e.activation(
                    out=g_.t[:, hf * 512:(hf + 1) * 512], in_=pp.t[:], func=AF.Silu),
                    reads=[pp.r], writes=[g_.r])
            if not isb:
                bt = BT[h % 2]
                src = bass.AP(rbx_d.tensor, h * 767, [[1, 128], [1, 640]])
                P.dma("sp", lambda e, bt=bt, src=src: e.dma_start(out=bt.t[:], in_=src), writes=[bt.r])
                P.op("pool", lambda e, bt=bt: e.tensor_tensor(out=bt.t[:], in0=bt.t[:], in1=cmask.t[:], op=ALU.add),
                     reads=[bt.r, cmask.r], writes=[bt.r])
            for s in range(2):
                U = NU[s]
                if isb:
                    dx = dexp[(2 * h + s) % 2]
                    for i in range(4):
                        P.op("pool", lambda e, dx=dx, i=i, s=s, h=h: e.tensor_tensor(
                            out=dx.t[:, i, :], in0=ident.t[:], in1=ND[s].t[:, 3 - i, h:h + 1].to_broadcast([128, 128]),
                            op=ALU.mult), reads=[ident.r, ND[s].r], writes=[dx.r])
                    pq = psG[(2 * h + s) % 4]
                    P.op("pe", lambda e, dx=dx, pq=pq: e.matmul(pq.t[:], lhsT=C.ones_b.t[:],
                                                                rhs=dx.t[:].rearrange("p i q -> p (i q)"),
                                                                start=True, stop=True),
                         reads=[C.ones_b.r, dx.r], writes=[pq.r])
                    P.op("act", lambda e, pq=pq: e.activation(out=NDQ.t[:], in_=pq.t[:], func=AF.Copy),
                         reads=[pq.r], writes=[NDQ.r])
                    P.op("pool", lambda e: e.tensor_tensor(
                        out=NDQd.t[:].rearrange("p (i q) -> p i q", i=4), in0=NDQ.t[:].rearrange("p (i q) -> p i q", i=4),
                        in1=trim.t[:].unsqueeze(1).to_broadcast([128, 4, 128]), op=ALU.add),
                        reads=[NDQ.r, trim.r], writes=[NDQd.r])
                first = True
                for ch in range(U // 8):
                    kx = kxc[nK % 2]
                    vx = vxc[nK % 2]
                    nK += 1
                    P.dma("sp", lambda e, kx=kx, s=s, h=h, ch=ch: e.dma_start(
                        out=kx.t[:], in_=kT_d[s][h, :, ch * 1024:(ch + 1) * 1024]), writes=[kx.r])
                    P.dma("sp", lambda e, vx=vx, s=s, h=h, ch=ch: e.dma_start(
                        out=vx.t[:], in_=V_d[s][ch * 8:(ch + 1) * 8, :, h * 128:(h + 1) * 128].rearrange("u p d -> p u d")),
                        writes=[vx.r])
                    if ch == 0:
                        order = [3, 4, 0, 1, 2, 5, 6, 7] if not isb else [3, 2, 1, 0, 4, 5, 6, 7]
                    else:
                        order = list(range(8))
                    for ul in order:
                        u = ch * 8 + ul
                        if not isb:
                            ilo, ihi = max(0, u - 4), min(3, u)
                            rlo = ilo + 4 - u
                            ncol = (ihi - ilo + 1) * 128
                            adds = [(0, ncol, BT[h % 2].t[:, rlo * 128:rlo * 128 + ncol], BT[h % 2].r)]
                            sc_ap = kmask.t[:, s, u:u + 1]
                            ares = [kmask.r]
                        else:
                            if u < 4:
                                ilo, ihi = 3 - u, 3
                                ncol = (ihi - ilo + 1) * 128
                                adds = [(0, 128, NDQd.t[:, ilo * 128:(ilo + 1) * 128], NDQd.r)]
                                if ncol > 128:
                                    adds.append((128, ncol - 128, NDQ.t[:, (ilo + 1) * 128:512], NDQ.r))
                            else:
                                ilo, ihi = 0, 3
                                ncol = 512
                                adds = [(0, 512, NDQ.t[:, :], NDQ.r)]
                            sc_ap = SC[s].t[:, u, h:h + 1]
                            ares = [SC[s].r]
                        c0 = ilo * 128
                        attn_unit(C, kx.t[:, ul * 128:(ul + 1) * 128], vx.t[:, ul, :],
                                  q_.t[:, s * 512 + c0:s * 512 + c0 + ncol], ncol, sc_ap, adds, psO, psSm, c0, first,
                                  kx.r, vx.r, q_.r, ares, psS[nUnit % 2], Sb[nUnit % 2], PT[nUnit % 2])
                        first = False
                        nUnit += 1
                P.op("dve", lambda e: e.reciprocal(out=rs.t[:], in_=psSm.t[:]), reads=[psSm.r], writes=[rs.r])
                P.op("pool", lambda e, g_=g_, s=s: e.tensor_tensor(out=wgt.t[:], in0=rs.t[:],
                                                                   in1=g_.t[:, s * 512:(s + 1) * 512], op=ALU.mult),
                     reads=[rs.r, g_.r], writes=[wgt.r])
                P.op("dve", lambda e, hh=hh, s=s: e.tensor_tensor(out=og.t[:, hh, s * 512:(s + 1) * 512], in0=psO.t[:],
                                                                  in1=wgt.t[:], op=ALU.mult),
                     reads=[psO.r, wgt.r], writes=[og.r])
        k = 0
        for c in range(NCH):
            for hf in range(2):
                pp = psG[k % 4]
                k += 1
                for hh in range(HG):
                    P.op("pe", lambda e, pp=pp, hh=hh, c=c, hf=hf: e.matmul(
                        pp.t[:], lhsT=wob.t[:, hh, c * 128:(c + 1) * 128], rhs=og.t[:, hh, hf * 512:(hf + 1) * 512],
                        start=(hh == 0), stop=(hh == HG - 1)), reads=[wob.r, og.r], writes=[pp.r])
                P.op("dve", lambda e, pp=pp, c=c, hf=hf: e.tensor_tensor(
                    out=C.xT.t[:, c, hf * 512:(hf + 1) * 512], in0=pp.t[:], in1=C.xT.t[:, c, hf * 512:(hf + 1) * 512],
                    op=ALU.add), reads=[pp.r, C.xr[c]], writes=[C.xr[c]])
    for c0 in range(0, NCH, 4):
        C.outs.append(P.dma("sp", lambda e, c0=c0: e.dma_start(out=xo_d[:, c0:c0 + 4, :], in_=C.xT.t[:, c0:c0 + 4, :]),
                            reads=C.xr[c0:c0 + 4]))
    if isb:
        gf = C.sb("gfin", [128, NCH], F32)
        P.dma("sp", lambda e: e.dma_start(out=gf.t[:], in_=gfin_d), writes=[gf.r])
        pss = (psG[0], psG[1])
        for c in range(NCH):
            sq = sg[c % 2]
            P.op("act", lambda e, c=c, sq=sq: e.activation(out=sq.t[:], in_=C.xT.t[:, c, :], func=AF.Square),
                 reads=[C.xr[c]], writes=[sq.r])
            for hf in range(2):
                P.op("pe", lambda e, c=c, sq=sq, hf=hf: e.matmul(
                    pss[hf].t[:], lhsT=C.ones_f.t[:], rhs=sq.t[:, hf * 512:(hf + 1) * 512],
                    start=(c == 0), stop=(c == NCH - 1)), reads=[sq.r, C.ones_f.r], writes=[pss[hf].r])
        for hf in range(2):
            sl = slice(hf * 512, (hf + 1) * 512)
            P.op("act", lambda e, hf=hf, sl=sl: e.activation(
                out=C.rstd.t[:, sl], in_=pss[hf].t[:], func=AF.Sqrt, bias=EPS, scale=1.0 / D_MODEL),
                reads=[pss[hf].r], writes=[C.rstd.r])
        P.op("dve", lambda e: e.reciprocal(out=C.rstd.t[:], in_=C.rstd.t[:]), reads=[C.rstd.r], writes=[C.rstd.r])
        for c in range(NCH):
            yb = sg[c % 2]
            P.op("dve", lambda e, c=c, yb=yb: e.scalar_tensor_tensor(
                out=yb.t[:], in0=C.xT.t[:, c, :], scalar=gf.t[:, c:c + 1], in1=C.rstd.t[:],
                op0=ALU.mult, op1=ALU.mult), reads=[C.xr[c], gf.r, C.rstd.r], writes=[yb.r])
            C.outs.append(P.dma("sp", lambda e, c=c, yb=yb: e.dma_start(out=yo_d[:, c, :], in_=yb.t[:]), reads=[yb.r]))
    return C.finish()


def _tile_wo(Wo):
    return np.ascontiguousarray(Wo.reshape(NH // HG, HG, 128, D_MODEL).transpose(0, 2, 1, 3))


def _gather_seq(res, key):
    out = []
    for b in range(BATCH):
        if key == "kT":
            g = np.zeros((NH, 128, SEQ), dtype=res[0][key].dtype)
            for j in range(4 * b, 4 * b + 4):
                for s, sgm in enumerate(core_segments(j)):
                    g[:, :, sgm * SEG:(sgm + 1) * SEG] = res[j][key][:, :, s * SEG:(s + 1) * SEG]
        else:
            w = res[0][key].shape[2]
            g = np.zeros((SEQ // 128, 128, w), dtype=res[0][key].dtype)
            for j in range(4 * b, 4 * b + 4):
                for s, sgm in enumerate(core_segments(j)):
                    g[sgm * 4:(sgm + 1) * 4] = res[j][key][s * 4:(s + 1) * 4]
        out.append(g)
    return out


def run_a(xT_cores, gn, W_in, rel_bias, W_out, kT_g, V_g):
    nc = get_nc("a")
    wq = tile_w_cols(W_in, 0, D_INNER, 128)
    wg = tile_w_cols(W_in, 3 * D_INNER, D_INNER, 128)
    wo = _tile_wo(W_out)
    g = tile_vec(gn)
    idx = np.clip(np.arange(767) - 127, -256, 256) + 256
    rbx = np.ascontiguousarray(rel_bias[:, idx])
    cmask = np.zeros((128, 5, 128), np.float32)
    cmask[:64, 0, :64] = NEG
    cmask[64:, 4, 64:] = NEG
    cmask = cmask.reshape(128, 640)
    in_maps = []
    for j in range(8):
        b = j // 4
        m = {"xT": xT_cores[j], "gn": g, "wq": wq, "wg": wg, "wo": wo, "rbx": rbx, "cmask": cmask}
        kmask = np.zeros((128, 2, 8), np.float32)
        for s, sgm in enumerate(core_segments(j)):
            kT = np.zeros((NH, 128, 2 * SEG), dtype=kT_g[b].dtype)
            V = np.zeros((8, 128, D_INNER), dtype=V_g[b].dtype)
            kT[:, :, SEG:] = kT_g[b][:, :, sgm * SEG:(sgm + 1) * SEG]
            V[4:] = V_g[b][sgm * 4:(sgm + 1) * 4]
            if sgm > 0:
                kT[:, :, :SEG] = kT_g[b][:, :, (sgm - 1) * SEG:sgm * SEG]
                V[:4] = V_g[b][(sgm - 1) * 4:sgm * 4]
            else:
                kmask[:, s, :4] = NEG
            m["kTs%d" % s] = np.ascontiguousarray(kT.reshape(NH, 128, 8, 128)[:, :, :, ::-1]).reshape(NH, 128, 1024)
            m["Vs%d" % s] = np.ascontiguousarray(V[:, ::-1, :])
        m["kmask"] = kmask
        in_maps.append(m)
    res = run_bass_kernel_spmd(nc, in_maps, core_ids=list(range(8)))
    return [r["xo"] for r in res.results]


def run_b(xT_cores, gn, W_in, W_out, kT_g, V_g, nlf_g, gfin):
    nc = get_nc("b")
    wq = tile_w_cols(W_in, 0, D_INNER, 128)
    wg = tile_w_cols(W_in, D_INNER, D_INNER, 128)
    wo = _tile_wo(W_out)
    g = tile_vec(gn)
    gf = tile_vec(gfin)
    ident = np.eye(128, dtype=np.float32)
    kl = np.arange(128)
    tri = (kl[:, None] > kl[None, :]).astype(np.float32)
    trim = np.where(kl[:, None] > kl[None, :], NEG, 0.0).astype(np.float32)
    in_maps = []
    for j in range(8):
        b = j // 4
        m = {"xT": xT_cores[j], "gn": g, "wq": wq, "wg": wg, "wo": wo, "gfin": gf, "ident": ident, "tri": tri,
             "trim": trim}
        for s, sgm in enumerate(core_segments(j)):
            U = UMAX[s]
            tiles = [4 * sgm + 3 - u for u in range(4 * sgm + 4)]
            kT = np.zeros((NH, 128, U * 128), dtype=kT_g[b].dtype)
            V = np.zeros((U, 128, D_INNER), dtype=V_g[b].dtype)
            nlf = np.zeros((128, U, NH), np.float32)
            kval = np.full((128, U), NEG, np.float32)
            for u, tix in enumerate(tiles):
                kT[:, :, u * 128:(u + 1) * 128] = kT_g[b][:, :, tix * 128:(tix + 1) * 128]
                V[u] = V_g[b][tix]
                nlf[:, u, :] = nlf_g[b][tix]
                kval[:, u] = 0.0
            m["kTs%d" % s] = kT
            m["Vs%d" % s] = V
            m["nlfs%d" % s] = nlf
            m["kval%d" % s] = kval
        in_maps.append(m)
    res = run_bass_kernel_spmd(nc, in_maps, core_ids=list(range(8)))
    return [r["xo"] for r in res.results], [r["yo"] for r in res.results]


def kernel_unfused(x, a_norm, a_w_in, a_rel_bias, a_w_out, kv_norm, kv_w, f_w, f_b, b_norm, b_w_in, b_w_out, final_norm):
    x = np.asarray(x, np.float32)
    xT = [to_xT(x[j // 4][core_tokens(j)]) for j in range(8)]
    f_w = np.asarray(f_w, np.float32)
    f_b = np.asarray(f_b, np.float32)
    for l in range(2):
        W = np.asarray(a_w_in[l], np.float32)
        r = run_kv(xT, np.asarray(a_norm[l], np.float32), W, D_INNER, 2 * D_INNER, f_w, f_b)
        kT_g = _gather_seq(r, "kT")
        V_g = _gather_seq(r, "V")
        xT = run_a(xT, np.asarray(a_norm[l], np.float32), W, np.asarray(a_rel_bias[l], np.float32),
                   np.asarray(a_w_out[l], np.float32), kT_g, V_g)
    r = run_kv(xT, np.asarray(kv_norm, np.float32), np.asarray(kv_w, np.float32), 0, D_INNER, f_w, f_b)
    kT_g = _gather_seq(r, "kT")
    V_g = _gather_seq(r, "V")
    nlf_g = _gather_seq(r, "nlf")
    yT = None
    for l in range(2):
        xT, yT = run_b(xT, np.asarray(b_norm[l], np.float32), np.asarray(b_w_in[l], np.float32),
                       np.asarray(b_w_out[l], np.float32), kT_g, V_g, nlf_g, np.asarray(final_norm, np.float32))
    out = np.zeros((BATCH, SEQ, D_MODEL), np.float32)
    for j in range(8):
        out[j // 4][core_tokens(j)] = from_xT(yT[j])
    return out


NSEGP = 11
KSEGE = 1024 * 512
VSEGE = 512 * 512
NSEGE = 512 * NH
GROUPS = [[0, 1, 2, 3], [4, 5, 6, 7]]


def build_fused():
    C = Ctx()
    P = C.P
    nc = C.nc
    xT_d = C.dram("xT", [128, NCH, T], F32, "ExternalInput")
    gns_d = C.dram("gns", [6, 128, NCH], F32, "ExternalInput")
    a_wq_d = C.dram("a_wq", [2, NH, 128, NCH, 128], F32, "ExternalInput")
    a_wg_d = C.dram("a_wg", [2, NH, 128, NCH, 128], F32, "ExternalInput")
    a_wk_d = C.dram("a_wk", [2, NH, 128, NCH, 128], F32, "ExternalInput")
    a_wv_d = C.dram("a_wv", [2, 8, 128, NCH, 512], F32, "ExternalInput")
    a_wo_d = C.dram("a_wo", [2, NH // HG, 128, HG, D_MODEL], F32, "ExternalInput")
    s_wk_d = C.dram("s_wk", [NH, 128, NCH, 128], F32, "ExternalInput")
    s_wv_d = C.dram("s_wv", [8, 128, NCH, 512], F32, "ExternalInput")
    b_wq_d = C.dram("b_wq", [2, NH, 128, NCH, 128], F32, "ExternalInput")
    b_wg_d = C.dram("b_wg", [2, NH, 128, NCH, 128], F32, "ExternalInput")
    b_wo_d = C.dram("b_wo", [2, NH // HG, 128, HG, D_MODEL], F32, "ExternalInput")
    fw_d = C.dram("fw", [128, NCH, NH], F32, "ExternalInput")
    fb_d = C.dram("fb", [128, NH], F32, "ExternalInput")
    rbx_d = C.dram("rbx", [2, NH, 767], F32, "ExternalInput")
    cmask_d = C.dram("cmask", [128, 640], F32, "ExternalInput")
    kmask_d = C.dram("kmask", [128, 2, 8], F32, "ExternalInput")
    kval_d = [C.dram("kval%d" % s, [128, UMAX[s]], F32, "ExternalInput") for s in range(2)]
    cst_d = C.dram("cst", [4, 128, 128], F32, "ExternalInput")
    yo_d = C.dram("yo", [128, NCH, T], F32, "ExternalOutput")
    kTo = [nc.dram_tensor("kTo%d" % i, [4 * 2 * 1024, 512], BF16) for i in range(1)]
    Vo = [nc.dram_tensor("Vo%d" % i, [8 * 2 * 512, 512], BF16) for i in range(1)]
    nlfo = nc.dram_tensor("nlfo", [T, NH], F32)
    KL = [nc.dram_tensor("KL%d" % i, [4 * NSEGP * 1024, 512], BF16) for i in range(2)]
    VL = [nc.dram_tensor("VL%d" % i, [8 * NSEGP * 512, 512], BF16) for i in range(2)]
    NL = nc.dram_tensor("NL", [NSEGP * 512, NH], F32)
    kTo_r = [[Res() for _ in range(NH)] for _ in range(2)]
    Vo_r = [[Res() for _ in range(8)] for _ in range(2)]
    nlfo_r = Res()
    KL_r = [[[Res() for _ in range(2)] for _ in range(4)] for _ in range(2)]
    VL_r = [[[Res() for _ in range(2)] for _ in range(8)] for _ in range(2)]
    NL_r = [Res() for _ in range(2)]
    pad_r = Res()
    Kloc = nc.dram_tensor("Kloc", [4 * 8 * 1024, 512], BF16)
    Vloc = nc.dram_tensor("Vloc", [8 * 8 * 512, 512], BF16)
    Nloc = nc.dram_tensor("Nloc", [8 * 512, NH], F32)
    Kloc_r = [Res() for _ in range(4)]
    Vloc_r = [Res() for _ in range(4)]
    Nloc_r = Res()

    ps = [C.ps("ps%d" % i) for i in range(8)]
    psS = ps[0:4]
    psO = ps[4]
    psSm = ps[5]
    psG = [ps[6], ps[7], ps[0], ps[1]]
    psG2 = [ps[6], ps[7]]
    phase_consts(C)
    phase_load_x(C, xT_d)

    sg = [C.sb("sg%d" % i, [128, T], F32) for i in range(2)]
    C.xsq = sg
    C.hT = C.sb("hT", [128, NCH, T], BF16)
    C.hr = [Res("h%d" % c) for c in range(NCH)]
    C.rstd = C.sb("rstd", [128, T], F32)
    WB0 = C.sb("WB0", [128, 8192], BF16)
    WB1 = C.sb("WB1", [128, 8192], BF16)
    wq_r = [Res(), Res()]
    wg_r = [Res(), Res()]
    WB1_rs = wq_r + wg_r

    def wvb_ap(i):
        return (WB0 if i == 0 else WB1).t[:].rearrange("p (c n) -> p c n", c=NCH)

    def wvb_res(i):
        return [WB0.r] if i == 0 else WB1_rs

    wob_ap = WB0.t[:].rearrange("p (h n) -> p h n", h=HG)

    def wqb_ap(i):
        return WB1.t[:, i * 2048:(i + 1) * 2048].rearrange("p (c n) -> p c n", c=NCH)

    def wgb_ap(i):
        return WB1.t[:, 4096 + i * 2048:4096 + (i + 1) * 2048].rearrange("p (c n) -> p c n", c=NCH)

    kvbuf = [C.sb("kvbuf%d" % i, [128, 2048], BF16) for i in range(3)]
    qT = [C.sb("qT%d" % i, [128, T], BF16) for i in range(2)]
    PT = [C.sb("PT%d" % i, [128, 512], BF16) for i in range(4)]
    vo = PT
    Sb = [C.sb("Sb%d" % i, [128, 512], F32) for i in range(4)]
    og = C.sb("og", [128, HG, T], BF16)
    rs = C.sb("rs", [128, 512], F32)
    wgt = rs
    racc = [C.sb("racc%d" % i, [128, 512], F32) for i in range(2)]
    BT = [C.sb("BT%d" % i, [128, 640], F32) for i in range(2)]
    cmask = C.sb("cmask", [128, 640], F32)
    kmask = C.sb("kmask", [128, 2, 8], F32)
    cst = C.sb("cst", [128, 4, 128], F32)
    ND = [C.sb("ND%d" % s, [128, UMAX[s], NH], F32) for s in range(2)]
    SC = [C.sb("SC%d" % s, [128, UMAX[s], NH], F32) for s in range(2)]
    kval = [C.sb("kval%d" % s, [128, UMAX[s]], F32) for s in range(2)]
    NDQ = BT[0]
    NDQd = BT[1]
    dexp = [C.sb("dexp%d" % i, [128, 4, 128], BF16) for i in range(2)]
    fb = C.sb("fb", [128, NH], F32)
    ones_col = C.sb("ones_col", [128, 1], F32)
    xsqc = racc
    zs = [C.sb("zs%d" % i, [128, NH], F32) for i in range(2)]
    rt = [C.sb("rt%d" % i, [128, 1], F32) for i in range(2)]
    zero = kvbuf[0]

    ident_ap = cst.t[:, 0, :]
    tri_ap = cst.t[:, 1, :]
    trim_ap = cst.t[:, 2, :]
    J_ap = cst.t[:, 3, :]
    for (t_, d_) in ((cmask, cmask_d), (kmask, kmask_d), (fb, fb_d), (kval[0], kval_d[0]), (kval[1], kval_d[1])):
        P.dma("sp", lambda e, t_=t_, d_=d_: e.dma_start(out=t_.t[:], in_=d_), writes=[t_.r])
    P.dma("sp", lambda e: e.dma_start(out=cst.t[:], in_=cst_d.rearrange("k p n -> p k n")), writes=[cst.r])
    P.op("pool", lambda e: e.memset(ones_col.t[:], 1.0), writes=[ones_col.r])
    P.op("pool", lambda e: e.memset(zero.t[:], 0.0), writes=[zero.r])
    for st in range(1):
        segs = (0, 1, 2) if st == 0 else (2,)
        for i in range(4):
            for sgp in segs:
                for half in range(2):
                    r0 = (i * NSEGP + sgp) * 1024 + half * 512
                    P.dma("sp", lambda e, st=st, r0=r0: e.dma_start(
                        out=KL[st][r0:r0 + 512, :].rearrange("(p a) n -> p (a n)", p=128), in_=zero.t[:]),
                        reads=[zero.r], writes=[pad_r])
        for b in range(8):
            for sgp in segs:
                r0 = (b * NSEGP + sgp) * 512
                P.dma("sp", lambda e, st=st, r0=r0: e.dma_start(
                    out=VL[st][r0:r0 + 512, :].rearrange("(p a) n -> p (a n)", p=128), in_=zero.t[:]),
                    reads=[zero.r], writes=[pad_r])
    P.dma("sp", lambda e: e.dma_start(out=NL[0:3 * 512, :].rearrange("(p a) n -> p (a n)", p=128),
                                      in_=zero.t[:, 0:384].bitcast(F32) if False else zero.t[:, 0:768].bitcast(F32)),
          reads=[zero.r], writes=[pad_r])

    dyn = {}

    def pre_sp(e):
        jj = e.snap(e.partition_id() % 4, min_val=0, max_val=3)
        dyn["k"] = e.snap(jj * KSEGE, min_val=0, max_val=3 * KSEGE)

    def pre_pool(e):
        jj = e.snap(e.partition_id() % 4, min_val=0, max_val=3)
        dyn["v"] = e.snap(jj * VSEGE, min_val=0, max_val=3 * VSEGE)
        dyn["n"] = e.snap(jj * NSEGE, min_val=0, max_val=3 * NSEGE)

    P.pre = {"sp": pre_sp, "pool": pre_pool}

    def localize(gates):
        for i in range(4):
            P.dma("sp", lambda e, i=i: e.dma_start(
                out=bass.AP(Kloc, i * 8 * KSEGE, [[32768, 128], [1, 32768]]),
                in_=bass.AP(KL[0], dyn["k"] + i * NSEGP * KSEGE, [[32768, 128], [1, 32768]])),
                reads=[KL_r[0][i][0], KL_r[0][i][1], pad_r], writes=[Kloc_r[i]])
        for p_ in range(4):
            P.dma("pool", lambda e, p_=p_: e.dma_start(
                out=bass.AP(Vloc, 2 * p_ * 8 * VSEGE, [[8 * VSEGE, 2], [16384, 128], [1, 16384]]),
                in_=bass.AP(VL[0], dyn["v"] + 2 * p_ * NSEGP * VSEGE, [[NSEGP * VSEGE, 2], [16384, 128], [1, 16384]])),
                reads=[VL_r[0][2 * p_][0], VL_r[0][2 * p_][1], VL_r[0][2 * p_ + 1][0], VL_r[0][2 * p_ + 1][1], pad_r],
                writes=[Vloc_r[p_]])
        if gates:
            P.dma("pool", lambda e: e.dma_start(
                out=bass.AP(Nloc, 0, [[1024, 128], [1, 1024]]),
                in_=bass.AP(NL, dyn["n"], [[1024, 128], [1, 1024]])),
                reads=[NL_r[0], NL_r[1], pad_r], writes=[Nloc_r])

    def kv_phase(st, gidx, wk_ap, wv_ap, gates):
        gn = phase_norm(C, gns_d[gidx], "g%d" % gidx, psG[0], psG[1])
        NT = T // 128
        if gates:
            fw_ap = Sb[0].t[:].rearrange("p (c h) -> p c h", c=NCH)
            gfw_ap = Sb[1].t[:].rearrange("p (c h) -> p c h", c=NCH)
            P.dma("sp", lambda e: e.dma_start(out=fw_ap, in_=fw_d), writes=[Sb[0].r])
            for c in range(NCH):
                P.op("pool", lambda e, c=c: e.tensor_scalar(out=gfw_ap[:, c, :], in0=fw_ap[:, c, :],
                                                            scalar1=gn.t[:, c:c + 1], scalar2=None, op0=ALU.mult),
                     reads=[Sb[0].r, gn.r], writes=[Sb[1].r])
            k = 0
            for tt in range(NT):
                pz = psS[tt % 2]
                pq = (psO, psSm)[tt % 2]
                tsl = slice(tt * 128, (tt + 1) * 128)
                for c in range(NCH):
                    sq = xsqc[k % 2]
                    k += 1
                    P.op("act", lambda e, c=c, sq=sq, tsl=tsl: e.activation(out=sq.t[:, 0:128], in_=C.xT.t[:, c, tsl],
                                                                            func=AF.Square),
                         reads=[C.xr[c]], writes=[sq.r])
                    P.op("pe", lambda e, c=c, sq=sq, pq=pq: e.matmul(pq.t[:, 0:1], lhsT=sq.t[:, 0:128], rhs=ones_col.t[:],
                                                                     start=(c == 0), stop=(c == NCH - 1)),
                         reads=[sq.r, ones_col.r], writes=[pq.r])
                    P.op("pe", lambda e, c=c, tsl=tsl, pz=pz: e.matmul(pz.t[:, 0:NH], lhsT=C.xT.t[:, c, tsl],
                                                                       rhs=gfw_ap[:, c, :],
                                                                       start=(c == 0), stop=(c == NCH - 1)),
                         reads=[C.xr[c], Sb[1].r], writes=[pz.r])
                r1 = rt[tt % 2]
                z = zs[tt % 2]
                P.op("act", lambda e, r1=r1, pq=pq: e.activation(out=r1.t[:], in_=pq.t[:, 0:1], func=AF.Sqrt, bias=EPS,
                                                                 scale=1.0 / D_MODEL), reads=[pq.r], writes=[r1.r])
                P.op("dve", lambda e, r1=r1: e.reciprocal(out=r1.t[:], in_=r1.t[:]), reads=[r1.r], writes=[r1.r])
                P.op("dve", lambda e, r1=r1, z=z, pz=pz: e.scalar_tensor_tensor(
                    out=z.t[:], in0=pz.t[:, 0:NH], scalar=r1.t[:, 0:1], in1=fb.t[:], op0=ALU.mult, op1=ALU.add),
                    reads=[pz.r, r1.r, fb.r], writes=[z.r])
                P.op("act", lambda e, z=z: e.activation(out=z.t[:], in_=z.t[:], func=AF.Exp, scale=-1.0),
                     reads=[z.r], writes=[z.r])
                P.op("act", lambda e, z=z: e.activation(out=z.t[:], in_=z.t[:], func=AF.Ln, bias=1.0, scale=1.0),
                     reads=[z.r], writes=[z.r])
                P.dma("sp", lambda e, z=z, tt=tt: e.dma_start(out=nlfo[tt * 128:(tt + 1) * 128, :], in_=z.t[:]),
                      reads=[z.r], writes=[nlfo_r])
            for sl in range(2):
                P.coll(("n", sl), lambda e, sl=sl: e.collective_compute(
                    "AllGather", ALU.bypass, replica_groups=GROUPS, ins=[nlfo[sl * 512:(sl + 1) * 512, :]],
                    outs=[NL[(3 + sl * 4) * 512:(3 + sl * 4 + 4) * 512, :]]), reads=[nlfo_r], writes=[NL_r[sl]])
        def load_wk(h):
            kb_ = kvbuf[h % 2]
            P.dma("pool", lambda e, kb_=kb_, h=h: e.dma_start(out=kb_.t[:].rearrange("p (c n) -> p c n", c=NCH),
                                                             in_=wk_ap(h)), writes=[kb_.r])
        load_wk(0)
        for h in range(NH):
            kb = kvbuf[h % 2]
            w_ap = kb.t[:].rearrange("p (c n) -> p c n", c=NCH)
            o = qT[h % 2]
            if h + 1 < NH:
                load_wk(h + 1)
            for hf in range(2):
                pp = psG[(2 * h + hf) % 4]
                for c in range(NCH):
                    P.op("pe", lambda e, c=c, pp=pp, w_ap=w_ap, hf=hf: e.matmul(
                        pp.t[:], lhsT=w_ap[:, c, :], rhs=C.hT.t[:, c, hf * 512:(hf + 1) * 512],
                        start=(c == 0), stop=(c == NCH - 1)), reads=[kb.r, C.hr[c]], writes=[pp.r])
                if hf == 0:
                    P.op("act", lambda e, o=o, pp=pp: e.activation(out=o.t[:, 0:512], in_=pp.t[:], func=AF.Copy),
                         reads=[pp.r], writes=[o.r])
                else:
                    P.op("dve", lambda e, o=o, pp=pp: e.tensor_copy(out=o.t[:, 512:1024], in_=pp.t[:]),
                         reads=[pp.r], writes=[o.r])
            for sl in range(2):
                r0 = ((h // 8) * 2 + sl) * 1024 + (h % 8) * 128
                P.dma("sp", lambda e, o=o, r0=r0, sl=sl: e.dma_start(out=kTo[st][r0:r0 + 128, :],
                                                                     in_=o.t[:, sl * 512:(sl + 1) * 512]),
                      reads=[o.r], writes=[kTo_r[st][h]])
            if h % 8 == 7:
                i = h // 8
                for sl in range(2):
                    P.coll(("k", i, sl), lambda e, i=i, sl=sl: e.collective_compute(
                        "AllGather", ALU.bypass, replica_groups=GROUPS,
                        ins=[kTo[st][(i * 2 + sl) * 1024:(i * 2 + sl + 1) * 1024, :]],
                        outs=[KL[st][(i * NSEGP + 3 + sl * 4) * 1024:(i * NSEGP + 3 + sl * 4 + 4) * 1024, :]]),
                        reads=kTo_r[st][i * 8:(i + 1) * 8], writes=[KL_r[st][i][sl]])
        k = 0
        def load_wv(b):
            P.dma("pool", lambda e, b=b: e.dma_start(out=wvb_ap(b % 2), in_=wv_ap(b)), writes=wvb_res(b % 2))
        load_wv(0)
        for b in range(8):
            w_ap = wvb_ap(b % 2)
            w_rs = wvb_res(b % 2)
            if b + 1 < 8:
                load_wv(b + 1)
            for tt in range(NT):
                pp = psG[k % 4]
                o = vo[k % 4]
                for c in range(NCH):
                    P.op("pe", lambda e, c=c, tt=tt, w_ap=w_ap, pp=pp: e.matmul(
                        pp.t[:], lhsT=C.hT.t[:, c, tt * 128:(tt + 1) * 128], rhs=w_ap[:, c, :],
                        start=(c == 0), stop=(c == NCH - 1)), reads=w_rs + [C.hr[c]], writes=[pp.r])
                if k % 2 == 0:
                    P.op("act", lambda e, o=o, pp=pp: e.activation(out=o.t[:], in_=pp.t[:], func=AF.Copy),
                         reads=[pp.r], writes=[o.r])
                else:
                    P.op("dve", lambda e, o=o, pp=pp: e.tensor_copy(out=o.t[:], in_=pp.t[:]), reads=[pp.r], writes=[o.r])
                r0 = (b * 2 + tt // 4) * 512 + (tt % 4) * 128
                P.dma("sp", lambda e, o=o, r0=r0: e.dma_start(out=Vo[st][r0:r0 + 128, :], in_=o.t[:]),
                      reads=[o.r], writes=[Vo_r[st][b]])
                k += 1
            for sl in range(2):
                P.coll(("v", b, sl), lambda e, b=b, sl=sl: e.collective_compute(
                    "AllGather", ALU.bypass, replica_groups=GROUPS,
                    ins=[Vo[st][(b * 2 + sl) * 512:(b * 2 + sl + 1) * 512, :]],
                    outs=[VL[st][(b * NSEGP + 3 + sl * 4) * 512:(b * NSEGP + 3 + sl * 4 + 4) * 512, :]]),
                    reads=[Vo_r[st][b]], writes=[VL_r[st][b][sl]])

    def decay_prep():
        nlf_t = sg[0].t[:].rearrange("p (u h) -> p u h", u=32)
        tot_t = sg[1].t[:].rearrange("p (u h) -> p u h", u=32)
        for s in range(2):
            U = UMAX[s]
            for a in range(U // 4):
                off = (3 + 4 * s - a) * NSEGE
                P.dma("sp", lambda e, a=a, off=off: e.dma_start(
                    out=nlf_t[:, 4 * a:4 * a + 4, :],
                    in_=bass.AP(Nloc, off, [[NH, 128], [128 * NH, 4], [1, NH]])),
                    reads=[Nloc_r], writes=[sg[0].r])
            nflat = sg[0].t[:, 0:U * NH]
            ndflat = ND[s].t[:].rearrange("p u h -> p (u h)")
            totflat = sg[1].t[:, 0:U * NH]
            for j in range(U * NH // 512):
                sl_ = slice(j * 512, (j + 1) * 512)
                pa = psG[(2 * j) % 4]
                pb = psG[(2 * j + 1) % 4]
                P.op("pe", lambda e, pa=pa, sl_=sl_, nflat=nflat: e.matmul(pa.t[:], lhsT=tri_ap, rhs=nflat[:, sl_],
                                                                           start=True, stop=True),
                     reads=[cst.r, sg[0].r], writes=[pa.r])
                P.op("pe", lambda e, pb=pb, sl_=sl_, nflat=nflat: e.matmul(pb.t[:], lhsT=C.ones_f.t[:], rhs=nflat[:, sl_],
                                                                           start=True, stop=True),
                     reads=[C.ones_f.r, sg[0].r], writes=[pb.r])
                P.op("act", lambda e, pa=pa, sl_=sl_, ndflat=ndflat: e.activation(out=ndflat[:, sl_], in_=pa.t[:],
                                                                                  func=AF.Copy),
                     reads=[pa.r], writes=[ND[s].r])
                P.op("dve", lambda e, pb=pb, sl_=sl_, totflat=totflat: e.tensor_copy(out=totflat[:, sl_], in_=pb.t[:]),
                     reads=[pb.r], writes=[sg[1].r])

            def uof(v):
                return 4 * (v // 4) + 3 - (v % 4)
            for v in range(1, U):
                u1, u0 = uof(v), uof(v - 1)
                P.op("dve", lambda e, s=s, u1=u1, u0=u0: e.tensor_tensor(out=ND[s].t[:, u1, :], in0=ND[s].t[:, u1, :],
                                                                         in1=tot_t[:, u0, :], op=ALU.add),
                     reads=[sg[1].r, ND[s].r], writes=[ND[s].r])
                if v < U - 1:
                    P.op("dve", lambda e, u1=u1, u0=u0: e.tensor_tensor(out=tot_t[:, u1, :], in0=tot_t[:, u1, :],
                                                                        in1=tot_t[:, u0, :], op=ALU.add),
                         reads=[sg[1].r], writes=[sg[1].r])
            P.op("dve", lambda e, s=s, U=U: e.tensor_tensor(
                out=SC[s].t[:], in0=kval[s].t[:].unsqueeze(2).to_broadcast([128, U, NH]), in1=ND[s].t[:],
                op=ALU.subtract), reads=[kval[s].r, ND[s].r], writes=[SC[s].r])

    cnt = {"K": 0, "U": 0, "A": 0}

    def mix_phase(isb, st, gidx, wq_ap, wg_ap, wo_ap, rb_layer):
        NU = UMAX if isb else (8, 8)
        if isb:
            phase_norm(C, gns_d[gidx], "g%d" % gidx, psG[0], psG[1])
        if isb and gidx == 3:
            decay_prep()
        for grp in range(NH // HG):
            P.dma("pool", lambda e, grp=grp: e.dma_start(out=wob_ap, in_=wo_ap(grp)), writes=[WB0.r])
            for hh in range(HG):
                h = grp * HG + hh
                i8 = h // 8
                b4 = h // 4
                wq_, wg_ = wqb_ap(h % 2), wgb_ap(h % 2)

                def load_qg(h2):
                    P.dma("pool", lambda e, h2=h2: e.dma_start(out=wqb_ap(h2 % 2), in_=wq_ap(h2)), writes=[wq_r[h2 % 2]])
                    P.dma("pool", lambda e, h2=h2: e.dma_start(out=wgb_ap(h2 % 2), in_=wg_ap(h2)), writes=[wg_r[h2 % 2]])
                if h == 0:
                    load_qg(0)
                if h + 1 < NH:
                    load_qg(h + 1)
                q_ = qT[h % 2]
                g_ = sg[h % 2]
                for hf in range(2):
                    pp = psG2[hf]
                    for c in range(NCH):
                        P.op("pe", lambda e, c=c, pp=pp, wq_=wq_, hf=hf: e.matmul(
                            pp.t[:], lhsT=wq_[:, c, :], rhs=C.hT.t[:, c, hf * 512:(hf + 1) * 512],
                            start=(c == 0), stop=(c == NCH - 1)), reads=[wq_r[h % 2], C.hr[c]], writes=[pp.r])
                    P.op("dve", lambda e, q_=q_, pp=pp, hf=hf: e.tensor_scalar(
                        out=q_.t[:, hf * 512:(hf + 1) * 512], in0=pp.t[:], scalar1=SCALE, scalar2=None, op0=ALU.mult),
                        reads=[pp.r], writes=[q_.r])
                for hf in range(2):
                    pp = psG2[hf]
                    for c in range(NCH):
                        P.op("pe", lambda e, c=c, pp=pp, wg_=wg_, hf=hf: e.matmul(
                            pp.t[:], lhsT=wg_[:, c, :], rhs=C.hT.t[:, c, hf * 512:(hf + 1) * 512],
                            start=(c == 0), stop=(c == NCH - 1)), reads=[wg_r[h % 2], C.hr[c]], writes=[pp.r])
                    P.op("act", lambda e, g_=g_, pp=pp, hf=hf: e.activation(
                        out=g_.t[:, hf * 512:(hf + 1) * 512], in_=pp.t[:], func=AF.Silu),
                        reads=[pp.r], writes=[g_.r])
                if not isb:
                    bt = BT[h % 2]
                    src = bass.AP(rbx_d.tensor, (rb_layer * NH + h) * 767, [[1, 128], [1, 640]])
                    P.dma("sp", lambda e, bt=bt, src=src: e.dma_start(out=bt.t[:], in_=src), writes=[bt.r])
                    pj = (psG2[0], psG2[1])
                    P.op("pe", lambda e, bt=bt: e.matmul(pj[0].t[:], lhsT=J_ap, rhs=bt.t[:, 0:512], start=True, stop=True),
                         reads=[cst.r, bt.r], writes=[pj[0].r])
                    P.op("pe", lambda e, bt=bt: e.matmul(pj[1].t[:, 0:128], lhsT=J_ap, rhs=bt.t[:, 512:640], start=True,
                                                         stop=True), reads=[cst.r, bt.r], writes=[pj[1].r])
                    P.op("dve", lambda e, bt=bt: e.tensor_tensor(out=bt.t[:, 0:512], in0=pj[0].t[:], in1=cmask.t[:, 0:512],
                                                                 op=ALU.add), reads=[pj[0].r, cmask.r], writes=[bt.r])
                    P.op("dve", lambda e, bt=bt: e.tensor_tensor(out=bt.t[:, 512:640], in0=pj[1].t[:, 0:128],
                                                                 in1=cmask.t[:, 512:640], op=ALU.add),
                         reads=[pj[1].r, cmask.r], writes=[bt.r])
                for s in range(2):
                    U = NU[s]
                    if isb:
                        dx = dexp[(2 * h + s) % 2]
                        for i in range(4):
                            P.op("pool", lambda e, dx=dx, i=i, s=s, h=h: e.tensor_tensor(
                                out=dx.t[:, i, :], in0=ident_ap, in1=ND[s].t[:, i, h:h + 1].to_broadcast([128, 128]),
                                op=ALU.mult), reads=[cst.r, ND[s].r], writes=[dx.r])
                        pq = psG2[(2 * h + s) % 2]
                        P.op("pe", lambda e, dx=dx, pq=pq: e.matmul(pq.t[:], lhsT=C.ones_b.t[:],
                                                                    rhs=dx.t[:].rearrange("p i q -> p (i q)"),
                                                                    start=True, stop=True),
                             reads=[C.ones_b.r, dx.r], writes=[pq.r])
                        P.op("act", lambda e, pq=pq: e.activation(out=NDQ.t[:, 0:512], in_=pq.t[:], func=AF.Copy),
                             reads=[pq.r], writes=[NDQ.r])
                        P.op("pool", lambda e: e.tensor_tensor(
                            out=NDQd.t[:, 0:512].rearrange("p (i q) -> p i q", i=4),
                            in0=NDQ.t[:, 0:512].rearrange("p (i q) -> p i q", i=4),
                            in1=trim_ap.unsqueeze(1).to_broadcast([128, 4, 128]), op=ALU.add),
                            reads=[NDQ.r, cst.r], writes=[NDQd.r])
                    first = True
                    units = []
                    chunk_loads = []
                    for ch in range(U // 8):
                        kb = kvbuf[cnt["K"] % 3]
                        cnt["K"] += 1
                        kx_ap = kb.t[:, 0:1024]
                        vx_ap = kb.t[:, 1024:2048].rearrange("p (u d) -> p u d", u=8)
                        kdeps = [Kloc_r[i8]]
                        vdeps = [Vloc_r[b4 // 2]]
                        def load_chunk(kb=kb, kx_ap=kx_ap, vx_ap=vx_ap, ch=ch, kdeps=kdeps, vdeps=vdeps, s=s, h=h,
                                       i8=i8, b4=b4):
                            for s2 in range(2):
                                if isb:
                                    sgp = 3 + 4 * s - (2 * ch + s2)
                                else:
                                    sgp = 3 + 4 * s - 1 + s2
                                koff = ((i8 * 8 + sgp) * 1024 + (h % 8) * 128) * 512
                                voff = ((b4 * 8 + sgp) * 512) * 512 + (h % 4) * 128
                                P.dma("sp", lambda e, s2=s2, koff=koff: e.dma_start(
                                    out=kx_ap[:, s2 * 512:(s2 + 1) * 512],
                                    in_=bass.AP(Kloc, koff, [[512, 128], [1, 512]])),
                                    reads=kdeps, writes=[kb.r])
                                P.dma("sp", lambda e, s2=s2, voff=voff: e.dma_start(
                                    out=vx_ap[:, s2 * 4:(s2 + 1) * 4, :],
                                    in_=bass.AP(Vloc, voff, [[512, 128], [128 * 512, 4], [1, 128]])),
                                    reads=vdeps, writes=[kb.r])
                        chunk_loads.append(load_chunk)
                        if ch == 0:
                            order = [3, 4, 0, 1, 2, 5, 6, 7] if not isb else list(range(8))
                        else:
                            order = list(range(8))
                        for ul in order:
                            u = ch * 8 + ul
                            if not isb:
                                ilo, ihi = max(0, u - 4), min(3, u)
                                rlo = ilo + 4 - u
                                ncol = (ihi - ilo + 1) * 128
                                adds = [(0, ncol, BT[h % 2].t[:, rlo * 128:rlo * 128 + ncol], BT[h % 2].r)]
                                sc_ap = kmask.t[:, s, u:u + 1]
                                ares = [kmask.r]
                            else:
                                if u < 4:
                                    ilo, ihi = u, 3
                                    ncol = (ihi - ilo + 1) * 128
                                    adds = [(0, 128, NDQd.t[:, ilo * 128:(ilo + 1) * 128], NDQd.r)]
                                    if ncol > 128:
                                        adds.append((128, ncol - 128, NDQ.t[:, (ilo + 1) * 128:512], NDQ.r))
                                else:
                                    ilo, ihi = 0, 3
                                    ncol = 512
                                    adds = [(0, 512, NDQ.t[:, 0:512], NDQ.r)]
                                sc_ap = SC[s].t[:, u, h:h + 1]
                                ares = [SC[s].r]
                            c0 = ilo * 128
                            n_ = cnt["U"]
                            cnt["U"] += 1
                            units.append((kx_ap[:, ul * 128:(ul + 1) * 128], vx_ap[:, ul, :],
                                          q_.t[:, s * 512 + c0:s * 512 + c0 + ncol], ncol, sc_ap, adds, c0, first,
                                          kb.r, q_.r, ares, psS[n_ % 4], Sb[n_ % 4], PT[n_ % 4]))
                            first = False
                    LA = 2
                    for idx in range(len(units) + LA):
                        if idx < len(units) and idx % 8 == 0:
                            ch_ = idx // 8
                            if ch_ == 0:
                                chunk_loads[0]()
                            if ch_ + 1 < len(chunk_loads):
                                chunk_loads[ch_ + 1]()
                        if idx < len(units):
                            (kt_, v_, qa_, ncol, sc_ap, adds, c0, fst, kr_, qr_, ares, pS_, Sb_, PT_) = units[idx]
                            P.op("pe", lambda e, pS_=pS_, ncol=ncol, kt_=kt_, qa_=qa_: e.matmul(
                                pS_.t[:, 0:ncol], lhsT=kt_, rhs=qa_, start=True, stop=True),
                                reads=[kr_, qr_], writes=[pS_.r])
                        if idx >= LA:
                            (kt_, v_, qa_, ncol, sc_ap, adds, c0, fst, kr_, qr_, ares, pS_, Sb_, PT_) = units[idx - LA]
                            for (o_, n2, ap_, r_) in adds:
                                P.op("dve", lambda e, o_=o_, n2=n2, ap_=ap_, pS_=pS_, Sb_=Sb_, sc_ap=sc_ap: e.scalar_tensor_tensor(
                                    out=Sb_.t[:, o_:o_ + n2], in0=pS_.t[:, o_:o_ + n2], scalar=sc_ap, in1=ap_,
                                    op0=ALU.add, op1=ALU.add), reads=[pS_.r, r_] + list(ares), writes=[Sb_.r])
                            P.op("act", lambda e, PT_=PT_, Sb_=Sb_, ncol=ncol: e.activation(
                                out=PT_.t[:, 0:ncol], in_=Sb_.t[:, 0:ncol], func=AF.Exp), reads=[Sb_.r], writes=[PT_.r])
                            P.op("pe", lambda e, v_=v_, PT_=PT_, ncol=ncol, c0=c0, fst=fst: e.matmul(
                                psO.t[:, c0:c0 + ncol], lhsT=v_, rhs=PT_.t[:, 0:ncol], start=fst, stop=False,
                                skip_group_check=True), reads=[kr_, PT_.r], writes=[psO.r])
                            ra_ = racc[cnt["A"] % 2]
                            if fst:
                                P.op("pool", lambda e, PT_=PT_, ra_=ra_: e.tensor_copy(out=ra_.t[:], in_=PT_.t[:]),
                                     reads=[PT_.r], writes=[ra_.r])
                            else:
                                P.op("pool", lambda e, PT_=PT_, ra_=ra_, ncol=ncol, c0=c0: e.tensor_tensor(
                                    out=ra_.t[:, c0:c0 + ncol], in0=ra_.t[:, c0:c0 + ncol], in1=PT_.t[:, 0:ncol],
                                    op=ALU.add), reads=[PT_.r, ra_.r], writes=[ra_.r])
                    ra_ = racc[cnt["A"] % 2]
                    cnt["A"] += 1
                    P.op("pe", lambda e, ra_=ra_: e.matmul(psSm.t[:], lhsT=C.ones_f.t[:], rhs=ra_.t[:], start=True, stop=True),
                         reads=[C.ones_f.r, ra_.r], writes=[psSm.r])
                    P.op("dve", lambda e: e.reciprocal(out=rs.t[:], in_=psSm.t[:]), reads=[psSm.r], writes=[rs.r])
                    P.op("pool", lambda e, g_=g_, s=s: e.tensor_tensor(out=wgt.t[:], in0=rs.t[:],
                                                                       in1=g_.t[:, s * 512:(s + 1) * 512], op=ALU.mult),
                         reads=[rs.r, g_.r], writes=[wgt.r])
                    P.op("dve", lambda e, hh=hh, s=s: e.tensor_tensor(out=og.t[:, hh, s * 512:(s + 1) * 512],
                                                                      in0=psO.t[:], in1=wgt.t[:], op=ALU.mult),
                         reads=[psO.r, wgt.r], writes=[og.r])
            k = 0
            for c in range(NCH):
                for hf in range(2):
                    pp = psG2[k % 2]
                    k += 1
                    for hh in range(HG):
                        P.op("pe", lambda e, pp=pp, hh=hh, c=c, hf=hf: e.matmul(
                            pp.t[:], lhsT=wob_ap[:, hh, c * 128:(c + 1) * 128], rhs=og.t[:, hh, hf * 512:(hf + 1) * 512],
                            start=(hh == 0), stop=(hh == HG - 1)), reads=[WB0.r, og.r], writes=[pp.r])
                    P.op("dve", lambda e, pp=pp, c=c, hf=hf: e.tensor_tensor(
                        out=C.xT.t[:, c, hf * 512:(hf + 1) * 512], in0=pp.t[:], in1=C.xT.t[:, c, hf * 512:(hf + 1) * 512],
                        op=ALU.add), reads=[pp.r, C.xr[c]], writes=[C.xr[c]])

    for l in range(2):
        kv_phase(0, l, lambda h, l=l: a_wk_d[l, h], lambda b, l=l: a_wv_d[l, b], False)
        localize(False)
        mix_phase(False, 0, l, lambda h, l=l: a_wq_d[l, h], lambda h, l=l: a_wg_d[l, h], lambda g, l=l: a_wo_d[l, g], l)
    kv_phase(0, 2, lambda h: s_wk_d[h], lambda b: s_wv_d[b], True)
    localize(True)
    for l in range(2):
        mix_phase(True, 0, 3 + l, lambda h, l=l: b_wq_d[l, h], lambda h, l=l: b_wg_d[l, h], lambda g, l=l: b_wo_d[l, g], 0)

    gf = C.sb("gfin", [128, NCH], F32)
    P.dma("sp", lambda e: e.dma_start(out=gf.t[:], in_=gns_d[5]), writes=[gf.r])
    pss = (psG[0], psG[1])
    for c in range(NCH):
        sq = sg[c % 2]
        P.op("act", lambda e, c=c, sq=sq: e.activation(out=sq.t[:], in_=C.xT.t[:, c, :], func=AF.Square),
             reads=[C.xr[c]], writes=[sq.r])
        for hf in range(2):
            P.op("pe", lambda e, c=c, sq=sq, hf=hf: e.matmul(
                pss[hf].t[:], lhsT=C.ones_f.t[:], rhs=sq.t[:, hf * 512:(hf + 1) * 512],
                start=(c == 0), stop=(c == NCH - 1)), reads=[sq.r, C.ones_f.r], writes=[pss[hf].r])
    for hf in range(2):
        sl = slice(hf * 512, (hf + 1) * 512)
        P.op("act", lambda e, hf=hf, sl=sl: e.activation(
            out=C.rstd.t[:, sl], in_=pss[hf].t[:], func=AF.Sqrt, bias=EPS, scale=1.0 / D_MODEL),
            reads=[pss[hf].r], writes=[C.rstd.r])
    P.op("dve", lambda e: e.reciprocal(out=C.rstd.t[:], in_=C.rstd.t[:]), reads=[C.rstd.r], writes=[C.rstd.r])
    for c in range(NCH):
        yb = sg[c % 2]
        P.op("dve", lambda e, c=c, yb=yb: e.scalar_tensor_tensor(
            out=yb.t[:], in0=C.xT.t[:, c, :], scalar=gf.t[:, c:c + 1], in1=C.rstd.t[:],
            op0=ALU.mult, op1=ALU.mult), reads=[C.xr[c], gf.r, C.rstd.r], writes=[yb.r])
        C.outs.append(P.dma("sp", lambda e, c=c, yb=yb: e.dma_start(out=yo_d[:, c, :], in_=yb.t[:]), reads=[yb.r]))
    return C.finish()


def kernel(x, a_norm, a_w_in, a_rel_bias, a_w_out, kv_norm, kv_w, f_w, f_b, b_norm, b_w_in, b_w_out, final_norm):
    f = lambda a: np.asarray(a, np.float32)
    x = f(x)
    a_w_in, a_w_out, kv_w, b_w_in, b_w_out = f(a_w_in), f(a_w_out), f(kv_w), f(b_w_in), f(b_w_out)
    gns = np.stack([tile_vec(f(a_norm)[0]), tile_vec(f(a_norm)[1]), tile_vec(f(kv_norm)), tile_vec(f(b_norm)[0]),
                    tile_vec(f(b_norm)[1]), tile_vec(f(final_norm))])
    shared = {
        "gns": gns,
        "a_wq": np.stack([tile_w_cols(a_w_in[l], 0, D_INNER, 128) for l in range(2)]),
        "a_wk": np.stack([tile_w_cols(a_w_in[l], D_INNER, D_INNER, 128) for l in range(2)]),
        "a_wv": np.stack([tile_w_cols(a_w_in[l], 2 * D_INNER, D_INNER, 512) for l in range(2)]),
        "a_wg": np.stack([tile_w_cols(a_w_in[l], 3 * D_INNER, D_INNER, 128) for l in range(2)]),
        "a_wo": np.stack([_tile_wo(a_w_out[l]) for l in range(2)]),
        "s_wk": tile_w_cols(kv_w, 0, D_INNER, 128),
        "s_wv": tile_w_cols(kv_w, D_INNER, D_INNER, 512),
        "b_wq": np.stack([tile_w_cols(b_w_in[l], 0, D_INNER, 128) for l in range(2)]),
        "b_wg": np.stack([tile_w_cols(b_w_in[l], D_INNER, D_INNER, 128) for l in range(2)]),
        "b_wo": np.stack([_tile_wo(b_w_out[l]) for l in range(2)]),
        "fw": np.ascontiguousarray(f(f_w).reshape(NCH, 128, NH).transpose(1, 0, 2)),
        "fb": np.ascontiguousarray(np.broadcast_to(f(f_b)[None, :], (128, NH))),
    }
    idx = np.clip(np.arange(767) - 127, -256, 256) + 256
    shared["rbx"] = np.ascontiguousarray(f(a_rel_bias)[:, :, idx])
    cmask = np.zeros((128, 5, 128), np.float32)
    cmask[64:, 0, :64] = NEG
    cmask[:64, 4, 64:] = NEG
    shared["cmask"] = cmask.reshape(128, 640)
    kl = np.arange(128)
    cst = np.zeros((4, 128, 128), np.float32)
    cst[0] = np.eye(128, dtype=np.float32)
    cst[1] = (kl[:, None] > kl[None, :]).astype(np.float32)
    cst[2] = np.where(kl[:, None] > kl[None, :], NEG, 0.0)
    cst[3] = np.eye(128, dtype=np.float32)[::-1]
    shared["cst"] = cst
    in_maps = []
    for j in range(8):
        jj = j % 4
        m = dict(shared)
        m["xT"] = to_xT(x[j // 4][core_tokens(j)])
        kmask = np.zeros((128, 2, 8), np.float32)
        if jj == 0:
            kmask[:, 0, :4] = NEG
        m["kmask"] = kmask
        for s in range(2):
            kv = np.full((128, UMAX[s]), NEG, np.float32)
            for u in range(UMAX[s]):
                if jj + 4 * s - u // 4 >= 0:
                    kv[:, u] = 0.0
            m["kval%d" % s] = kv
        in_maps.append(m)
    if "fused" not in _NC_CACHE:
        _NC_CACHE["fused"] = build_fused()
    res = run_bass_kernel_spmd(_NC_CACHE["fused"], in_maps, core_ids=list(range(8)))
    out = np.zeros((BATCH, SEQ, D_MODEL), np.float32)
    for j in range(8):
        out[j // 4][core_tokens(j)] = from_xT(res.results[j]["yo"])
    return out
```

```python
import numpy as np
from contextlib import ExitStack
import ml_dtypes
import concourse.bass as bass
import concourse.mybir as mybir
from concourse.bass_utils import run_bass_kernel_spmd

F32 = mybir.dt.float32
BF16 = mybir.dt.bfloat16
AF = mybir.ActivationFunctionType
ALU = mybir.AluOpType
NPBF = ml_dtypes.bfloat16

D_MODEL = 2048
NCH = 16
D_INNER = 4096
NH = 32
DH = 128
SEQ = 4096
BATCH = 2
T = 1024
SEG = 512
NSEG = 2
EPS = 1e-6
NEG = -30000.0
SCALE = DH ** -0.5
HG = 4
UMAX = (16, 32)


class Res:
    __slots__ = ("name", "w", "rs")

    def __init__(self, name=""):
        self.name = name
        self.w = None
        self.rs = {}


class Op:
    __slots__ = ("eng", "fn", "deps", "dma", "sig", "need", "n", "cc")

    def __init__(self, eng, fn, dma, cc=None):
        self.eng = eng
        self.fn = fn
        self.dma = dma
        self.deps = []
        self.sig = None
        self.need = dma
        self.n = 0
        self.cc = cc


class Prog:
    ENGS = ("pe", "act", "dve", "pool", "sp")
    NDSEM = 24
    EPOCH = 30000

    def __init__(self, nc):
        self.nc = nc
        self.ops = {e: [] for e in self.ENGS}
        self.ndma = {e: 0 for e in self.ENGS}
        self.count = 0
        self.ccs = {}

    def _track(self, o, reads, writes):
        deps = {}
        for r in reads:
            if r.w is not None:
                deps[id(r.w)] = r.w
        for r in writes:
            if r.w is not None:
                deps[id(r.w)] = r.w
            for x in r.rs.values():
                deps[id(x)] = x
        for d in deps.values():
            if d is o:
                continue
            if (not d.dma) and (not o.dma) and d.eng == "pe" and o.eng == "pe":
                continue
            d.need = True
            o.deps.append(d)
        for r in reads:
            key = ("dma", self.count) if o.dma else o.eng
            r.rs[key] = o
        for r in writes:
            r.w = o
            r.rs = {}
        self.count += 1

    def op(self, eng, fn, reads=(), writes=()):
        o = Op(eng, fn, False)
        self._track(o, reads, writes)
        self.ops[eng].append(o)
        return o

    def dma(self, q, fn, reads=(), writes=()):
        o = Op(q, fn, True)
        self._track(o, reads, writes)
        self.ops[q].append(o)
        return o

    def coll(self, key, fn, reads=(), writes=()):
        o = Op("pool", fn, True, cc=key)
        self._track(o, reads, writes)
        self.ops["pool"].append(o)
        return o

    def emit(self, es, final_deps):
        nc = self.nc
        fin = Op("sp", None, False)
        for d in final_deps:
            d.need = True
            fin.deps.append(d)
        self.ops["sp"].append(fin)
        sems = {}
        for e in self.ENGS:
            cnt = 0
            nd = 0
            esems = []
            dsems = []
            for o in self.ops[e]:
                if o.cc is not None:
                    if o.cc not in self.ccs:
                        self.ccs[o.cc] = [es.enter_context(nc.semaphore("cc_%s" % str(o.cc))), 0]
                    self.ccs[o.cc][1] += 1
                    o.sig = (self.ccs[o.cc][0], self.ccs[o.cc][1])
                elif o.dma:
                    k = nd % self.NDSEM
                    if k >= len(dsems):
                        dsems.append(es.enter_context(nc.semaphore("d_%s_%d" % (e, k))))
                    o.sig = (dsems[k], 16 * (nd // self.NDSEM + 1))
                    o.n = nd
                    nd += 1
                elif o.need:
                    ep = cnt // self.EPOCH
                    if ep >= len(esems):
                        esems.append(es.enter_context(nc.semaphore("c_%s_%d" % (e, ep))))
                    o.sig = (esems[ep], cnt % self.EPOCH + 1)
                    cnt += 1
            sems[e] = (esems, dsems)
        engobj = {"pe": nc.tensor, "act": nc.scalar, "dve": nc.vector, "pool": nc.gpsimd, "sp": nc.sync}
        block = es.enter_context(nc.Block())

        def make(e):
            def body(eng):
                waited = {}
                pre = getattr(self, "pre", {}).get(e)
                if pre is not None:
                    pre(eng)
                for o in self.ops[e]:
                    ws = []
                    for d in o.deps:
                        ws.append(d.sig)
                    if o.dma and o.cc is None and o.n >= self.NDSEM:
                        ws.append((o.sig[0], o.sig[1] - 16))
                    for (s, v) in ws:
                        if waited.get(id(s), 0) < v:
                            waited[id(s)] = v
                            eng.wait_ge(s, v)
                    if o.fn is None:
                        continue
                    ins = o.fn(eng)
                    if o.cc is not None:
                        ins.then_inc(o.sig[0], 1)
                    elif o.dma:
                        ins.then_inc(o.sig[0], 16)
                    elif o.sig is not None:
                        ins.then_inc(o.sig[0], 1)
            return body

        block.tensor(make("pe"))
        block.scalar(make("act"))
        block.vector(make("dve"))
        block.gpsimd(make("pool"))
        block.sync(make("sp"))


class Tl:
    __slots__ = ("t", "r")

    def __init__(self, t, name):
        self.t = t
        self.r = Res(name)


class Ctx:
    def __init__(self):
        self.nc = bass.Bass("TRN2", target_bir_lowering=False)
        self.P = Prog(self.nc)
        self.es = ExitStack()
        self.outs = []
        self.nps = 0

    def dram(self, name, shape, dt, kind):
        return self.nc.dram_tensor(name, list(shape), dt, kind=kind).ap()

    def sb(self, name, shape, dt):
        return Tl(self.es.enter_context(self.nc.sbuf_tensor("s_" + name, list(shape), dt)), name)

    def ps(self, name, dt=F32):
        n = 512 if dt == F32 else 1024
        return Tl(self.es.enter_context(self.nc.psum_tensor("p_" + name, [128, n], dt)), name)

    def finish(self):
        self.P.emit(self.es, self.outs)
        self.es.close()
        return self.nc


def phase_consts(C):
    P = C.P
    C.ones_f = C.sb("ones_f", [128, 128], F32)
    C.ones_b = C.sb("ones_b", [128, 128], BF16)
    P.op("pool", lambda e: e.memset(C.ones_f.t[:], 1.0), writes=[C.ones_f.r])
    P.op("pool", lambda e: e.memset(C.ones_b.t[:], 1.0), writes=[C.ones_b.r])


def phase_load_x(C, xT_d):
    P = C.P
    C.xT = C.sb("xT", [128, NCH, T], F32)
    C.xr = [Res("x%d" % c) for c in range(NCH)]
    for c0 in range(0, NCH, 4):
        P.dma("sp", lambda e, c0=c0: e.dma_start(out=C.xT.t[:, c0:c0 + 4, :], in_=xT_d[:, c0:c0 + 4, :]),
              writes=C.xr[c0:c0 + 4])


def phase_norm(C, gn_d, tag, psA, psB, want_tok_rstd=False):
    P = C.P
    if not hasattr(C, "hT"):
        C.hT = C.sb("hT", [128, NCH, T], BF16)
        C.hr = [Res("h%d" % c) for c in range(NCH)]
        C.xsq = [C.sb("xsq%d" % i, [128, T], F32) for i in range(2)]
        C.rstd = C.sb("rstd", [128, T], F32)
    gn = C.sb("gn_" + tag, [128, NCH], F32)
    P.dma("sp", lambda e: e.dma_start(out=gn.t[:], in_=gn_d), writes=[gn.r])
    pss = (psA, psB)
    for c in range(NCH):
        sq = C.xsq[c % 2]
        P.op("act", lambda e, c=c, sq=sq: e.activation(out=sq.t[:], in_=C.xT.t[:, c, :], func=AF.Square),
             reads=[C.xr[c]], writes=[sq.r])
        for hf in range(2):
            P.op("pe", lambda e, c=c, sq=sq, hf=hf: e.matmul(
                pss[hf].t[:], lhsT=C.ones_f.t[:], rhs=sq.t[:, hf * 512:(hf + 1) * 512],
                start=(c == 0), stop=(c == NCH - 1)),
                reads=[sq.r, C.ones_f.r], writes=[pss[hf].r])
    for hf in range(2):
        sl = slice(hf * 512, (hf + 1) * 512)
        P.op("act", lambda e, hf=hf, sl=sl: e.activation(
            out=C.rstd.t[:, sl], in_=pss[hf].t[:], func=AF.Sqrt, bias=EPS, scale=1.0 / D_MODEL),
            reads=[pss[hf].r], writes=[C.rstd.r])
    P.op("dve", lambda e: e.reciprocal(out=C.rstd.t[:], in_=C.rstd.t[:]), reads=[C.rstd.r], writes=[C.rstd.r])
    for c in range(NCH):
        P.op("dve", lambda e, c=c: e.scalar_tensor_tensor(
            out=C.hT.t[:, c, :], in0=C.xT.t[:, c, :], scalar=gn.t[:, c:c + 1], in1=C.rstd.t[:],
            op0=ALU.mult, op1=ALU.mult), reads=[C.xr[c], gn.r, C.rstd.r], writes=[C.hr[c]])
    return gn


def proj_fm(C, wtile, ps, hf, extra_reads=()):
    P = C.P
    for c in range(NCH):
        P.op("pe", lambda e, c=c: e.matmul(ps.t[:], lhsT=wtile.t[:, c, :], rhs=C.hT.t[:, c, hf * 512:(hf + 1) * 512],
                                           start=(c == 0), stop=(c == NCH - 1)),
             reads=[wtile.r, C.hr[c]] + list(extra_reads), writes=[ps.r])


def build_kv():
    C = Ctx()
    P = C.P
    xT_d = C.dram("xT", [128, NCH, T], F32, "ExternalInput")
    gn_d = C.dram("gn", [128, NCH], F32, "ExternalInput")
    wk_d = C.dram("wk", [NH, 128, NCH, 128], F32, "ExternalInput")
    wv_d = C.dram("wv", [8, 128, NCH, 512], F32, "ExternalInput")
    fw_d = C.dram("fw", [128, NCH, NH], F32, "ExternalInput")
    fb_d = C.dram("fb", [128, NH], F32, "ExternalInput")
    kT_o = C.dram("kT", [NH, 128, T], BF16, "ExternalOutput")
    V_o = C.dram("V", [T // 128, 128, D_INNER], BF16, "ExternalOutput")
    nlf_o = C.dram("nlf", [T // 128, 128, NH], F32, "ExternalOutput")

    ps = [C.ps("ps%d" % i) for i in range(8)]
    phase_consts(C)
    phase_load_x(C, xT_d)
    gn = phase_norm(C, gn_d, "kv", ps[0], ps[1])

    fw = C.sb("fw", [128, NCH, NH], F32)
    fb = C.sb("fb", [128, NH], F32)
    P.dma("sp", lambda e: e.dma_start(out=fw.t[:], in_=fw_d), writes=[fw.r])
    P.dma("sp", lambda e: e.dma_start(out=fb.t[:], in_=fb_d), writes=[fb.r])
    gfw = C.sb("gfw", [128, NCH, NH], F32)
    for c in range(NCH):
        P.op("pool", lambda e, c=c: e.tensor_scalar(out=gfw.t[:, c, :], in0=fw.t[:, c, :], scalar1=gn.t[:, c:c + 1],
                                                    scalar2=None, op0=ALU.mult),
             reads=[fw.r, gn.r], writes=[gfw.r])
    ones_col = C.sb("ones_col", [128, 1], F32)
    P.op("pool", lambda e: e.memset(ones_col.t[:], 1.0), writes=[ones_col.r])
    xsqc = [C.sb("xsqc%d" % i, [128, 128], F32) for i in range(2)]
    zs = [C.sb("zs%d" % i, [128, NH], F32) for i in range(2)]
    rt = [C.sb("rt%d" % i, [128, 1], F32) for i in range(2)]
    NT = T // 128
    k = 0
    for tt in range(NT):
        pz = ps[2 + (tt % 2) * 2]
        pq = ps[3 + (tt % 2) * 2]
        tsl = slice(tt * 128, (tt + 1) * 128)
        for c in range(NCH):
            sq = xsqc[k % 2]
            k += 1
            P.op("act", lambda e, c=c, sq=sq, tsl=tsl: e.activation(out=sq.t[:], in_=C.xT.t[:, c, tsl], func=AF.Square),
                 reads=[C.xr[c]], writes=[sq.r])
            P.op("pe", lambda e, c=c, sq=sq, pq=pq: e.matmul(pq.t[:, 0:1], lhsT=sq.t[:], rhs=ones_col.t[:],
                                                             start=(c == 0), stop=(c == NCH - 1)),
                 reads=[sq.r, ones_col.r], writes=[pq.r])
            P.op("pe", lambda e, c=c, tsl=tsl, pz=pz: e.matmul(pz.t[:, 0:NH], lhsT=C.xT.t[:, c, tsl], rhs=gfw.t[:, c, :],
                                                               start=(c == 0), stop=(c == NCH - 1)),
                 reads=[C.xr[c], gfw.r], writes=[pz.r])
        r1 = rt[tt % 2]
        z = zs[tt % 2]
        P.op("act", lambda e, r1=r1, pq=pq: e.activation(out=r1.t[:], in_=pq.t[:, 0:1], func=AF.Sqrt, bias=EPS,
                                                         scale=1.0 / D_MODEL), reads=[pq.r], writes=[r1.r])
        P.op("dve", lambda e, r1=r1: e.reciprocal(out=r1.t[:], in_=r1.t[:]), reads=[r1.r], writes=[r1.r])
        P.op("dve", lambda e, r1=r1, z=z, pz=pz: e.scalar_tensor_tensor(
            out=z.t[:], in0=pz.t[:, 0:NH], scalar=r1.t[:, 0:1], in1=fb.t[:], op0=ALU.mult, op1=ALU.add),
            reads=[pz.r, r1.r, fb.r], writes=[z.r])
        P.op("act", lambda e, z=z: e.activation(out=z.t[:], in_=z.t[:], func=AF.Exp, scale=-1.0),
             reads=[z.r], writes=[z.r])
        P.op("act", lambda e, z=z: e.activation(out=z.t[:], in_=z.t[:], func=AF.Ln, bias=1.0, scale=1.0),
             reads=[z.r], writes=[z.r])
        C.outs.append(P.dma("sp", lambda e, z=z, tt=tt: e.dma_start(out=nlf_o[tt], in_=z.t[:]), reads=[z.r]))

    wkb = [C.sb("wkb%d" % i, [128, NCH, 128], BF16) for i in range(2)]
    ko = [C.sb("ko%d" % i, [128, T], BF16) for i in range(2)]
    for h in range(NH):
        w = wkb[h % 2]
        o = ko[h % 2]
        P.dma("pool", lambda e, w=w, h=h: e.dma_start(out=w.t[:], in_=wk_d[h]), writes=[w.r])
        for hf in range(2):
            pp = ps[(2 * h + hf) % 4]
            proj_fm(C, w, pp, hf)
            if hf == 0:
                P.op("act", lambda e, o=o, pp=pp: e.activation(out=o.t[:, 0:512], in_=pp.t[:], func=AF.Copy),
                     reads=[pp.r], writes=[o.r])
            else:
                P.op("dve", lambda e, o=o, pp=pp: e.tensor_copy(out=o.t[:, 512:1024], in_=pp.t[:]),
                     reads=[pp.r], writes=[o.r])
        C.outs.append(P.dma("sp", lambda e, o=o, h=h: e.dma_start(out=kT_o[h], in_=o.t[:]), reads=[o.r]))

    wvb = [C.sb("wvb%d" % i, [128, NCH, 512], BF16) for i in range(2)]
    vo = [C.sb("vo%d" % i, [128, 512], BF16) for i in range(4)]
    k = 0
    for b in range(8):
        w = wvb[b % 2]
        P.dma("pool", lambda e, w=w, b=b: e.dma_start(out=w.t[:], in_=wv_d[b]), writes=[w.r])
        for tt in range(NT):
            pp = ps[4 + k % 4]
            o = vo[k % 4]
            for c in range(NCH):
                P.op("pe", lambda e, c=c, tt=tt, w=w, pp=pp: e.matmul(
                    pp.t[:], lhsT=C.hT.t[:, c, tt * 128:(tt + 1) * 128], rhs=w.t[:, c, :],
                    start=(c == 0), stop=(c == NCH - 1)), reads=[w.r, C.hr[c]], writes=[pp.r])
            if k % 2 == 0:
                P.op("act", lambda e, o=o, pp=pp: e.activation(out=o.t[:], in_=pp.t[:], func=AF.Copy),
                     reads=[pp.r], writes=[o.r])
            else:
                P.op("dve", lambda e, o=o, pp=pp: e.tensor_copy(out=o.t[:], in_=pp.t[:]), reads=[pp.r], writes=[o.r])
            C.outs.append(P.dma("sp", lambda e, o=o, tt=tt, b=b: e.dma_start(
                out=V_o[tt, :, b * 512:(b + 1) * 512], in_=o.t[:]), reads=[o.r]))
            k += 1
    return C.finish()


def core_segments(j):
    jj = j % 4
    return (jj, jj + 4)


def core_tokens(j):
    s0, s1 = core_segments(j)
    return np.concatenate([np.arange(s0 * SEG, (s0 + 1) * SEG), np.arange(s1 * SEG, (s1 + 1) * SEG)])


def tile_w_cols(W, col0, ncols, blk):
    Wc = W[:, col0:col0 + ncols]
    nb = ncols // blk
    return np.ascontiguousarray(Wc.reshape(NCH, 128, nb, blk).transpose(2, 1, 0, 3))


def tile_vec(g):
    return np.ascontiguousarray(g.reshape(NCH, 128).T)


def to_xT(xtok):
    t = xtok.shape[0]
    return np.ascontiguousarray(xtok.T.reshape(NCH, 128, t).transpose(1, 0, 2))


def from_xT(xT):
    t = xT.shape[2]
    return np.ascontiguousarray(xT.transpose(1, 0, 2).reshape(D_MODEL, t).T)


_NC_CACHE = {}


def get_nc(kind):
    if kind not in _NC_CACHE:
        _NC_CACHE[kind] = build_kv() if kind == "kv" else build_mix(kind)
    return _NC_CACHE[kind]


def run_kv(xT_cores, gn, W, koff, voff, f_w, f_b):
    nc = get_nc("kv")
    wk = tile_w_cols(W, koff, D_INNER, 128)
    wv = tile_w_cols(W, voff, D_INNER, 512)
    fw = np.ascontiguousarray(f_w.reshape(NCH, 128, NH).transpose(1, 0, 2))
    fb = np.ascontiguousarray(np.broadcast_to(f_b[None, :], (128, NH)))
    g = tile_vec(gn)
    in_maps = [{"xT": xT_cores[j], "gn": g, "wk": wk, "wv": wv, "fw": fw, "fb": fb} for j in range(8)]
    res = run_bass_kernel_spmd(nc, in_maps, core_ids=list(range(8)))
    return res.results


def attn_unit(C, kt_ap, v_ap, q_ap, ncol, sc_ap, adds, acc_o, acc_s, c0, first, kres, vres, qres, ares, psS, Sb, PT):
    P = C.P
    P.op("pe", lambda e: e.matmul(psS.t[:, 0:ncol], lhsT=kt_ap, rhs=q_ap, start=True, stop=True),
         reads=[kres, qres], writes=[psS.r])
    for (o, n, ap, r) in adds:
        P.op("dve", lambda e, o=o, n=n, ap=ap: e.scalar_tensor_tensor(
            out=Sb.t[:, o:o + n], in0=psS.t[:, o:o + n], scalar=sc_ap, in1=ap, op0=ALU.add, op1=ALU.add),
            reads=[psS.r, r] + list(ares), writes=[Sb.r])
    P.op("act", lambda e: e.activation(out=PT.t[:, 0:ncol], in_=Sb.t[:, 0:ncol], func=AF.Exp),
         reads=[Sb.r], writes=[PT.r])
    P.op("pe", lambda e: e.matmul(acc_o.t[:, c0:c0 + ncol], lhsT=v_ap, rhs=PT.t[:, 0:ncol], start=first, stop=False,
                                  skip_group_check=True),
         reads=[vres, PT.r], writes=[acc_o.r])
    P.op("pe", lambda e: e.matmul(acc_s.t[:, c0:c0 + ncol], lhsT=C.ones_b.t[:], rhs=PT.t[:, 0:ncol], start=first,
                                  stop=False, skip_group_check=True),
         reads=[C.ones_b.r, PT.r], writes=[acc_s.r])


def build_mix(kind):
    C = Ctx()
    P = C.P
    isb = kind == "b"
    xT_d = C.dram("xT", [128, NCH, T], F32, "ExternalInput")
    gn_d = C.dram("gn", [128, NCH], F32, "ExternalInput")
    wq_d = C.dram("wq", [NH, 128, NCH, 128], F32, "ExternalInput")
    wg_d = C.dram("wg", [NH, 128, NCH, 128], F32, "ExternalInput")
    wo_d = C.dram("wo", [NH // HG, 128, HG, D_MODEL], F32, "ExternalInput")
    xo_d = C.dram("xo", [128, NCH, T], F32, "ExternalOutput")
    if not isb:
        NU = (8, 8)
        kT_d = [C.dram("kTs%d" % s, [NH, 128, 8 * 128], BF16, "ExternalInput") for s in range(2)]
        V_d = [C.dram("Vs%d" % s, [8, 128, D_INNER], BF16, "ExternalInput") for s in range(2)]
        kmask_d = C.dram("kmask", [128, 2, 8], F32, "ExternalInput")
        rbx_d = C.dram("rbx", [NH, 767], F32, "ExternalInput")
        cmask_d = C.dram("cmask", [128, 640], F32, "ExternalInput")
    else:
        NU = UMAX
        kT_d = [C.dram("kTs%d" % s, [NH, 128, NU[s] * 128], BF16, "ExternalInput") for s in range(2)]
        V_d = [C.dram("Vs%d" % s, [NU[s], 128, D_INNER], BF16, "ExternalInput") for s in range(2)]
        nlf_d = [C.dram("nlfs%d" % s, [128, NU[s], NH], F32, "ExternalInput") for s in range(2)]
        kval_d = [C.dram("kval%d" % s, [128, NU[s]], F32, "ExternalInput") for s in range(2)]
        gfin_d = C.dram("gfin", [128, NCH], F32, "ExternalInput")
        ident_d = C.dram("ident", [128, 128], F32, "ExternalInput")
        tri_d = C.dram("tri", [128, 128], F32, "ExternalInput")
        trim_d = C.dram("trim", [128, 128], F32, "ExternalInput")
        yo_d = C.dram("yo", [128, NCH, T], F32, "ExternalOutput")

    ps = [C.ps("ps%d" % i) for i in range(8)]
    psS = ps[0:2]
    psO = ps[2]
    psSm = ps[3]
    psG = ps[4:8]
    phase_consts(C)
    phase_load_x(C, xT_d)
    sg = [C.sb("sg%d" % i, [128, T], F32) for i in range(2)]
    C.xsq = sg
    C.hT = C.sb("hT", [128, NCH, T], BF16)
    C.hr = [Res("h%d" % c) for c in range(NCH)]
    C.rstd = C.sb("rstd", [128, T], F32)
    phase_norm(C, gn_d, "n1", psG[0], psG[1])

    wqb = [C.sb("wqb%d" % i, [128, NCH, 128], BF16) for i in range(2)]
    wgb = [C.sb("wgb%d" % i, [128, NCH, 128], BF16) for i in range(2)]
    wob = C.sb("wob", [128, HG, D_MODEL], BF16)
    qT = [C.sb("qT%d" % i, [128, T], BF16) for i in range(2)]
    og = C.sb("og", [128, HG, T], BF16)
    kxc = [C.sb("kxc%d" % i, [128, 8 * 128], BF16) for i in range(2)]
    vxc = [C.sb("vxc%d" % i, [128, 8, 128], BF16) for i in range(2)]
    Sb = [C.sb("Sb%d" % i, [128, 512], F32) for i in range(2)]
    PT = [C.sb("PT%d" % i, [128, 512], BF16) for i in range(2)]
    rs = C.sb("rs", [128, 512], F32)
    wgt = C.sb("wgt", [128, 512], F32)

    if not isb:
        kmask = C.sb("kmask", [128, 2, 8], F32)
        P.dma("sp", lambda e: e.dma_start(out=kmask.t[:], in_=kmask_d), writes=[kmask.r])
        cmask = C.sb("cmask", [128, 640], F32)
        P.dma("sp", lambda e: e.dma_start(out=cmask.t[:], in_=cmask_d), writes=[cmask.r])
        BT = [C.sb("BT%d" % i, [128, 640], F32) for i in range(2)]
    else:
        ident = C.sb("ident", [128, 128], F32)
        tri = C.sb("tri", [128, 128], F32)
        trim = C.sb("trim", [128, 128], F32)
        for (t_, d_) in ((ident, ident_d), (tri, tri_d), (trim, trim_d)):
            P.dma("sp", lambda e, t_=t_, d_=d_: e.dma_start(out=t_.t[:], in_=d_), writes=[t_.r])
        nlf_t = C.sb("nlf_t", [128, 32, NH], F32)
        tot_t = C.sb("tot_t", [128, 32, NH], F32)
        ND = [C.sb("ND%d" % s, [128, NU[s], NH], F32) for s in range(2)]
        SC = [C.sb("SC%d" % s, [128, NU[s], NH], F32) for s in range(2)]
        kval = [C.sb("kval%d" % s, [128, NU[s]], F32) for s in range(2)]
        NDQ = C.sb("NDQ", [128, 512], F32)
        NDQd = C.sb("NDQd", [128, 512], F32)
        dexp = [C.sb("dexp%d" % i, [128, 4, 128], BF16) for i in range(2)]
        for s in range(2):
            U = NU[s]
            P.dma("sp", lambda e, s=s, U=U: e.dma_start(out=nlf_t.t[:, 0:U, :], in_=nlf_d[s]), writes=[nlf_t.r])
            P.dma("sp", lambda e, s=s: e.dma_start(out=kval[s].t[:], in_=kval_d[s]), writes=[kval[s].r])
            nflat = nlf_t.t[:, 0:U, :].rearrange("p u h -> p (u h)")
            ndflat = ND[s].t[:].rearrange("p u h -> p (u h)")
            totflat = tot_t.t[:, 0:U, :].rearrange("p u h -> p (u h)")
            for j in range(U * NH // 512):
                sl = slice(j * 512, (j + 1) * 512)
                pa = psG[(2 * j) % 4]
                pb = psG[(2 * j + 1) % 4]
                P.op("pe", lambda e, pa=pa, sl=sl, nflat=nflat: e.matmul(pa.t[:], lhsT=tri.t[:], rhs=nflat[:, sl],
                                                                         start=True, stop=True),
                     reads=[tri.r, nlf_t.r], writes=[pa.r])
                P.op("pe", lambda e, pb=pb, sl=sl, nflat=nflat: e.matmul(pb.t[:], lhsT=C.ones_f.t[:], rhs=nflat[:, sl],
                                                                         start=True, stop=True),
                     reads=[C.ones_f.r, nlf_t.r], writes=[pb.r])
                P.op("act", lambda e, pa=pa, sl=sl, ndflat=ndflat: e.activation(out=ndflat[:, sl], in_=pa.t[:], func=AF.Copy),
                     reads=[pa.r], writes=[ND[s].r])
                P.op("dve", lambda e, pb=pb, sl=sl, totflat=totflat: e.tensor_copy(out=totflat[:, sl], in_=pb.t[:]),
                     reads=[pb.r], writes=[tot_t.r])
            for u in range(1, U - 1):
                P.op("dve", lambda e, u=u: e.tensor_tensor(out=tot_t.t[:, u, :], in0=tot_t.t[:, u, :],
                                                           in1=tot_t.t[:, u - 1, :], op=ALU.add),
                     reads=[tot_t.r], writes=[tot_t.r])
            P.op("dve", lambda e, s=s, U=U: e.tensor_tensor(out=ND[s].t[:, 1:U, :], in0=ND[s].t[:, 1:U, :],
                                                            in1=tot_t.t[:, 0:U - 1, :], op=ALU.add),
                 reads=[tot_t.r, ND[s].r], writes=[ND[s].r])
            P.op("dve", lambda e, s=s, U=U: e.tensor_tensor(
                out=SC[s].t[:], in0=kval[s].t[:].unsqueeze(2).to_broadcast([128, U, NH]), in1=ND[s].t[:],
                op=ALU.subtract), reads=[kval[s].r, ND[s].r], writes=[SC[s].r])

    nK = 0
    nUnit = 0
    for grp in range(NH // HG):
        P.dma("pool", lambda e, grp=grp: e.dma_start(out=wob.t[:], in_=wo_d[grp]), writes=[wob.r])
        for hh in range(HG):
            h = grp * HG + hh
            wq_, wg_ = wqb[h % 2], wgb[h % 2]
            P.dma("pool", lambda e, wq_=wq_, h=h: e.dma_start(out=wq_.t[:], in_=wq_d[h]), writes=[wq_.r])
            P.dma("pool", lambda e, wg_=wg_, h=h: e.dma_start(out=wg_.t[:], in_=wg_d[h]), writes=[wg_.r])
            q_ = qT[h % 2]
            g_ = sg[h % 2]
            for hf in range(2):
                pp = psG[hf]
                proj_fm(C, wq_, pp, hf)
                P.op("dve", lambda e, q_=q_, pp=pp, hf=hf: e.tensor_scalar(
                    out=q_.t[:, hf * 512:(hf + 1) * 512], in0=pp.t[:], scalar1=SCALE, scalar2=None, op0=ALU.mult),
                    reads=[pp.r], writes=[q_.r])
            for hf in range(2):
                pp = psG[2 + hf]
                proj_fm(C, wg_, pp, hf)
                P.op("act", lambda e, g_=g_, pp=pp, hf=hf: e.activation(
                    out=g_.t[:, hf * 512:(hf + 1) * 512], in_=pp.t[:], func=AF.Silu),
                    reads=[pp.r], writes=[g_.r])
            if not isb:
                bt = BT[h % 2]
                src = bass.AP(rbx_d.tensor, h * 767, [[1, 128], [1, 640]])
                P.dma("sp", lambda e, bt=bt, src=src: e.dma_start(out=bt.t[:], in_=src), writes=[bt.r])
                P.op("pool", lambda e, bt=bt: e.tensor_tensor(out=bt.t[:], in0=bt.t[:], in1=cmask.t[:], op=ALU.add),
                     reads=[bt.r, cmask.r], writes=[bt.r])
            for s in range(2):
                U = NU[s]
                if isb:
                    dx = dexp[(2 * h + s) % 2]
                    for i in range(4):
                        P.op("pool", lambda e, dx=dx, i=i, s=s, h=h: e.tensor_tensor(
                            out=dx.t[:, i, :], in0=ident.t[:], in1=ND[s].t[:, 3 - i, h:h + 1].to_broadcast([128, 128]),
                            op=ALU.mult), reads=[ident.r, ND[s].r], writes=[dx.r])
                    pq = psG[(2 * h + s) % 4]
                    P.op("pe", lambda e, dx=dx, pq=pq: e.matmul(pq.t[:], lhsT=C.ones_b.t[:],
                                                                rhs=dx.t[:].rearrange("p i q -> p (i q)"),
                                                                start=True, stop=True),
                         reads=[C.ones_b.r, dx.r], writes=[pq.r])
                    P.op("act", lambda e, pq=pq: e.activation(out=NDQ.t[:], in_=pq.t[:], func=AF.Copy),
                         reads=[pq.r], writes=[NDQ.r])
                    P.op("pool", lambda e: e.tensor_tensor(
                        out=NDQd.t[:].rearrange("p (i q) -> p i q", i=4), in0=NDQ.t[:].rearrange("p (i q) -> p i q", i=4),
                        in1=trim.t[:].unsqueeze(1).to_broadcast([128, 4, 128]), op=ALU.add),
                        reads=[NDQ.r, trim.r], writes=[NDQd.r])
                first = True
                for ch in range(U // 8):
                    kx = kxc[nK % 2]
                    vx = vxc[nK % 2]
                    nK += 1
                    P.dma("sp", lambda e, kx=kx, s=s, h=h, ch=ch: e.dma_start(
                        out=kx.t[:], in_=kT_d[s][h, :, ch * 1024:(ch + 1) * 1024]), writes=[kx.r])
                    P.dma("sp", lambda e, vx=vx, s=s, h=h, ch=ch: e.dma_start(
                        out=vx.t[:], in_=V_d[s][ch * 8:(ch + 1) * 8, :, h * 128:(h + 1) * 128].rearrange("u p d -> p u d")),
                        writes=[vx.r])
                    if ch == 0:
                        order = [3, 4, 0, 1, 2, 5, 6, 7] if not isb else [3, 2, 1, 0, 4, 5, 6, 7]
                    else:
                        order = list(range(8))
                    for ul in order:
                        u = ch * 8 + ul
                        if not isb:
                            ilo, ihi = max(0, u - 4), min(3, u)
                            rlo = ilo + 4 - u
                            ncol = (ihi - ilo + 1) * 128
                            adds = [(0, ncol, BT[h % 2].t[:, rlo * 128:rlo * 128 + ncol], BT[h % 2].r)]
                            sc_ap = kmask.t[:, s, u:u + 1]
                            ares = [kmask.r]
                        else:
                            if u < 4:
                                ilo, ihi = 3 - u, 3
                                ncol = (ihi - ilo + 1) * 128
                                adds = [(0, 128, NDQd.t[:, ilo * 128:(ilo + 1) * 128], NDQd.r)]
                                if ncol > 128:
                                    adds.append((128, ncol - 128, NDQ.t[:, (ilo + 1) * 128:512], NDQ.r))
                            else:
                                ilo, ihi = 0, 3
                                ncol = 512
                                adds = [(0, 512, NDQ.t[:, :], NDQ.r)]
                            sc_ap = SC[s].t[:, u, h:h + 1]
                            ares = [SC[s].r]
                        c0 = ilo * 128
                        attn_unit(C, kx.t[:, ul * 128:(ul + 1) * 128], vx.t[:, ul, :],
                                  q_.t[:, s * 512 + c0:s * 512 + c0 + ncol], ncol, sc_ap, adds, psO, psSm, c0, first,
                                  kx.r, vx.r, q_.r, ares, psS[nUnit % 2], Sb[nUnit % 2], PT[nUnit % 2])
                        first = False
                        nUnit += 1
                P.op("dve", lambda e: e.reciprocal(out=rs.t[:], in_=psSm.t[:]), reads=[psSm.r], writes=[rs.r])
                P.op("pool", lambda e, g_=g_, s=s: e.tensor_tensor(out=wgt.t[:], in0=rs.t[:],
                                                                   in1=g_.t[:, s * 512:(s + 1) * 512], op=ALU.mult),
                     reads=[rs.r, g_.r], writes=[wgt.r])
                P.op("dve", lambda e, hh=hh, s=s: e.tensor_tensor(out=og.t[:, hh, s * 512:(s + 1) * 512], in0=psO.t[:],
                                                                  in1=wgt.t[:], op=ALU.mult),
                     reads=[psO.r, wgt.r], writes=[og.r])
        k = 0
        for c in range(NCH):
            for hf in range(2):
                pp = psG[k % 4]
                k += 1
                for hh in range(HG):
                    P.op("pe", lambda e, pp=pp, hh=hh, c=c, hf=hf: e.matmul(
                        pp.t[:], lhsT=wob.t[:, hh, c * 128:(c + 1) * 128], rhs=og.t[:, hh, hf * 512:(hf + 1) * 512],
                        start=(hh == 0), stop=(hh == HG - 1)), reads=[wob.r, og.r], writes=[pp.r])
                P.op("dve", lambda e, pp=pp, c=c, hf=hf: e.tensor_tensor(
                    out=C.xT.t[:, c, hf * 512:(hf + 1) * 512], in0=pp.t[:], in1=C.xT.t[:, c, hf * 512:(hf + 1) * 512],
                    op=ALU.add), reads=[pp.r, C.xr[c]], writes=[C.xr[c]])
    for c0 in range(0, NCH, 4):
        C.outs.append(P.dma("sp", lambda e, c0=c0: e.dma_start(out=xo_d[:, c0:c0 + 4, :], in_=C.xT.t[:, c0:c0 + 4, :]),
                            reads=C.xr[c0:c0 + 4]))
    if isb:
        gf = C.sb("gfin", [128, NCH], F32)
        P.dma("sp", lambda e: e.dma_start(out=gf.t[:], in_=gfin_d), writes=[gf.r])
        pss = (psG[0], psG[1])
        for c in range(NCH):
            sq = sg[c % 2]
            P.op("act", lambda e, c=c, sq=sq: e.activation(out=sq.t[:], in_=C.xT.t[:, c, :], func=AF.Square),
                 reads=[C.xr[c]], writes=[sq.r])
            for hf in range(2):
                P.op("pe", lambda e, c=c, sq=sq, hf=hf: e.matmul(
                    pss[hf].t[:], lhsT=C.ones_f.t[:], rhs=sq.t[:, hf * 512:(hf + 1) * 512],
                    start=(c == 0), stop=(c == NCH - 1)), reads=[sq.r, C.ones_f.r], writes=[pss[hf].r])
        for hf in range(2):
            sl = slice(hf * 512, (hf + 1) * 512)
            P.op("act", lambda e, hf=hf, sl=sl: e.activation(
                out=C.rstd.t[:, sl], in_=pss[hf].t[:], func=AF.Sqrt, bias=EPS, scale=1.0 / D_MODEL),
                reads=[pss[hf].r], writes=[C.rstd.r])
        P.op("dve", lambda e: e.reciprocal(out=C.rstd.t[:], in_=C.rstd.t[:]), reads=[C.rstd.r], writes=[C.rstd.r])
        for c in range(NCH):
            yb = sg[c % 2]
            P.op("dve", lambda e, c=c, yb=yb: e.scalar_tensor_tensor(
                out=yb.t[:], in0=C.xT.t[:, c, :], scalar=gf.t[:, c:c + 1], in1=C.rstd.t[:],
                op0=ALU.mult, op1=ALU.mult), reads=[C.xr[c], gf.r, C.rstd.r], writes=[yb.r])
            C.outs.append(P.dma("sp", lambda e, c=c, yb=yb: e.dma_start(out=yo_d[:, c, :], in_=yb.t[:]), reads=[yb.r]))
    return C.finish()


def _tile_wo(Wo):
    return np.ascontiguousarray(Wo.reshape(NH // HG, HG, 128, D_MODEL).transpose(0, 2, 1, 3))


def _gather_seq(res, key):
    out = []
    for b in range(BATCH):
        if key == "kT":
            g = np.zeros((NH, 128, SEQ), dtype=res[0][key].dtype)
            for j in range(4 * b, 4 * b + 4):
                for s, sgm in enumerate(core_segments(j)):
                    g[:, :, sgm * SEG:(sgm + 1) * SEG] = res[j][key][:, :, s * SEG:(s + 1) * SEG]
        else:
            w = res[0][key].shape[2]
            g = np.zeros((SEQ // 128, 128, w), dtype=res[0][key].dtype)
            for j in range(4 * b, 4 * b + 4):
                for s, sgm in enumerate(core_segments(j)):
                    g[sgm * 4:(sgm + 1) * 4] = res[j][key][s * 4:(s + 1) * 4]
        out.append(g)
    return out


def run_a(xT_cores, gn, W_in, rel_bias, W_out, kT_g, V_g):
    nc = get_nc("a")
    wq = tile_w_cols(W_in, 0, D_INNER, 128)
    wg = tile_w_cols(W_in, 3 * D_INNER, D_INNER, 128)
    wo = _tile_wo(W_out)
    g = tile_vec(gn)
    idx = np.clip(np.arange(767) - 127, -256, 256) + 256
    rbx = np.ascontiguousarray(rel_bias[:, idx])
    cmask = np.zeros((128, 5, 128), np.float32)
    cmask[:64, 0, :64] = NEG
    cmask[64:, 4, 64:] = NEG
    cmask = cmask.reshape(128, 640)
    in_maps = []
    for j in range(8):
        b = j // 4
        m = {"xT": xT_cores[j], "gn": g, "wq": wq, "wg": wg, "wo": wo, "rbx": rbx, "cmask": cmask}
        kmask = np.zeros((128, 2, 8), np.float32)
        for s, sgm in enumerate(core_segments(j)):
            kT = np.zeros((NH, 128, 2 * SEG), dtype=kT_g[b].dtype)
            V = np.zeros((8, 128, D_INNER), dtype=V_g[b].dtype)
            kT[:, :, SEG:] = kT_g[b][:, :, sgm * SEG:(sgm + 1) * SEG]
            V[4:] = V_g[b][sgm * 4:(sgm + 1) * 4]
            if sgm > 0:
                kT[:, :, :SEG] = kT_g[b][:, :, (sgm - 1) * SEG:sgm * SEG]
                V[:4] = V_g[b][(sgm - 1) * 4:sgm * 4]
            else:
                kmask[:, s, :4] = NEG
            m["kTs%d" % s] = np.ascontiguousarray(kT.reshape(NH, 128, 8, 128)[:, :, :, ::-1]).reshape(NH, 128, 1024)
            m["Vs%d" % s] = np.ascontiguousarray(V[:, ::-1, :])
        m["kmask"] = kmask
        in_maps.append(m)
    res = run_bass_kernel_spmd(nc, in_maps, core_ids=list(range(8)))
    return [r["xo"] for r in res.results]


def run_b(xT_cores, gn, W_in, W_out, kT_g, V_g, nlf_g, gfin):
    nc = get_nc("b")
    wq = tile_w_cols(W_in, 0, D_INNER, 128)
    wg = tile_w_cols(W_in, D_INNER, D_INNER, 128)
    wo = _tile_wo(W_out)
    g = tile_vec(gn)
    gf = tile_vec(gfin)
    ident = np.eye(128, dtype=np.float32)
    kl = np.arange(128)
    tri = (kl[:, None] > kl[None, :]).astype(np.float32)
    trim = np.where(kl[:, None] > kl[None, :], NEG, 0.0).astype(np.float32)
    in_maps = []
    for j in range(8):
        b = j // 4
        m = {"xT": xT_cores[j], "gn": g, "wq": wq, "wg": wg, "wo": wo, "gfin": gf, "ident": ident, "tri": tri,
             "trim": trim}
        for s, sgm in enumerate(core_segments(j)):
            U = UMAX[s]
            tiles = [4 * sgm + 3 - u for u in range(4 * sgm + 4)]
            kT = np.zeros((NH, 128, U * 128), dtype=kT_g[b].dtype)
            V = np.zeros((U, 128, D_INNER), dtype=V_g[b].dtype)
            nlf = np.zeros((128, U, NH), np.float32)
            kval = np.full((128, U), NEG, np.float32)
            for u, tix in enumerate(tiles):
                kT[:, :, u * 128:(u + 1) * 128] = kT_g[b][:, :, tix * 128:(tix + 1) * 128]
                V[u] = V_g[b][tix]
                nlf[:, u, :] = nlf_g[b][tix]
                kval[:, u] = 0.0
            m["kTs%d" % s] = kT
            m["Vs%d" % s] = V
            m["nlfs%d" % s] = nlf
            m["kval%d" % s] = kval
        in_maps.append(m)
    res = run_bass_kernel_spmd(nc, in_maps, core_ids=list(range(8)))
    return [r["xo"] for r in res.results], [r["yo"] for r in res.results]


def kernel_unfused(x, a_norm, a_w_in, a_rel_bias, a_w_out, kv_norm, kv_w, f_w, f_b, b_norm, b_w_in, b_w_out, final_norm):
    x = np.asarray(x, np.float32)
    xT = [to_xT(x[j // 4][core_tokens(j)]) for j in range(8)]
    f_w = np.asarray(f_w, np.float32)
    f_b = np.asarray(f_b, np.float32)
    for l in range(2):
        W = np.asarray(a_w_in[l], np.float32)
        r = run_kv(xT, np.asarray(a_norm[l], np.float32), W, D_INNER, 2 * D_INNER, f_w, f_b)
        kT_g = _gather_seq(r, "kT")
        V_g = _gather_seq(r, "V")
        xT = run_a(xT, np.asarray(a_norm[l], np.float32), W, np.asarray(a_rel_bias[l], np.float32),
                   np.asarray(a_w_out[l], np.float32), kT_g, V_g)
    r = run_kv(xT, np.asarray(kv_norm, np.float32), np.asarray(kv_w, np.float32), 0, D_INNER, f_w, f_b)
    kT_g = _gather_seq(r, "kT")
    V_g = _gather_seq(r, "V")
    nlf_g = _gather_seq(r, "nlf")
    yT = None
    for l in range(2):
        xT, yT = run_b(xT, np.asarray(b_norm[l], np.float32), np.asarray(b_w_in[l], np.float32),
                       np.asarray(b_w_out[l], np.float32), kT_g, V_g, nlf_g, np.asarray(final_norm, np.float32))
    out = np.zeros((BATCH, SEQ, D_MODEL), np.float32)
    for j in range(8):
        out[j // 4][core_tokens(j)] = from_xT(yT[j])
    return out


NSEGP = 11
KSEGE = 1024 * 512
VSEGE = 512 * 512
NSEGE = 512 * NH
GROUPS = [[0, 1, 2, 3], [4, 5, 6, 7]]


def build_fused():
    C = Ctx()
    P = C.P
    nc = C.nc
    xT_d = C.dram("xT", [128, NCH, T], F32, "ExternalInput")
    gns_d = C.dram("gns", [6, 128, NCH], F32, "ExternalInput")
    a_wq_d = C.dram("a_wq", [2, NH, 128, NCH, 128], F32, "ExternalInput")
    a_wg_d = C.dram("a_wg", [2, NH, 128, NCH, 128], F32, "ExternalInput")
    a_wk_d = C.dram("a_wk", [2, NH, 128, NCH, 128], F32, "ExternalInput")
    a_wv_d = C.dram("a_wv", [2, 8, 128, NCH, 512], F32, "ExternalInput")
    a_wo_d = C.dram("a_wo", [2, NH // HG, 128, HG, D_MODEL], F32, "ExternalInput")
    s_wk_d = C.dram("s_wk", [NH, 128, NCH, 128], F32, "ExternalInput")
    s_wv_d = C.dram("s_wv", [8, 128, NCH, 512], F32, "ExternalInput")
    b_wq_d = C.dram("b_wq", [2, NH, 128, NCH, 128], F32, "ExternalInput")
    b_wg_d = C.dram("b_wg", [2, NH, 128, NCH, 128], F32, "ExternalInput")
    b_wo_d = C.dram("b_wo", [2, NH // HG, 128, HG, D_MODEL], F32, "ExternalInput")
    fw_d = C.dram("fw", [128, NCH, NH], F32, "ExternalInput")
    fb_d = C.dram("fb", [128, NH], F32, "ExternalInput")
    rbx_d = C.dram("rbx", [2, NH, 767], F32, "ExternalInput")
    cmask_d = C.dram("cmask", [128, 640], F32, "ExternalInput")
    kmask_d = C.dram("kmask", [128, 2, 8], F32, "ExternalInput")
    kval_d = [C.dram("kval%d" % s, [128, UMAX[s]], F32, "ExternalInput") for s in range(2)]
    cst_d = C.dram("cst", [4, 128, 128], F32, "ExternalInput")
    yo_d = C.dram("yo", [128, NCH, T], F32, "ExternalOutput")
    kTo = [nc.dram_tensor("kTo%d" % i, [4 * 2 * 1024, 512], BF16) for i in range(1)]
    Vo = [nc.dram_tensor("Vo%d" % i, [8 * 2 * 512, 512], BF16) for i in range(1)]
    nlfo = nc.dram_tensor("nlfo", [T, NH], F32)
    KL = [nc.dram_tensor("KL%d" % i, [4 * NSEGP * 1024, 512], BF16) for i in range(2)]
    VL = [nc.dram_tensor("VL%d" % i, [8 * NSEGP * 512, 512], BF16) for i in range(2)]
    NL = nc.dram_tensor("NL", [NSEGP * 512, NH], F32)
    kTo_r = [[Res() for _ in range(NH)] for _ in range(2)]
    Vo_r = [[Res() for _ in range(8)] for _ in range(2)]
    nlfo_r = Res()
    KL_r = [[[Res() for _ in range(2)] for _ in range(4)] for _ in range(2)]
    VL_r = [[[Res() for _ in range(2)] for _ in range(8)] for _ in range(2)]
    NL_r = [Res() for _ in range(2)]
    pad_r = Res()
    Kloc = nc.dram_tensor("Kloc", [4 * 8 * 1024, 512], BF16)
    Vloc = nc.dram_tensor("Vloc", [8 * 8 * 512, 512], BF16)
    Nloc = nc.dram_tensor("Nloc", [8 * 512, NH], F32)
    Kloc_r = [Res() for _ in range(4)]
    Vloc_r = [Res() for _ in range(4)]
    Nloc_r = Res()

    ps = [C.ps("ps%d" % i) for i in range(8)]
    psS = ps[0:3]
    psOs = [ps[3], ps[4]]
    psSms = [ps[5], ps[6]]
    psO = psOs[0]
    psSm = psSms[0]
    psG = [ps[6], ps[7], ps[0], ps[1]]
    psG2 = [ps[7], ps[2]]
    phase_consts(C)
    phase_load_x(C, xT_d)

    sg = [C.sb("sg%d" % i, [128, T], F32) for i in range(2)]
    C.xsq = sg
    C.hT = C.sb("hT", [128, NCH, T], BF16)
    C.hr = [Res("h%d" % c) for c in range(NCH)]
    C.rstd = C.sb("rstd", [128, T], F32)
    WB0 = C.sb("WB0", [128, 8192], BF16)
    WB1 = C.sb("WB1", [128, 8192], BF16)
    wq_r = [Res(), Res()]
    wg_r = [Res(), Res()]
    WB1_rs = wq_r + wg_r

    def wvb_ap(i):
        return (WB0 if i == 0 else WB1).t[:].rearrange("p (c n) -> p c n", c=NCH)

    def wvb_res(i):
        return [WB0.r] if i == 0 else WB1_rs

    wob_ap = WB0.t[:].rearrange("p (h n) -> p h n", h=HG)

    def wqb_ap(i):
        return WB1.t[:, i * 2048:(i + 1) * 2048].rearrange("p (c n) -> p c n", c=NCH)

    def wgb_ap(i):
        return WB1.t[:, 4096 + i * 2048:4096 + (i + 1) * 2048].rearrange("p (c n) -> p c n", c=NCH)

    kvbuf = [C.sb("kvbuf%d" % i, [128, 2048], BF16) for i in range(3)]
    qT = [C.sb("qT%d" % i, [128, T], BF16) for i in range(2)]
    PT = [C.sb("PT%d" % i, [128, 512], BF16) for i in range(4)]
    vo = PT
    Sb = [C.sb("Sb%d" % i, [128, 512], F32) for i in range(4)]
    og = C.sb("og", [128, HG, T], BF16)
    rs = C.sb("rs", [128, 512], F32)
    wgt = C.sb("wgt", [128, 512], F32)
    BT = [C.sb("BT%d" % i, [128, 640], F32) for i in range(2)]
    cmask = C.sb("cmask", [128, 640], F32)
    kmask = C.sb("kmask", [128, 2, 8], F32)
    cst = C.sb("cst", [128, 4, 128], F32)
    ND = [C.sb("ND%d" % s, [128, UMAX[s], NH], F32) for s in range(2)]
    SC = [C.sb("SC%d" % s, [128, UMAX[s], NH], F32) for s in range(2)]
    kval = [C.sb("kval%d" % s, [128, UMAX[s]], F32) for s in range(2)]
    NDQ = BT[0]
    NDQd = BT[1]
    dexp = [C.sb("dexp%d" % i, [128, 4, 128], BF16) for i in range(2)]
    fb = C.sb("fb", [128, NH], F32)
    ones_col = C.sb("ones_col", [128, 1], F32)
    xsqc = [C.sb("xsqc%d" % i, [128, 128], F32) for i in range(2)]
    zs = [C.sb("zs%d" % i, [128, NH], F32) for i in range(2)]
    rt = [C.sb("rt%d" % i, [128, 1], F32) for i in range(2)]
    zero = kvbuf[0]

    ident_ap = cst.t[:, 0, :]
    tri_ap = cst.t[:, 1, :]
    trim_ap = cst.t[:, 2, :]
    J_ap = cst.t[:, 3, :]
    for (t_, d_) in ((cmask, cmask_d), (kmask, kmask_d), (fb, fb_d), (kval[0], kval_d[0]), (kval[1], kval_d[1])):
        P.dma("sp", lambda e, t_=t_, d_=d_: e.dma_start(out=t_.t[:], in_=d_), writes=[t_.r])
    P.dma("sp", lambda e: e.dma_start(out=cst.t[:], in_=cst_d.rearrange("k p n -> p k n")), writes=[cst.r])
    P.op("pool", lambda e: e.memset(ones_col.t[:], 1.0), writes=[ones_col.r])
    P.op("pool", lambda e: e.memset(zero.t[:], 0.0), writes=[zero.r])
    for st in range(1):
        segs = (0, 1, 2) if st == 0 else (2,)
        for i in range(4):
            for sgp in segs:
                for half in range(2):
                    r0 = (i * NSEGP + sgp) * 1024 + half * 512
                    P.dma("sp", lambda e, st=st, r0=r0: e.dma_start(
                        out=KL[st][r0:r0 + 512, :].rearrange("(p a) n -> p (a n)", p=128), in_=zero.t[:]),
                        reads=[zero.r], writes=[pad_r])
        for b in range(8):
            for sgp in segs:
                r0 = (b * NSEGP + sgp) * 512
                P.dma("sp", lambda e, st=st, r0=r0: e.dma_start(
                    out=VL[st][r0:r0 + 512, :].rearrange("(p a) n -> p (a n)", p=128), in_=zero.t[:]),
                    reads=[zero.r], writes=[pad_r])
    P.dma("sp", lambda e: e.dma_start(out=NL[0:3 * 512, :].rearrange("(p a) n -> p (a n)", p=128),
                                      in_=zero.t[:, 0:384].bitcast(F32) if False else zero.t[:, 0:768].bitcast(F32)),
          reads=[zero.r], writes=[pad_r])

    dyn = {}

    def pre_sp(e):
        jj = e.snap(e.partition_id() % 4, min_val=0, max_val=3)
        dyn["k"] = e.snap(jj * KSEGE, min_val=0, max_val=3 * KSEGE)

    def pre_pool(e):
        jj = e.snap(e.partition_id() % 4, min_val=0, max_val=3)
        dyn["v"] = e.snap(jj * VSEGE, min_val=0, max_val=3 * VSEGE)
        dyn["n"] = e.snap(jj * NSEGE, min_val=0, max_val=3 * NSEGE)

    P.pre = {"sp": pre_sp, "pool": pre_pool}

    def localize(gates):
        for i in range(4):
            P.dma("sp", lambda e, i=i: e.dma_start(
                out=bass.AP(Kloc, i * 8 * KSEGE, [[32768, 128], [1, 32768]]),
                in_=bass.AP(KL[0], dyn["k"] + i * NSEGP * KSEGE, [[32768, 128], [1, 32768]])),
                reads=[KL_r[0][i][0], KL_r[0][i][1], pad_r], writes=[Kloc_r[i]])
        for p_ in range(4):
            P.dma("pool", lambda e, p_=p_: e.dma_start(
                out=bass.AP(Vloc, 2 * p_ * 8 * VSEGE, [[8 * VSEGE, 2], [16384, 128], [1, 16384]]),
                in_=bass.AP(VL[0], dyn["v"] + 2 * p_ * NSEGP * VSEGE, [[NSEGP * VSEGE, 2], [16384, 128], [1, 16384]])),
                reads=[VL_r[0][2 * p_][0], VL_r[0][2 * p_][1], VL_r[0][2 * p_ + 1][0], VL_r[0][2 * p_ + 1][1], pad_r],
                writes=[Vloc_r[p_]])
        if gates:
            P.dma("pool", lambda e: e.dma_start(
                out=bass.AP(Nloc, 0, [[1024, 128], [1, 1024]]),
                in_=bass.AP(NL, dyn["n"], [[1024, 128], [1, 1024]])),
                reads=[NL_r[0], NL_r[1], pad_r], writes=[Nloc_r])

    def kv_phase(st, gidx, wk_ap, wv_ap, gates):
        gn = phase_norm(C, gns_d[gidx], "g%d" % gidx, psG[0], psG[1])
        NT = T // 128
        if gates:
            fw_ap = Sb[0].t[:].rearrange("p (c h) -> p c h", c=NCH)
            gfw_ap = Sb[1].t[:].rearrange("p (c h) -> p c h", c=NCH)
            P.dma("sp", lambda e: e.dma_start(out=fw_ap, in_=fw_d), writes=[Sb[0].r])
            for c in range(NCH):
                P.op("pool", lambda e, c=c: e.tensor_scalar(out=gfw_ap[:, c, :], in0=fw_ap[:, c, :],
                                                            scalar1=gn.t[:, c:c + 1], scalar2=None, op0=ALU.mult),
                     reads=[Sb[0].r, gn.r], writes=[Sb[1].r])
            k = 0
            for tt in range(NT):
                pz = psS[tt % 2]
                pq = (psO, psSm)[tt % 2]
                tsl = slice(tt * 128, (tt + 1) * 128)
                for c in range(NCH):
                    sq = xsqc[k % 2]
                    k += 1
                    P.op("act", lambda e, c=c, sq=sq, tsl=tsl: e.activation(out=sq.t[:], in_=C.xT.t[:, c, tsl],
                                                                            func=AF.Square),
                         reads=[C.xr[c]], writes=[sq.r])
                    P.op("pe", lambda e, c=c, sq=sq, pq=pq: e.matmul(pq.t[:, 0:1], lhsT=sq.t[:], rhs=ones_col.t[:],
                                                                     start=(c == 0), stop=(c == NCH - 1)),
                         reads=[sq.r, ones_col.r], writes=[pq.r])
                    P.op("pe", lambda e, c=c, tsl=tsl, pz=pz: e.matmul(pz.t[:, 0:NH], lhsT=C.xT.t[:, c, tsl],
                                                                       rhs=gfw_ap[:, c, :],
                                                                       start=(c == 0), stop=(c == NCH - 1)),
                         reads=[C.xr[c], Sb[1].r], writes=[pz.r])
                r1 = rt[tt % 2]
                z = zs[tt % 2]
                P.op("act", lambda e, r1=r1, pq=pq: e.activation(out=r1.t[:], in_=pq.t[:, 0:1], func=AF.Sqrt, bias=EPS,
                                                                 scale=1.0 / D_MODEL), reads=[pq.r], writes=[r1.r])
                P.op("dve", lambda e, r1=r1: e.reciprocal(out=r1.t[:], in_=r1.t[:]), reads=[r1.r], writes=[r1.r])
                P.op("dve", lambda e, r1=r1, z=z, pz=pz: e.scalar_tensor_tensor(
                    out=z.t[:], in0=pz.t[:, 0:NH], scalar=r1.t[:, 0:1], in1=fb.t[:], op0=ALU.mult, op1=ALU.add),
                    reads=[pz.r, r1.r, fb.r], writes=[z.r])
                P.op("act", lambda e, z=z: e.activation(out=z.t[:], in_=z.t[:], func=AF.Exp, scale=-1.0),
                     reads=[z.r], writes=[z.r])
                P.op("act", lambda e, z=z: e.activation(out=z.t[:], in_=z.t[:], func=AF.Ln, bias=1.0, scale=1.0),
                     reads=[z.r], writes=[z.r])
                P.dma("sp", lambda e, z=z, tt=tt: e.dma_start(out=nlfo[tt * 128:(tt + 1) * 128, :], in_=z.t[:]),
                      reads=[z.r], writes=[nlfo_r])
            for sl in range(2):
                P.coll(("n", sl), lambda e, sl=sl: e.collective_compute(
                    "AllGather", ALU.bypass, replica_groups=GROUPS, ins=[nlfo[sl * 512:(sl + 1) * 512, :]],
                    outs=[NL[(3 + sl * 4) * 512:(3 + sl * 4 + 4) * 512, :]]), reads=[nlfo_r], writes=[NL_r[sl]])
        def load_wk(h):
            kb_ = kvbuf[h % 2]
            P.dma("pool", lambda e, kb_=kb_, h=h: e.dma_start(out=kb_.t[:].rearrange("p (c n) -> p c n", c=NCH),
                                                             in_=wk_ap(h)), writes=[kb_.r])
        load_wk(0)
        for h in range(NH):
            kb = kvbuf[h % 2]
            w_ap = kb.t[:].rearrange("p (c n) -> p c n", c=NCH)
            o = qT[h % 2]
            if h + 1 < NH:
                load_wk(h + 1)
            for hf in range(2):
                pp = psG[(2 * h + hf) % 4]
                for c in range(NCH):
                    P.op("pe", lambda e, c=c, pp=pp, w_ap=w_ap, hf=hf: e.matmul(
                        pp.t[:], lhsT=w_ap[:, c, :], rhs=C.hT.t[:, c, hf * 512:(hf + 1) * 512],
                        start=(c == 0), stop=(c == NCH - 1)), reads=[kb.r, C.hr[c]], writes=[pp.r])
                if hf == 0:
                    P.op("act", lambda e, o=o, pp=pp: e.activation(out=o.t[:, 0:512], in_=pp.t[:], func=AF.Copy),
                         reads=[pp.r], writes=[o.r])
                else:
                    P.op("dve", lambda e, o=o, pp=pp: e.tensor_copy(out=o.t[:, 512:1024], in_=pp.t[:]),
                         reads=[pp.r], writes=[o.r])
            for sl in range(2):
                r0 = ((h // 8) * 2 + sl) * 1024 + (h % 8) * 128
                P.dma("sp", lambda e, o=o, r0=r0, sl=sl: e.dma_start(out=kTo[st][r0:r0 + 128, :],
                                                                     in_=o.t[:, sl * 512:(sl + 1) * 512]),
                      reads=[o.r], writes=[kTo_r[st][h]])
            if h % 8 == 7:
                i = h // 8
                for sl in range(2):
                    P.coll(("k", i, sl), lambda e, i=i, sl=sl: e.collective_compute(
                        "AllGather", ALU.bypass, replica_groups=GROUPS,
                        ins=[kTo[st][(i * 2 + sl) * 1024:(i * 2 + sl + 1) * 1024, :]],
                        outs=[KL[st][(i * NSEGP + 3 + sl * 4) * 1024:(i * NSEGP + 3 + sl * 4 + 4) * 1024, :]]),
                        reads=kTo_r[st][i * 8:(i + 1) * 8], writes=[KL_r[st][i][sl]])
        k = 0
        def load_wv(b):
            P.dma("pool", lambda e, b=b: e.dma_start(out=wvb_ap(b % 2), in_=wv_ap(b)), writes=wvb_res(b % 2))
        load_wv(0)
        for b in range(8):
            w_ap = wvb_ap(b % 2)
            w_rs = wvb_res(b % 2)
            if b + 1 < 8:
                load_wv(b + 1)
            for tt in range(NT):
                pp = psG[k % 4]
                o = vo[k % 4]
                for c in range(NCH):
                    P.op("pe", lambda e, c=c, tt=tt, w_ap=w_ap, pp=pp: e.matmul(
                        pp.t[:], lhsT=C.hT.t[:, c, tt * 128:(tt + 1) * 128], rhs=w_ap[:, c, :],
                        start=(c == 0), stop=(c == NCH - 1)), reads=w_rs + [C.hr[c]], writes=[pp.r])
                if k % 2 == 0:
                    P.op("act", lambda e, o=o, pp=pp: e.activation(out=o.t[:], in_=pp.t[:], func=AF.Copy),
                         reads=[pp.r], writes=[o.r])
                else:
                    P.op("dve", lambda e, o=o, pp=pp: e.tensor_copy(out=o.t[:], in_=pp.t[:]), reads=[pp.r], writes=[o.r])
                r0 = (b * 2 + tt // 4) * 512 + (tt % 4) * 128
                P.dma("sp", lambda e, o=o, r0=r0: e.dma_start(out=Vo[st][r0:r0 + 128, :], in_=o.t[:]),
                      reads=[o.r], writes=[Vo_r[st][b]])
                k += 1
            for sl in range(2):
                P.coll(("v", b, sl), lambda e, b=b, sl=sl: e.collective_compute(
                    "AllGather", ALU.bypass, replica_groups=GROUPS,
                    ins=[Vo[st][(b * 2 + sl) * 512:(b * 2 + sl + 1) * 512, :]],
                    outs=[VL[st][(b * NSEGP + 3 + sl * 4) * 512:(b * NSEGP + 3 + sl * 4 + 4) * 512, :]]),
                    reads=[Vo_r[st][b]], writes=[VL_r[st][b][sl]])

    def decay_prep():
        nlf_t = sg[0].t[:].rearrange("p (u h) -> p u h", u=32)
        tot_t = sg[1].t[:].rearrange("p (u h) -> p u h", u=32)
        for s in range(2):
            U = UMAX[s]
            for a in range(U // 4):
                off = (3 + 4 * s - a) * NSEGE
                P.dma("sp", lambda e, a=a, off=off: e.dma_start(
                    out=nlf_t[:, 4 * a:4 * a + 4, :],
                    in_=bass.AP(Nloc, off, [[NH, 128], [128 * NH, 4], [1, NH]])),
                    reads=[Nloc_r], writes=[sg[0].r])
            nflat = sg[0].t[:, 0:U * NH]
            ndflat = ND[s].t[:].rearrange("p u h -> p (u h)")
            totflat = sg[1].t[:, 0:U * NH]
            for j in range(U * NH // 512):
                sl_ = slice(j * 512, (j + 1) * 512)
                pa = psG[(2 * j) % 4]
                pb = psG[(2 * j + 1) % 4]
                P.op("pe", lambda e, pa=pa, sl_=sl_, nflat=nflat: e.matmul(pa.t[:], lhsT=tri_ap, rhs=nflat[:, sl_],
                                                                           start=True, stop=True),
                     reads=[cst.r, sg[0].r], writes=[pa.r])
                P.op("pe", lambda e, pb=pb, sl_=sl_, nflat=nflat: e.matmul(pb.t[:], lhsT=C.ones_f.t[:], rhs=nflat[:, sl_],
                                                                           start=True, stop=True),
                     reads=[C.ones_f.r, sg[0].r], writes=[pb.r])
                P.op("act", lambda e, pa=pa, sl_=sl_, ndflat=ndflat: e.activation(out=ndflat[:, sl_], in_=pa.t[:],
                                                                                  func=AF.Copy),
                     reads=[pa.r], writes=[ND[s].r])
                P.op("dve", lambda e, pb=pb, sl_=sl_, totflat=totflat: e.tensor_copy(out=totflat[:, sl_], in_=pb.t[:]),
                     reads=[pb.r], writes=[sg[1].r])

            def uof(v):
                return 4 * (v // 4) + 3 - (v % 4)
            for v in range(1, U):
                u1, u0 = uof(v), uof(v - 1)
                P.op("dve", lambda e, s=s, u1=u1, u0=u0: e.tensor_tensor(out=ND[s].t[:, u1, :], in0=ND[s].t[:, u1, :],
                                                                         in1=tot_t[:, u0, :], op=ALU.add),
                     reads=[sg[1].r, ND[s].r], writes=[ND[s].r])
                if v < U - 1:
                    P.op("dve", lambda e, u1=u1, u0=u0: e.tensor_tensor(out=tot_t[:, u1, :], in0=tot_t[:, u1, :],
                                                                        in1=tot_t[:, u0, :], op=ALU.add),
                         reads=[sg[1].r], writes=[sg[1].r])
            P.op("dve", lambda e, s=s, U=U: e.tensor_tensor(
                out=SC[s].t[:], in0=kval[s].t[:].unsqueeze(2).to_broadcast([128, U, NH]), in1=ND[s].t[:],
                op=ALU.subtract), reads=[kval[s].r, ND[s].r], writes=[SC[s].r])

    cnt = {"K": 0, "U": 0, "A": 0}

    def mix_phase(isb, st, gidx, wq_ap, wg_ap, wo_ap, rb_layer):
        NU = UMAX if isb else (8, 8)
        if isb:
            phase_norm(C, gns_d[gidx], "g%d" % gidx, psG[0], psG[1])
        if isb and gidx == 3:
            decay_prep()
        for grp in range(NH // HG):
            P.dma("pool", lambda e, grp=grp: e.dma_start(out=wob_ap, in_=wo_ap(grp)), writes=[WB0.r])
            for hh in range(HG):
                h = grp * HG + hh
                i8 = h // 8
                b4 = h // 4
                wq_, wg_ = wqb_ap(h % 2), wgb_ap(h % 2)

                def load_qg(h2):
                    P.dma("pool", lambda e, h2=h2: e.dma_start(out=wqb_ap(h2 % 2), in_=wq_ap(h2)), writes=[wq_r[h2 % 2]])
                    P.dma("pool", lambda e, h2=h2: e.dma_start(out=wgb_ap(h2 % 2), in_=wg_ap(h2)), writes=[wg_r[h2 % 2]])
                if h == 0:
                    load_qg(0)
                if h + 1 < NH:
                    load_qg(h + 1)
                q_ = qT[h % 2]
                g_ = sg[h % 2]
                for hf in range(2):
                    pp = psG2[hf]
                    for c in range(NCH):
                        P.op("pe", lambda e, c=c, pp=pp, wq_=wq_, hf=hf: e.matmul(
                            pp.t[:], lhsT=wq_[:, c, :], rhs=C.hT.t[:, c, hf * 512:(hf + 1) * 512],
                            start=(c == 0), stop=(c == NCH - 1)), reads=[wq_r[h % 2], C.hr[c]], writes=[pp.r])
                    P.op("dve", lambda e, q_=q_, pp=pp, hf=hf: e.tensor_scalar(
                        out=q_.t[:, hf * 512:(hf + 1) * 512], in0=pp.t[:], scalar1=SCALE, scalar2=None, op0=ALU.mult),
                        reads=[pp.r], writes=[q_.r])
                for hf in range(2):
                    pp = psG2[hf]
                    for c in range(NCH):
                        P.op("pe", lambda e, c=c, pp=pp, wg_=wg_, hf=hf: e.matmul(
                            pp.t[:], lhsT=wg_[:, c, :], rhs=C.hT.t[:, c, hf * 512:(hf + 1) * 512],
                            start=(c == 0), stop=(c == NCH - 1)), reads=[wg_r[h % 2], C.hr[c]], writes=[pp.r])
                    P.op("act", lambda e, g_=g_, pp=pp, hf=hf: e.activation(
                        out=g_.t[:, hf * 512:(hf + 1) * 512], in_=pp.t[:], func=AF.Silu),
                        reads=[pp.r], writes=[g_.r])
                if not isb:
                    bt = BT[h % 2]
                    src = bass.AP(rbx_d.tensor, (rb_layer * NH + h) * 767, [[1, 128], [1, 640]])
                    P.dma("sp", lambda e, bt=bt, src=src: e.dma_start(out=bt.t[:], in_=src), writes=[bt.r])
                    pj = (psG2[0], psG2[1])
                    P.op("pe", lambda e, bt=bt: e.matmul(pj[0].t[:], lhsT=J_ap, rhs=bt.t[:, 0:512], start=True, stop=True),
                         reads=[cst.r, bt.r], writes=[pj[0].r])
                    P.op("pe", lambda e, bt=bt: e.matmul(pj[1].t[:, 0:128], lhsT=J_ap, rhs=bt.t[:, 512:640], start=True,
                                                         stop=True), reads=[cst.r, bt.r], writes=[pj[1].r])
                    P.op("dve", lambda e, bt=bt: e.tensor_tensor(out=bt.t[:, 0:512], in0=pj[0].t[:], in1=cmask.t[:, 0:512],
                                                                 op=ALU.add), reads=[pj[0].r, cmask.r], writes=[bt.r])
                    P.op("dve", lambda e, bt=bt: e.tensor_tensor(out=bt.t[:, 512:640], in0=pj[1].t[:, 0:128],
                                                                 in1=cmask.t[:, 512:640], op=ALU.add),
                         reads=[pj[1].r, cmask.r], writes=[bt.r])
                for s in range(2):
                    U = NU[s]
                    if isb:
                        dx = dexp[(2 * h + s) % 2]
                        for i in range(4):
                            P.op("pool", lambda e, dx=dx, i=i, s=s, h=h: e.tensor_tensor(
                                out=dx.t[:, i, :], in0=ident_ap, in1=ND[s].t[:, i, h:h + 1].to_broadcast([128, 128]),
                                op=ALU.mult), reads=[cst.r, ND[s].r], writes=[dx.r])
                        pq = psG2[(2 * h + s) % 2]
                        P.op("pe", lambda e, dx=dx, pq=pq: e.matmul(pq.t[:], lhsT=C.ones_b.t[:],
                                                                    rhs=dx.t[:].rearrange("p i q -> p (i q)"),
                                                                    start=True, stop=True),
                             reads=[C.ones_b.r, dx.r], writes=[pq.r])
                        P.op("act", lambda e, pq=pq: e.activation(out=NDQ.t[:, 0:512], in_=pq.t[:], func=AF.Copy),
                             reads=[pq.r], writes=[NDQ.r])
                        P.op("pool", lambda e: e.tensor_tensor(
                            out=NDQd.t[:, 0:512].rearrange("p (i q) -> p i q", i=4),
                            in0=NDQ.t[:, 0:512].rearrange("p (i q) -> p i q", i=4),
                            in1=trim_ap.unsqueeze(1).to_broadcast([128, 4, 128]), op=ALU.add),
                            reads=[NDQ.r, cst.r], writes=[NDQd.r])
                    first = True
                    units = []
                    chunk_loads = []
                    psO = psOs[cnt["A"] % 2]
                    psSm = psSms[cnt["A"] % 2]
                    cnt["A"] += 1
                    for ch in range(U // 8):
                        kb = kvbuf[cnt["K"] % 3]
                        cnt["K"] += 1
                        kx_ap = kb.t[:, 0:1024]
                        vx_ap = kb.t[:, 1024:2048].rearrange("p (u d) -> p u d", u=8)
                        kdeps = [Kloc_r[i8]]
                        vdeps = [Vloc_r[b4 // 2]]
                        def load_chunk(kb=kb, kx_ap=kx_ap, vx_ap=vx_ap, ch=ch, kdeps=kdeps, vdeps=vdeps, s=s, h=h,
                                       i8=i8, b4=b4):
                            for s2 in range(2):
                                if isb:
                                    sgp = 3 + 4 * s - (2 * ch + s2)
                                else:
                                    sgp = 3 + 4 * s - 1 + s2
                                koff = ((i8 * 8 + sgp) * 1024 + (h % 8) * 128) * 512
                                voff = ((b4 * 8 + sgp) * 512) * 512 + (h % 4) * 128
                                P.dma("sp", lambda e, s2=s2, koff=koff: e.dma_start(
                                    out=kx_ap[:, s2 * 512:(s2 + 1) * 512],
                                    in_=bass.AP(Kloc, koff, [[512, 128], [1, 512]])),
                                    reads=kdeps, writes=[kb.r])
                                P.dma("sp", lambda e, s2=s2, voff=voff: e.dma_start(
                                    out=vx_ap[:, s2 * 4:(s2 + 1) * 4, :],
                                    in_=bass.AP(Vloc, voff, [[512, 128], [128 * 512, 4], [1, 128]])),
                                    reads=vdeps, writes=[kb.r])
                        chunk_loads.append(load_chunk)
                        if ch == 0:
                            order = [3, 4, 0, 1, 2, 5, 6, 7] if not isb else list(range(8))
                        else:
                            order = list(range(8))
                        for ul in order:
                            u = ch * 8 + ul
                            if not isb:
                                ilo, ihi = max(0, u - 4), min(3, u)
                                rlo = ilo + 4 - u
                                ncol = (ihi - ilo + 1) * 128
                                adds = [(0, ncol, BT[h % 2].t[:, rlo * 128:rlo * 128 + ncol], BT[h % 2].r)]
                                sc_ap = kmask.t[:, s, u:u + 1]
                                ares = [kmask.r]
                            else:
                                if u < 4:
                                    ilo, ihi = u, 3
                                    ncol = (ihi - ilo + 1) * 128
                                    adds = [(0, 128, NDQd.t[:, ilo * 128:(ilo + 1) * 128], NDQd.r)]
                                    if ncol > 128:
                                        adds.append((128, ncol - 128, NDQ.t[:, (ilo + 1) * 128:512], NDQ.r))
                                else:
                                    ilo, ihi = 0, 3
                                    ncol = 512
                                    adds = [(0, 512, NDQ.t[:, 0:512], NDQ.r)]
                                sc_ap = SC[s].t[:, u, h:h + 1]
                                ares = [SC[s].r]
                            c0 = ilo * 128
                            n_ = cnt["U"]
                            cnt["U"] += 1
                            units.append((kx_ap[:, ul * 128:(ul + 1) * 128], vx_ap[:, ul, :],
                                          q_.t[:, s * 512 + c0:s * 512 + c0 + ncol], ncol, sc_ap, adds, c0, first,
                                          kb.r, q_.r, ares, psS[n_ % 3], Sb[n_ % 4], PT[n_ % 4]))
                            first = False
                    LA = 2
                    for idx in range(len(units) + LA):
                        if idx < len(units) and idx % 8 == 0:
                            ch_ = idx // 8
                            if ch_ == 0:
                                chunk_loads[0]()
                            if ch_ + 1 < len(chunk_loads):
                                chunk_loads[ch_ + 1]()
                        if idx < len(units):
                            (kt_, v_, qa_, ncol, sc_ap, adds, c0, fst, kr_, qr_, ares, pS_, Sb_, PT_) = units[idx]
                            P.op("pe", lambda e, pS_=pS_, ncol=ncol, kt_=kt_, qa_=qa_: e.matmul(
                                pS_.t[:, 0:ncol], lhsT=kt_, rhs=qa_, start=True, stop=True),
                                reads=[kr_, qr_], writes=[pS_.r])
                        if idx >= LA:
                            (kt_, v_, qa_, ncol, sc_ap, adds, c0, fst, kr_, qr_, ares, pS_, Sb_, PT_) = units[idx - LA]
                            for (o_, n2, ap_, r_) in adds:
                                P.op("dve", lambda e, o_=o_, n2=n2, ap_=ap_, pS_=pS_, Sb_=Sb_, sc_ap=sc_ap: e.scalar_tensor_tensor(
                                    out=Sb_.t[:, o_:o_ + n2], in0=pS_.t[:, o_:o_ + n2], scalar=sc_ap, in1=ap_,
                                    op0=ALU.add, op1=ALU.add), reads=[pS_.r, r_] + list(ares), writes=[Sb_.r])
                            P.op("act", lambda e, PT_=PT_, Sb_=Sb_, ncol=ncol: e.activation(
                                out=PT_.t[:, 0:ncol], in_=Sb_.t[:, 0:ncol], func=AF.Exp), reads=[Sb_.r], writes=[PT_.r])
                            P.op("pe", lambda e, v_=v_, PT_=PT_, ncol=ncol, c0=c0, fst=fst, psO=psO: e.matmul(
                                psO.t[:, c0:c0 + ncol], lhsT=v_, rhs=PT_.t[:, 0:ncol], start=fst, stop=False,
                                skip_group_check=True), reads=[kr_, PT_.r], writes=[psO.r])
                            P.op("pe", lambda e, PT_=PT_, ncol=ncol, c0=c0, fst=fst, psSm=psSm: e.matmul(
                                psSm.t[:, c0:c0 + ncol], lhsT=C.ones_b.t[:], rhs=PT_.t[:, 0:ncol], start=fst, stop=False,
                                skip_group_check=True), reads=[C.ones_b.r, PT_.r], writes=[psSm.r])
                    P.op("dve", lambda e, psSm=psSm: e.reciprocal(out=rs.t[:], in_=psSm.t[:]), reads=[psSm.r], writes=[rs.r])
                    P.op("pool", lambda e, g_=g_, s=s: e.tensor_tensor(out=wgt.t[:], in0=rs.t[:],
                                                                       in1=g_.t[:, s * 512:(s + 1) * 512], op=ALU.mult),
                         reads=[rs.r, g_.r], writes=[wgt.r])
                    P.op("dve", lambda e, hh=hh, s=s, psO=psO: e.tensor_tensor(out=og.t[:, hh, s * 512:(s + 1) * 512],
                                                                      in0=psO.t[:], in1=wgt.t[:], op=ALU.mult),
                         reads=[psO.r, wgt.r], writes=[og.r])
            k = 0
            for c in range(NCH):
                for hf in range(2):
                    pp = psG2[k % 2]
                    k += 1
                    for hh in range(HG):
                        P.op("pe", lambda e, pp=pp, hh=hh, c=c, hf=hf: e.matmul(
                            pp.t[:], lhsT=wob_ap[:, hh, c * 128:(c + 1) * 128], rhs=og.t[:, hh, hf * 512:(hf + 1) * 512],
                            start=(hh == 0), stop=(hh == HG - 1)), reads=[WB0.r, og.r], writes=[pp.r])
                    P.op("dve", lambda e, pp=pp, c=c, hf=hf: e.tensor_tensor(
                        out=C.xT.t[:, c, hf * 512:(hf + 1) * 512], in0=pp.t[:], in1=C.xT.t[:, c, hf * 512:(hf + 1) * 512],
                        op=ALU.add), reads=[pp.r, C.xr[c]], writes=[C.xr[c]])

    for l in range(2):
        kv_phase(0, l, lambda h, l=l: a_wk_d[l, h], lambda b, l=l: a_wv_d[l, b], False)
        localize(False)
        mix_phase(False, 0, l, lambda h, l=l: a_wq_d[l, h], lambda h, l=l: a_wg_d[l, h], lambda g, l=l: a_wo_d[l, g], l)
    kv_phase(0, 2, lambda h: s_wk_d[h], lambda b: s_wv_d[b], True)
    localize(True)
    for l in range(2):
        mix_phase(True, 0, 3 + l, lambda h, l=l: b_wq_d[l, h], lambda h, l=l: b_wg_d[l, h], lambda g, l=l: b_wo_d[l, g], 0)

    gf = C.sb("gfin", [128, NCH], F32)
    P.dma("sp", lambda e: e.dma_start(out=gf.t[:], in_=gns_d[5]), writes=[gf.r])
    pss = (psG[0], psG[1])
    for c in range(NCH):
        sq = sg[c % 2]
        P.op("act", lambda e, c=c, sq=sq: e.activation(out=sq.t[:], in_=C.xT.t[:, c, :], func=AF.Square),
             reads=[C.xr[c]], writes=[sq.r])
        for hf in range(2):
            P.op("pe", lambda e, c=c, sq=sq, hf=hf: e.matmul(
                pss[hf].t[:], lhsT=C.ones_f.t[:], rhs=sq.t[:, hf * 512:(hf + 1) * 512],
                start=(c == 0), stop=(c == NCH - 1)), reads=[sq.r, C.ones_f.r], writes=[pss[hf].r])
    for hf in range(2):
        sl = slice(hf * 512, (hf + 1) * 512)
        P.op("act", lambda e, hf=hf, sl=sl: e.activation(
            out=C.rstd.t[:, sl], in_=pss[hf].t[:], func=AF.Sqrt, bias=EPS, scale=1.0 / D_MODEL),
            reads=[pss[hf].r], writes=[C.rstd.r])
    P.op("dve", lambda e: e.reciprocal(out=C.rstd.t[:], in_=C.rstd.t[:]), reads=[C.rstd.r], writes=[C.rstd.r])
    for c in range(NCH):
        yb = sg[c % 2]
        P.op("dve", lambda e, c=c, yb=yb: e.scalar_tensor_tensor(
            out=yb.t[:], in0=C.xT.t[:, c, :], scalar=gf.t[:, c:c + 1], in1=C.rstd.t[:],
            op0=ALU.mult, op1=ALU.mult), reads=[C.xr[c], gf.r, C.rstd.r], writes=[yb.r])
        C.outs.append(P.dma("sp", lambda e, c=c, yb=yb: e.dma_start(out=yo_d[:, c, :], in_=yb.t[:]), reads=[yb.r]))
    return C.finish()


def kernel(x, a_norm, a_w_in, a_rel_bias, a_w_out, kv_norm, kv_w, f_w, f_b, b_norm, b_w_in, b_w_out, final_norm):
    f = lambda a: np.asarray(a, np.float32)
    x = f(x)
    a_w_in, a_w_out, kv_w, b_w_in, b_w_out = f(a_w_in), f(a_w_out), f(kv_w), f(b_w_in), f(b_w_out)
    gns = np.stack([tile_vec(f(a_norm)[0]), tile_vec(f(a_norm)[1]), tile_vec(f(kv_norm)), tile_vec(f(b_norm)[0]),
                    tile_vec(f(b_norm)[1]), tile_vec(f(final_norm))])
    shared = {
        "gns": gns,
        "a_wq": np.stack([tile_w_cols(a_w_in[l], 0, D_INNER, 128) for l in range(2)]),
        "a_wk": np.stack([tile_w_cols(a_w_in[l], D_INNER, D_INNER, 128) for l in range(2)]),
        "a_wv": np.stack([tile_w_cols(a_w_in[l], 2 * D_INNER, D_INNER, 512) for l in range(2)]),
        "a_wg": np.stack([tile_w_cols(a_w_in[l], 3 * D_INNER, D_INNER, 128) for l in range(2)]),
        "a_wo": np.stack([_tile_wo(a_w_out[l]) for l in range(2)]),
        "s_wk": tile_w_cols(kv_w, 0, D_INNER, 128),
        "s_wv": tile_w_cols(kv_w, D_INNER, D_INNER, 512),
        "b_wq": np.stack([tile_w_cols(b_w_in[l], 0, D_INNER, 128) for l in range(2)]),
        "b_wg": np.stack([tile_w_cols(b_w_in[l], D_INNER, D_INNER, 128) for l in range(2)]),
        "b_wo": np.stack([_tile_wo(b_w_out[l]) for l in range(2)]),
        "fw": np.ascontiguousarray(f(f_w).reshape(NCH, 128, NH).transpose(1, 0, 2)),
        "fb": np.ascontiguousarray(np.broadcast_to(f(f_b)[None, :], (128, NH))),
    }
    idx = np.clip(np.arange(767) - 127, -256, 256) + 256
    shared["rbx"] = np.ascontiguousarray(f(a_rel_bias)[:, :, idx])
    cmask = np.zeros((128, 5, 128), np.float32)
    cmask[64:, 0, :64] = NEG
    cmask[:64, 4, 64:] = NEG
    shared["cmask"] = cmask.reshape(128, 640)
    kl = np.arange(128)
    cst = np.zeros((4, 128, 128), np.float32)
    cst[0] = np.eye(128, dtype=np.float32)
    cst[1] = (kl[:, None] > kl[None, :]).astype(np.float32)
    cst[2] = np.where(kl[:, None] > kl[None, :], NEG, 0.0)
    cst[3] = np.eye(128, dtype=np.float32)[::-1]
    shared["cst"] = cst
    in_maps = []
    for j in range(8):
        jj = j % 4
        m = dict(shared)
        m["xT"] = to_xT(x[j // 4][core_tokens(j)])
        kmask = np.zeros((128, 2, 8), np.float32)
        if jj == 0:
            kmask[:, 0, :4] = NEG
        m["kmask"] = kmask
        for s in range(2):
            kv = np.full((128, UMAX[s]), NEG, np.float32)
            for u in range(UMAX[s]):
                if jj + 4 * s - u // 4 >= 0:
                    kv[:, u] = 0.0
            m["kval%d" % s] = kv
        in_maps.append(m)
    if "fused" not in _NC_CACHE:
        _NC_CACHE["fused"] = build_fused()
    res = run_bass_kernel_spmd(_NC_CACHE["fused"], in_maps, core_ids=list(range(8)))
    out = np.zeros((BATCH, SEQ, D_MODEL), np.float32)
    for j in range(8):
        out[j // 4][core_tokens(j)] = from_xT(res.results[j]["yo"])
    return out
```

```python
import numpy as np
from contextlib import ExitStack
import ml_dtypes
import concourse.bass as bass
import concourse.mybir as mybir
from concourse.bass_utils import run_bass_kernel_spmd

F32 = mybir.dt.float32
BF16 = mybir.dt.bfloat16
AF = mybir.ActivationFunctionType
ALU = mybir.AluOpType
NPBF = ml_dtypes.bfloat16

D_MODEL = 2048
NCH = 16
D_INNER = 4096
NH = 32
DH = 128
SEQ = 4096
BATCH = 2
T = 1024
SEG = 512
NSEG = 2
EPS = 1e-6
NEG = -30000.0
SCALE = DH ** -0.5
HG = 4
UMAX = (16, 32)


class Res:
    __slots__ = ("name", "w", "rs")

    def __init__(self, name=""):
        self.name = name
        self.w = None
        self.rs = {}


class Op:
    __slots__ = ("eng", "fn", "deps", "dma", "sig", "need", "n", "cc")

    def __init__(self, eng, fn, dma, cc=None):
        self.eng = eng
        self.fn = fn
        self.dma = dma
        self.deps = []
        self.sig = None
        self.need = dma
        self.n = 0
        self.cc = cc


class Prog:
    ENGS = ("pe", "act", "dve", "pool", "sp")
    NDSEM = 24
    EPOCH = 30000

    def __init__(self, nc):
        self.nc = nc
        self.ops = {e: [] for e in self.ENGS}
        self.ndma = {e: 0 for e in self.ENGS}
        self.count = 0
        self.ccs = {}

    def _track(self, o, reads, writes):
        deps = {}
        for r in reads:
            if r.w is not None:
                deps[id(r.w)] = r.w
        for r in writes:
            if r.w is not None:
                deps[id(r.w)] = r.w
            for x in r.rs.values():
                deps[id(x)] = x
        for d in deps.values():
            if d is o:
                continue
            if (not d.dma) and (not o.dma) and d.eng == "pe" and o.eng == "pe":
                continue
            d.need = True
            o.deps.append(d)
        for r in reads:
            key = ("dma", self.count) if o.dma else o.eng
            r.rs[key] = o
        for r in writes:
            r.w = o
            r.rs = {}
        self.count += 1

    def op(self, eng, fn, reads=(), writes=()):
        o = Op(eng, fn, False)
        self._track(o, reads, writes)
        self.ops[eng].append(o)
        return o

    def dma(self, q, fn, reads=(), writes=()):
        o = Op(q, fn, True)
        self._track(o, reads, writes)
        self.ops[q].append(o)
        return o

    def coll(self, key, fn, reads=(), writes=()):
        o = Op("pool", fn, True, cc=key)
        self._track(o, reads, writes)
        self.ops["pool"].append(o)
        return o

    def emit(self, es, final_deps):
        nc = self.nc
        fin = Op("sp", None, False)
        for d in final_deps:
            d.need = True
            fin.deps.append(d)
        self.ops["sp"].append(fin)
        sems = {}
        for e in self.ENGS:
            cnt = 0
            nd = 0
            esems = []
            dsems = []
            for o in self.ops[e]:
                if o.cc is not None:
                    if o.cc not in self.ccs:
                        self.ccs[o.cc] = [es.enter_context(nc.semaphore("cc_%s" % str(o.cc))), 0]
                    self.ccs[o.cc][1] += 1
                    o.sig = (self.ccs[o.cc][0], self.ccs[o.cc][1])
                elif o.dma:
                    k = nd % self.NDSEM
                    if k >= len(dsems):
                        dsems.append(es.enter_context(nc.semaphore("d_%s_%d" % (e, k))))
                    o.sig = (dsems[k], 16 * (nd // self.NDSEM + 1))
                    o.n = nd
                    nd += 1
                elif o.need:
                    ep = cnt // self.EPOCH
                    if ep >= len(esems):
                        esems.append(es.enter_context(nc.semaphore("c_%s_%d" % (e, ep))))
                    o.sig = (esems[ep], cnt % self.EPOCH + 1)
                    cnt += 1
            sems[e] = (esems, dsems)
        engobj = {"pe": nc.tensor, "act": nc.scalar, "dve": nc.vector, "pool": nc.gpsimd, "sp": nc.sync}
        block = es.enter_context(nc.Block())

        def make(e):
            def body(eng):
                waited = {}
                pre = getattr(self, "pre", {}).get(e)
                if pre is not None:
                    pre(eng)
                for o in self.ops[e]:
                    ws = []
                    for d in o.deps:
                        ws.append(d.sig)
                    if o.dma and o.cc is None and o.n >= self.NDSEM:
                        ws.append((o.sig[0], o.sig[1] - 16))
                    for (s, v) in ws:
                        if waited.get(id(s), 0) < v:
                            waited[id(s)] = v
                            eng.wait_ge(s, v)
                    if o.fn is None:
                        continue
                    ins = o.fn(eng)
                    if o.cc is not None:
                        ins.then_inc(o.sig[0], 1)
                    elif o.dma:
                        ins.then_inc(o.sig[0], 16)
                    elif o.sig is not None:
                        ins.then_inc(o.sig[0], 1)
            return body

        block.tensor(make("pe"))
        block.scalar(make("act"))
        block.vector(make("dve"))
        block.gpsimd(make("pool"))
        block.sync(make("sp"))


class Tl:
    __slots__ = ("t", "r")

    def __init__(self, t, name):
        self.t = t
        self.r = Res(name)


class Ctx:
    def __init__(self):
        self.nc = bass.Bass("TRN2", target_bir_lowering=False)
        self.P = Prog(self.nc)
        self.es = ExitStack()
        self.outs = []
        self.nps = 0

    def dram(self, name, shape, dt, kind):
        return self.nc.dram_tensor(name, list(shape), dt, kind=kind).ap()

    def sb(self, name, shape, dt):
        return Tl(self.es.enter_context(self.nc.sbuf_tensor("s_" + name, list(shape), dt)), name)

    def ps(self, name, dt=F32):
        n = 512 if dt == F32 else 1024
        return Tl(self.es.enter_context(self.nc.psum_tensor("p_" + name, [128, n], dt)), name)

    def finish(self):
        self.P.emit(self.es, self.outs)
        self.es.close()
        return self.nc


def phase_consts(C):
    P = C.P
    C.ones_f = C.sb("ones_f", [128, 128], F32)
    C.ones_b = C.sb("ones_b", [128, 128], BF16)
    P.op("pool", lambda e: e.memset(C.ones_f.t[:], 1.0), writes=[C.ones_f.r])
    P.op("pool", lambda e: e.memset(C.ones_b.t[:], 1.0), writes=[C.ones_b.r])


def phase_load_x(C, xT_d):
    P = C.P
    C.xT = C.sb("xT", [128, NCH, T], F32)
    C.xr = [Res("x%d" % c) for c in range(NCH)]
    for c0 in range(0, NCH, 4):
        P.dma("sp", lambda e, c0=c0: e.dma_start(out=C.xT.t[:, c0:c0 + 4, :], in_=xT_d[:, c0:c0 + 4, :]),
              writes=C.xr[c0:c0 + 4])


def phase_norm(C, gn_d, tag, psA, psB, want_tok_rstd=False):
    P = C.P
    if not hasattr(C, "hT"):
        C.hT = C.sb("hT", [128, NCH, T], BF16)
        C.hr = [Res("h%d" % c) for c in range(NCH)]
        C.xsq = [C.sb("xsq%d" % i, [128, T], F32) for i in range(2)]
        C.rstd = C.sb("rstd", [128, T], F32)
    gn = C.sb("gn_" + tag, [128, NCH], F32)
    P.dma("sp", lambda e: e.dma_start(out=gn.t[:], in_=gn_d), writes=[gn.r])
    pss = (psA, psB)
    for c in range(NCH):
        sq = C.xsq[c % 2]
        P.op("act", lambda e, c=c, sq=sq: e.activation(out=sq.t[:], in_=C.xT.t[:, c, :], func=AF.Square),
             reads=[C.xr[c]], writes=[sq.r])
        for hf in range(2):
            P.op("pe", lambda e, c=c, sq=sq, hf=hf: e.matmul(
                pss[hf].t[:], lhsT=C.ones_f.t[:], rhs=sq.t[:, hf * 512:(hf + 1) * 512],
                start=(c == 0), stop=(c == NCH - 1)),
                reads=[sq.r, C.ones_f.r], writes=[pss[hf].r])
    for hf in range(2):
        sl = slice(hf * 512, (hf + 1) * 512)
        P.op("act", lambda e, hf=hf, sl=sl: e.activation(
            out=C.rstd.t[:, sl], in_=pss[hf].t[:], func=AF.Sqrt, bias=EPS, scale=1.0 / D_MODEL),
            reads=[pss[hf].r], writes=[C.rstd.r])
    P.op("dve", lambda e: e.reciprocal(out=C.rstd.t[:], in_=C.rstd.t[:]), reads=[C.rstd.r], writes=[C.rstd.r])
    for c in range(NCH):
        P.op("dve", lambda e, c=c: e.scalar_tensor_tensor(
            out=C.hT.t[:, c, :], in0=C.xT.t[:, c, :], scalar=gn.t[:, c:c + 1], in1=C.rstd.t[:],
            op0=ALU.mult, op1=ALU.mult), reads=[C.xr[c], gn.r, C.rstd.r], writes=[C.hr[c]])
    return gn


def proj_fm(C, wtile, ps, hf, extra_reads=()):
    P = C.P
    for c in range(NCH):
        P.op("pe", lambda e, c=c: e.matmul(ps.t[:], lhsT=wtile.t[:, c, :], rhs=C.hT.t[:, c, hf * 512:(hf + 1) * 512],
                                           start=(c == 0), stop=(c == NCH - 1)),
             reads=[wtile.r, C.hr[c]] + list(extra_reads), writes=[ps.r])


def build_kv():
    C = Ctx()
    P = C.P
    xT_d = C.dram("xT", [128, NCH, T], F32, "ExternalInput")
    gn_d = C.dram("gn", [128, NCH], F32, "ExternalInput")
    wk_d = C.dram("wk", [NH, 128, NCH, 128], F32, "ExternalInput")
    wv_d = C.dram("wv", [8, 128, NCH, 512], F32, "ExternalInput")
    fw_d = C.dram("fw", [128, NCH, NH], F32, "ExternalInput")
    fb_d = C.dram("fb", [128, NH], F32, "ExternalInput")
    kT_o = C.dram("kT", [NH, 128, T], BF16, "ExternalOutput")
    V_o = C.dram("V", [T // 128, 128, D_INNER], BF16, "ExternalOutput")
    nlf_o = C.dram("nlf", [T // 128, 128, NH], F32, "ExternalOutput")

    ps = [C.ps("ps%d" % i) for i in range(8)]
    phase_consts(C)
    phase_load_x(C, xT_d)
    gn = phase_norm(C, gn_d, "kv", ps[0], ps[1])

    fw = C.sb("fw", [128, NCH, NH], F32)
    fb = C.sb("fb", [128, NH], F32)
    P.dma("sp", lambda e: e.dma_start(out=fw.t[:], in_=fw_d), writes=[fw.r])
    P.dma("sp", lambda e: e.dma_start(out=fb.t[:], in_=fb_d), writes=[fb.r])
    gfw = C.sb("gfw", [128, NCH, NH], F32)
    for c in range(NCH):
        P.op("pool", lambda e, c=c: e.tensor_scalar(out=gfw.t[:, c, :], in0=fw.t[:, c, :], scalar1=gn.t[:, c:c + 1],
                                                    scalar2=None, op0=ALU.mult),
             reads=[fw.r, gn.r], writes=[gfw.r])
    ones_col = C.sb("ones_col", [128, 1], F32)
    P.op("pool", lambda e: e.memset(ones_col.t[:], 1.0), writes=[ones_col.r])
    xsqc = [C.sb("xsqc%d" % i, [128, 128], F32) for i in range(2)]
    zs = [C.sb("zs%d" % i, [128, NH], F32) for i in range(2)]
    rt = [C.sb("rt%d" % i, [128, 1], F32) for i in range(2)]
    NT = T // 128
    k = 0
    for tt in range(NT):
        pz = ps[2 + (tt % 2) * 2]
        pq = ps[3 + (tt % 2) * 2]
        tsl = slice(tt * 128, (tt + 1) * 128)
        for c in range(NCH):
            sq = xsqc[k % 2]
            k += 1
            P.op("act", lambda e, c=c, sq=sq, tsl=tsl: e.activation(out=sq.t[:], in_=C.xT.t[:, c, tsl], func=AF.Square),
                 reads=[C.xr[c]], writes=[sq.r])
            P.op("pe", lambda e, c=c, sq=sq, pq=pq: e.matmul(pq.t[:, 0:1], lhsT=sq.t[:], rhs=ones_col.t[:],
                                                             start=(c == 0), stop=(c == NCH - 1)),
                 reads=[sq.r, ones_col.r], writes=[pq.r])
            P.op("pe", lambda e, c=c, tsl=tsl, pz=pz: e.matmul(pz.t[:, 0:NH], lhsT=C.xT.t[:, c, tsl], rhs=gfw.t[:, c, :],
                                                               start=(c == 0), stop=(c == NCH - 1)),
                 reads=[C.xr[c], gfw.r], writes=[pz.r])
        r1 = rt[tt % 2]
        z = zs[tt % 2]
        P.op("act", lambda e, r1=r1, pq=pq: e.activation(out=r1.t[:], in_=pq.t[:, 0:1], func=AF.Sqrt, bias=EPS,
                                                         scale=1.0 / D_MODEL), reads=[pq.r], writes=[r1.r])
        P.op("dve", lambda e, r1=r1: e.reciprocal(out=r1.t[:], in_=r1.t[:]), reads=[r1.r], writes=[r1.r])
        P.op("dve", lambda e, r1=r1, z=z, pz=pz: e.scalar_tensor_tensor(
            out=z.t[:], in0=pz.t[:, 0:NH], scalar=r1.t[:, 0:1], in1=fb.t[:], op0=ALU.mult, op1=ALU.add),
            reads=[pz.r, r1.r, fb.r], writes=[z.r])
        P.op("act", lambda e, z=z: e.activation(out=z.t[:], in_=z.t[:], func=AF.Exp, scale=-1.0),
             reads=[z.r], writes=[z.r])
        P.op("act", lambda e, z=z: e.activation(out=z.t[:], in_=z.t[:], func=AF.Ln, bias=1.0, scale=1.0),
             reads=[z.r], writes=[z.r])
        C.outs.append(P.dma("sp", lambda e, z=z, tt=tt: e.dma_start(out=nlf_o[tt], in_=z.t[:]), reads=[z.r]))

    wkb = [C.sb("wkb%d" % i, [128, NCH, 128], BF16) for i in range(2)]
    ko = [C.sb("ko%d" % i, [128, T], BF16) for i in range(2)]
    for h in range(NH):
        w = wkb[h % 2]
        o = ko[h % 2]
        P.dma("pool", lambda e, w=w, h=h: e.dma_start(out=w.t[:], in_=wk_d[h]), writes=[w.r])
        for hf in range(2):
            pp = ps[(2 * h + hf) % 4]
            proj_fm(C, w, pp, hf)
            if hf == 0:
                P.op("act", lambda e, o=o, pp=pp: e.activation(out=o.t[:, 0:512], in_=pp.t[:], func=AF.Copy),
                     reads=[pp.r], writes=[o.r])
            else:
                P.op("dve", lambda e, o=o, pp=pp: e.tensor_copy(out=o.t[:, 512:1024], in_=pp.t[:]),
                     reads=[pp.r], writes=[o.r])
        C.outs.append(P.dma("sp", lambda e, o=o, h=h: e.dma_start(out=kT_o[h], in_=o.t[:]), reads=[o.r]))

    wvb = [C.sb("wvb%d" % i, [128, NCH, 512], BF16) for i in range(2)]
    vo = [C.sb("vo%d" % i, [128, 512], BF16) for i in range(4)]
    k = 0
    for b in range(8):
        w = wvb[b % 2]
        P.dma("pool", lambda e, w=w, b=b: e.dma_start(out=w.t[:], in_=wv_d[b]), writes=[w.r])
        for tt in range(NT):
            pp = ps[4 + k % 4]
            o = vo[k % 4]
            for c in range(NCH):
                P.op("pe", lambda e, c=c, tt=tt, w=w, pp=pp: e.matmul(
                    pp.t[:], lhsT=C.hT.t[:, c, tt * 128:(tt + 1) * 128], rhs=w.t[:, c, :],
                    start=(c == 0), stop=(c == NCH - 1)), reads=[w.r, C.hr[c]], writes=[pp.r])
            if k % 2 == 0:
                P.op("act", lambda e, o=o, pp=pp: e.activation(out=o.t[:], in_=pp.t[:], func=AF.Copy),
                     reads=[pp.r], writes=[o.r])
            else:
                P.op("dve", lambda e, o=o, pp=pp: e.tensor_copy(out=o.t[:], in_=pp.t[:]), reads=[pp.r], writes=[o.r])
            C.outs.append(P.dma("sp", lambda e, o=o, tt=tt, b=b: e.dma_start(
                out=V_o[tt, :, b * 512:(b + 1) * 512], in_=o.t[:]), reads=[o.r]))
            k += 1
    return C.finish()


def core_segments(j):
    jj = j % 4
    return (jj, jj + 4)


def core_tokens(j):
    s0, s1 = core_segments(j)
    return np.concatenate([np.arange(s0 * SEG, (s0 + 1) * SEG), np.arange(s1 * SEG, (s1 + 1) * SEG)])


def tile_w_cols(W, col0, ncols, blk):
    Wc = W[:, col0:col0 + ncols]
    nb = ncols // blk
    return np.ascontiguousarray(Wc.reshape(NCH, 128, nb, blk).transpose(2, 1, 0, 3))


def tile_vec(g):
    return np.ascontiguousarray(g.reshape(NCH, 128).T)


def to_xT(xtok):
    t = xtok.shape[0]
    return np.ascontiguousarray(xtok.T.reshape(NCH, 128, t).transpose(1, 0, 2))


def from_xT(xT):
    t = xT.shape[2]
    return np.ascontiguousarray(xT.transpose(1, 0, 2).reshape(D_MODEL, t).T)


_NC_CACHE = {}


def get_nc(kind):
    if kind not in _NC_CACHE:
        _NC_CACHE[kind] = build_kv() if kind == "kv" else build_mix(kind)
    return _NC_CACHE[kind]


def run_kv(xT_cores, gn, W, koff, voff, f_w, f_b):
    nc = get_nc("kv")
    wk = tile_w_cols(W, koff, D_INNER, 128)
    wv = tile_w_cols(W, voff, D_INNER, 512)
    fw = np.ascontiguousarray(f_w.reshape(NCH, 128, NH).transpose(1, 0, 2))
    fb = np.ascontiguousarray(np.broadcast_to(f_b[None, :], (128, NH)))
    g = tile_vec(gn)
    in_maps = [{"xT": xT_cores[j], "gn": g, "wk": wk, "wv": wv, "fw": fw, "fb": fb} for j in range(8)]
    res = run_bass_kernel_spmd(nc, in_maps, core_ids=list(range(8)))
    return res.results


def attn_unit(C, kt_ap, v_ap, q_ap, ncol, sc_ap, adds, acc_o, acc_s, c0, first, kres, vres, qres, ares, psS, Sb, PT):
    P = C.P
    P.op("pe", lambda e: e.matmul(psS.t[:, 0:ncol], lhsT=kt_ap, rhs=q_ap, start=True, stop=True),
         reads=[kres, qres], writes=[psS.r])
    for (o, n, ap, r) in adds:
        P.op("dve", lambda e, o=o, n=n, ap=ap: e.scalar_tensor_tensor(
            out=Sb.t[:, o:o + n], in0=psS.t[:, o:o + n], scalar=sc_ap, in1=ap, op0=ALU.add, op1=ALU.add),
            reads=[psS.r, r] + list(ares), writes=[Sb.r])
    P.op("act", lambda e: e.activation(out=PT.t[:, 0:ncol], in_=Sb.t[:, 0:ncol], func=AF.Exp),
         reads=[Sb.r], writes=[PT.r])
    P.op("pe", lambda e: e.matmul(acc_o.t[:, c0:c0 + ncol], lhsT=v_ap, rhs=PT.t[:, 0:ncol], start=first, stop=False,
                                  skip_group_check=True),
         reads=[vres, PT.r], writes=[acc_o.r])
    P.op("pe", lambda e: e.matmul(acc_s.t[:, c0:c0 + ncol], lhsT=C.ones_b.t[:], rhs=PT.t[:, 0:ncol], start=first,
                                  stop=False, skip_group_check=True),
         reads=[C.ones_b.r, PT.r], writes=[acc_s.r])


def build_mix(kind):
    C = Ctx()
    P = C.P
    isb = kind == "b"
    xT_d = C.dram("xT", [128, NCH, T], F32, "ExternalInput")
    gn_d = C.dram("gn", [128, NCH], F32, "ExternalInput")
    wq_d = C.dram("wq", [NH, 128, NCH, 128], F32, "ExternalInput")
    wg_d = C.dram("wg", [NH, 128, NCH, 128], F32, "ExternalInput")
    wo_d = C.dram("wo", [NH // HG, 128, HG, D_MODEL], F32, "ExternalInput")
    xo_d = C.dram("xo", [128, NCH, T], F32, "ExternalOutput")
    if not isb:
        NU = (8, 8)
        kT_d = [C.dram("kTs%d" % s, [NH, 128, 8 * 128], BF16, "ExternalInput") for s in range(2)]
        V_d = [C.dram("Vs%d" % s, [8, 128, D_INNER], BF16, "ExternalInput") for s in range(2)]
        kmask_d = C.dram("kmask", [128, 2, 8], F32, "ExternalInput")
        rbx_d = C.dram("rbx", [NH, 767], F32, "ExternalInput")
        cmask_d = C.dram("cmask", [128, 640], F32, "ExternalInput")
    else:
        NU = UMAX
        kT_d = [C.dram("kTs%d" % s, [NH, 128, NU[s] * 128], BF16, "ExternalInput") for s in range(2)]
        V_d = [C.dram("Vs%d" % s, [NU[s], 128, D_INNER], BF16, "ExternalInput") for s in range(2)]
        nlf_d = [C.dram("nlfs%d" % s, [128, NU[s], NH], F32, "ExternalInput") for s in range(2)]
        kval_d = [C.dram("kval%d" % s, [128, NU[s]], F32, "ExternalInput") for s in range(2)]
        gfin_d = C.dram("gfin", [128, NCH], F32, "ExternalInput")
        ident_d = C.dram("ident", [128, 128], F32, "ExternalInput")
        tri_d = C.dram("tri", [128, 128], F32, "ExternalInput")
        trim_d = C.dram("trim", [128, 128], F32, "ExternalInput")
        yo_d = C.dram("yo", [128, NCH, T], F32, "ExternalOutput")

    ps = [C.ps("ps%d" % i) for i in range(8)]
    psS = ps[0:2]
    psO = ps[2]
    psSm = ps[3]
    psG = ps[4:8]
    phase_consts(C)
    phase_load_x(C, xT_d)
    sg = [C.sb("sg%d" % i, [128, T], F32) for i in range(2)]
    C.xsq = sg
    C.hT = C.sb("hT", [128, NCH, T], BF16)
    C.hr = [Res("h%d" % c) for c in range(NCH)]
    C.rstd = C.sb("rstd", [128, T], F32)
    phase_norm(C, gn_d, "n1", psG[0], psG[1])

    wqb = [C.sb("wqb%d" % i, [128, NCH, 128], BF16) for i in range(2)]
    wgb = [C.sb("wgb%d" % i, [128, NCH, 128], BF16) for i in range(2)]
    wob = C.sb("wob", [128, HG, D_MODEL], BF16)
    qT = [C.sb("qT%d" % i, [128, T], BF16) for i in range(2)]
    og = C.sb("og", [128, HG, T], BF16)
    kxc = [C.sb("kxc%d" % i, [128, 8 * 128], BF16) for i in range(2)]
    vxc = [C.sb("vxc%d" % i, [128, 8, 128], BF16) for i in range(2)]
    Sb = [C.sb("Sb%d" % i, [128, 512], F32) for i in range(2)]
    PT = [C.sb("PT%d" % i, [128, 512], BF16) for i in range(2)]
    rs = C.sb("rs", [128, 512], F32)
    wgt = C.sb("wgt", [128, 512], F32)

    if not isb:
        kmask = C.sb("kmask", [128, 2, 8], F32)
        P.dma("sp", lambda e: e.dma_start(out=kmask.t[:], in_=kmask_d), writes=[kmask.r])
        cmask = C.sb("cmask", [128, 640], F32)
        P.dma("sp", lambda e: e.dma_start(out=cmask.t[:], in_=cmask_d), writes=[cmask.r])
        BT = [C.sb("BT%d" % i, [128, 640], F32) for i in range(2)]
    else:
        ident = C.sb("ident", [128, 128], F32)
        tri = C.sb("tri", [128, 128], F32)
        trim = C.sb("trim", [128, 128], F32)
        for (t_, d_) in ((ident, ident_d), (tri, tri_d), (trim, trim_d)):
            P.dma("sp", lambda e, t_=t_, d_=d_: e.dma_start(out=t_.t[:], in_=d_), writes=[t_.r])
        nlf_t = C.sb("nlf_t", [128, 32, NH], F32)
        tot_t = C.sb("tot_t", [128, 32, NH], F32)
        ND = [C.sb("ND%d" % s, [128, NU[s], NH], F32) for s in range(2)]
        SC = [C.sb("SC%d" % s, [128, NU[s], NH], F32) for s in range(2)]
        kval = [C.sb("kval%d" % s, [128, NU[s]], F32) for s in range(2)]
        NDQ = C.sb("NDQ", [128, 512], F32)
        NDQd = C.sb("NDQd", [128, 512], F32)
        dexp = [C.sb("dexp%d" % i, [128, 4, 128], BF16) for i in range(2)]
        for s in range(2):
            U = NU[s]
            P.dma("sp", lambda e, s=s, U=U: e.dma_start(out=nlf_t.t[:, 0:U, :], in_=nlf_d[s]), writes=[nlf_t.r])
            P.dma("sp", lambda e, s=s: e.dma_start(out=kval[s].t[:], in_=kval_d[s]), writes=[kval[s].r])
            nflat = nlf_t.t[:, 0:U, :].rearrange("p u h -> p (u h)")
            ndflat = ND[s].t[:].rearrange("p u h -> p (u h)")
            totflat = tot_t.t[:, 0:U, :].rearrange("p u h -> p (u h)")
            for j in range(U * NH // 512):
                sl = slice(j * 512, (j + 1) * 512)
                pa = psG[(2 * j) % 4]
                pb = psG[(2 * j + 1) % 4]
                P.op("pe", lambda e, pa=pa, sl=sl, nflat=nflat: e.matmul(pa.t[:], lhsT=tri.t[:], rhs=nflat[:, sl],
                                                                         start=True, stop=True),
                     reads=[tri.r, nlf_t.r], writes=[pa.r])
                P.op("pe", lambda e, pb=pb, sl=sl, nflat=nflat: e.matmul(pb.t[:], lhsT=C.ones_f.t[:], rhs=nflat[:, sl],
                                                                         start=True, stop=True),
                     reads=[C.ones_f.r, nlf_t.r], writes=[pb.r])
                P.op("act", lambda e, pa=pa, sl=sl, ndflat=ndflat: e.activation(out=ndflat[:, sl], in_=pa.t[:], func=AF.Copy),
                     reads=[pa.r], writes=[ND[s].r])
                P.op("dve", lambda e, pb=pb, sl=sl, totflat=totflat: e.tensor_copy(out=totflat[:, sl], in_=pb.t[:]),
                     reads=[pb.r], writes=[tot_t.r])
            for u in range(1, U - 1):
                P.op("dve", lambda e, u=u: e.tensor_tensor(out=tot_t.t[:, u, :], in0=tot_t.t[:, u, :],
                                                           in1=tot_t.t[:, u - 1, :], op=ALU.add),
                     reads=[tot_t.r], writes=[tot_t.r])
            P.op("dve", lambda e, s=s, U=U: e.tensor_tensor(out=ND[s].t[:, 1:U, :], in0=ND[s].t[:, 1:U, :],
                                                            in1=tot_t.t[:, 0:U - 1, :], op=ALU.add),
                 reads=[tot_t.r, ND[s].r], writes=[ND[s].r])
            P.op("dve", lambda e, s=s, U=U: e.tensor_tensor(
                out=SC[s].t[:], in0=kval[s].t[:].unsqueeze(2).to_broadcast([128, U, NH]), in1=ND[s].t[:],
                op=ALU.subtract), reads=[kval[s].r, ND[s].r], writes=[SC[s].r])

    nK = 0
    nUnit = 0
    for grp in range(NH // HG):
        P.dma("pool", lambda e, grp=grp: e.dma_start(out=wob.t[:], in_=wo_d[grp]), writes=[wob.r])
        for hh in range(HG):
            h = grp * HG + hh
            wq_, wg_ = wqb[h % 2], wgb[h % 2]
            P.dma("pool", lambda e, wq_=wq_, h=h: e.dma_start(out=wq_.t[:], in_=wq_d[h]), writes=[wq_.r])
            P.dma("pool", lambda e, wg_=wg_, h=h: e.dma_start(out=wg_.t[:], in_=wg_d[h]), writes=[wg_.r])
            q_ = qT[h % 2]
            g_ = sg[h % 2]
            for hf in range(2):
                pp = psG[hf]
                proj_fm(C, wq_, pp, hf)
                P.op("dve", lambda e, q_=q_, pp=pp, hf=hf: e.tensor_scalar(
                    out=q_.t[:, hf * 512:(hf + 1) * 512], in0=pp.t[:], scalar1=SCALE, scalar2=None, op0=ALU.mult),
                    reads=[pp.r], writes=[q_.r])
            for hf in range(2):
                pp = psG[2 + hf]
                proj_fm(C, wg_, pp, hf)
                P.op("act", lambda e, g_=g_, pp=pp, hf=hf: e.activation(
                    out=g_.t[:, hf * 512:(hf + 1) * 512], in_=pp.t[:], func=AF.Silu),
                    reads=[pp.r], writes=[g_.r])
            if not isb:
                bt = BT[h % 2]
                src = bass.AP(rbx_d.tensor, h * 767, [[1, 128], [1, 640]])
                P.dma("sp", lambda e, bt=bt, src=src: e.dma_start(out=bt.t[:], in_=src), writes=[bt.r])
                P.op("pool", lambda e, bt=bt: e.tensor_tensor(out=bt.t[:], in0=bt.t[:], in1=cmask.t[:], op=ALU.add),
                     reads=[bt.r, cmask.r], writes=[bt.r])
            for s in range(2):
                U = NU[s]
                if isb:
                    dx = dexp[(2 * h + s) % 2]
                    for i in range(4):
                        P.op("pool", lambda e, dx=dx, i=i, s=s, h=h: e.tensor_tensor(
                            out=dx.t[:, i, :], in0=ident.t[:], in1=ND[s].t[:, 3 - i, h:h + 1].to_broadcast([128, 128]),
                            op=ALU.mult), reads=[ident.r, ND[s].r], writes=[dx.r])
                    pq = psG[(2 * h + s) % 4]
                    P.op("pe", lambda e, dx=dx, pq=pq: e.matmul(pq.t[:], lhsT=C.ones_b.t[:],
                                                                rhs=dx.t[:].rearrange("p i q -> p (i q)"),
                                                                start=True, stop=True),
                         reads=[C.ones_b.r, dx.r], writes=[pq.r])
                    P.op("act", lambda e, pq=pq: e.activation(out=NDQ.t[:], in_=pq.t[:], func=AF.Copy),
                         reads=[pq.r], writes=[NDQ.r])
                    P.op("pool", lambda e: e.tensor_tensor(
                        out=NDQd.t[:].rearrange("p (i q) -> p i q", i=4), in0=NDQ.t[:].rearrange("p (i q) -> p i q", i=4),
                        in1=trim.t[:].unsqueeze(1).to_broadcast([128, 4, 128]), op=ALU.add),
                        reads=[NDQ.r, trim.r], writes=[NDQd.r])
                first = True
                for ch in range(U // 8):
                    kx = kxc[nK % 2]
                    vx = vxc[nK % 2]
                    nK += 1
                    P.dma("sp", lambda e, kx=kx, s=s, h=h, ch=ch: e.dma_start(
                        out=kx.t[:], in_=kT_d[s][h, :, ch * 1024:(ch + 1) * 1024]), writes=[kx.r])
                    P.dma("sp", lambda e, vx=vx, s=s, h=h, ch=ch: e.dma_start(
                        out=vx.t[:], in_=V_d[s][ch * 8:(ch + 1) * 8, :, h * 128:(h + 1) * 128].rearrange("u p d -> p u d")),
                        writes=[vx.r])
                    if ch == 0:
                        order = [3, 4, 0, 1, 2, 5, 6, 7] if not isb else [3, 2, 1, 0, 4, 5, 6, 7]
                    else:
                        order = list(range(8))
                    for ul in order:
                        u = ch * 8 + ul
                        if not isb:
                            ilo, ihi = max(0, u - 4), min(3, u)
                            rlo = ilo + 4 - u
                            ncol = (ihi - ilo + 1) * 128
                            adds = [(0, ncol, BT[h % 2].t[:, rlo * 128:rlo * 128 + ncol], BT[h % 2].r)]
                            sc_ap = kmask.t[:, s, u:u + 1]
                            ares = [kmask.r]
                        else:
                            if u < 4:
                                ilo, ihi = 3 - u, 3
                                ncol = (ihi - ilo + 1) * 128
                                adds = [(0, 128, NDQd.t[:, ilo * 128:(ilo + 1) * 128], NDQd.r)]
                                if ncol > 128:
                                    adds.append((128, ncol - 128, NDQ.t[:, (ilo + 1) * 128:512], NDQ.r))
                            else:
                                ilo, ihi = 0, 3
                                ncol = 512
                                adds = [(0, 512, NDQ.t[:, :], NDQ.r)]
                            sc_ap = SC[s].t[:, u, h:h + 1]
                            ares = [SC[s].r]
                        c0 = ilo * 128
                        attn_unit(C, kx.t[:, ul * 128:(ul + 1) * 128], vx.t[:, ul, :],
                                  q_.t[:, s * 512 + c0:s * 512 + c0 + ncol], ncol, sc_ap, adds, psO, psSm, c0, first,
                                  kx.r, vx.r, q_.r, ares, psS[nUnit % 2], Sb[nUnit % 2], PT[nUnit % 2])
                        first = False
                        nUnit += 1
                P.op("dve", lambda e: e.reciprocal(out=rs.t[:], in_=psSm.t[:]), reads=[psSm.r], writes=[rs.r])
                P.op("pool", lambda e, g_=g_, s=s: e.tensor_tensor(out=wgt.t[:], in0=rs.t[:],
                                                                   in1=g_.t[:, s * 512:(s + 1) * 512], op=ALU.mult),
                     reads=[rs.r, g_.r], writes=[wgt.r])
                P.op("dve", lambda e, hh=hh, s=s: e.tensor_tensor(out=og.t[:, hh, s * 512:(s + 1) * 512], in0=psO.t[:],
                                                                  in1=wgt.t[:], op=ALU.mult),
                     reads=[psO.r, wgt.r], writes=[og.r])
        k = 0
        for c in range(NCH):
            for hf in range(2):
                pp = psG[k % 4]
                k += 1
                for hh in range(HG):
                    P.op("pe", lambda e, pp=pp, hh=hh, c=c, hf=hf: e.matmul(
                        pp.t[:], lhsT=wob.t[:, hh, c * 128:(c + 1) * 128], rhs=og.t[:, hh, hf * 512:(hf + 1) * 512],
                        start=(hh == 0), stop=(hh == HG - 1)), reads=[wob.r, og.r], writes=[pp.r])
                P.op("dve", lambda e, pp=pp, c=c, hf=hf: e.tensor_tensor(
                    out=C.xT.t[:, c, hf * 512:(hf + 1) * 512], in0=pp.t[:], in1=C.xT.t[:, c, hf * 512:(hf + 1) * 512],
                    op=ALU.add), reads=[pp.r, C.xr[c]], writes=[C.xr[c]])
    for c0 in range(0, NCH, 4):
        C.outs.append(P.dma("sp", lambda e, c0=c0: e.dma_start(out=xo_d[:, c0:c0 + 4, :], in_=C.xT.t[:, c0:c0 + 4, :]),
                            reads=C.xr[c0:c0 + 4]))
    if isb:
        gf = C.sb("gfin", [128, NCH], F32)
        P.dma("sp", lambda e: e.dma_start(out=gf.t[:], in_=gfin_d), writes=[gf.r])
        pss = (psG[0], psG[1])
        for c in range(NCH):
            sq = sg[c % 2]
            P.op("act", lambda e, c=c, sq=sq: e.activation(out=sq.t[:], in_=C.xT.t[:, c, :], func=AF.Square),
                 reads=[C.xr[c]], writes=[sq.r])
            for hf in range(2):
                P.op("pe", lambda e, c=c, sq=sq, hf=hf: e.matmul(
                    pss[hf].t[:], lhsT=C.ones_f.t[:], rhs=sq.t[:, hf * 512:(hf + 1) * 512],
                    start=(c == 0), stop=(c == NCH - 1)), reads=[sq.r, C.ones_f.r], writes=[pss[hf].r])
        for hf in range(2):
            sl = slice(hf * 512, (hf + 1) * 512)
            P.op("act", lambda e, hf=hf, sl=sl: e.activation(
                out=C.rstd.t[:, sl], in_=pss[hf].t[:], func=AF.Sqrt, bias=EPS, scale=1.0 / D_MODEL),
                reads=[pss[hf].r], writes=[C.rstd.r])
        P.op("dve", lambda e: e.reciprocal(out=C.rstd.t[:], in_=C.rstd.t[:]), reads=[C.rstd.r], writes=[C.rstd.r])
        for c in range(NCH):
            yb = sg[c % 2]
            P.op("dve", lambda e, c=c, yb=yb: e.scalar_tensor_tensor(
                out=yb.t[:], in0=C.xT.t[:, c, :], scalar=gf.t[:, c:c + 1], in1=C.rstd.t[:],
                op0=ALU.mult, op1=ALU.mult), reads=[C.xr[c], gf.r, C.rstd.r], writes=[yb.r])
            C.outs.append(P.dma("sp", lambda e, c=c, yb=yb: e.dma_start(out=yo_d[:, c, :], in_=yb.t[:]), reads=[yb.r]))
    return C.finish()


def _tile_wo(Wo):
    return np.ascontiguousarray(Wo.reshape(NH // HG, HG, 128, D_MODEL).transpose(0, 2, 1, 3))


def _gather_seq(res, key):
    out = []
    for b in range(BATCH):
        if key == "kT":
            g = np.zeros((NH, 128, SEQ), dtype=res[0][key].dtype)
            for j in range(4 * b, 4 * b + 4):
                for s, sgm in enumerate(core_segments(j)):
                    g[:, :, sgm * SEG:(sgm + 1) * SEG] = res[j][key][:, :, s * SEG:(s + 1) * SEG]
        else:
            w = res[0][key].shape[2]
            g = np.zeros((SEQ // 128, 128, w), dtype=res[0][key].dtype)
            for j in range(4 * b, 4 * b + 4):
                for s, sgm in enumerate(core_segments(j)):
                    g[sgm * 4:(sgm + 1) * 4] = res[j][key][s * 4:(s + 1) * 4]
        out.append(g)
    return out


def run_a(xT_cores, gn, W_in, rel_bias, W_out, kT_g, V_g):
    nc = get_nc("a")
    wq = tile_w_cols(W_in, 0, D_INNER, 128)
    wg = tile_w_cols(W_in, 3 * D_INNER, D_INNER, 128)
    wo = _tile_wo(W_out)
    g = tile_vec(gn)
    idx = np.clip(np.arange(767) - 127, -256, 256) + 256
    rbx = np.ascontiguousarray(rel_bias[:, idx])
    cmask = np.zeros((128, 5, 128), np.float32)
    cmask[:64, 0, :64] = NEG
    cmask[64:, 4, 64:] = NEG
    cmask = cmask.reshape(128, 640)
    in_maps = []
    for j in range(8):
        b = j // 4
        m = {"xT": xT_cores[j], "gn": g, "wq": wq, "wg": wg, "wo": wo, "rbx": rbx, "cmask": cmask}
        kmask = np.zeros((128, 2, 8), np.float32)
        for s, sgm in enumerate(core_segments(j)):
            kT = np.zeros((NH, 128, 2 * SEG), dtype=kT_g[b].dtype)
            V = np.zeros((8, 128, D_INNER), dtype=V_g[b].dtype)
            kT[:, :, SEG:] = kT_g[b][:, :, sgm * SEG:(sgm + 1) * SEG]
            V[4:] = V_g[b][sgm * 4:(sgm + 1) * 4]
            if sgm > 0:
                kT[:, :, :SEG] = kT_g[b][:, :, (sgm - 1) * SEG:sgm * SEG]
                V[:4] = V_g[b][(sgm - 1) * 4:sgm * 4]
            else:
                kmask[:, s, :4] = NEG
            m["kTs%d" % s] = np.ascontiguousarray(kT.reshape(NH, 128, 8, 128)[:, :, :, ::-1]).reshape(NH, 128, 1024)
            m["Vs%d" % s] = np.ascontiguousarray(V[:, ::-1, :])
        m["kmask"] = kmask
        in_maps.append(m)
    res = run_bass_kernel_spmd(nc, in_maps, core_ids=list(range(8)))
    return [r["xo"] for r in res.results]


def run_b(xT_cores, gn, W_in, W_out, kT_g, V_g, nlf_g, gfin):
    nc = get_nc("b")
    wq = tile_w_cols(W_in, 0, D_INNER, 128)
    wg = tile_w_cols(W_in, D_INNER, D_INNER, 128)
    wo = _tile_wo(W_out)
    g = tile_vec(gn)
    gf = tile_vec(gfin)
    ident = np.eye(128, dtype=np.float32)
    kl = np.arange(128)
    tri = (kl[:, None] > kl[None, :]).astype(np.float32)
    trim = np.where(kl[:, None] > kl[None, :], NEG, 0.0).astype(np.float32)
    in_maps = []
    for j in range(8):
        b = j // 4
        m = {"xT": xT_cores[j], "gn": g, "wq": wq, "wg": wg, "wo": wo, "gfin": gf, "ident": ident, "tri": tri,
             "trim": trim}
        for s, sgm in enumerate(core_segments(j)):
            U = UMAX[s]
            tiles = [4 * sgm + 3 - u for u in range(4 * sgm + 4)]
            kT = np.zeros((NH, 128, U * 128), dtype=kT_g[b].dtype)
            V = np.zeros((U, 128, D_INNER), dtype=V_g[b].dtype)
            nlf = np.zeros((128, U, NH), np.float32)
            kval = np.full((128, U), NEG, np.float32)
            for u, tix in enumerate(tiles):
                kT[:, :, u * 128:(u + 1) * 128] = kT_g[b][:, :, tix * 128:(tix + 1) * 128]
                V[u] = V_g[b][tix]
                nlf[:, u, :] = nlf_g[b][tix]
                kval[:, u] = 0.0
            m["kTs%d" % s] = kT
            m["Vs%d" % s] = V
            m["nlfs%d" % s] = nlf
            m["kval%d" % s] = kval
        in_maps.append(m)
    res = run_bass_kernel_spmd(nc, in_maps, core_ids=list(range(8)))
    return [r["xo"] for r in res.results], [r["yo"] for r in res.results]


def kernel_unfused(x, a_norm, a_w_in, a_rel_bias, a_w_out, kv_norm, kv_w, f_w, f_b, b_norm, b_w_in, b_w_out, final_norm):
    x = np.asarray(x, np.float32)
    xT = [to_xT(x[j // 4][core_tokens(j)]) for j in range(8)]
    f_w = np.asarray(f_w, np.float32)
    f_b = np.asarray(f_b, np.float32)
    for l in range(2):
        W = np.asarray(a_w_in[l], np.float32)
        r = run_kv(xT, np.asarray(a_norm[l], np.float32), W, D_INNER, 2 * D_INNER, f_w, f_b)
        kT_g = _gather_seq(r, "kT")
        V_g = _gather_seq(r, "V")
        xT = run_a(xT, np.asarray(a_norm[l], np.float32), W, np.asarray(a_rel_bias[l], np.float32),
                   np.asarray(a_w_out[l], np.float32), kT_g, V_g)
    r = run_kv(xT, np.asarray(kv_norm, np.float32), np.asarray(kv_w, np.float32), 0, D_INNER, f_w, f_b)
    kT_g = _gather_seq(r, "kT")
    V_g = _gather_seq(r, "V")
    nlf_g = _gather_seq(r, "nlf")
    yT = None
    for l in range(2):
        xT, yT = run_b(xT, np.asarray(b_norm[l], np.float32), np.asarray(b_w_in[l], np.float32),
                       np.asarray(b_w_out[l], np.float32), kT_g, V_g, nlf_g, np.asarray(final_norm, np.float32))
    out = np.zeros((BATCH, SEQ, D_MODEL), np.float32)
    for j in range(8):
        out[j // 4][core_tokens(j)] = from_xT(yT[j])
    return out


NSEGP = 11
KSEGE = 1024 * 512
VSEGE = 512 * 512
NSEGE = 512 * NH
GROUPS = [[0, 1, 2, 3], [4, 5, 6, 7]]


def build_fused():
    C = Ctx()
    P = C.P
    nc = C.nc
    xT_d = C.dram("xT", [128, NCH, T], F32, "ExternalInput")
    gns_d = C.dram("gns", [6, 128, NCH], F32, "ExternalInput")
    a_wq_d = C.dram("a_wq", [2, NH, 128, NCH, 128], F32, "ExternalInput")
    a_wg_d = C.dram("a_wg", [2, NH, 128, NCH, 128], F32, "ExternalInput")
    a_wk_d = C.dram("a_wk", [2, NH, 128, NCH, 128], F32, "ExternalInput")
    a_wv_d = C.dram("a_wv", [2, 8, 128, NCH, 512], F32, "ExternalInput")
    a_wo_d = C.dram("a_wo", [2, NH // HG, 128, HG, D_MODEL], F32, "ExternalInput")
    s_wk_d = C.dram("s_wk", [NH, 128, NCH, 128], F32, "ExternalInput")
    s_wv_d = C.dram("s_wv", [8, 128, NCH, 512], F32, "ExternalInput")
    b_wq_d = C.dram("b_wq", [2, NH, 128, NCH, 128], F32, "ExternalInput")
    b_wg_d = C.dram("b_wg", [2, NH, 128, NCH, 128], F32, "ExternalInput")
    b_wo_d = C.dram("b_wo", [2, NH // HG, 128, HG, D_MODEL], F32, "ExternalInput")
    fw_d = C.dram("fw", [128, NCH, NH], F32, "ExternalInput")
    fb_d = C.dram("fb", [128, NH], F32, "ExternalInput")
    rbx_d = C.dram("rbx", [2, NH, 767], F32, "ExternalInput")
    cmask_d = C.dram("cmask", [128, 640], F32, "ExternalInput")
    kmask_d = C.dram("kmask", [128, 2, 8], F32, "ExternalInput")
    kval_d = [C.dram("kval%d" % s, [128, UMAX[s]], F32, "ExternalInput") for s in range(2)]
    cst_d = C.dram("cst", [4, 128, 128], F32, "ExternalInput")
    yo_d = C.dram("yo", [128, NCH, T], F32, "ExternalOutput")
    kTo = [nc.dram_tensor("kTo%d" % i, [4 * 2 * 1024, 512], BF16) for i in range(1)]
    Vo = [nc.dram_tensor("Vo%d" % i, [8 * 2 * 512, 512], BF16) for i in range(1)]
    nlfo = nc.dram_tensor("nlfo", [T, NH], F32)
    KL = [nc.dram_tensor("KL%d" % i, [4 * NSEGP * 1024, 512], BF16) for i in range(2)]
    VL = [nc.dram_tensor("VL%d" % i, [4 * NSEGP * 1024, 512], BF16) for i in range(2)]
    NL = nc.dram_tensor("NL", [NSEGP * 512, NH], F32)
    kTo_r = [[Res() for _ in range(NH)] for _ in range(2)]
    Vo_r = [[Res() for _ in range(8)] for _ in range(2)]
    nlfo_r = Res()
    KL_r = [[[Res() for _ in range(2)] for _ in range(4)] for _ in range(2)]
    VL_r = [[[Res() for _ in range(2)] for _ in range(4)] for _ in range(2)]
    NL_r = [Res() for _ in range(2)]
    pad_r = Res()
    Kloc = nc.dram_tensor("Kloc", [4 * 8 * 1024, 512], BF16)
    Vloc = nc.dram_tensor("Vloc", [4 * 8 * 1024, 512], BF16)
    Nloc = nc.dram_tensor("Nloc", [8 * 512, NH], F32)
    Kloc_r = [Res() for _ in range(4)]
    Vloc_r = [Res() for _ in range(4)]
    Nloc_r = Res()

    ps = [C.ps("ps%d" % i) for i in range(8)]
    psS = ps[0:3]
    psOs = [ps[3], ps[4]]
    psSms = [ps[5], ps[6]]
    psO = psOs[0]
    psSm = psSms[0]
    psG = [ps[6], ps[7], ps[0], ps[1]]
    psG2 = [ps[7], ps[2]]
    phase_consts(C)
    phase_load_x(C, xT_d)

    sg = [C.sb("sg%d" % i, [128, T], F32) for i in range(2)]
    C.xsq = sg
    C.hT = C.sb("hT", [128, NCH, T], BF16)
    C.hr = [Res("h%d" % c) for c in range(NCH)]
    C.rstd = C.sb("rstd", [128, T], F32)
    WB0 = C.sb("WB0", [128, 8192], BF16)
    WB1 = C.sb("WB1", [128, 8192], BF16)
    wq_r = [Res(), Res()]
    wg_r = [Res(), Res()]
    WB1_rs = wq_r + wg_r

    def wvb_ap(i):
        return (WB0 if i == 0 else WB1).t[:].rearrange("p (c n) -> p c n", c=NCH)

    def wvb_res(i):
        return [WB0.r] if i == 0 else WB1_rs

    wob_ap = WB0.t[:].rearrange("p (h n) -> p h n", h=HG)

    def wqb_ap(i):
        return WB1.t[:, i * 2048:(i + 1) * 2048].rearrange("p (c n) -> p c n", c=NCH)

    def wgb_ap(i):
        return WB1.t[:, 4096 + i * 2048:4096 + (i + 1) * 2048].rearrange("p (c n) -> p c n", c=NCH)

    kvbuf = [C.sb("kvbuf%d" % i, [128, 2048], BF16) for i in range(3)]
    qT = [C.sb("qT%d" % i, [128, T], BF16) for i in range(2)]
    PT = [C.sb("PT%d" % i, [128, 512], BF16) for i in range(4)]
    vo = PT
    Sb = [C.sb("Sb%d" % i, [128, 512], F32) for i in range(4)]
    og = C.sb("og", [128, HG, T], BF16)
    rs = C.sb("rs", [128, 512], F32)
    wgt = C.sb("wgt", [128, 512], F32)
    BT = [C.sb("BT%d" % i, [128, 640], F32) for i in range(2)]
    cmask = C.sb("cmask", [128, 640], F32)
    kmask = C.sb("kmask", [128, 2, 8], F32)
    cst = C.sb("cst", [128, 4, 128], F32)
    ND = [C.sb("ND%d" % s, [128, UMAX[s], NH], F32) for s in range(2)]
    SC = [C.sb("SC%d" % s, [128, UMAX[s], NH], F32) for s in range(2)]
    kval = [C.sb("kval%d" % s, [128, UMAX[s]], F32) for s in range(2)]
    NDQ = BT[0]
    NDQd = BT[1]
    dexp = [C.sb("dexp%d" % i, [128, 4, 128], BF16) for i in range(2)]
    fb = C.sb("fb", [128, NH], F32)
    ones_col = C.sb("ones_col", [128, 1], F32)
    xsqc = [C.sb("xsqc%d" % i, [128, 128], F32) for i in range(2)]
    zs = [C.sb("zs%d" % i, [128, NH], F32) for i in range(2)]
    rt = [C.sb("rt%d" % i, [128, 1], F32) for i in range(2)]
    zero = kvbuf[0]

    ident_ap = cst.t[:, 0, :]
    tri_ap = cst.t[:, 1, :]
    trim_ap = cst.t[:, 2, :]
    J_ap = cst.t[:, 3, :]
    for (t_, d_) in ((cmask, cmask_d), (kmask, kmask_d), (fb, fb_d), (kval[0], kval_d[0]), (kval[1], kval_d[1])):
        P.dma("sp", lambda e, t_=t_, d_=d_: e.dma_start(out=t_.t[:], in_=d_), writes=[t_.r])
    P.dma("sp", lambda e: e.dma_start(out=cst.t[:], in_=cst_d.rearrange("k p n -> p k n")), writes=[cst.r])
    P.op("pool", lambda e: e.memset(ones_col.t[:], 1.0), writes=[ones_col.r])
    P.op("pool", lambda e: e.memset(zero.t[:], 0.0), writes=[zero.r])
    for st in range(1):
        segs = (0, 1, 2) if st == 0 else (2,)
        for i in range(4):
            for sgp in segs:
                for half in range(2):
                    r0 = (i * NSEGP + sgp) * 1024 + half * 512
                    P.dma("sp", lambda e, st=st, r0=r0: e.dma_start(
                        out=KL[st][r0:r0 + 512, :].rearrange("(p a) n -> p (a n)", p=128), in_=zero.t[:]),
                        reads=[zero.r], writes=[pad_r])
        for b in range(4):
            for sgp in segs:
                for half in range(2):
                    r0 = (b * NSEGP + sgp) * 1024 + half * 512
                    P.dma("sp", lambda e, st=st, r0=r0: e.dma_start(
                        out=VL[st][r0:r0 + 512, :].rearrange("(p a) n -> p (a n)", p=128), in_=zero.t[:]),
                        reads=[zero.r], writes=[pad_r])
    P.dma("sp", lambda e: e.dma_start(out=NL[0:3 * 512, :].rearrange("(p a) n -> p (a n)", p=128),
                                      in_=zero.t[:, 0:384].bitcast(F32) if False else zero.t[:, 0:768].bitcast(F32)),
          reads=[zero.r], writes=[pad_r])

    dyn = {}

    def pre_sp(e):
        jj = e.snap(e.partition_id() % 4, min_val=0, max_val=3)
        dyn["k"] = e.snap(jj * KSEGE, min_val=0, max_val=3 * KSEGE)

    def pre_pool(e):
        jj = e.snap(e.partition_id() % 4, min_val=0, max_val=3)
        dyn["v"] = e.snap(jj * KSEGE, min_val=0, max_val=3 * KSEGE)
        dyn["n"] = e.snap(jj * NSEGE, min_val=0, max_val=3 * NSEGE)

    P.pre = {"sp": pre_sp, "pool": pre_pool}

    def localize(gates):
        for i in range(4):
            P.dma("sp", lambda e, i=i: e.dma_start(
                out=bass.AP(Kloc, i * 8 * KSEGE, [[32768, 128], [1, 32768]]),
                in_=bass.AP(KL[0], dyn["k"] + i * NSEGP * KSEGE, [[32768, 128], [1, 32768]])),
                reads=[KL_r[0][i][0], KL_r[0][i][1], pad_r], writes=[Kloc_r[i]])
        for p_ in range(4):
            P.dma("pool", lambda e, p_=p_: e.dma_start(
                out=bass.AP(Vloc, p_ * 8 * KSEGE, [[32768, 128], [1, 32768]]),
                in_=bass.AP(VL[0], dyn["v"] + p_ * NSEGP * KSEGE, [[32768, 128], [1, 32768]])),
                reads=[VL_r[0][p_][0], VL_r[0][p_][1], pad_r], writes=[Vloc_r[p_]])
        if gates:
            P.dma("pool", lambda e: e.dma_start(
                out=bass.AP(Nloc, 0, [[1024, 128], [1, 1024]]),
                in_=bass.AP(NL, dyn["n"], [[1024, 128], [1, 1024]])),
                reads=[NL_r[0], NL_r[1], pad_r], writes=[Nloc_r])

    def kv_phase(st, gidx, wk_ap, wv_ap, gates):
        gn = phase_norm(C, gns_d[gidx], "g%d" % gidx, psG[0], psG[1])
        NT = T // 128
        if gates:
            fw_ap = Sb[0].t[:].rearrange("p (c h) -> p c h", c=NCH)
            gfw_ap = Sb[1].t[:].rearrange("p (c h) -> p c h", c=NCH)
            P.dma("sp", lambda e: e.dma_start(out=fw_ap, in_=fw_d), writes=[Sb[0].r])
            for c in range(NCH):
                P.op("pool", lambda e, c=c: e.tensor_scalar(out=gfw_ap[:, c, :], in0=fw_ap[:, c, :],
                                                            scalar1=gn.t[:, c:c + 1], scalar2=None, op0=ALU.mult),
                     reads=[Sb[0].r, gn.r], writes=[Sb[1].r])
            k = 0
            for tt in range(NT):
                pz = psS[tt % 2]
                pq = (psO, psSm)[tt % 2]
                tsl = slice(tt * 128, (tt + 1) * 128)
                for c in range(NCH):
                    sq = xsqc[k % 2]
                    k += 1
                    P.op("act", lambda e, c=c, sq=sq, tsl=tsl: e.activation(out=sq.t[:], in_=C.xT.t[:, c, tsl],
                                                                            func=AF.Square),
                         reads=[C.xr[c]], writes=[sq.r])
                    P.op("pe", lambda e, c=c, sq=sq, pq=pq: e.matmul(pq.t[:, 0:1], lhsT=sq.t[:], rhs=ones_col.t[:],
                                                                     start=(c == 0), stop=(c == NCH - 1)),
                         reads=[sq.r, ones_col.r], writes=[pq.r])
                    P.op("pe", lambda e, c=c, tsl=tsl, pz=pz: e.matmul(pz.t[:, 0:NH], lhsT=C.xT.t[:, c, tsl],
                                                                       rhs=gfw_ap[:, c, :],
                                                                       start=(c == 0), stop=(c == NCH - 1)),
                         reads=[C.xr[c], Sb[1].r], writes=[pz.r])
                r1 = rt[tt % 2]
                z = zs[tt % 2]
                P.op("act", lambda e, r1=r1, pq=pq: e.activation(out=r1.t[:], in_=pq.t[:, 0:1], func=AF.Sqrt, bias=EPS,
                                                                 scale=1.0 / D_MODEL), reads=[pq.r], writes=[r1.r])
                P.op("dve", lambda e, r1=r1: e.reciprocal(out=r1.t[:], in_=r1.t[:]), reads=[r1.r], writes=[r1.r])
                P.op("dve", lambda e, r1=r1, z=z, pz=pz: e.scalar_tensor_tensor(
                    out=z.t[:], in0=pz.t[:, 0:NH], scalar=r1.t[:, 0:1], in1=fb.t[:], op0=ALU.mult, op1=ALU.add),
                    reads=[pz.r, r1.r, fb.r], writes=[z.r])
                P.op("act", lambda e, z=z: e.activation(out=z.t[:], in_=z.t[:], func=AF.Exp, scale=-1.0),
                     reads=[z.r], writes=[z.r])
                P.op("act", lambda e, z=z: e.activation(out=z.t[:], in_=z.t[:], func=AF.Ln, bias=1.0, scale=1.0),
                     reads=[z.r], writes=[z.r])
                P.dma("sp", lambda e, z=z, tt=tt: e.dma_start(out=nlfo[tt * 128:(tt + 1) * 128, :], in_=z.t[:]),
                      reads=[z.r], writes=[nlfo_r])
            for sl in range(2):
                P.coll(("n", sl), lambda e, sl=sl: e.collective_compute(
                    "AllGather", ALU.bypass, replica_groups=GROUPS, ins=[nlfo[sl * 512:(sl + 1) * 512, :]],
                    outs=[NL[(3 + sl * 4) * 512:(3 + sl * 4 + 4) * 512, :]]), reads=[nlfo_r], writes=[NL_r[sl]])
        def load_wk(h):
            kb_ = kvbuf[h % 2]
            P.dma("pool", lambda e, kb_=kb_, h=h: e.dma_start(out=kb_.t[:].rearrange("p (c n) -> p c n", c=NCH),
                                                             in_=wk_ap(h)), writes=[kb_.r])
        load_wk(0)
        for h in range(NH):
            kb = kvbuf[h % 2]
            w_ap = kb.t[:].rearrange("p (c n) -> p c n", c=NCH)
            o = qT[h % 2]
            if h + 1 < NH:
                load_wk(h + 1)
            for hf in range(2):
                pp = psG[(2 * h + hf) % 4]
                for c in range(NCH):
                    P.op("pe", lambda e, c=c, pp=pp, w_ap=w_ap, hf=hf: e.matmul(
                        pp.t[:], lhsT=w_ap[:, c, :], rhs=C.hT.t[:, c, hf * 512:(hf + 1) * 512],
                        start=(c == 0), stop=(c == NCH - 1)), reads=[kb.r, C.hr[c]], writes=[pp.r])
                if hf == 0:
                    P.op("act", lambda e, o=o, pp=pp: e.activation(out=o.t[:, 0:512], in_=pp.t[:], func=AF.Copy),
                         reads=[pp.r], writes=[o.r])
                else:
                    P.op("dve", lambda e, o=o, pp=pp: e.tensor_copy(out=o.t[:, 512:1024], in_=pp.t[:]),
                         reads=[pp.r], writes=[o.r])
            for sl in range(2):
                r0 = ((h // 8) * 2 + sl) * 1024 + (h % 8) * 128
                P.dma("sp", lambda e, o=o, r0=r0, sl=sl: e.dma_start(out=kTo[st][r0:r0 + 128, :],
                                                                     in_=o.t[:, sl * 512:(sl + 1) * 512]),
                      reads=[o.r], writes=[kTo_r[st][h]])
            if h % 8 == 7:
                i = h // 8
                for sl in range(2):
                    P.coll(("k", i, sl), lambda e, i=i, sl=sl: e.collective_compute(
                        "AllGather", ALU.bypass, replica_groups=GROUPS,
                        ins=[kTo[st][(i * 2 + sl) * 1024:(i * 2 + sl + 1) * 1024, :]],
                        outs=[KL[st][(i * NSEGP + 3 + sl * 4) * 1024:(i * NSEGP + 3 + sl * 4 + 4) * 1024, :]]),
                        reads=kTo_r[st][i * 8:(i + 1) * 8], writes=[KL_r[st][i][sl]])
        k = 0
        def load_wv(b):
            P.dma("pool", lambda e, b=b: e.dma_start(out=wvb_ap(b % 2), in_=wv_ap(b)), writes=wvb_res(b % 2))
        load_wv(0)
        for b in range(8):
            w_ap = wvb_ap(b % 2)
            w_rs = wvb_res(b % 2)
            if b + 1 < 8:
                load_wv(b + 1)
            for tt in range(NT):
                pp = psG[k % 4]
                o = vo[k % 4]
                for c in range(NCH):
                    P.op("pe", lambda e, c=c, tt=tt, w_ap=w_ap, pp=pp: e.matmul(
                        pp.t[:], lhsT=C.hT.t[:, c, tt * 128:(tt + 1) * 128], rhs=w_ap[:, c, :],
                        start=(c == 0), stop=(c == NCH - 1)), reads=w_rs + [C.hr[c]], writes=[pp.r])
                if k % 2 == 0:
                    P.op("act", lambda e, o=o, pp=pp: e.activation(out=o.t[:], in_=pp.t[:], func=AF.Copy),
                         reads=[pp.r], writes=[o.r])
                else:
                    P.op("dve", lambda e, o=o, pp=pp: e.tensor_copy(out=o.t[:], in_=pp.t[:]), reads=[pp.r], writes=[o.r])
                r0 = (((b // 2) * 2 + tt // 4) * 2 + b % 2) * 512 + (tt % 4) * 128
                P.dma("sp", lambda e, o=o, r0=r0: e.dma_start(out=Vo[st][r0:r0 + 128, :], in_=o.t[:]),
                      reads=[o.r], writes=[Vo_r[st][b]])
                k += 1
            if b % 2 == 1:
                p_ = b // 2
                for sl in range(2):
                    P.coll(("v", p_, sl), lambda e, p_=p_, sl=sl: e.collective_compute(
                        "AllGather", ALU.bypass, replica_groups=GROUPS,
                        ins=[Vo[st][(p_ * 2 + sl) * 1024:(p_ * 2 + sl + 1) * 1024, :]],
                        outs=[VL[st][(p_ * NSEGP + 3 + sl * 4) * 1024:(p_ * NSEGP + 3 + sl * 4 + 4) * 1024, :]]),
                        reads=[Vo_r[st][b - 1], Vo_r[st][b]], writes=[VL_r[st][p_][sl]])

    def decay_prep():
        nlf_t = sg[0].t[:].rearrange("p (u h) -> p u h", u=32)
        tot_t = sg[1].t[:].rearrange("p (u h) -> p u h", u=32)
        for s in range(2):
            U = UMAX[s]
            for a in range(U // 4):
                off = (3 + 4 * s - a) * NSEGE
                P.dma("sp", lambda e, a=a, off=off: e.dma_start(
                    out=nlf_t[:, 4 * a:4 * a + 4, :],
                    in_=bass.AP(Nloc, off, [[NH, 128], [128 * NH, 4], [1, NH]])),
                    reads=[Nloc_r], writes=[sg[0].r])
            nflat = sg[0].t[:, 0:U * NH]
            ndflat = ND[s].t[:].rearrange("p u h -> p (u h)")
            totflat = sg[1].t[:, 0:U * NH]
            for j in range(U * NH // 512):
                sl_ = slice(j * 512, (j + 1) * 512)
                pa = psG[(2 * j) % 4]
                pb = psG[(2 * j + 1) % 4]
                P.op("pe", lambda e, pa=pa, sl_=sl_, nflat=nflat: e.matmul(pa.t[:], lhsT=tri_ap, rhs=nflat[:, sl_],
                                                                           start=True, stop=True),
                     reads=[cst.r, sg[0].r], writes=[pa.r])
                P.op("pe", lambda e, pb=pb, sl_=sl_, nflat=nflat: e.matmul(pb.t[:], lhsT=C.ones_f.t[:], rhs=nflat[:, sl_],
                                                                           start=True, stop=True),
                     reads=[C.ones_f.r, sg[0].r], writes=[pb.r])
                P.op("act", lambda e, pa=pa, sl_=sl_, ndflat=ndflat: e.activation(out=ndflat[:, sl_], in_=pa.t[:],
                                                                                  func=AF.Copy),
                     reads=[pa.r], writes=[ND[s].r])
                P.op("dve", lambda e, pb=pb, sl_=sl_, totflat=totflat: e.tensor_copy(out=totflat[:, sl_], in_=pb.t[:]),
                     reads=[pb.r], writes=[sg[1].r])

            def uof(v):
                return 4 * (v // 4) + 3 - (v % 4)
            for v in range(1, U):
                u1, u0 = uof(v), uof(v - 1)
                P.op("dve", lambda e, s=s, u1=u1, u0=u0: e.tensor_tensor(out=ND[s].t[:, u1, :], in0=ND[s].t[:, u1, :],
                                                                         in1=tot_t[:, u0, :], op=ALU.add),
                     reads=[sg[1].r, ND[s].r], writes=[ND[s].r])
                if v < U - 1:
                    P.op("dve", lambda e, u1=u1, u0=u0: e.tensor_tensor(out=tot_t[:, u1, :], in0=tot_t[:, u1, :],
                                                                        in1=tot_t[:, u0, :], op=ALU.add),
                         reads=[sg[1].r], writes=[sg[1].r])
            P.op("dve", lambda e, s=s, U=U: e.tensor_tensor(
                out=SC[s].t[:], in0=kval[s].t[:].unsqueeze(2).to_broadcast([128, U, NH]), in1=ND[s].t[:],
                op=ALU.subtract), reads=[kval[s].r, ND[s].r], writes=[SC[s].r])

    cnt = {"K": 0, "U": 0, "A": 0}

    def mix_phase(isb, st, gidx, wq_ap, wg_ap, wo_ap, rb_layer):
        NU = UMAX if isb else (8, 8)
        if isb:
            phase_norm(C, gns_d[gidx], "g%d" % gidx, psG[0], psG[1])
        if isb and gidx == 3:
            decay_prep()
        for grp in range(NH // HG):
            P.dma("pool", lambda e, grp=grp: e.dma_start(out=wob_ap, in_=wo_ap(grp)), writes=[WB0.r])
            for hh in range(HG):
                h = grp * HG + hh
                i8 = h // 8
                b4 = h // 4
                wq_, wg_ = wqb_ap(h % 2), wgb_ap(h % 2)

                def load_qg(h2):
                    P.dma("pool", lambda e, h2=h2: e.dma_start(out=wqb_ap(h2 % 2), in_=wq_ap(h2)), writes=[wq_r[h2 % 2]])
                    P.dma("pool", lambda e, h2=h2: e.dma_start(out=wgb_ap(h2 % 2), in_=wg_ap(h2)), writes=[wg_r[h2 % 2]])
                if h == 0:
                    load_qg(0)
                if h + 1 < NH:
                    load_qg(h + 1)
                q_ = qT[h % 2]
                g_ = sg[h % 2]
                for hf in range(2):
                    pp = psG2[hf]
                    for c in range(NCH):
                        P.op("pe", lambda e, c=c, pp=pp, wq_=wq_, hf=hf: e.matmul(
                            pp.t[:], lhsT=wq_[:, c, :], rhs=C.hT.t[:, c, hf * 512:(hf + 1) * 512],
                            start=(c == 0), stop=(c == NCH - 1)), reads=[wq_r[h % 2], C.hr[c]], writes=[pp.r])
                    P.op("dve", lambda e, q_=q_, pp=pp, hf=hf: e.tensor_scalar(
                        out=q_.t[:, hf * 512:(hf + 1) * 512], in0=pp.t[:], scalar1=SCALE, scalar2=None, op0=ALU.mult),
                        reads=[pp.r], writes=[q_.r])
                for hf in range(2):
                    pp = psG2[hf]
                    for c in range(NCH):
                        P.op("pe", lambda e, c=c, pp=pp, wg_=wg_, hf=hf: e.matmul(
                            pp.t[:], lhsT=wg_[:, c, :], rhs=C.hT.t[:, c, hf * 512:(hf + 1) * 512],
                            start=(c == 0), stop=(c == NCH - 1)), reads=[wg_r[h % 2], C.hr[c]], writes=[pp.r])
                    P.op("act", lambda e, g_=g_, pp=pp, hf=hf: e.activation(
                        out=g_.t[:, hf * 512:(hf + 1) * 512], in_=pp.t[:], func=AF.Silu),
                        reads=[pp.r], writes=[g_.r])
                if not isb:
                    bt = BT[h % 2]
                    src = bass.AP(rbx_d.tensor, (rb_layer * NH + h) * 767, [[1, 128], [1, 640]])
                    P.dma("sp", lambda e, bt=bt, src=src: e.dma_start(out=bt.t[:], in_=src), writes=[bt.r])
                    pj = (psG2[0], psG2[1])
                    P.op("pe", lambda e, bt=bt: e.matmul(pj[0].t[:], lhsT=J_ap, rhs=bt.t[:, 0:512], start=True, stop=True),
                         reads=[cst.r, bt.r], writes=[pj[0].r])
                    P.op("pe", lambda e, bt=bt: e.matmul(pj[1].t[:, 0:128], lhsT=J_ap, rhs=bt.t[:, 512:640], start=True,
                                                         stop=True), reads=[cst.r, bt.r], writes=[pj[1].r])
                    P.op("dve", lambda e, bt=bt: e.tensor_tensor(out=bt.t[:, 0:512], in0=pj[0].t[:], in1=cmask.t[:, 0:512],
                                                                 op=ALU.add), reads=[pj[0].r, cmask.r], writes=[bt.r])
                    P.op("dve", lambda e, bt=bt: e.tensor_tensor(out=bt.t[:, 512:640], in0=pj[1].t[:, 0:128],
                                                                 in1=cmask.t[:, 512:640], op=ALU.add),
                         reads=[pj[1].r, cmask.r], writes=[bt.r])
                for s in range(2):
                    U = NU[s]
                    if isb:
                        dx = dexp[(2 * h + s) % 2]
                        for i in range(4):
                            P.op("pool", lambda e, dx=dx, i=i, s=s, h=h: e.tensor_tensor(
                                out=dx.t[:, i, :], in0=ident_ap, in1=ND[s].t[:, i, h:h + 1].to_broadcast([128, 128]),
                                op=ALU.mult), reads=[cst.r, ND[s].r], writes=[dx.r])
                        pq = psG2[(2 * h + s) % 2]
                        P.op("pe", lambda e, dx=dx, pq=pq: e.matmul(pq.t[:], lhsT=C.ones_b.t[:],
                                                                    rhs=dx.t[:].rearrange("p i q -> p (i q)"),
                                                                    start=True, stop=True),
                             reads=[C.ones_b.r, dx.r], writes=[pq.r])
                        P.op("act", lambda e, pq=pq: e.activation(out=NDQ.t[:, 0:512], in_=pq.t[:], func=AF.Copy),
                             reads=[pq.r], writes=[NDQ.r])
                        P.op("pool", lambda e: e.tensor_tensor(
                            out=NDQd.t[:, 0:512].rearrange("p (i q) -> p i q", i=4),
                            in0=NDQ.t[:, 0:512].rearrange("p (i q) -> p i q", i=4),
                            in1=trim_ap.unsqueeze(1).to_broadcast([128, 4, 128]), op=ALU.add),
                            reads=[NDQ.r, cst.r], writes=[NDQd.r])
                    first = True
                    units = []
                    chunk_loads = []
                    psO = psOs[cnt["A"] % 2]
                    psSm = psSms[cnt["A"] % 2]
                    cnt["A"] += 1
                    for ch in range(U // 8):
                        kb = kvbuf[cnt["K"] % 3]
                        cnt["K"] += 1
                        kx_ap = kb.t[:, 0:1024]
                        vx_ap = kb.t[:, 1024:2048].rearrange("p (u d) -> p u d", u=8)
                        kdeps = [Kloc_r[i8]]
                        vdeps = [Vloc_r[b4 // 2]]
                        def load_chunk(kb=kb, kx_ap=kx_ap, vx_ap=vx_ap, ch=ch, kdeps=kdeps, vdeps=vdeps, s=s, h=h,
                                       i8=i8, b4=b4):
                            for s2 in range(2):
                                if isb:
                                    sgp = 3 + 4 * s - (2 * ch + s2)
                                else:
                                    sgp = 3 + 4 * s - 1 + s2
                                koff = ((i8 * 8 + sgp) * 1024 + (h % 8) * 128) * 512
                                voff = ((((b4 // 2) * 8 + sgp) * 2 + b4 % 2) * 512) * 512 + (h % 4) * 128
                                P.dma("sp", lambda e, s2=s2, koff=koff: e.dma_start(
                                    out=kx_ap[:, s2 * 512:(s2 + 1) * 512],
                                    in_=bass.AP(Kloc, koff, [[512, 128], [1, 512]])),
                                    reads=kdeps, writes=[kb.r])
                                P.dma("sp", lambda e, s2=s2, voff=voff: e.dma_start(
                                    out=vx_ap[:, s2 * 4:(s2 + 1) * 4, :],
                                    in_=bass.AP(Vloc, voff, [[512, 128], [128 * 512, 4], [1, 128]])),
                                    reads=vdeps, writes=[kb.r])
                        chunk_loads.append(load_chunk)
                        if ch == 0:
                            order = [3, 4, 0, 1, 2, 5, 6, 7] if not isb else list(range(8))
                        else:
                            order = list(range(8))
                        for ul in order:
                            u = ch * 8 + ul
                            if not isb:
                                ilo, ihi = max(0, u - 4), min(3, u)
                                rlo = ilo + 4 - u
                                ncol = (ihi - ilo + 1) * 128
                                adds = [(0, ncol, BT[h % 2].t[:, rlo * 128:rlo * 128 + ncol], BT[h % 2].r)]
                                sc_ap = kmask.t[:, s, u:u + 1]
                                ares = [kmask.r]
                            else:
                                if u < 4:
                                    ilo, ihi = u, 3
                                    ncol = (ihi - ilo + 1) * 128
                                    adds = [(0, 128, NDQd.t[:, ilo * 128:(ilo + 1) * 128], NDQd.r)]
                                    if ncol > 128:
                                        adds.append((128, ncol - 128, NDQ.t[:, (ilo + 1) * 128:512], NDQ.r))
                                else:
                                    ilo, ihi = 0, 3
                                    ncol = 512
                                    adds = [(0, 512, NDQ.t[:, 0:512], NDQ.r)]
                                sc_ap = SC[s].t[:, u, h:h + 1]
                                ares = [SC[s].r]
                            c0 = ilo * 128
                            n_ = cnt["U"]
                            cnt["U"] += 1
                            units.append((kx_ap[:, ul * 128:(ul + 1) * 128], vx_ap[:, ul, :],
                                          q_.t[:, s * 512 + c0:s * 512 + c0 + ncol], ncol, sc_ap, adds, c0, first,
                                          kb.r, q_.r, ares, psS[n_ % 3], Sb[n_ % 4], PT[n_ % 4]))
                            first = False
                    LA = 2
                    for idx in range(len(units) + LA):
                        if idx < len(units) and idx % 8 == 0:
                            ch_ = idx // 8
                            if ch_ == 0:
                                chunk_loads[0]()
                            if ch_ + 1 < len(chunk_loads):
                                chunk_loads[ch_ + 1]()
                        if idx < len(units):
                            (kt_, v_, qa_, ncol, sc_ap, adds, c0, fst, kr_, qr_, ares, pS_, Sb_, PT_) = units[idx]
                            P.op("pe", lambda e, pS_=pS_, ncol=ncol, kt_=kt_, qa_=qa_: e.matmul(
                                pS_.t[:, 0:ncol], lhsT=kt_, rhs=qa_, start=True, stop=True),
                                reads=[kr_, qr_], writes=[pS_.r])
                        if idx >= LA:
                            (kt_, v_, qa_, ncol, sc_ap, adds, c0, fst, kr_, qr_, ares, pS_, Sb_, PT_) = units[idx - LA]
                            for (o_, n2, ap_, r_) in adds:
                                P.op("dve", lambda e, o_=o_, n2=n2, ap_=ap_, pS_=pS_, Sb_=Sb_, sc_ap=sc_ap: e.scalar_tensor_tensor(
                                    out=Sb_.t[:, o_:o_ + n2], in0=pS_.t[:, o_:o_ + n2], scalar=sc_ap, in1=ap_,
                                    op0=ALU.add, op1=ALU.add), reads=[pS_.r, r_] + list(ares), writes=[Sb_.r])
                            P.op("act", lambda e, PT_=PT_, Sb_=Sb_, ncol=ncol: e.activation(
                                out=PT_.t[:, 0:ncol], in_=Sb_.t[:, 0:ncol], func=AF.Exp), reads=[Sb_.r], writes=[PT_.r])
                            P.op("pe", lambda e, v_=v_, PT_=PT_, ncol=ncol, c0=c0, fst=fst, psO=psO: e.matmul(
                                psO.t[:, c0:c0 + ncol], lhsT=v_, rhs=PT_.t[:, 0:ncol], start=fst, stop=False,
                                skip_group_check=True), reads=[kr_, PT_.r], writes=[psO.r])
                            P.op("pe", lambda e, PT_=PT_, ncol=ncol, c0=c0, fst=fst, psSm=psSm: e.matmul(
                                psSm.t[:, c0:c0 + ncol], lhsT=C.ones_b.t[:], rhs=PT_.t[:, 0:ncol], start=fst, stop=False,
                                skip_group_check=True), reads=[C.ones_b.r, PT_.r], writes=[psSm.r])
                    P.op("dve", lambda e, psSm=psSm: e.reciprocal(out=rs.t[:], in_=psSm.t[:]), reads=[psSm.r], writes=[rs.r])
                    P.op("pool", lambda e, g_=g_, s=s: e.tensor_tensor(out=wgt.t[:], in0=rs.t[:],
                                                                       in1=g_.t[:, s * 512:(s + 1) * 512], op=ALU.mult),
                         reads=[rs.r, g_.r], writes=[wgt.r])
                    P.op("dve", lambda e, hh=hh, s=s, psO=psO: e.tensor_tensor(out=og.t[:, hh, s * 512:(s + 1) * 512],
                                                                      in0=psO.t[:], in1=wgt.t[:], op=ALU.mult),
                         reads=[psO.r, wgt.r], writes=[og.r])
            k = 0
            for c in range(NCH):
                for hf in range(2):
                    pp = psG2[k % 2]
                    k += 1
                    for hh in range(HG):
                        P.op("pe", lambda e, pp=pp, hh=hh, c=c, hf=hf: e.matmul(
                            pp.t[:], lhsT=wob_ap[:, hh, c * 128:(c + 1) * 128], rhs=og.t[:, hh, hf * 512:(hf + 1) * 512],
                            start=(hh == 0), stop=(hh == HG - 1)), reads=[WB0.r, og.r], writes=[pp.r])
                    P.op("dve", lambda e, pp=pp, c=c, hf=hf: e.tensor_tensor(
                        out=C.xT.t[:, c, hf * 512:(hf + 1) * 512], in0=pp.t[:], in1=C.xT.t[:, c, hf * 512:(hf + 1) * 512],
                        op=ALU.add), reads=[pp.r, C.xr[c]], writes=[C.xr[c]])

    for l in range(2):
        kv_phase(0, l, lambda h, l=l: a_wk_d[l, h], lambda b, l=l: a_wv_d[l, b], False)
        localize(False)
        mix_phase(False, 0, l, lambda h, l=l: a_wq_d[l, h], lambda h, l=l: a_wg_d[l, h], lambda g, l=l: a_wo_d[l, g], l)
    kv_phase(0, 2, lambda h: s_wk_d[h], lambda b: s_wv_d[b], True)
    localize(True)
    for l in range(2):
        mix_phase(True, 0, 3 + l, lambda h, l=l: b_wq_d[l, h], lambda h, l=l: b_wg_d[l, h], lambda g, l=l: b_wo_d[l, g], 0)

    gf = C.sb("gfin", [128, NCH], F32)
    P.dma("sp", lambda e: e.dma_start(out=gf.t[:], in_=gns_d[5]), writes=[gf.r])
    pss = (psG[0], psG[1])
    for c in range(NCH):
        sq = sg[c % 2]
        P.op("act", lambda e, c=c, sq=sq: e.activation(out=sq.t[:], in_=C.xT.t[:, c, :], func=AF.Square),
             reads=[C.xr[c]], writes=[sq.r])
        for hf in range(2):
            P.op("pe", lambda e, c=c, sq=sq, hf=hf: e.matmul(
                pss[hf].t[:], lhsT=C.ones_f.t[:], rhs=sq.t[:, hf * 512:(hf + 1) * 512],
                start=(c == 0), stop=(c == NCH - 1)), reads=[sq.r, C.ones_f.r], writes=[pss[hf].r])
    for hf in range(2):
        sl = slice(hf * 512, (hf + 1) * 512)
        P.op("act", lambda e, hf=hf, sl=sl: e.activation(
            out=C.rstd.t[:, sl], in_=pss[hf].t[:], func=AF.Sqrt, bias=EPS, scale=1.0 / D_MODEL),
            reads=[pss[hf].r], writes=[C.rstd.r])
    P.op("dve", lambda e: e.reciprocal(out=C.rstd.t[:], in_=C.rstd.t[:]), reads=[C.rstd.r], writes=[C.rstd.r])
    for c in range(NCH):
        yb = sg[c % 2]
        P.op("dve", lambda e, c=c, yb=yb: e.scalar_tensor_tensor(
            out=yb.t[:], in0=C.xT.t[:, c, :], scalar=gf.t[:, c:c + 1], in1=C.rstd.t[:],
            op0=ALU.mult, op1=ALU.mult), reads=[C.xr[c], gf.r, C.rstd.r], writes=[yb.r])
        C.outs.append(P.dma("sp", lambda e, c=c, yb=yb: e.dma_start(out=yo_d[:, c, :], in_=yb.t[:]), reads=[yb.r]))
    return C.finish()


def kernel(x, a_norm, a_w_in, a_rel_bias, a_w_out, kv_norm, kv_w, f_w, f_b, b_norm, b_w_in, b_w_out, final_norm):
    f = lambda a: np.asarray(a, np.float32)
    x = f(x)
    a_w_in, a_w_out, kv_w, b_w_in, b_w_out = f(a_w_in), f(a_w_out), f(kv_w), f(b_w_in), f(b_w_out)
    gns = np.stack([tile_vec(f(a_norm)[0]), tile_vec(f(a_norm)[1]), tile_vec(f(kv_norm)), tile_vec(f(b_norm)[0]),
                    tile_vec(f(b_norm)[1]), tile_vec(f(final_norm))])
    shared = {
        "gns": gns,
        "a_wq": np.stack([tile_w_cols(a_w_in[l], 0, D_INNER, 128) for l in range(2)]),
        "a_wk": np.stack([tile_w_cols(a_w_in[l], D_INNER, D_INNER, 128) for l in range(2)]),
        "a_wv": np.stack([tile_w_cols(a_w_in[l], 2 * D_INNER, D_INNER, 512) for l in range(2)]),
        "a_wg": np.stack([tile_w_cols(a_w_in[l], 3 * D_INNER, D_INNER, 128) for l in range(2)]),
        "a_wo": np.stack([_tile_wo(a_w_out[l]) for l in range(2)]),
        "s_wk": tile_w_cols(kv_w, 0, D_INNER, 128),
        "s_wv": tile_w_cols(kv_w, D_INNER, D_INNER, 512),
        "b_wq": np.stack([tile_w_cols(b_w_in[l], 0, D_INNER, 128) for l in range(2)]),
        "b_wg": np.stack([tile_w_cols(b_w_in[l], D_INNER, D_INNER, 128) for l in range(2)]),
        "b_wo": np.stack([_tile_wo(b_w_out[l]) for l in range(2)]),
        "fw": np.ascontiguousarray(f(f_w).reshape(NCH, 128, NH).transpose(1, 0, 2)),
        "fb": np.ascontiguousarray(np.broadcast_to(f(f_b)[None, :], (128, NH))),
    }
    idx = np.clip(np.arange(767) - 127, -256, 256) + 256
    shared["rbx"] = np.ascontiguousarray(f(a_rel_bias)[:, :, idx])
    cmask = np.zeros((128, 5, 128), np.float32)
    cmask[64:, 0, :64] = NEG
    cmask[:64, 4, 64:] = NEG
    shared["cmask"] = cmask.reshape(128, 640)
    kl = np.arange(128)
    cst = np.zeros((4, 128, 128), np.float32)
    cst[0] = np.eye(128, dtype=np.float32)
    cst[1] = (kl[:, None] > kl[None, :]).astype(np.float32)
    cst[2] = np.where(kl[:, None] > kl[None, :], NEG, 0.0)
    cst[3] = np.eye(128, dtype=np.float32)[::-1]
    shared["cst"] = cst
    in_maps = []
    for j in range(8):
        jj = j % 4
        m = dict(shared)
        m["xT"] = to_xT(x[j // 4][core_tokens(j)])
        kmask = np.zeros((128, 2, 8), np.float32)
        if jj == 0:
            kmask[:, 0, :4] = NEG
        m["kmask"] = kmask
        for s in range(2):
            kv = np.full((128, UMAX[s]), NEG, np.float32)
            for u in range(UMAX[s]):
                if jj + 4 * s - u // 4 >= 0:
                    kv[:, u] = 0.0
            m["kval%d" % s] = kv
        in_maps.append(m)
    if "fused" not in _NC_CACHE:
        _NC_CACHE["fused"] = build_fused()
    res = run_bass_kernel_spmd(_NC_CACHE["fused"], in_maps, core_ids=list(range(8)))
    out = np.zeros((BATCH, SEQ, D_MODEL), np.float32)
    for j in range(8):
        out[j // 4][core_tokens(j)] = from_xT(res.results[j]["yo"])
    return out
```

```python
import numpy as np
from contextlib import ExitStack
import ml_dtypes
import concourse.bass as bass
import concourse.mybir as mybir
from concourse.bass_utils import run_bass_kernel_spmd

F32 = mybir.dt.float32
BF16 = mybir.dt.bfloat16
AF = mybir.ActivationFunctionType
ALU = mybir.AluOpType
NPBF = ml_dtypes.bfloat16

D_MODEL = 2048
NCH = 16
D_INNER = 4096
NH = 32
DH = 128
SEQ = 4096
BATCH = 2
T = 1024
SEG = 512
NSEG = 2
EPS = 1e-6
NEG = -30000.0
SCALE = DH ** -0.5
HG = 4
UMAX = (16, 32)


class Res:
    __slots__ = ("name", "w", "rs")

    def __init__(self, name=""):
        self.name = name
        self.w = None
        self.rs = {}


class Op:
    __slots__ = ("eng", "fn", "deps", "dma", "sig", "need", "n", "cc")

    def __init__(self, eng, fn, dma, cc=None):
        self.eng = eng
        self.fn = fn
        self.dma = dma
        self.deps = []
        self.sig = None
        self.need = dma
        self.n = 0
        self.cc = cc


class Prog:
    ENGS = ("pe", "act", "dve", "pool", "sp")
    NDSEM = 24
    EPOCH = 30000

    def __init__(self, nc):
        self.nc = nc
        self.ops = {e: [] for e in self.ENGS}
        self.ndma = {e: 0 for e in self.ENGS}
        self.count = 0
        self.ccs = {}

    def _track(self, o, reads, writes):
        deps = {}
        for r in reads:
            if r.w is not None:
                deps[id(r.w)] = r.w
        for r in writes:
            if r.w is not None:
                deps[id(r.w)] = r.w
            for x in r.rs.values():
                deps[id(x)] = x
        for d in deps.values():
            if d is o:
                continue
            if (not d.dma) and (not o.dma) and d.eng == "pe" and o.eng == "pe":
                continue
            d.need = True
            o.deps.append(d)
        for r in reads:
            key = ("dma", self.count) if o.dma else o.eng
            r.rs[key] = o
        for r in writes:
            r.w = o
            r.rs = {}
        self.count += 1

    def op(self, eng, fn, reads=(), writes=()):
        o = Op(eng, fn, False)
        self._track(o, reads, writes)
        self.ops[eng].append(o)
        return o

    def dma(self, q, fn, reads=(), writes=()):
        o = Op(q, fn, True)
        self._track(o, reads, writes)
        self.ops[q].append(o)
        return o

    def coll(self, key, fn, reads=(), writes=()):
        o = Op("pool", fn, True, cc=key)
        self._track(o, reads, writes)
        self.ops["pool"].append(o)
        return o

    def emit(self, es, final_deps):
        nc = self.nc
        fin = Op("sp", None, False)
        for d in final_deps:
            d.need = True
            fin.deps.append(d)
        self.ops["sp"].append(fin)
        sems = {}
        for e in self.ENGS:
            cnt = 0
            nd = 0
            esems = []
            dsems = []
            for o in self.ops[e]:
                if o.cc is not None:
                    if o.cc not in self.ccs:
                        self.ccs[o.cc] = [es.enter_context(nc.semaphore("cc_%s" % str(o.cc))), 0]
                    self.ccs[o.cc][1] += 1
                    o.sig = (self.ccs[o.cc][0], self.ccs[o.cc][1])
                elif o.dma:
                    k = nd % self.NDSEM
                    if k >= len(dsems):
                        dsems.append(es.enter_context(nc.semaphore("d_%s_%d" % (e, k))))
                    o.sig = (dsems[k], 16 * (nd // self.NDSEM + 1))
                    o.n = nd
                    nd += 1
                elif o.need:
                    ep = cnt // self.EPOCH
                    if ep >= len(esems):
                        esems.append(es.enter_context(nc.semaphore("c_%s_%d" % (e, ep))))
                    o.sig = (esems[ep], cnt % self.EPOCH + 1)
                    cnt += 1
            sems[e] = (esems, dsems)
        engobj = {"pe": nc.tensor, "act": nc.scalar, "dve": nc.vector, "pool": nc.gpsimd, "sp": nc.sync}
        block = es.enter_context(nc.Block())

        def make(e):
            def body(eng):
                waited = {}
                pre = getattr(self, "pre", {}).get(e)
                if pre is not None:
                    pre(eng)
                for o in self.ops[e]:
                    ws = []
                    for d in o.deps:
                        ws.append(d.sig)
                    if o.dma and o.cc is None and o.n >= self.NDSEM:
                        ws.append((o.sig[0], o.sig[1] - 16))
                    for (s, v) in ws:
                        if waited.get(id(s), 0) < v:
                            waited[id(s)] = v
                            eng.wait_ge(s, v)
                    if o.fn is None:
                        continue
                    ins = o.fn(eng)
                    if o.cc is not None:
                        ins.then_inc(o.sig[0], 1)
                    elif o.dma:
                        ins.then_inc(o.sig[0], 16)
                    elif o.sig is not None:
                        ins.then_inc(o.sig[0], 1)
            return body

        block.tensor(make("pe"))
        block.scalar(make("act"))
        block.vector(make("dve"))
        block.gpsimd(make("pool"))
        block.sync(make("sp"))


class Tl:
    __slots__ = ("t", "r")

    def __init__(self, t, name):
        self.t = t
        self.r = Res(name)


class Ctx:
    def __init__(self):
        self.nc = bass.Bass("TRN2", target_bir_lowering=False)
        self.P = Prog(self.nc)
        self.es = ExitStack()
        self.outs = []
        self.nps = 0

    def dram(self, name, shape, dt, kind):
        return self.nc.dram_tensor(name, list(shape), dt, kind=kind).ap()

    def sb(self, name, shape, dt):
        return Tl(self.es.enter_context(self.nc.sbuf_tensor("s_" + name, list(shape), dt)), name)

    def ps(self, name, dt=F32):
        n = 512 if dt == F32 else 1024
        return Tl(self.es.enter_context(self.nc.psum_tensor("p_" + name, [128, n], dt)), name)

    def finish(self):
        self.P.emit(self.es, self.outs)
        self.es.close()
        return self.nc


def phase_consts(C):
    P = C.P
    C.ones_f = C.sb("ones_f", [128, 128], F32)
    C.ones_b = C.sb("ones_b", [128, 128], BF16)
    P.op("pool", lambda e: e.memset(C.ones_f.t[:], 1.0), writes=[C.ones_f.r])
    P.op("pool", lambda e: e.memset(C.ones_b.t[:], 1.0), writes=[C.ones_b.r])


def phase_load_x(C, xT_d):
    P = C.P
    C.xT = C.sb("xT", [128, NCH, T], F32)
    C.xr = [Res("x%d" % c) for c in range(NCH)]
    for c0 in range(0, NCH, 4):
        P.dma("sp", lambda e, c0=c0: e.dma_start(out=C.xT.t[:, c0:c0 + 4, :], in_=xT_d[:, c0:c0 + 4, :]),
              writes=C.xr[c0:c0 + 4])


def phase_norm(C, gn_d, tag, psA, psB, want_tok_rstd=False):
    P = C.P
    if not hasattr(C, "hT"):
        C.hT = C.sb("hT", [128, NCH, T], BF16)
        C.hr = [Res("h%d" % c) for c in range(NCH)]
        C.xsq = [C.sb("xsq%d" % i, [128, T], F32) for i in range(2)]
        C.rstd = C.sb("rstd", [128, T], F32)
    gn = C.sb("gn_" + tag, [128, NCH], F32)
    P.dma("sp", lambda e: e.dma_start(out=gn.t[:], in_=gn_d), writes=[gn.r])
    pss = (psA, psB)
    for c in range(NCH):
        sq = C.xsq[c % 2]
        P.op("act", lambda e, c=c, sq=sq: e.activation(out=sq.t[:], in_=C.xT.t[:, c, :], func=AF.Square),
             reads=[C.xr[c]], writes=[sq.r])
        for hf in range(2):
            P.op("pe", lambda e, c=c, sq=sq, hf=hf: e.matmul(
                pss[hf].t[:], lhsT=C.ones_f.t[:], rhs=sq.t[:, hf * 512:(hf + 1) * 512],
                start=(c == 0), stop=(c == NCH - 1)),
                reads=[sq.r, C.ones_f.r], writes=[pss[hf].r])
    for hf in range(2):
        sl = slice(hf * 512, (hf + 1) * 512)
        P.op("act", lambda e, hf=hf, sl=sl: e.activation(
            out=C.rstd.t[:, sl], in_=pss[hf].t[:], func=AF.Sqrt, bias=EPS, scale=1.0 / D_MODEL),
            reads=[pss[hf].r], writes=[C.rstd.r])
    P.op("dve", lambda e: e.reciprocal(out=C.rstd.t[:], in_=C.rstd.t[:]), reads=[C.rstd.r], writes=[C.rstd.r])
    for c in range(NCH):
        P.op("dve", lambda e, c=c: e.scalar_tensor_tensor(
            out=C.hT.t[:, c, :], in0=C.xT.t[:, c, :], scalar=gn.t[:, c:c + 1], in1=C.rstd.t[:],
            op0=ALU.mult, op1=ALU.mult), reads=[C.xr[c], gn.r, C.rstd.r], writes=[C.hr[c]])
    return gn


def proj_fm(C, wtile, ps, hf, extra_reads=()):
    P = C.P
    for c in range(NCH):
        P.op("pe", lambda e, c=c: e.matmul(ps.t[:], lhsT=wtile.t[:, c, :], rhs=C.hT.t[:, c, hf * 512:(hf + 1) * 512],
                                           start=(c == 0), stop=(c == NCH - 1)),
             reads=[wtile.r, C.hr[c]] + list(extra_reads), writes=[ps.r])


def build_kv():
    C = Ctx()
    P = C.P
    xT_d = C.dram("xT", [128, NCH, T], F32, "ExternalInput")
    gn_d = C.dram("gn", [128, NCH], F32, "ExternalInput")
    wk_d = C.dram("wk", [NH, 128, NCH, 128], F32, "ExternalInput")
    wv_d = C.dram("wv", [8, 128, NCH, 512], F32, "ExternalInput")
    fw_d = C.dram("fw", [128, NCH, NH], F32, "ExternalInput")
    fb_d = C.dram("fb", [128, NH], F32, "ExternalInput")
    kT_o = C.dram("kT", [NH, 128, T], BF16, "ExternalOutput")
    V_o = C.dram("V", [T // 128, 128, D_INNER], BF16, "ExternalOutput")
    nlf_o = C.dram("nlf", [T // 128, 128, NH], F32, "ExternalOutput")

    ps = [C.ps("ps%d" % i) for i in range(8)]
    phase_consts(C)
    phase_load_x(C, xT_d)
    gn = phase_norm(C, gn_d, "kv", ps[0], ps[1])

    fw = C.sb("fw", [128, NCH, NH], F32)
    fb = C.sb("fb", [128, NH], F32)
    P.dma("sp", lambda e: e.dma_start(out=fw.t[:], in_=fw_d), writes=[fw.r])
    P.dma("sp", lambda e: e.dma_start(out=fb.t[:], in_=fb_d), writes=[fb.r])
    gfw = C.sb("gfw", [128, NCH, NH], F32)
    for c in range(NCH):
        P.op("pool", lambda e, c=c: e.tensor_scalar(out=gfw.t[:, c, :], in0=fw.t[:, c, :], scalar1=gn.t[:, c:c + 1],
                                                    scalar2=None, op0=ALU.mult),
             reads=[fw.r, gn.r], writes=[gfw.r])
    ones_col = C.sb("ones_col", [128, 1], F32)
    P.op("pool", lambda e: e.memset(ones_col.t[:], 1.0), writes=[ones_col.r])
    xsqc = [C.sb("xsqc%d" % i, [128, 128], F32) for i in range(2)]
    zs = [C.sb("zs%d" % i, [128, NH], F32) for i in range(2)]
    rt = [C.sb("rt%d" % i, [128, 1], F32) for i in range(2)]
    NT = T // 128
    k = 0
    for tt in range(NT):
        pz = ps[2 + (tt % 2) * 2]
        pq = ps[3 + (tt % 2) * 2]
        tsl = slice(tt * 128, (tt + 1) * 128)
        for c in range(NCH):
            sq = xsqc[k % 2]
            k += 1
            P.op("act", lambda e, c=c, sq=sq, tsl=tsl: e.activation(out=sq.t[:], in_=C.xT.t[:, c, tsl], func=AF.Square),
                 reads=[C.xr[c]], writes=[sq.r])
            P.op("pe", lambda e, c=c, sq=sq, pq=pq: e.matmul(pq.t[:, 0:1], lhsT=sq.t[:], rhs=ones_col.t[:],
                                                             start=(c == 0), stop=(c == NCH - 1)),
                 reads=[sq.r, ones_col.r], writes=[pq.r])
            P.op("pe", lambda e, c=c, tsl=tsl, pz=pz: e.matmul(pz.t[:, 0:NH], lhsT=C.xT.t[:, c, tsl], rhs=gfw.t[:, c, :],
                                                               start=(c == 0), stop=(c == NCH - 1)),
                 reads=[C.xr[c], gfw.r], writes=[pz.r])
        r1 = rt[tt % 2]
        z = zs[tt % 2]
        P.op("act", lambda e, r1=r1, pq=pq: e.activation(out=r1.t[:], in_=pq.t[:, 0:1], func=AF.Sqrt, bias=EPS,
                                                         scale=1.0 / D_MODEL), reads=[pq.r], writes=[r1.r])
        P.op("dve", lambda e, r1=r1: e.reciprocal(out=r1.t[:], in_=r1.t[:]), reads=[r1.r], writes=[r1.r])
        P.op("dve", lambda e, r1=r1, z=z, pz=pz: e.scalar_tensor_tensor(
            out=z.t[:], in0=pz.t[:, 0:NH], scalar=r1.t[:, 0:1], in1=fb.t[:], op0=ALU.mult, op1=ALU.add),
            reads=[pz.r, r1.r, fb.r], writes=[z.r])
        P.op("act", lambda e, z=z: e.activation(out=z.t[:], in_=z.t[:], func=AF.Exp, scale=-1.0),
             reads=[z.r], writes=[z.r])
        P.op("act", lambda e, z=z: e.activation(out=z.t[:], in_=z.t[:], func=AF.Ln, bias=1.0, scale=1.0),
             reads=[z.r], writes=[z.r])
        C.outs.append(P.dma("sp", lambda e, z=z, tt=tt: e.dma_start(out=nlf_o[tt], in_=z.t[:]), reads=[z.r]))

    wkb = [C.sb("wkb%d" % i, [128, NCH, 128], BF16) for i in range(2)]
    ko = [C.sb("ko%d" % i, [128, T], BF16) for i in range(2)]
    for h in range(NH):
        w = wkb[h % 2]
        o = ko[h % 2]
        P.dma("pool", lambda e, w=w, h=h: e.dma_start(out=w.t[:], in_=wk_d[h]), writes=[w.r])
        for hf in range(2):
            pp = ps[(2 * h + hf) % 4]
            proj_fm(C, w, pp, hf)
            if hf == 0:
                P.op("act", lambda e, o=o, pp=pp: e.activation(out=o.t[:, 0:512], in_=pp.t[:], func=AF.Copy),
                     reads=[pp.r], writes=[o.r])
            else:
                P.op("dve", lambda e, o=o, pp=pp: e.tensor_copy(out=o.t[:, 512:1024], in_=pp.t[:]),
                     reads=[pp.r], writes=[o.r])
        C.outs.append(P.dma("sp", lambda e, o=o, h=h: e.dma_start(out=kT_o[h], in_=o.t[:]), reads=[o.r]))

    wvb = [C.sb("wvb%d" % i, [128, NCH, 512], BF16) for i in range(2)]
    vo = [C.sb("vo%d" % i, [128, 512], BF16) for i in range(4)]
    k = 0
    for b in range(8):
        w = wvb[b % 2]
        P.dma("pool", lambda e, w=w, b=b: e.dma_start(out=w.t[:], in_=wv_d[b]), writes=[w.r])
        for tt in range(NT):
            pp = ps[4 + k % 4]
            o = vo[k % 4]
            for c in range(NCH):
                P.op("pe", lambda e, c=c, tt=tt, w=w, pp=pp: e.matmul(
                    pp.t[:], lhsT=C.hT.t[:, c, tt * 128:(tt + 1) * 128], rhs=w.t[:, c, :],
                    start=(c == 0), stop=(c == NCH - 1)), reads=[w.r, C.hr[c]], writes=[pp.r])
            if k % 2 == 0:
                P.op("act", lambda e, o=o, pp=pp: e.activation(out=o.t[:], in_=pp.t[:], func=AF.Copy),
                     reads=[pp.r], writes=[o.r])
            else:
                P.op("dve", lambda e, o=o, pp=pp: e.tensor_copy(out=o.t[:], in_=pp.t[:]), reads=[pp.r], writes=[o.r])
            C.outs.append(P.dma("sp", lambda e, o=o, tt=tt, b=b: e.dma_start(
                out=V_o[tt, :, b * 512:(b + 1) * 512], in_=o.t[:]), reads=[o.r]))
            k += 1
    return C.finish()


def core_segments(j):
    jj = j % 4
    return (jj, jj + 4)


def core_tokens(j):
    s0, s1 = core_segments(j)
    return np.concatenate([np.arange(s0 * SEG, (s0 + 1) * SEG), np.arange(s1 * SEG, (s1 + 1) * SEG)])


def tile_w_cols(W, col0, ncols, blk):
    Wc = W[:, col0:col0 + ncols]
    nb = ncols // blk
    return np.ascontiguousarray(Wc.reshape(NCH, 128, nb, blk).transpose(2, 1, 0, 3))


def tile_vec(g):
    return np.ascontiguousarray(g.reshape(NCH, 128).T)


def to_xT(xtok):
    t = xtok.shape[0]
    return np.ascontiguousarray(xtok.T.reshape(NCH, 128, t).transpose(1, 0, 2))


def from_xT(xT):
    t = xT.shape[2]
    return np.ascontiguousarray(xT.transpose(1, 0, 2).reshape(D_MODEL, t).T)


_NC_CACHE = {}


def get_nc(kind):
    if kind not in _NC_CACHE:
        _NC_CACHE[kind] = build_kv() if kind == "kv" else build_mix(kind)
    return _NC_CACHE[kind]


def run_kv(xT_cores, gn, W, koff, voff, f_w, f_b):
    nc = get_nc("kv")
    wk = tile_w_cols(W, koff, D_INNER, 128)
    wv = tile_w_cols(W, voff, D_INNER, 512)
    fw = np.ascontiguousarray(f_w.reshape(NCH, 128, NH).transpose(1, 0, 2))
    fb = np.ascontiguousarray(np.broadcast_to(f_b[None, :], (128, NH)))
    g = tile_vec(gn)
    in_maps = [{"xT": xT_cores[j], "gn": g, "wk": wk, "wv": wv, "fw": fw, "fb": fb} for j in range(8)]
    res = run_bass_kernel_spmd(nc, in_maps, core_ids=list(range(8)))
    return res.results


def attn_unit(C, kt_ap, v_ap, q_ap, ncol, sc_ap, adds, acc_o, acc_s, c0, first, kres, vres, qres, ares, psS, Sb, PT):
    P = C.P
    P.op("pe", lambda e: e.matmul(psS.t[:, 0:ncol], lhsT=kt_ap, rhs=q_ap, start=True, stop=True),
         reads=[kres, qres], writes=[psS.r])
    for (o, n, ap, r) in adds:
        P.op("dve", lambda e, o=o, n=n, ap=ap: e.scalar_tensor_tensor(
            out=Sb.t[:, o:o + n], in0=psS.t[:, o:o + n], scalar=sc_ap, in1=ap, op0=ALU.add, op1=ALU.add),
            reads=[psS.r, r] + list(ares), writes=[Sb.r])
    P.op("act", lambda e: e.activation(out=PT.t[:, 0:ncol], in_=Sb.t[:, 0:ncol], func=AF.Exp),
         reads=[Sb.r], writes=[PT.r])
    P.op("pe", lambda e: e.matmul(acc_o.t[:, c0:c0 + ncol], lhsT=v_ap, rhs=PT.t[:, 0:ncol], start=first, stop=False,
                                  skip_group_check=True),
         reads=[vres, PT.r], writes=[acc_o.r])
    P.op("pe", lambda e: e.matmul(acc_s.t[:, c0:c0 + ncol], lhsT=C.ones_b.t[:], rhs=PT.t[:, 0:ncol], start=first,
                                  stop=False, skip_group_check=True),
         reads=[C.ones_b.r, PT.r], writes=[acc_s.r])


def build_mix(kind):
    C = Ctx()
    P = C.P
    isb = kind == "b"
    xT_d = C.dram("xT", [128, NCH, T], F32, "ExternalInput")
    gn_d = C.dram("gn", [128, NCH], F32, "ExternalInput")
    wq_d = C.dram("wq", [NH, 128, NCH, 128], F32, "ExternalInput")
    wg_d = C.dram("wg", [NH, 128, NCH, 128], F32, "ExternalInput")
    wo_d = C.dram("wo", [NH // HG, 128, HG, D_MODEL], F32, "ExternalInput")
    xo_d = C.dram("xo", [128, NCH, T], F32, "ExternalOutput")
    if not isb:
        NU = (8, 8)
        kT_d = [C.dram("kTs%d" % s, [NH, 128, 8 * 128], BF16, "ExternalInput") for s in range(2)]
        V_d = [C.dram("Vs%d" % s, [8, 128, D_INNER], BF16, "ExternalInput") for s in range(2)]
        kmask_d = C.dram("kmask", [128, 2, 8], F32, "ExternalInput")
        rbx_d = C.dram("rbx", [NH, 767], F32, "ExternalInput")
        cmask_d = C.dram("cmask", [128, 640], F32, "ExternalInput")
    else:
        NU = UMAX
        kT_d = [C.dram("kTs%d" % s, [NH, 128, NU[s] * 128], BF16, "ExternalInput") for s in range(2)]
        V_d = [C.dram("Vs%d" % s, [NU[s], 128, D_INNER], BF16, "ExternalInput") for s in range(2)]
        nlf_d = [C.dram("nlfs%d" % s, [128, NU[s], NH], F32, "ExternalInput") for s in range(2)]
        kval_d = [C.dram("kval%d" % s, [128, NU[s]], F32, "ExternalInput") for s in range(2)]
        gfin_d = C.dram("gfin", [128, NCH], F32, "ExternalInput")
        ident_d = C.dram("ident", [128, 128], F32, "ExternalInput")
        tri_d = C.dram("tri", [128, 128], F32, "ExternalInput")
        trim_d = C.dram("trim", [128, 128], F32, "ExternalInput")
        yo_d = C.dram("yo", [128, NCH, T], F32, "ExternalOutput")

    ps = [C.ps("ps%d" % i) for i in range(8)]
    psS = ps[0:2]
    psO = ps[2]
    psSm = ps[3]
    psG = ps[4:8]
    phase_consts(C)
    phase_load_x(C, xT_d)
    sg = [C.sb("sg%d" % i, [128, T], F32) for i in range(2)]
    C.xsq = sg
    C.hT = C.sb("hT", [128, NCH, T], BF16)
    C.hr = [Res("h%d" % c) for c in range(NCH)]
    C.rstd = C.sb("rstd", [128, T], F32)
    phase_norm(C, gn_d, "n1", psG[0], psG[1])

    wqb = [C.sb("wqb%d" % i, [128, NCH, 128], BF16) for i in range(2)]
    wgb = [C.sb("wgb%d" % i, [128, NCH, 128], BF16) for i in range(2)]
    wob = C.sb("wob", [128, HG, D_MODEL], BF16)
    qT = [C.sb("qT%d" % i, [128, T], BF16) for i in range(2)]
    og = C.sb("og", [128, HG, T], BF16)
    kxc = [C.sb("kxc%d" % i, [128, 8 * 128], BF16) for i in range(2)]
    vxc = [C.sb("vxc%d" % i, [128, 8, 128], BF16) for i in range(2)]
    Sb = [C.sb("Sb%d" % i, [128, 512], F32) for i in range(2)]
    PT = [C.sb("PT%d" % i, [128, 512], BF16) for i in range(2)]
    rs = C.sb("rs", [128, 512], F32)
    wgt = C.sb("wgt", [128, 512], F32)

    if not isb:
        kmask = C.sb("kmask", [128, 2, 8], F32)
        P.dma("sp", lambda e: e.dma_start(out=kmask.t[:], in_=kmask_d), writes=[kmask.r])
        cmask = C.sb("cmask", [128, 640], F32)
        P.dma("sp", lambda e: e.dma_start(out=cmask.t[:], in_=cmask_d), writes=[cmask.r])
        BT = [C.sb("BT%d" % i, [128, 640], F32) for i in range(2)]
    else:
        ident = C.sb("ident", [128, 128], F32)
        tri = C.sb("tri", [128, 128], F32)
        trim = C.sb("trim", [128, 128], F32)
        for (t_, d_) in ((ident, ident_d), (tri, tri_d), (trim, trim_d)):
            P.dma("sp", lambda e, t_=t_, d_=d_: e.dma_start(out=t_.t[:], in_=d_), writes=[t_.r])
        nlf_t = C.sb("nlf_t", [128, 32, NH], F32)
        tot_t = C.sb("tot_t", [128, 32, NH], F32)
        ND = [C.sb("ND%d" % s, [128, NU[s], NH], F32) for s in range(2)]
        SC = [C.sb("SC%d" % s, [128, NU[s], NH], F32) for s in range(2)]
        kval = [C.sb("kval%d" % s, [128, NU[s]], F32) for s in range(2)]
        NDQ = C.sb("NDQ", [128, 512], F32)
        NDQd = C.sb("NDQd", [128, 512], F32)
        dexp = [C.sb("dexp%d" % i, [128, 4, 128], BF16) for i in range(2)]
        for s in range(2):
            U = NU[s]
            P.dma("sp", lambda e, s=s, U=U: e.dma_start(out=nlf_t.t[:, 0:U, :], in_=nlf_d[s]), writes=[nlf_t.r])
            P.dma("sp", lambda e, s=s: e.dma_start(out=kval[s].t[:], in_=kval_d[s]), writes=[kval[s].r])
            nflat = nlf_t.t[:, 0:U, :].rearrange("p u h -> p (u h)")
            ndflat = ND[s].t[:].rearrange("p u h -> p (u h)")
            totflat = tot_t.t[:, 0:U, :].rearrange("p u h -> p (u h)")
            for j in range(U * NH // 512):
                sl = slice(j * 512, (j + 1) * 512)
                pa = psG[(2 * j) % 4]
                pb = psG[(2 * j + 1) % 4]
                P.op("pe", lambda e, pa=pa, sl=sl, nflat=nflat: e.matmul(pa.t[:], lhsT=tri.t[:], rhs=nflat[:, sl],
                                                                         start=True, stop=True),
                     reads=[tri.r, nlf_t.r], writes=[pa.r])
                P.op("pe", lambda e, pb=pb, sl=sl, nflat=nflat: e.matmul(pb.t[:], lhsT=C.ones_f.t[:], rhs=nflat[:, sl],
                                                                         start=True, stop=True),
                     reads=[C.ones_f.r, nlf_t.r], writes=[pb.r])
                P.op("act", lambda e, pa=pa, sl=sl, ndflat=ndflat: e.activation(out=ndflat[:, sl], in_=pa.t[:], func=AF.Copy),
                     reads=[pa.r], writes=[ND[s].r])
                P.op("dve", lambda e, pb=pb, sl=sl, totflat=totflat: e.tensor_copy(out=totflat[:, sl], in_=pb.t[:]),
                     reads=[pb.r], writes=[tot_t.r])
            for u in range(1, U - 1):
                P.op("dve", lambda e, u=u: e.tensor_tensor(out=tot_t.t[:, u, :], in0=tot_t.t[:, u, :],
                                                           in1=tot_t.t[:, u - 1, :], op=ALU.add),
                     reads=[tot_t.r], writes=[tot_t.r])
            P.op("dve", lambda e, s=s, U=U: e.tensor_tensor(out=ND[s].t[:, 1:U, :], in0=ND[s].t[:, 1:U, :],
                                                            in1=tot_t.t[:, 0:U - 1, :], op=ALU.add),
                 reads=[tot_t.r, ND[s].r], writes=[ND[s].r])
            P.op("dve", lambda e, s=s, U=U: e.tensor_tensor(
                out=SC[s].t[:], in0=kval[s].t[:].unsqueeze(2).to_broadcast([128, U, NH]), in1=ND[s].t[:],
                op=ALU.subtract), reads=[kval[s].r, ND[s].r], writes=[SC[s].r])

    nK = 0
    nUnit = 0
    for grp in range(NH // HG):
        P.dma("pool", lambda e, grp=grp: e.dma_start(out=wob.t[:], in_=wo_d[grp]), writes=[wob.r])
        for hh in range(HG):
            h = grp * HG + hh
            wq_, wg_ = wqb[h % 2], wgb[h % 2]
            P.dma("pool", lambda e, wq_=wq_, h=h: e.dma_start(out=wq_.t[:], in_=wq_d[h]), writes=[wq_.r])
            P.dma("pool", lambda e, wg_=wg_, h=h: e.dma_start(out=wg_.t[:], in_=wg_d[h]), writes=[wg_.r])
            q_ = qT[h % 2]
            g_ = sg[h % 2]
            for hf in range(2):
                pp = psG[hf]
                proj_fm(C, wq_, pp, hf)
                P.op("dve", lambda e, q_=q_, pp=pp, hf=hf: e.tensor_scalar(
                    out=q_.t[:, hf * 512:(hf + 1) * 512], in0=pp.t[:], scalar1=SCALE, scalar2=None, op0=ALU.mult),
                    reads=[pp.r], writes=[q_.r])
            for hf in range(2):
                pp = psG[2 + hf]
                proj_fm(C, wg_, pp, hf)
                P.op("act", lambda e, g_=g_, pp=pp, hf=hf: e.activation(
                    out=g_.t[:, hf * 512:(hf + 1) * 512], in_=pp.t[:], func=AF.Silu),
                    reads=[pp.r], writes=[g_.r])
            if not isb:
                bt = BT[h % 2]
                src = bass.AP(rbx_d.tensor, h * 767, [[1, 128], [1, 640]])
                P.dma("sp", lambda e, bt=bt, src=src: e.dma_start(out=bt.t[:], in_=src), writes=[bt.r])
                P.op("pool", lambda e, bt=bt: e.tensor_tensor(out=bt.t[:], in0=bt.t[:], in1=cmask.t[:], op=ALU.add),
                     reads=[bt.r, cmask.r], writes=[bt.r])
            for s in range(2):
                U = NU[s]
                if isb:
                    dx = dexp[(2 * h + s) % 2]
                    for i in range(4):
                        P.op("pool", lambda e, dx=dx, i=i, s=s, h=h: e.tensor_tensor(
                            out=dx.t[:, i, :], in0=ident.t[:], in1=ND[s].t[:, 3 - i, h:h + 1].to_broadcast([128, 128]),
                            op=ALU.mult), reads=[ident.r, ND[s].r], writes=[dx.r])
                    pq = psG[(2 * h + s) % 4]
                    P.op("pe", lambda e, dx=dx, pq=pq: e.matmul(pq.t[:], lhsT=C.ones_b.t[:],
                                                                rhs=dx.t[:].rearrange("p i q -> p (i q)"),
                                                                start=True, stop=True),
                         reads=[C.ones_b.r, dx.r], writes=[pq.r])
                    P.op("act", lambda e, pq=pq: e.activation(out=NDQ.t[:], in_=pq.t[:], func=AF.Copy),
                         reads=[pq.r], writes=[NDQ.r])
                    P.op("pool", lambda e: e.tensor_tensor(
                        out=NDQd.t[:].rearrange("p (i q) -> p i q", i=4), in0=NDQ.t[:].rearrange("p (i q) -> p i q", i=4),
                        in1=trim.t[:].unsqueeze(1).to_broadcast([128, 4, 128]), op=ALU.add),
                        reads=[NDQ.r, trim.r], writes=[NDQd.r])
                first = True
                for ch in range(U // 8):
                    kx = kxc[nK % 2]
                    vx = vxc[nK % 2]
                    nK += 1
                    P.dma("sp", lambda e, kx=kx, s=s, h=h, ch=ch: e.dma_start(
                        out=kx.t[:], in_=kT_d[s][h, :, ch * 1024:(ch + 1) * 1024]), writes=[kx.r])
                    P.dma("sp", lambda e, vx=vx, s=s, h=h, ch=ch: e.dma_start(
                        out=vx.t[:], in_=V_d[s][ch * 8:(ch + 1) * 8, :, h * 128:(h + 1) * 128].rearrange("u p d -> p u d")),
                        writes=[vx.r])
                    if ch == 0:
                        order = [3, 4, 0, 1, 2, 5, 6, 7] if not isb else [3, 2, 1, 0, 4, 5, 6, 7]
                    else:
                        order = list(range(8))
                    for ul in order:
                        u = ch * 8 + ul
                        if not isb:
                            ilo, ihi = max(0, u - 4), min(3, u)
                            rlo = ilo + 4 - u
                            ncol = (ihi - ilo + 1) * 128
                            adds = [(0, ncol, BT[h % 2].t[:, rlo * 128:rlo * 128 + ncol], BT[h % 2].r)]
                            sc_ap = kmask.t[:, s, u:u + 1]
                            ares = [kmask.r]
                        else:
                            if u < 4:
                                ilo, ihi = 3 - u, 3
                                ncol = (ihi - ilo + 1) * 128
                                adds = [(0, 128, NDQd.t[:, ilo * 128:(ilo + 1) * 128], NDQd.r)]
                                if ncol > 128:
                                    adds.append((128, ncol - 128, NDQ.t[:, (ilo + 1) * 128:512], NDQ.r))
                            else:
                                ilo, ihi = 0, 3
                                ncol = 512
                                adds = [(0, 512, NDQ.t[:, :], NDQ.r)]
                            sc_ap = SC[s].t[:, u, h:h + 1]
                            ares = [SC[s].r]
                        c0 = ilo * 128
                        attn_unit(C, kx.t[:, ul * 128:(ul + 1) * 128], vx.t[:, ul, :],
                                  q_.t[:, s * 512 + c0:s * 512 + c0 + ncol], ncol, sc_ap, adds, psO, psSm, c0, first,
                                  kx.r, vx.r, q_.r, ares, psS[nUnit % 2], Sb[nUnit % 2], PT[nUnit % 2])
                        first = False
                        nUnit += 1
                P.op("dve", lambda e: e.reciprocal(out=rs.t[:], in_=psSm.t[:]), reads=[psSm.r], writes=[rs.r])
                P.op("pool", lambda e, g_=g_, s=s: e.tensor_tensor(out=wgt.t[:], in0=rs.t[:],
                                                                   in1=g_.t[:, s * 512:(s + 1) * 512], op=ALU.mult),
                     reads=[rs.r, g_.r], writes=[wgt.r])
                P.op("dve", lambda e, hh=hh, s=s: e.tensor_tensor(out=og.t[:, hh, s * 512:(s + 1) * 512], in0=psO.t[:],
                                                                  in1=wgt.t[:], op=ALU.mult),
                     reads=[psO.r, wgt.r], writes=[og.r])
        k = 0
        for c in range(NCH):
            for hf in range(2):
                pp = psG[k % 4]
                k += 1
                for hh in range(HG):
                    P.op("pe", lambda e, pp=pp, hh=hh, c=c, hf=hf: e.matmul(
                        pp.t[:], lhsT=wob.t[:, hh, c * 128:(c + 1) * 128], rhs=og.t[:, hh, hf * 512:(hf + 1) * 512],
                        start=(hh == 0), stop=(hh == HG - 1)), reads=[wob.r, og.r], writes=[pp.r])
                P.op("dve", lambda e, pp=pp, c=c, hf=hf: e.tensor_tensor(
                    out=C.xT.t[:, c, hf * 512:(hf + 1) * 512], in0=pp.t[:], in1=C.xT.t[:, c, hf * 512:(hf + 1) * 512],
                    op=ALU.add), reads=[pp.r, C.xr[c]], writes=[C.xr[c]])
    for c0 in range(0, NCH, 4):
        C.outs.append(P.dma("sp", lambda e, c0=c0: e.dma_start(out=xo_d[:, c0:c0 + 4, :], in_=C.xT.t[:, c0:c0 + 4, :]),
                            reads=C.xr[c0:c0 + 4]))
    if isb:
        gf = C.sb("gfin", [128, NCH], F32)
        P.dma("sp", lambda e: e.dma_start(out=gf.t[:], in_=gfin_d), writes=[gf.r])
        pss = (psG[0], psG[1])
        for c in range(NCH):
            sq = sg[c % 2]
            P.op("act", lambda e, c=c, sq=sq: e.activation(out=sq.t[:], in_=C.xT.t[:, c, :], func=AF.Square),
                 reads=[C.xr[c]], writes=[sq.r])
            for hf in range(2):
                P.op("pe", lambda e, c=c, sq=sq, hf=hf: e.matmul(
                    pss[hf].t[:], lhsT=C.ones_f.t[:], rhs=sq.t[:, hf * 512:(hf + 1) * 512],
                    start=(c == 0), stop=(c == NCH - 1)), reads=[sq.r, C.ones_f.r], writes=[pss[hf].r])
        for hf in range(2):
            sl = slice(hf * 512, (hf + 1) * 512)
            P.op("act", lambda e, hf=hf, sl=sl: e.activation(
                out=C.rstd.t[:, sl], in_=pss[hf].t[:], func=AF.Sqrt, bias=EPS, scale=1.0 / D_MODEL),
                reads=[pss[hf].r], writes=[C.rstd.r])
        P.op("dve", lambda e: e.reciprocal(out=C.rstd.t[:], in_=C.rstd.t[:]), reads=[C.rstd.r], writes=[C.rstd.r])
        for c in range(NCH):
            yb = sg[c % 2]
            P.op("dve", lambda e, c=c, yb=yb: e.scalar_tensor_tensor(
                out=yb.t[:], in0=C.xT.t[:, c, :], scalar=gf.t[:, c:c + 1], in1=C.rstd.t[:],
                op0=ALU.mult, op1=ALU.mult), reads=[C.xr[c], gf.r, C.rstd.r], writes=[yb.r])
            C.outs.append(P.dma("sp", lambda e, c=c, yb=yb: e.dma_start(out=yo_d[:, c, :], in_=yb.t[:]), reads=[yb.r]))
    return C.finish()


def _tile_wo(Wo):
    return np.ascontiguousarray(Wo.reshape(NH // HG, HG, 128, D_MODEL).transpose(0, 2, 1, 3))


def _gather_seq(res, key):
    out = []
    for b in range(BATCH):
        if key == "kT":
            g = np.zeros((NH, 128, SEQ), dtype=res[0][key].dtype)
            for j in range(4 * b, 4 * b + 4):
                for s, sgm in enumerate(core_segments(j)):
                    g[:, :, sgm * SEG:(sgm + 1) * SEG] = res[j][key][:, :, s * SEG:(s + 1) * SEG]
        else:
            w = res[0][key].shape[2]
            g = np.zeros((SEQ // 128, 128, w), dtype=res[0][key].dtype)
            for j in range(4 * b, 4 * b + 4):
                for s, sgm in enumerate(core_segments(j)):
                    g[sgm * 4:(sgm + 1) * 4] = res[j][key][s * 4:(s + 1) * 4]
        out.append(g)
    return out


def run_a(xT_cores, gn, W_in, rel_bias, W_out, kT_g, V_g):
    nc = get_nc("a")
    wq = tile_w_cols(W_in, 0, D_INNER, 128)
    wg = tile_w_cols(W_in, 3 * D_INNER, D_INNER, 128)
    wo = _tile_wo(W_out)
    g = tile_vec(gn)
    idx = np.clip(np.arange(767) - 127, -256, 256) + 256
    rbx = np.ascontiguousarray(rel_bias[:, idx])
    cmask = np.zeros((128, 5, 128), np.float32)
    cmask[:64, 0, :64] = NEG
    cmask[64:, 4, 64:] = NEG
    cmask = cmask.reshape(128, 640)
    in_maps = []
    for j in range(8):
        b = j // 4
        m = {"xT": xT_cores[j], "gn": g, "wq": wq, "wg": wg, "wo": wo, "rbx": rbx, "cmask": cmask}
        kmask = np.zeros((128, 2, 8), np.float32)
        for s, sgm in enumerate(core_segments(j)):
            kT = np.zeros((NH, 128, 2 * SEG), dtype=kT_g[b].dtype)
            V = np.zeros((8, 128, D_INNER), dtype=V_g[b].dtype)
            kT[:, :, SEG:] = kT_g[b][:, :, sgm * SEG:(sgm + 1) * SEG]
            V[4:] = V_g[b][sgm * 4:(sgm + 1) * 4]
            if sgm > 0:
                kT[:, :, :SEG] = kT_g[b][:, :, (sgm - 1) * SEG:sgm * SEG]
                V[:4] = V_g[b][(sgm - 1) * 4:sgm * 4]
            else:
                kmask[:, s, :4] = NEG
            m["kTs%d" % s] = np.ascontiguousarray(kT.reshape(NH, 128, 8, 128)[:, :, :, ::-1]).reshape(NH, 128, 1024)
            m["Vs%d" % s] = np.ascontiguousarray(V[:, ::-1, :])
        m["kmask"] = kmask
        in_maps.append(m)
    res = run_bass_kernel_spmd(nc, in_maps, core_ids=list(range(8)))
    return [r["xo"] for r in res.results]


def run_b(xT_cores, gn, W_in, W_out, kT_g, V_g, nlf_g, gfin):
    nc = get_nc("b")
    wq = tile_w_cols(W_in, 0, D_INNER, 128)
    wg = tile_w_cols(W_in, D_INNER, D_INNER, 128)
    wo = _tile_wo(W_out)
    g = tile_vec(gn)
    gf = tile_vec(gfin)
    ident = np.eye(128, dtype=np.float32)
    kl = np.arange(128)
    tri = (kl[:, None] > kl[None, :]).astype(np.float32)
    trim = np.where(kl[:, None] > kl[None, :], NEG, 0.0).astype(np.float32)
    in_maps = []
    for j in range(8):
        b = j // 4
        m = {"xT": xT_cores[j], "gn": g, "wq": wq, "wg": wg, "wo": wo, "gfin": gf, "ident": ident, "tri": tri,
             "trim": trim}
        for s, sgm in enumerate(core_segments(j)):
            U = UMAX[s]
            tiles = [4 * sgm + 3 - u for u in range(4 * sgm + 4)]
            kT = np.zeros((NH, 128, U * 128), dtype=kT_g[b].dtype)
            V = np.zeros((U, 128, D_INNER), dtype=V_g[b].dtype)
            nlf = np.zeros((128, U, NH), np.float32)
            kval = np.full((128, U), NEG, np.float32)
            for u, tix in enumerate(tiles):
                kT[:, :, u * 128:(u + 1) * 128] = kT_g[b][:, :, tix * 128:(tix + 1) * 128]
                V[u] = V_g[b][tix]
                nlf[:, u, :] = nlf_g[b][tix]
                kval[:, u] = 0.0
            m["kTs%d" % s] = kT
            m["Vs%d" % s] = V
            m["nlfs%d" % s] = nlf
            m["kval%d" % s] = kval
        in_maps.append(m)
    res = run_bass_kernel_spmd(nc, in_maps, core_ids=list(range(8)))
    return [r["xo"] for r in res.results], [r["yo"] for r in res.results]


def kernel_unfused(x, a_norm, a_w_in, a_rel_bias, a_w_out, kv_norm, kv_w, f_w, f_b, b_norm, b_w_in, b_w_out, final_norm):
    x = np.asarray(x, np.float32)
    xT = [to_xT(x[j // 4][core_tokens(j)]) for j in range(8)]
    f_w = np.asarray(f_w, np.float32)
    f_b = np.asarray(f_b, np.float32)
    for l in range(2):
        W = np.asarray(a_w_in[l], np.float32)
        r = run_kv(xT, np.asarray(a_norm[l], np.float32), W, D_INNER, 2 * D_INNER, f_w, f_b)
        kT_g = _gather_seq(r, "kT")
        V_g = _gather_seq(r, "V")
        xT = run_a(xT, np.asarray(a_norm[l], np.float32), W, np.asarray(a_rel_bias[l], np.float32),
                   np.asarray(a_w_out[l], np.float32), kT_g, V_g)
    r = run_kv(xT, np.asarray(kv_norm, np.float32), np.asarray(kv_w, np.float32), 0, D_INNER, f_w, f_b)
    kT_g = _gather_seq(r, "kT")
    V_g = _gather_seq(r, "V")
    nlf_g = _gather_seq(r, "nlf")
    yT = None
    for l in range(2):
        xT, yT = run_b(xT, np.asarray(b_norm[l], np.float32), np.asarray(b_w_in[l], np.float32),
                       np.asarray(b_w_out[l], np.float32), kT_g, V_g, nlf_g, np.asarray(final_norm, np.float32))
    out = np.zeros((BATCH, SEQ, D_MODEL), np.float32)
    for j in range(8):
        out[j // 4][core_tokens(j)] = from_xT(yT[j])
    return out


NSEGP = 11
KSEGE = 1024 * 512
VSEGE = 512 * 512
NSEGE = 512 * NH
GROUPS = [[0, 1, 2, 3], [4, 5, 6, 7]]


def build_fused():
    C = Ctx()
    P = C.P
    nc = C.nc
    xT_d = C.dram("xT", [128, NCH, T], F32, "ExternalInput")
    gns_d = C.dram("gns", [6, 128, NCH], F32, "ExternalInput")
    a_wq_d = C.dram("a_wq", [2, NH, 128, NCH, 128], F32, "ExternalInput")
    a_wg_d = C.dram("a_wg", [2, NH, 128, NCH, 128], F32, "ExternalInput")
    a_wk_d = C.dram("a_wk", [2, NH, 128, NCH, 128], F32, "ExternalInput")
    a_wv_d = C.dram("a_wv", [2, 8, 128, NCH, 512], F32, "ExternalInput")
    a_wo_d = C.dram("a_wo", [2, NH // HG, 128, HG, D_MODEL], F32, "ExternalInput")
    s_wk_d = C.dram("s_wk", [NH, 128, NCH, 128], F32, "ExternalInput")
    s_wv_d = C.dram("s_wv", [8, 128, NCH, 512], F32, "ExternalInput")
    b_wq_d = C.dram("b_wq", [2, NH, 128, NCH, 128], F32, "ExternalInput")
    b_wg_d = C.dram("b_wg", [2, NH, 128, NCH, 128], F32, "ExternalInput")
    b_wo_d = C.dram("b_wo", [2, NH // HG, 128, HG, D_MODEL], F32, "ExternalInput")
    fw_d = C.dram("fw", [128, NCH, NH], F32, "ExternalInput")
    fb_d = C.dram("fb", [128, NH], F32, "ExternalInput")
    rbx_d = C.dram("rbx", [2, NH, 767], F32, "ExternalInput")
    cmask_d = C.dram("cmask", [128, 640], F32, "ExternalInput")
    kmask_d = C.dram("kmask", [128, 2, 8], F32, "ExternalInput")
    kval_d = [C.dram("kval%d" % s, [128, UMAX[s]], F32, "ExternalInput") for s in range(2)]
    cst_d = C.dram("cst", [4, 128, 128], F32, "ExternalInput")
    yo_d = C.dram("yo", [128, NCH, T], F32, "ExternalOutput")
    kTo = [nc.dram_tensor("kTo%d" % i, [4 * 2 * 1024, 512], BF16) for i in range(1)]
    Vo = [nc.dram_tensor("Vo%d" % i, [8 * 2 * 512, 512], BF16) for i in range(1)]
    nlfo = nc.dram_tensor("nlfo", [T, NH], F32)
    KL = [nc.dram_tensor("KL%d" % i, [4 * NSEGP * 1024, 512], BF16) for i in range(2)]
    VL = [nc.dram_tensor("VL%d" % i, [4 * NSEGP * 1024, 512], BF16) for i in range(2)]
    NL = nc.dram_tensor("NL", [NSEGP * 512, NH], F32)
    kTo_r = [[Res() for _ in range(NH)] for _ in range(2)]
    Vo_r = [[Res() for _ in range(8)] for _ in range(2)]
    nlfo_r = Res()
    KL_r = [[[Res() for _ in range(2)] for _ in range(4)] for _ in range(2)]
    VL_r = [[[Res() for _ in range(2)] for _ in range(4)] for _ in range(2)]
    NL_r = [Res() for _ in range(2)]
    pad_r = Res()
    Kloc = nc.dram_tensor("Kloc", [4 * 8 * 1024, 512], BF16)
    Vloc = nc.dram_tensor("Vloc", [4 * 8 * 1024, 512], BF16)
    Nloc = nc.dram_tensor("Nloc", [8 * 512, NH], F32)
    Kloc_r = [Res() for _ in range(4)]
    Vloc_r = [Res() for _ in range(4)]
    Nloc_r = Res()

    ps = [C.ps("ps%d" % i) for i in range(8)]
    psS = ps[0:3]
    psOs = [ps[3], ps[4]]
    psSms = [ps[5], ps[6]]
    psO = psOs[0]
    psSm = psSms[0]
    psG = [ps[6], ps[7], ps[0], ps[1]]
    psG2 = [ps[7], ps[2]]
    phase_consts(C)
    phase_load_x(C, xT_d)

    sg = [C.sb("sg%d" % i, [128, T], F32) for i in range(2)]
    C.xsq = sg
    C.hT = C.sb("hT", [128, NCH, T], BF16)
    C.hr = [Res("h%d" % c) for c in range(NCH)]
    C.rstd = C.sb("rstd", [128, T], F32)
    WB0 = C.sb("WB0", [128, 8192], BF16)
    WB1 = C.sb("WB1", [128, 8192], BF16)
    wq_r = [Res(), Res()]
    wg_r = [Res(), Res()]
    WB1_rs = wq_r + wg_r

    def wvb_ap(i):
        return (WB0 if i == 0 else WB1).t[:].rearrange("p (c n) -> p c n", c=NCH)

    def wvb_res(i):
        return [WB0.r] if i == 0 else WB1_rs

    wob_ap = WB0.t[:].rearrange("p (h n) -> p h n", h=HG)

    def wqb_ap(i):
        return WB1.t[:, i * 2048:(i + 1) * 2048].rearrange("p (c n) -> p c n", c=NCH)

    def wgb_ap(i):
        return WB1.t[:, 4096 + i * 2048:4096 + (i + 1) * 2048].rearrange("p (c n) -> p c n", c=NCH)

    kvbuf = [C.sb("kvbuf%d" % i, [128, 2048], BF16) for i in range(3)]
    qT = [C.sb("qT%d" % i, [128, T], BF16) for i in range(2)]
    PT = [C.sb("PT%d" % i, [128, 512], BF16) for i in range(4)]
    vo = PT
    Sb = [C.sb("Sb%d" % i, [128, 512], F32) for i in range(4)]
    og = C.sb("og", [128, HG, T], BF16)
    rs = C.sb("rs", [128, 512], F32)
    wgt = C.sb("wgt", [128, 512], F32)
    BT = [C.sb("BT%d" % i, [128, 640], F32) for i in range(2)]
    cmask = C.sb("cmask", [128, 640], F32)
    kmask = C.sb("kmask", [128, 2, 8], F32)
    cst = C.sb("cst", [128, 4, 128], F32)
    ND = [C.sb("ND%d" % s, [128, UMAX[s], NH], F32) for s in range(2)]
    SC = [C.sb("SC%d" % s, [128, UMAX[s], NH], F32) for s in range(2)]
    kval = [C.sb("kval%d" % s, [128, UMAX[s]], F32) for s in range(2)]
    NDQ = BT[0]
    NDQd = BT[1]
    dexp = [C.sb("dexp%d" % i, [128, 4, 128], BF16) for i in range(2)]
    fb = C.sb("fb", [128, NH], F32)
    ones_col = C.sb("ones_col", [128, 1], F32)
    xsqc = [C.sb("xsqc%d" % i, [128, 128], F32) for i in range(2)]
    zs = [C.sb("zs%d" % i, [128, NH], F32) for i in range(2)]
    rt = [C.sb("rt%d" % i, [128, 1], F32) for i in range(2)]
    zero = kvbuf[0]

    ident_ap = cst.t[:, 0, :]
    tri_ap = cst.t[:, 1, :]
    trim_ap = cst.t[:, 2, :]
    J_ap = cst.t[:, 3, :]
    for (t_, d_) in ((cmask, cmask_d), (kmask, kmask_d), (fb, fb_d), (kval[0], kval_d[0]), (kval[1], kval_d[1])):
        P.dma("sp", lambda e, t_=t_, d_=d_: e.dma_start(out=t_.t[:], in_=d_), writes=[t_.r])
    P.dma("sp", lambda e: e.dma_start(out=cst.t[:], in_=cst_d.rearrange("k p n -> p k n")), writes=[cst.r])
    P.op("pool", lambda e: e.memset(ones_col.t[:], 1.0), writes=[ones_col.r])
    P.op("pool", lambda e: e.memset(zero.t[:], 0.0), writes=[zero.r])
    for st in range(1):
        segs = (0, 1, 2) if st == 0 else (2,)
        for i in range(4):
            for sgp in segs:
                for half in range(2):
                    r0 = (i * NSEGP + sgp) * 1024 + half * 512
                    P.dma("sp", lambda e, st=st, r0=r0: e.dma_start(
                        out=KL[st][r0:r0 + 512, :].rearrange("(p a) n -> p (a n)", p=128), in_=zero.t[:]),
                        reads=[zero.r], writes=[pad_r])
        for b in range(4):
            for sgp in segs:
                for half in range(2):
                    r0 = (b * NSEGP + sgp) * 1024 + half * 512
                    P.dma("sp", lambda e, st=st, r0=r0: e.dma_start(
                        out=VL[st][r0:r0 + 512, :].rearrange("(p a) n -> p (a n)", p=128), in_=zero.t[:]),
                        reads=[zero.r], writes=[pad_r])
    P.dma("sp", lambda e: e.dma_start(out=NL[0:3 * 512, :].rearrange("(p a) n -> p (a n)", p=128),
                                      in_=zero.t[:, 0:384].bitcast(F32) if False else zero.t[:, 0:768].bitcast(F32)),
          reads=[zero.r], writes=[pad_r])

    dyn = {}

    def pre_sp(e):
        jj = e.snap(e.partition_id() % 4, min_val=0, max_val=3)
        dyn["k"] = e.snap(jj * KSEGE, min_val=0, max_val=3 * KSEGE)

    def pre_pool(e):
        jj = e.snap(e.partition_id() % 4, min_val=0, max_val=3)
        dyn["v"] = e.snap(jj * KSEGE, min_val=0, max_val=3 * KSEGE)
        dyn["n"] = e.snap(jj * NSEGE, min_val=0, max_val=3 * NSEGE)

    P.pre = {"sp": pre_sp, "pool": pre_pool}

    def localize_k(i):
        P.dma("sp", lambda e, i=i: e.dma_start(
            out=bass.AP(Kloc, i * 8 * KSEGE, [[32768, 128], [1, 32768]]),
            in_=bass.AP(KL[0], dyn["k"] + i * NSEGP * KSEGE, [[32768, 128], [1, 32768]])),
            reads=[KL_r[0][i][0], KL_r[0][i][1], pad_r], writes=[Kloc_r[i]])

    def localize_v(p_):
        P.dma("pool", lambda e, p_=p_: e.dma_start(
            out=bass.AP(Vloc, p_ * 8 * KSEGE, [[32768, 128], [1, 32768]]),
            in_=bass.AP(VL[0], dyn["v"] + p_ * NSEGP * KSEGE, [[32768, 128], [1, 32768]])),
            reads=[VL_r[0][p_][0], VL_r[0][p_][1], pad_r], writes=[Vloc_r[p_]])

    def localize_n():
        P.dma("pool", lambda e: e.dma_start(
            out=bass.AP(Nloc, 0, [[1024, 128], [1, 1024]]),
            in_=bass.AP(NL, dyn["n"], [[1024, 128], [1, 1024]])),
            reads=[NL_r[0], NL_r[1], pad_r], writes=[Nloc_r])

    def kv_phase(st, gidx, wk_ap, wv_ap, gates):
        gn = phase_norm(C, gns_d[gidx], "g%d" % gidx, psG[0], psG[1])
        NT = T // 128
        if gates:
            fw_ap = Sb[0].t[:].rearrange("p (c h) -> p c h", c=NCH)
            gfw_ap = Sb[1].t[:].rearrange("p (c h) -> p c h", c=NCH)
            P.dma("sp", lambda e: e.dma_start(out=fw_ap, in_=fw_d), writes=[Sb[0].r])
            for c in range(NCH):
                P.op("pool", lambda e, c=c: e.tensor_scalar(out=gfw_ap[:, c, :], in0=fw_ap[:, c, :],
                                                            scalar1=gn.t[:, c:c + 1], scalar2=None, op0=ALU.mult),
                     reads=[Sb[0].r, gn.r], writes=[Sb[1].r])
            k = 0
            for tt in range(NT):
                pz = psS[tt % 2]
                pq = (psO, psSm)[tt % 2]
                tsl = slice(tt * 128, (tt + 1) * 128)
                for c in range(NCH):
                    sq = xsqc[k % 2]
                    k += 1
                    P.op("act", lambda e, c=c, sq=sq, tsl=tsl: e.activation(out=sq.t[:], in_=C.xT.t[:, c, tsl],
                                                                            func=AF.Square),
                         reads=[C.xr[c]], writes=[sq.r])
                    P.op("pe", lambda e, c=c, sq=sq, pq=pq: e.matmul(pq.t[:, 0:1], lhsT=sq.t[:], rhs=ones_col.t[:],
                                                                     start=(c == 0), stop=(c == NCH - 1)),
                         reads=[sq.r, ones_col.r], writes=[pq.r])
                    P.op("pe", lambda e, c=c, tsl=tsl, pz=pz: e.matmul(pz.t[:, 0:NH], lhsT=C.xT.t[:, c, tsl],
                                                                       rhs=gfw_ap[:, c, :],
                                                                       start=(c == 0), stop=(c == NCH - 1)),
                         reads=[C.xr[c], Sb[1].r], writes=[pz.r])
                r1 = rt[tt % 2]
                z = zs[tt % 2]
                P.op("act", lambda e, r1=r1, pq=pq: e.activation(out=r1.t[:], in_=pq.t[:, 0:1], func=AF.Sqrt, bias=EPS,
                                                                 scale=1.0 / D_MODEL), reads=[pq.r], writes=[r1.r])
                P.op("dve", lambda e, r1=r1: e.reciprocal(out=r1.t[:], in_=r1.t[:]), reads=[r1.r], writes=[r1.r])
                P.op("dve", lambda e, r1=r1, z=z, pz=pz: e.scalar_tensor_tensor(
                    out=z.t[:], in0=pz.t[:, 0:NH], scalar=r1.t[:, 0:1], in1=fb.t[:], op0=ALU.mult, op1=ALU.add),
                    reads=[pz.r, r1.r, fb.r], writes=[z.r])
                P.op("act", lambda e, z=z: e.activation(out=z.t[:], in_=z.t[:], func=AF.Exp, scale=-1.0),
                     reads=[z.r], writes=[z.r])
                P.op("act", lambda e, z=z: e.activation(out=z.t[:], in_=z.t[:], func=AF.Ln, bias=1.0, scale=1.0),
                     reads=[z.r], writes=[z.r])
                P.dma("sp", lambda e, z=z, tt=tt: e.dma_start(out=nlfo[tt * 128:(tt + 1) * 128, :], in_=z.t[:]),
                      reads=[z.r], writes=[nlfo_r])
            for sl in range(2):
                P.coll(("n", sl), lambda e, sl=sl: e.collective_compute(
                    "AllGather", ALU.bypass, replica_groups=GROUPS, ins=[nlfo[sl * 512:(sl + 1) * 512, :]],
                    outs=[NL[(3 + sl * 4) * 512:(3 + sl * 4 + 4) * 512, :]]), reads=[nlfo_r], writes=[NL_r[sl]])
        def load_wk(h):
            kb_ = kvbuf[h % 2]
            P.dma("pool", lambda e, kb_=kb_, h=h: e.dma_start(out=kb_.t[:].rearrange("p (c n) -> p c n", c=NCH),
                                                             in_=wk_ap(h)), writes=[kb_.r])
        load_wk(0)
        for h in range(NH):
            kb = kvbuf[h % 2]
            w_ap = kb.t[:].rearrange("p (c n) -> p c n", c=NCH)
            o = qT[h % 2]
            if h + 1 < NH:
                load_wk(h + 1)
            for hf in range(2):
                pp = psG[(2 * h + hf) % 4]
                for c in range(NCH):
                    P.op("pe", lambda e, c=c, pp=pp, w_ap=w_ap, hf=hf: e.matmul(
                        pp.t[:], lhsT=w_ap[:, c, :], rhs=C.hT.t[:, c, hf * 512:(hf + 1) * 512],
                        start=(c == 0), stop=(c == NCH - 1)), reads=[kb.r, C.hr[c]], writes=[pp.r])
                if hf == 0:
                    P.op("act", lambda e, o=o, pp=pp: e.activation(out=o.t[:, 0:512], in_=pp.t[:], func=AF.Copy),
                         reads=[pp.r], writes=[o.r])
                else:
                    P.op("dve", lambda e, o=o, pp=pp: e.tensor_copy(out=o.t[:, 512:1024], in_=pp.t[:]),
                         reads=[pp.r], writes=[o.r])
            for sl in range(2):
                r0 = ((h // 8) * 2 + sl) * 1024 + (h % 8) * 128
                P.dma("sp", lambda e, o=o, r0=r0, sl=sl: e.dma_start(out=kTo[st][r0:r0 + 128, :],
                                                                     in_=o.t[:, sl * 512:(sl + 1) * 512]),
                      reads=[o.r], writes=[kTo_r[st][h]])
            if h % 8 == 7:
                i = h // 8
                for sl in range(2):
                    P.coll(("k", i, sl), lambda e, i=i, sl=sl: e.collective_compute(
                        "AllGather", ALU.bypass, replica_groups=GROUPS,
                        ins=[kTo[st][(i * 2 + sl) * 1024:(i * 2 + sl + 1) * 1024, :]],
                        outs=[KL[st][(i * NSEGP + 3 + sl * 4) * 1024:(i * NSEGP + 3 + sl * 4 + 4) * 1024, :]]),
                        reads=kTo_r[st][i * 8:(i + 1) * 8], writes=[KL_r[st][i][sl]])
                if i >= 1:
                    localize_k(i - 1)
        k = 0
        def load_wv(b):
            P.dma("pool", lambda e, b=b: e.dma_start(out=wvb_ap(b % 2), in_=wv_ap(b)), writes=wvb_res(b % 2))
        load_wv(0)
        for b in range(8):
            w_ap = wvb_ap(b % 2)
            w_rs = wvb_res(b % 2)
            if b + 1 < 8:
                load_wv(b + 1)
            for tt in range(NT):
                pp = psG[k % 4]
                o = vo[k % 4]
                for c in range(NCH):
                    P.op("pe", lambda e, c=c, tt=tt, w_ap=w_ap, pp=pp: e.matmul(
                        pp.t[:], lhsT=C.hT.t[:, c, tt * 128:(tt + 1) * 128], rhs=w_ap[:, c, :],
                        start=(c == 0), stop=(c == NCH - 1)), reads=w_rs + [C.hr[c]], writes=[pp.r])
                if k % 2 == 0:
                    P.op("act", lambda e, o=o, pp=pp: e.activation(out=o.t[:], in_=pp.t[:], func=AF.Copy),
                         reads=[pp.r], writes=[o.r])
                else:
                    P.op("dve", lambda e, o=o, pp=pp: e.tensor_copy(out=o.t[:], in_=pp.t[:]), reads=[pp.r], writes=[o.r])
                r0 = (((b // 2) * 2 + tt // 4) * 2 + b % 2) * 512 + (tt % 4) * 128
                P.dma("act", lambda e, o=o, r0=r0: e.dma_start(out=Vo[st][r0:r0 + 128, :], in_=o.t[:]),
                      reads=[o.r], writes=[Vo_r[st][b]])
                k += 1
            if b % 2 == 1:
                p_ = b // 2
                for sl in range(2):
                    P.coll(("v", p_, sl), lambda e, p_=p_, sl=sl: e.collective_compute(
                        "AllGather", ALU.bypass, replica_groups=GROUPS,
                        ins=[Vo[st][(p_ * 2 + sl) * 1024:(p_ * 2 + sl + 1) * 1024, :]],
                        outs=[VL[st][(p_ * NSEGP + 3 + sl * 4) * 1024:(p_ * NSEGP + 3 + sl * 4 + 4) * 1024, :]]),
                        reads=[Vo_r[st][b - 1], Vo_r[st][b]], writes=[VL_r[st][p_][sl]])
                if p_ == 0:
                    localize_k(3)
                if p_ >= 1:
                    localize_v(p_ - 1)
        localize_v(3)
        if gates:
            localize_n()

    def decay_prep():
        nlf_t = sg[0].t[:].rearrange("p (u h) -> p u h", u=32)
        tot_t = sg[1].t[:].rearrange("p (u h) -> p u h", u=32)
        for s in range(2):
            U = UMAX[s]
            for a in range(U // 4):
                off = (3 + 4 * s - a) * NSEGE
                P.dma("sp", lambda e, a=a, off=off: e.dma_start(
                    out=nlf_t[:, 4 * a:4 * a + 4, :],
                    in_=bass.AP(Nloc, off, [[NH, 128], [128 * NH, 4], [1, NH]])),
                    reads=[Nloc_r], writes=[sg[0].r])
            nflat = sg[0].t[:, 0:U * NH]
            ndflat = ND[s].t[:].rearrange("p u h -> p (u h)")
            totflat = sg[1].t[:, 0:U * NH]
            for j in range(U * NH // 512):
                sl_ = slice(j * 512, (j + 1) * 512)
                pa = psG[(2 * j) % 4]
                pb = psG[(2 * j + 1) % 4]
                P.op("pe", lambda e, pa=pa, sl_=sl_, nflat=nflat: e.matmul(pa.t[:], lhsT=tri_ap, rhs=nflat[:, sl_],
                                                                           start=True, stop=True),
                     reads=[cst.r, sg[0].r], writes=[pa.r])
                P.op("pe", lambda e, pb=pb, sl_=sl_, nflat=nflat: e.matmul(pb.t[:], lhsT=C.ones_f.t[:], rhs=nflat[:, sl_],
                                                                           start=True, stop=True),
                     reads=[C.ones_f.r, sg[0].r], writes=[pb.r])
                P.op("act", lambda e, pa=pa, sl_=sl_, ndflat=ndflat: e.activation(out=ndflat[:, sl_], in_=pa.t[:],
                                                                                  func=AF.Copy),
                     reads=[pa.r], writes=[ND[s].r])
                P.op("dve", lambda e, pb=pb, sl_=sl_, totflat=totflat: e.tensor_copy(out=totflat[:, sl_], in_=pb.t[:]),
                     reads=[pb.r], writes=[sg[1].r])

            def uof(v):
                return 4 * (v // 4) + 3 - (v % 4)
            for v in range(1, U):
                u1, u0 = uof(v), uof(v - 1)
                P.op("dve", lambda e, s=s, u1=u1, u0=u0: e.tensor_tensor(out=ND[s].t[:, u1, :], in0=ND[s].t[:, u1, :],
                                                                         in1=tot_t[:, u0, :], op=ALU.add),
                     reads=[sg[1].r, ND[s].r], writes=[ND[s].r])
                if v < U - 1:
                    P.op("dve", lambda e, u1=u1, u0=u0: e.tensor_tensor(out=tot_t[:, u1, :], in0=tot_t[:, u1, :],
                                                                        in1=tot_t[:, u0, :], op=ALU.add),
                         reads=[sg[1].r], writes=[sg[1].r])
            P.op("dve", lambda e, s=s, U=U: e.tensor_tensor(
                out=SC[s].t[:], in0=kval[s].t[:].unsqueeze(2).to_broadcast([128, U, NH]), in1=ND[s].t[:],
                op=ALU.subtract), reads=[kval[s].r, ND[s].r], writes=[SC[s].r])

    cnt = {"K": 0, "U": 0, "A": 0}

    def mix_phase(isb, st, gidx, wq_ap, wg_ap, wo_ap, rb_layer):
        NU = UMAX if isb else (8, 8)
        if isb:
            phase_norm(C, gns_d[gidx], "g%d" % gidx, psG[0], psG[1])
        if isb and gidx == 3:
            decay_prep()
        for grp in range(NH // HG):
            P.dma("pool", lambda e, grp=grp: e.dma_start(out=wob_ap, in_=wo_ap(grp)), writes=[WB0.r])
            for hh in range(HG):
                h = grp * HG + hh
                i8 = h // 8
                b4 = h // 4
                wq_, wg_ = wqb_ap(h % 2), wgb_ap(h % 2)

                def load_qg(h2):
                    P.dma("pool", lambda e, h2=h2: e.dma_start(out=wqb_ap(h2 % 2), in_=wq_ap(h2)), writes=[wq_r[h2 % 2]])
                    P.dma("pool", lambda e, h2=h2: e.dma_start(out=wgb_ap(h2 % 2), in_=wg_ap(h2)), writes=[wg_r[h2 % 2]])
                if h == 0:
                    load_qg(0)
                if h + 1 < NH:
                    load_qg(h + 1)
                q_ = qT[h % 2]
                g_ = sg[h % 2]
                for hf in range(2):
                    pp = psG2[hf]
                    for c in range(NCH):
                        P.op("pe", lambda e, c=c, pp=pp, wq_=wq_, hf=hf: e.matmul(
                            pp.t[:], lhsT=wq_[:, c, :], rhs=C.hT.t[:, c, hf * 512:(hf + 1) * 512],
                            start=(c == 0), stop=(c == NCH - 1)), reads=[wq_r[h % 2], C.hr[c]], writes=[pp.r])
                    P.op("dve", lambda e, q_=q_, pp=pp, hf=hf: e.tensor_scalar(
                        out=q_.t[:, hf * 512:(hf + 1) * 512], in0=pp.t[:], scalar1=SCALE, scalar2=None, op0=ALU.mult),
                        reads=[pp.r], writes=[q_.r])
                for hf in range(2):
                    pp = psG2[hf]
                    for c in range(NCH):
                        P.op("pe", lambda e, c=c, pp=pp, wg_=wg_, hf=hf: e.matmul(
                            pp.t[:], lhsT=wg_[:, c, :], rhs=C.hT.t[:, c, hf * 512:(hf + 1) * 512],
                            start=(c == 0), stop=(c == NCH - 1)), reads=[wg_r[h % 2], C.hr[c]], writes=[pp.r])
                    P.op("act", lambda e, g_=g_, pp=pp, hf=hf: e.activation(
                        out=g_.t[:, hf * 512:(hf + 1) * 512], in_=pp.t[:], func=AF.Silu),
                        reads=[pp.r], writes=[g_.r])
                if not isb:
                    bt = BT[h % 2]
                    src = bass.AP(rbx_d.tensor, (rb_layer * NH + h) * 767, [[1, 128], [1, 640]])
                    P.dma("sp", lambda e, bt=bt, src=src: e.dma_start(out=bt.t[:], in_=src), writes=[bt.r])
                    pj = (psG2[0], psG2[1])
                    P.op("pe", lambda e, bt=bt: e.matmul(pj[0].t[:], lhsT=J_ap, rhs=bt.t[:, 0:512], start=True, stop=True),
                         reads=[cst.r, bt.r], writes=[pj[0].r])
                    P.op("pe", lambda e, bt=bt: e.matmul(pj[1].t[:, 0:128], lhsT=J_ap, rhs=bt.t[:, 512:640], start=True,
                                                         stop=True), reads=[cst.r, bt.r], writes=[pj[1].r])
                    P.op("dve", lambda e, bt=bt: e.tensor_tensor(out=bt.t[:, 0:512], in0=pj[0].t[:], in1=cmask.t[:, 0:512],
                                                                 op=ALU.add), reads=[pj[0].r, cmask.r], writes=[bt.r])
                    P.op("dve", lambda e, bt=bt: e.tensor_tensor(out=bt.t[:, 512:640], in0=pj[1].t[:, 0:128],
                                                                 in1=cmask.t[:, 512:640], op=ALU.add),
                         reads=[pj[1].r, cmask.r], writes=[bt.r])
                for s in range(2):
                    U = NU[s]
                    if isb:
                        dx = dexp[(2 * h + s) % 2]
                        for i in range(4):
                            P.op("pool", lambda e, dx=dx, i=i, s=s, h=h: e.tensor_tensor(
                                out=dx.t[:, i, :], in0=ident_ap, in1=ND[s].t[:, i, h:h + 1].to_broadcast([128, 128]),
                                op=ALU.mult), reads=[cst.r, ND[s].r], writes=[dx.r])
                        pq = psG2[(2 * h + s) % 2]
                        P.op("pe", lambda e, dx=dx, pq=pq: e.matmul(pq.t[:], lhsT=C.ones_b.t[:],
                                                                    rhs=dx.t[:].rearrange("p i q -> p (i q)"),
                                                                    start=True, stop=True),
                             reads=[C.ones_b.r, dx.r], writes=[pq.r])
                        P.op("act", lambda e, pq=pq: e.activation(out=NDQ.t[:, 0:512], in_=pq.t[:], func=AF.Copy),
                             reads=[pq.r], writes=[NDQ.r])
                        P.op("pool", lambda e: e.tensor_tensor(
                            out=NDQd.t[:, 0:512].rearrange("p (i q) -> p i q", i=4),
                            in0=NDQ.t[:, 0:512].rearrange("p (i q) -> p i q", i=4),
                            in1=trim_ap.unsqueeze(1).to_broadcast([128, 4, 128]), op=ALU.add),
                            reads=[NDQ.r, cst.r], writes=[NDQd.r])
                    first = True
                    units = []
                    chunk_loads = []
                    psO = psOs[cnt["A"] % 2]
                    psSm = psSms[cnt["A"] % 2]
                    cnt["A"] += 1
                    for ch in range(U // 8):
                        kb = kvbuf[cnt["K"] % 3]
                        cnt["K"] += 1
                        kx_ap = kb.t[:, 0:1024]
                        vx_ap = kb.t[:, 1024:2048].rearrange("p (u d) -> p u d", u=8)
                        kdeps = [Kloc_r[i8]]
                        vdeps = [Vloc_r[b4 // 2]]
                        def load_chunk(kb=kb, kx_ap=kx_ap, vx_ap=vx_ap, ch=ch, kdeps=kdeps, vdeps=vdeps, s=s, h=h,
                                       i8=i8, b4=b4):
                            for s2 in range(2):
                                if isb:
                                    sgp = 3 + 4 * s - (2 * ch + s2)
                                else:
                                    sgp = 3 + 4 * s - 1 + s2
                                koff = ((i8 * 8 + sgp) * 1024 + (h % 8) * 128) * 512
                                voff = ((((b4 // 2) * 8 + sgp) * 2 + b4 % 2) * 512) * 512 + (h % 4) * 128
                                P.dma("sp", lambda e, s2=s2, koff=koff: e.dma_start(
                                    out=kx_ap[:, s2 * 512:(s2 + 1) * 512],
                                    in_=bass.AP(Kloc, koff, [[512, 128], [1, 512]])),
                                    reads=kdeps, writes=[kb.r])
                                P.dma("sp", lambda e, s2=s2, voff=voff: e.dma_start(
                                    out=vx_ap[:, s2 * 4:(s2 + 1) * 4, :],
                                    in_=bass.AP(Vloc, voff, [[512, 128], [128 * 512, 4], [1, 128]])),
                                    reads=vdeps, writes=[kb.r])
                        chunk_loads.append(load_chunk)
                        if ch == 0:
                            order = [3, 4, 0, 1, 2, 5, 6, 7] if not isb else list(range(8))
                        else:
                            order = list(range(8))
                        for ul in order:
                            u = ch * 8 + ul
                            if not isb:
                                ilo, ihi = max(0, u - 4), min(3, u)
                                rlo = ilo + 4 - u
                                ncol = (ihi - ilo + 1) * 128
                                adds = [(0, ncol, BT[h % 2].t[:, rlo * 128:rlo * 128 + ncol], BT[h % 2].r)]
                                sc_ap = kmask.t[:, s, u:u + 1]
                                ares = [kmask.r]
                            else:
                                if u < 4:
                                    ilo, ihi = u, 3
                                    ncol = (ihi - ilo + 1) * 128
                                    adds = [(0, 128, NDQd.t[:, ilo * 128:(ilo + 1) * 128], NDQd.r)]
                                    if ncol > 128:
                                        adds.append((128, ncol - 128, NDQ.t[:, (ilo + 1) * 128:512], NDQ.r))
                                else:
                                    ilo, ihi = 0, 3
                                    ncol = 512
                                    adds = [(0, 512, NDQ.t[:, 0:512], NDQ.r)]
                                sc_ap = SC[s].t[:, u, h:h + 1]
                                ares = [SC[s].r]
                            c0 = ilo * 128
                            n_ = cnt["U"]
                            cnt["U"] += 1
                            units.append((kx_ap[:, ul * 128:(ul + 1) * 128], vx_ap[:, ul, :],
                                          q_.t[:, s * 512 + c0:s * 512 + c0 + ncol], ncol, sc_ap, adds, c0, first,
                                          kb.r, q_.r, ares, psS[n_ % 3], Sb[n_ % 4], PT[n_ % 4]))
                            first = False
                    LA = 2
                    for idx in range(len(units) + LA):
                        if idx < len(units) and idx % 8 == 0:
                            ch_ = idx // 8
                            if ch_ == 0:
                                chunk_loads[0]()
                            if ch_ + 1 < len(chunk_loads):
                                chunk_loads[ch_ + 1]()
                        if idx < len(units):
                            (kt_, v_, qa_, ncol, sc_ap, adds, c0, fst, kr_, qr_, ares, pS_, Sb_, PT_) = units[idx]
                            P.op("pe", lambda e, pS_=pS_, ncol=ncol, kt_=kt_, qa_=qa_: e.matmul(
                                pS_.t[:, 0:ncol], lhsT=kt_, rhs=qa_, start=True, stop=True),
                                reads=[kr_, qr_], writes=[pS_.r])
                        if idx >= LA:
                            (kt_, v_, qa_, ncol, sc_ap, adds, c0, fst, kr_, qr_, ares, pS_, Sb_, PT_) = units[idx - LA]
                            for (o_, n2, ap_, r_) in adds:
                                P.op("dve", lambda e, o_=o_, n2=n2, ap_=ap_, pS_=pS_, Sb_=Sb_, sc_ap=sc_ap: e.scalar_tensor_tensor(
                                    out=Sb_.t[:, o_:o_ + n2], in0=pS_.t[:, o_:o_ + n2], scalar=sc_ap, in1=ap_,
                                    op0=ALU.add, op1=ALU.add), reads=[pS_.r, r_] + list(ares), writes=[Sb_.r])
                            P.op("act", lambda e, PT_=PT_, Sb_=Sb_, ncol=ncol: e.activation(
                                out=PT_.t[:, 0:ncol], in_=Sb_.t[:, 0:ncol], func=AF.Exp), reads=[Sb_.r], writes=[PT_.r])
                            P.op("pe", lambda e, v_=v_, PT_=PT_, ncol=ncol, c0=c0, fst=fst, psO=psO: e.matmul(
                                psO.t[:, c0:c0 + ncol], lhsT=v_, rhs=PT_.t[:, 0:ncol], start=fst, stop=False,
                                skip_group_check=True), reads=[kr_, PT_.r], writes=[psO.r])
                            P.op("pe", lambda e, PT_=PT_, ncol=ncol, c0=c0, fst=fst, psSm=psSm: e.matmul(
                                psSm.t[:, c0:c0 + ncol], lhsT=C.ones_b.t[:], rhs=PT_.t[:, 0:ncol], start=fst, stop=False,
                                skip_group_check=True), reads=[C.ones_b.r, PT_.r], writes=[psSm.r])
                    P.op("dve", lambda e, psSm=psSm: e.reciprocal(out=rs.t[:], in_=psSm.t[:]), reads=[psSm.r], writes=[rs.r])
                    P.op("pool", lambda e, g_=g_, s=s: e.tensor_tensor(out=wgt.t[:], in0=rs.t[:],
                                                                       in1=g_.t[:, s * 512:(s + 1) * 512], op=ALU.mult),
                         reads=[rs.r, g_.r], writes=[wgt.r])
                    P.op("dve", lambda e, hh=hh, s=s, psO=psO: e.tensor_tensor(out=og.t[:, hh, s * 512:(s + 1) * 512],
                                                                      in0=psO.t[:], in1=wgt.t[:], op=ALU.mult),
                         reads=[psO.r, wgt.r], writes=[og.r])
            k = 0
            for c in range(NCH):
                for hf in range(2):
                    pp = psG2[k % 2]
                    k += 1
                    for hh in range(HG):
                        P.op("pe", lambda e, pp=pp, hh=hh, c=c, hf=hf: e.matmul(
                            pp.t[:], lhsT=wob_ap[:, hh, c * 128:(c + 1) * 128], rhs=og.t[:, hh, hf * 512:(hf + 1) * 512],
                            start=(hh == 0), stop=(hh == HG - 1)), reads=[WB0.r, og.r], writes=[pp.r])
                    P.op("dve", lambda e, pp=pp, c=c, hf=hf: e.tensor_tensor(
                        out=C.xT.t[:, c, hf * 512:(hf + 1) * 512], in0=pp.t[:], in1=C.xT.t[:, c, hf * 512:(hf + 1) * 512],
                        op=ALU.add), reads=[pp.r, C.xr[c]], writes=[C.xr[c]])

    for l in range(2):
        kv_phase(0, l, lambda h, l=l: a_wk_d[l, h], lambda b, l=l: a_wv_d[l, b], False)
        mix_phase(False, 0, l, lambda h, l=l: a_wq_d[l, h], lambda h, l=l: a_wg_d[l, h], lambda g, l=l: a_wo_d[l, g], l)
    kv_phase(0, 2, lambda h: s_wk_d[h], lambda b: s_wv_d[b], True)
    for l in range(2):
        mix_phase(True, 0, 3 + l, lambda h, l=l: b_wq_d[l, h], lambda h, l=l: b_wg_d[l, h], lambda g, l=l: b_wo_d[l, g], 0)

    gf = C.sb("gfin", [128, NCH], F32)
    P.dma("sp", lambda e: e.dma_start(out=gf.t[:], in_=gns_d[5]), writes=[gf.r])
    pss = (psG[0], psG[1])
    for c in range(NCH):
        sq = sg[c % 2]
        P.op("act", lambda e, c=c, sq=sq: e.activation(out=sq.t[:], in_=C.xT.t[:, c, :], func=AF.Square),
             reads=[C.xr[c]], writes=[sq.r])
        for hf in range(2):
            P.op("pe", lambda e, c=c, sq=sq, hf=hf: e.matmul(
                pss[hf].t[:], lhsT=C.ones_f.t[:], rhs=sq.t[:, hf * 512:(hf + 1) * 512],
                start=(c == 0), stop=(c == NCH - 1)), reads=[sq.r, C.ones_f.r], writes=[pss[hf].r])
    for hf in range(2):
        sl = slice(hf * 512, (hf + 1) * 512)
        P.op("act", lambda e, hf=hf, sl=sl: e.activation(
            out=C.rstd.t[:, sl], in_=pss[hf].t[:], func=AF.Sqrt, bias=EPS, scale=1.0 / D_MODEL),
            reads=[pss[hf].r], writes=[C.rstd.r])
    P.op("dve", lambda e: e.reciprocal(out=C.rstd.t[:], in_=C.rstd.t[:]), reads=[C.rstd.r], writes=[C.rstd.r])
    for c in range(NCH):
        yb = sg[c % 2]
        P.op("dve", lambda e, c=c, yb=yb: e.scalar_tensor_tensor(
            out=yb.t[:], in0=C.xT.t[:, c, :], scalar=gf.t[:, c:c + 1], in1=C.rstd.t[:],
            op0=ALU.mult, op1=ALU.mult), reads=[C.xr[c], gf.r, C.rstd.r], writes=[yb.r])
        C.outs.append(P.dma("sp", lambda e, c=c, yb=yb: e.dma_start(out=yo_d[:, c, :], in_=yb.t[:]), reads=[yb.r]))
    return C.finish()


def kernel(x, a_norm, a_w_in, a_rel_bias, a_w_out, kv_norm, kv_w, f_w, f_b, b_norm, b_w_in, b_w_out, final_norm):
    f = lambda a: np.asarray(a, np.float32)
    x = f(x)
    a_w_in, a_w_out, kv_w, b_w_in, b_w_out = f(a_w_in), f(a_w_out), f(kv_w), f(b_w_in), f(b_w_out)
    gns = np.stack([tile_vec(f(a_norm)[0]), tile_vec(f(a_norm)[1]), tile_vec(f(kv_norm)), tile_vec(f(b_norm)[0]),
                    tile_vec(f(b_norm)[1]), tile_vec(f(final_norm))])
    shared = {
        "gns": gns,
        "a_wq": np.stack([tile_w_cols(a_w_in[l], 0, D_INNER, 128) for l in range(2)]),
        "a_wk": np.stack([tile_w_cols(a_w_in[l], D_INNER, D_INNER, 128) for l in range(2)]),
        "a_wv": np.stack([tile_w_cols(a_w_in[l], 2 * D_INNER, D_INNER, 512) for l in range(2)]),
        "a_wg": np.stack([tile_w_cols(a_w_in[l], 3 * D_INNER, D_INNER, 128) for l in range(2)]),
        "a_wo": np.stack([_tile_wo(a_w_out[l]) for l in range(2)]),
        "s_wk": tile_w_cols(kv_w, 0, D_INNER, 128),
        "s_wv": tile_w_cols(kv_w, D_INNER, D_INNER, 512),
        "b_wq": np.stack([tile_w_cols(b_w_in[l], 0, D_INNER, 128) for l in range(2)]),
        "b_wg": np.stack([tile_w_cols(b_w_in[l], D_INNER, D_INNER, 128) for l in range(2)]),
        "b_wo": np.stack([_tile_wo(b_w_out[l]) for l in range(2)]),
        "fw": np.ascontiguousarray(f(f_w).reshape(NCH, 128, NH).transpose(1, 0, 2)),
        "fb": np.ascontiguousarray(np.broadcast_to(f(f_b)[None, :], (128, NH))),
    }
    idx = np.clip(np.arange(767) - 127, -256, 256) + 256
    shared["rbx"] = np.ascontiguousarray(f(a_rel_bias)[:, :, idx])
    cmask = np.zeros((128, 5, 128), np.float32)
    cmask[64:, 0, :64] = NEG
    cmask[:64, 4, 64:] = NEG
    shared["cmask"] = cmask.reshape(128, 640)
    kl = np.arange(128)
    cst = np.zeros((4, 128, 128), np.float32)
    cst[0] = np.eye(128, dtype=np.float32)
    cst[1] = (kl[:, None] > kl[None, :]).astype(np.float32)
    cst[2] = np.where(kl[:, None] > kl[None, :], NEG, 0.0)
    cst[3] = np.eye(128, dtype=np.float32)[::-1]
    shared["cst"] = cst
    in_maps = []
    for j in range(8):
        jj = j % 4
        m = dict(shared)
        m["xT"] = to_xT(x[j // 4][core_tokens(j)])
        kmask = np.zeros((128, 2, 8), np.float32)
        if jj == 0:
            kmask[:, 0, :4] = NEG
        m["kmask"] = kmask
        for s in range(2):
            kv = np.full((128, UMAX[s]), NEG, np.float32)
            for u in range(UMAX[s]):
                if jj + 4 * s - u // 4 >= 0:
                    kv[:, u] = 0.0
            m["kval%d" % s] = kv
        in_maps.append(m)
    if "fused" not in _NC_CACHE:
        _NC_CACHE["fused"] = build_fused()
    res = run_bass_kernel_spmd(_NC_CACHE["fused"], in_maps, core_ids=list(range(8)))
    out = np.zeros((BATCH, SEQ, D_MODEL), np.float32)
    for j in range(8):
        out[j // 4][core_tokens(j)] = from_xT(res.results[j]["yo"])
    return out
```

```python
import numpy as np
from contextlib import ExitStack
import ml_dtypes
import concourse.bass as bass
import concourse.mybir as mybir
from concourse.bass_utils import run_bass_kernel_spmd

F32 = mybir.dt.float32
BF16 = mybir.dt.bfloat16
AF = mybir.ActivationFunctionType
ALU = mybir.AluOpType
NPBF = ml_dtypes.bfloat16

D_MODEL = 2048
NCH = 16
D_INNER = 4096
NH = 32
DH = 128
SEQ = 4096
BATCH = 2
T = 1024
SEG = 512
NSEG = 2
EPS = 1e-6
NEG = -30000.0
SCALE = DH ** -0.5
HG = 4
UMAX = (16, 32)


class Res:
    __slots__ = ("name", "w", "rs")

    def __init__(self, name=""):
        self.name = name
        self.w = None
        self.rs = {}


class Op:
    __slots__ = ("eng", "fn", "deps", "dma", "sig", "need", "n", "cc")

    def __init__(self, eng, fn, dma, cc=None):
        self.eng = eng
        self.fn = fn
        self.dma = dma
        self.deps = []
        self.sig = None
        self.need = dma
        self.n = 0
        self.cc = cc


class Prog:
    ENGS = ("pe", "act", "dve", "pool", "sp")
    NDSEM = 24
    EPOCH = 30000

    def __init__(self, nc):
        self.nc = nc
        self.ops = {e: [] for e in self.ENGS}
        self.ndma = {e: 0 for e in self.ENGS}
        self.count = 0
        self.ccs = {}

    def _track(self, o, reads, writes):
        deps = {}
        for r in reads:
            if r.w is not None:
                deps[id(r.w)] = r.w
        for r in writes:
            if r.w is not None:
                deps[id(r.w)] = r.w
            for x in r.rs.values():
                deps[id(x)] = x
        for d in deps.values():
            if d is o:
                continue
            if (not d.dma) and (not o.dma) and d.eng == "pe" and o.eng == "pe":
                continue
            d.need = True
            o.deps.append(d)
        for r in reads:
            key = ("dma", self.count) if o.dma else o.eng
            r.rs[key] = o
        for r in writes:
            r.w = o
            r.rs = {}
        self.count += 1

    def op(self, eng, fn, reads=(), writes=()):
        o = Op(eng, fn, False)
        self._track(o, reads, writes)
        self.ops[eng].append(o)
        return o

    def dma(self, q, fn, reads=(), writes=()):
        o = Op(q, fn, True)
        self._track(o, reads, writes)
        self.ops[q].append(o)
        return o

    def coll(self, key, fn, reads=(), writes=()):
        o = Op("pool", fn, True, cc=key)
        self._track(o, reads, writes)
        self.ops["pool"].append(o)
        return o

    def emit(self, es, final_deps):
        nc = self.nc
        fin = Op("sp", None, False)
        for d in final_deps:
            d.need = True
            fin.deps.append(d)
        self.ops["sp"].append(fin)
        sems = {}
        for e in self.ENGS:
            cnt = 0
            nd = 0
            esems = []
            dsems = []
            for o in self.ops[e]:
                if o.cc is not None:
                    if o.cc not in self.ccs:
                        self.ccs[o.cc] = [es.enter_context(nc.semaphore("cc_%s" % str(o.cc))), 0]
                    self.ccs[o.cc][1] += 1
                    o.sig = (self.ccs[o.cc][0], self.ccs[o.cc][1])
                elif o.dma:
                    k = nd % self.NDSEM
                    if k >= len(dsems):
                        dsems.append(es.enter_context(nc.semaphore("d_%s_%d" % (e, k))))
                    o.sig = (dsems[k], 16 * (nd // self.NDSEM + 1))
                    o.n = nd
                    nd += 1
                elif o.need:
                    ep = cnt // self.EPOCH
                    if ep >= len(esems):
                        esems.append(es.enter_context(nc.semaphore("c_%s_%d" % (e, ep))))
                    o.sig = (esems[ep], cnt % self.EPOCH + 1)
                    cnt += 1
            sems[e] = (esems, dsems)
        engobj = {"pe": nc.tensor, "act": nc.scalar, "dve": nc.vector, "pool": nc.gpsimd, "sp": nc.sync}
        block = es.enter_context(nc.Block())

        def make(e):
            def body(eng):
                waited = {}
                pre = getattr(self, "pre", {}).get(e)
                if pre is not None:
                    pre(eng)
                for o in self.ops[e]:
                    ws = []
                    for d in o.deps:
                        ws.append(d.sig)
                    if o.dma and o.cc is None and o.n >= self.NDSEM:
                        ws.append((o.sig[0], o.sig[1] - 16))
                    for (s, v) in ws:
                        if waited.get(id(s), 0) < v:
                            waited[id(s)] = v
                            eng.wait_ge(s, v)
                    if o.fn is None:
                        continue
                    ins = o.fn(eng)
                    if o.cc is not None:
                        ins.then_inc(o.sig[0], 1)
                    elif o.dma:
                        ins.then_inc(o.sig[0], 16)
                    elif o.sig is not None:
                        ins.then_inc(o.sig[0], 1)
            return body

        block.tensor(make("pe"))
        block.scalar(make("act"))
        block.vector(make("dve"))
        block.gpsimd(make("pool"))
        block.sync(make("sp"))


class Tl:
    __slots__ = ("t", "r")

    def __init__(self, t, name):
        self.t = t
        self.r = Res(name)


class Ctx:
    def __init__(self):
        self.nc = bass.Bass("TRN2", target_bir_lowering=False)
        self.P = Prog(self.nc)
        self.es = ExitStack()
        self.outs = []
        self.nps = 0

    def dram(self, name, shape, dt, kind):
        return self.nc.dram_tensor(name, list(shape), dt, kind=kind).ap()

    def sb(self, name, shape, dt):
        return Tl(self.es.enter_context(self.nc.sbuf_tensor("s_" + name, list(shape), dt)), name)

    def ps(self, name, dt=F32):
        n = 512 if dt == F32 else 1024
        return Tl(self.es.enter_context(self.nc.psum_tensor("p_" + name, [128, n], dt)), name)

    def finish(self):
        self.P.emit(self.es, self.outs)
        self.es.close()
        return self.nc


def phase_consts(C):
    P = C.P
    C.ones_f = C.sb("ones_f", [128, 128], F32)
    C.ones_b = C.sb("ones_b", [128, 128], BF16)
    P.op("pool", lambda e: e.memset(C.ones_f.t[:], 1.0), writes=[C.ones_f.r])
    P.op("pool", lambda e: e.memset(C.ones_b.t[:], 1.0), writes=[C.ones_b.r])


def phase_load_x(C, xT_d):
    P = C.P
    C.xT = C.sb("xT", [128, NCH, T], F32)
    C.xr = [Res("x%d" % c) for c in range(NCH)]
    for c0 in range(0, NCH, 4):
        P.dma("sp", lambda e, c0=c0: e.dma_start(out=C.xT.t[:, c0:c0 + 4, :], in_=xT_d[:, c0:c0 + 4, :]),
              writes=C.xr[c0:c0 + 4])


def phase_norm(C, gn_d, tag, psA, psB, want_tok_rstd=False):
    P = C.P
    if not hasattr(C, "hT"):
        C.hT = C.sb("hT", [128, NCH, T], BF16)
        C.hr = [Res("h%d" % c) for c in range(NCH)]
        C.xsq = [C.sb("xsq%d" % i, [128, T], F32) for i in range(2)]
        C.rstd = C.sb("rstd", [128, T], F32)
    gn = C.sb("gn_" + tag, [128, NCH], F32)
    P.dma("sp", lambda e: e.dma_start(out=gn.t[:], in_=gn_d), writes=[gn.r])
    pss = (psA, psB)
    for c in range(NCH):
        sq = C.xsq[c % 2]
        P.op("act", lambda e, c=c, sq=sq: e.activation(out=sq.t[:], in_=C.xT.t[:, c, :], func=AF.Square),
             reads=[C.xr[c]], writes=[sq.r])
        for hf in range(2):
            P.op("pe", lambda e, c=c, sq=sq, hf=hf: e.matmul(
                pss[hf].t[:], lhsT=C.ones_f.t[:], rhs=sq.t[:, hf * 512:(hf + 1) * 512],
                start=(c == 0), stop=(c == NCH - 1)),
                reads=[sq.r, C.ones_f.r], writes=[pss[hf].r])
    for hf in range(2):
        sl = slice(hf * 512, (hf + 1) * 512)
        P.op("act", lambda e, hf=hf, sl=sl: e.activation(
            out=C.rstd.t[:, sl], in_=pss[hf].t[:], func=AF.Sqrt, bias=EPS, scale=1.0 / D_MODEL),
            reads=[pss[hf].r], writes=[C.rstd.r])
    P.op("dve", lambda e: e.reciprocal(out=C.rstd.t[:], in_=C.rstd.t[:]), reads=[C.rstd.r], writes=[C.rstd.r])
    for c in range(NCH):
        P.op("dve", lambda e, c=c: e.scalar_tensor_tensor(
            out=C.hT.t[:, c, :], in0=C.xT.t[:, c, :], scalar=gn.t[:, c:c + 1], in1=C.rstd.t[:],
            op0=ALU.mult, op1=ALU.mult), reads=[C.xr[c], gn.r, C.rstd.r], writes=[C.hr[c]])
    return gn


def proj_fm(C, wtile, ps, hf, extra_reads=()):
    P = C.P
    for c in range(NCH):
        P.op("pe", lambda e, c=c: e.matmul(ps.t[:], lhsT=wtile.t[:, c, :], rhs=C.hT.t[:, c, hf * 512:(hf + 1) * 512],
                                           start=(c == 0), stop=(c == NCH - 1)),
             reads=[wtile.r, C.hr[c]] + list(extra_reads), writes=[ps.r])


def build_kv():
    C = Ctx()
    P = C.P
    xT_d = C.dram("xT", [128, NCH, T], F32, "ExternalInput")
    gn_d = C.dram("gn", [128, NCH], F32, "ExternalInput")
    wk_d = C.dram("wk", [NH, 128, NCH, 128], F32, "ExternalInput")
    wv_d = C.dram("wv", [8, 128, NCH, 512], F32, "ExternalInput")
    fw_d = C.dram("fw", [128, NCH, NH], F32, "ExternalInput")
    fb_d = C.dram("fb", [128, NH], F32, "ExternalInput")
    kT_o = C.dram("kT", [NH, 128, T], BF16, "ExternalOutput")
    V_o = C.dram("V", [T // 128, 128, D_INNER], BF16, "ExternalOutput")
    nlf_o = C.dram("nlf", [T // 128, 128, NH], F32, "ExternalOutput")

    ps = [C.ps("ps%d" % i) for i in range(8)]
    phase_consts(C)
    phase_load_x(C, xT_d)
    gn = phase_norm(C, gn_d, "kv", ps[0], ps[1])

    fw = C.sb("fw", [128, NCH, NH], F32)
    fb = C.sb("fb", [128, NH], F32)
    P.dma("sp", lambda e: e.dma_start(out=fw.t[:], in_=fw_d), writes=[fw.r])
    P.dma("sp", lambda e: e.dma_start(out=fb.t[:], in_=fb_d), writes=[fb.r])
    gfw = C.sb("gfw", [128, NCH, NH], F32)
    for c in range(NCH):
        P.op("pool", lambda e, c=c: e.tensor_scalar(out=gfw.t[:, c, :], in0=fw.t[:, c, :], scalar1=gn.t[:, c:c + 1],
                                                    scalar2=None, op0=ALU.mult),
             reads=[fw.r, gn.r], writes=[gfw.r])
    ones_col = C.sb("ones_col", [128, 1], F32)
    P.op("pool", lambda e: e.memset(ones_col.t[:], 1.0), writes=[ones_col.r])
    xsqc = [C.sb("xsqc%d" % i, [128, 128], F32) for i in range(2)]
    zs = [C.sb("zs%d" % i, [128, NH], F32) for i in range(2)]
    rt = [C.sb("rt%d" % i, [128, 1], F32) for i in range(2)]
    NT = T // 128
    k = 0
    for tt in range(NT):
        pz = ps[2 + (tt % 2) * 2]
        pq = ps[3 + (tt % 2) * 2]
        tsl = slice(tt * 128, (tt + 1) * 128)
        for c in range(NCH):
            sq = xsqc[k % 2]
            k += 1
            P.op("act", lambda e, c=c, sq=sq, tsl=tsl: e.activation(out=sq.t[:], in_=C.xT.t[:, c, tsl], func=AF.Square),
                 reads=[C.xr[c]], writes=[sq.r])
            P.op("pe", lambda e, c=c, sq=sq, pq=pq: e.matmul(pq.t[:, 0:1], lhsT=sq.t[:], rhs=ones_col.t[:],
                                                             start=(c == 0), stop=(c == NCH - 1)),
                 reads=[sq.r, ones_col.r], writes=[pq.r])
            P.op("pe", lambda e, c=c, tsl=tsl, pz=pz: e.matmul(pz.t[:, 0:NH], lhsT=C.xT.t[:, c, tsl], rhs=gfw.t[:, c, :],
                                                               start=(c == 0), stop=(c == NCH - 1)),
                 reads=[C.xr[c], gfw.r], writes=[pz.r])
        r1 = rt[tt % 2]
        z = zs[tt % 2]
        P.op("act", lambda e, r1=r1, pq=pq: e.activation(out=r1.t[:], in_=pq.t[:, 0:1], func=AF.Sqrt, bias=EPS,
                                                         scale=1.0 / D_MODEL), reads=[pq.r], writes=[r1.r])
        P.op("dve", lambda e, r1=r1: e.reciprocal(out=r1.t[:], in_=r1.t[:]), reads=[r1.r], writes=[r1.r])
        P.op("dve", lambda e, r1=r1, z=z, pz=pz: e.scalar_tensor_tensor(
            out=z.t[:], in0=pz.t[:, 0:NH], scalar=r1.t[:, 0:1], in1=fb.t[:], op0=ALU.mult, op1=ALU.add),
            reads=[pz.r, r1.r, fb.r], writes=[z.r])
        P.op("act", lambda e, z=z: e.activation(out=z.t[:], in_=z.t[:], func=AF.Exp, scale=-1.0),
             reads=[z.r], writes=[z.r])
        P.op("act", lambda e, z=z: e.activation(out=z.t[:], in_=z.t[:], func=AF.Ln, bias=1.0, scale=1.0),
             reads=[z.r], writes=[z.r])
        C.outs.append(P.dma("sp", lambda e, z=z, tt=tt: e.dma_start(out=nlf_o[tt], in_=z.t[:]), reads=[z.r]))

    wkb = [C.sb("wkb%d" % i, [128, NCH, 128], BF16) for i in range(2)]
    ko = [C.sb("ko%d" % i, [128, T], BF16) for i in range(2)]
    for h in range(NH):
        w = wkb[h % 2]
        o = ko[h % 2]
        P.dma("pool", lambda e, w=w, h=h: e.dma_start(out=w.t[:], in_=wk_d[h]), writes=[w.r])
        for hf in range(2):
            pp = ps[(2 * h + hf) % 4]
            proj_fm(C, w, pp, hf)
            if hf == 0:
                P.op("act", lambda e, o=o, pp=pp: e.activation(out=o.t[:, 0:512], in_=pp.t[:], func=AF.Copy),
                     reads=[pp.r], writes=[o.r])
            else:
                P.op("dve", lambda e, o=o, pp=pp: e.tensor_copy(out=o.t[:, 512:1024], in_=pp.t[:]),
                     reads=[pp.r], writes=[o.r])
        C.outs.append(P.dma("sp", lambda e, o=o, h=h: e.dma_start(out=kT_o[h], in_=o.t[:]), reads=[o.r]))

    wvb = [C.sb("wvb%d" % i, [128, NCH, 512], BF16) for i in range(2)]
    vo = [C.sb("vo%d" % i, [128, 512], BF16) for i in range(4)]
    k = 0
    for b in range(8):
        w = wvb[b % 2]
        P.dma("pool", lambda e, w=w, b=b: e.dma_start(out=w.t[:], in_=wv_d[b]), writes=[w.r])
        for tt in range(NT):
            pp = ps[4 + k % 4]
            o = vo[k % 4]
            for c in range(NCH):
                P.op("pe", lambda e, c=c, tt=tt, w=w, pp=pp: e.matmul(
                    pp.t[:], lhsT=C.hT.t[:, c, tt * 128:(tt + 1) * 128], rhs=w.t[:, c, :],
                    start=(c == 0), stop=(c == NCH - 1)), reads=[w.r, C.hr[c]], writes=[pp.r])
            if k % 2 == 0:
                P.op("act", lambda e, o=o, pp=pp: e.activation(out=o.t[:], in_=pp.t[:], func=AF.Copy),
                     reads=[pp.r], writes=[o.r])
            else:
                P.op("dve", lambda e, o=o, pp=pp: e.tensor_copy(out=o.t[:], in_=pp.t[:]), reads=[pp.r], writes=[o.r])
            C.outs.append(P.dma("sp", lambda e, o=o, tt=tt, b=b: e.dma_start(
                out=V_o[tt, :, b * 512:(b + 1) * 512], in_=o.t[:]), reads=[o.r]))
            k += 1
    return C.finish()


def core_segments(j):
    jj = j % 4
    return (jj, jj + 4)


def core_tokens(j):
    s0, s1 = core_segments(j)
    return np.concatenate([np.arange(s0 * SEG, (s0 + 1) * SEG), np.arange(s1 * SEG, (s1 + 1) * SEG)])


def tile_w_cols(W, col0, ncols, blk):
    Wc = W[:, col0:col0 + ncols]
    nb = ncols // blk
    return np.ascontiguousarray(Wc.reshape(NCH, 128, nb, blk).transpose(2, 1, 0, 3))


def tile_vec(g):
    return np.ascontiguousarray(g.reshape(NCH, 128).T)


def to_xT(xtok):
    t = xtok.shape[0]
    return np.ascontiguousarray(xtok.T.reshape(NCH, 128, t).transpose(1, 0, 2))


def from_xT(xT):
    t = xT.shape[2]
    return np.ascontiguousarray(xT.transpose(1, 0, 2).reshape(D_MODEL, t).T)


_NC_CACHE = {}


def get_nc(kind):
    if kind not in _NC_CACHE:
        _NC_CACHE[kind] = build_kv() if kind == "kv" else build_mix(kind)
    return _NC_CACHE[kind]


def run_kv(xT_cores, gn, W, koff, voff, f_w, f_b):
    nc = get_nc("kv")
    wk = tile_w_cols(W, koff, D_INNER, 128)
    wv = tile_w_cols(W, voff, D_INNER, 512)
    fw = np.ascontiguousarray(f_w.reshape(NCH, 128, NH).transpose(1, 0, 2))
    fb = np.ascontiguousarray(np.broadcast_to(f_b[None, :], (128, NH)))
    g = tile_vec(gn)
    in_maps = [{"xT": xT_cores[j], "gn": g, "wk": wk, "wv": wv, "fw": fw, "fb": fb} for j in range(8)]
    res = run_bass_kernel_spmd(nc, in_maps, core_ids=list(range(8)))
    return res.results


def attn_unit(C, kt_ap, v_ap, q_ap, ncol, sc_ap, adds, acc_o, acc_s, c0, first, kres, vres, qres, ares, psS, Sb, PT):
    P = C.P
    P.op("pe", lambda e: e.matmul(psS.t[:, 0:ncol], lhsT=kt_ap, rhs=q_ap, start=True, stop=True),
         reads=[kres, qres], writes=[psS.r])
    for (o, n, ap, r) in adds:
        P.op("dve", lambda e, o=o, n=n, ap=ap: e.scalar_tensor_tensor(
            out=Sb.t[:, o:o + n], in0=psS.t[:, o:o + n], scalar=sc_ap, in1=ap, op0=ALU.add, op1=ALU.add),
            reads=[psS.r, r] + list(ares), writes=[Sb.r])
    P.op("act", lambda e: e.activation(out=PT.t[:, 0:ncol], in_=Sb.t[:, 0:ncol], func=AF.Exp),
         reads=[Sb.r], writes=[PT.r])
    P.op("pe", lambda e: e.matmul(acc_o.t[:, c0:c0 + ncol], lhsT=v_ap, rhs=PT.t[:, 0:ncol], start=first, stop=False,
                                  skip_group_check=True),
         reads=[vres, PT.r], writes=[acc_o.r])
    P.op("pe", lambda e: e.matmul(acc_s.t[:, c0:c0 + ncol], lhsT=C.ones_b.t[:], rhs=PT.t[:, 0:ncol], start=first,
                                  stop=False, skip_group_check=True),
         reads=[C.ones_b.r, PT.r], writes=[acc_s.r])


def build_mix(kind):
    C = Ctx()
    P = C.P
    isb = kind == "b"
    xT_d = C.dram("xT", [128, NCH, T], F32, "ExternalInput")
    gn_d = C.dram("gn", [128, NCH], F32, "ExternalInput")
    wq_d = C.dram("wq", [NH, 128, NCH, 128], F32, "ExternalInput")
    wg_d = C.dram("wg", [NH, 128, NCH, 128], F32, "ExternalInput")
    wo_d = C.dram("wo", [NH // HG, 128, HG, D_MODEL], F32, "ExternalInput")
    xo_d = C.dram("xo", [128, NCH, T], F32, "ExternalOutput")
    if not isb:
        NU = (8, 8)
        kT_d = [C.dram("kTs%d" % s, [NH, 128, 8 * 128], BF16, "ExternalInput") for s in range(2)]
        V_d = [C.dram("Vs%d" % s, [8, 128, D_INNER], BF16, "ExternalInput") for s in range(2)]
        kmask_d = C.dram("kmask", [128, 2, 8], F32, "ExternalInput")
        rbx_d = C.dram("rbx", [NH, 767], F32, "ExternalInput")
        cmask_d = C.dram("cmask", [128, 640], F32, "ExternalInput")
    else:
        NU = UMAX
        kT_d = [C.dram("kTs%d" % s, [NH, 128, NU[s] * 128], BF16, "ExternalInput") for s in range(2)]
        V_d = [C.dram("Vs%d" % s, [NU[s], 128, D_INNER], BF16, "ExternalInput") for s in range(2)]
        nlf_d = [C.dram("nlfs%d" % s, [128, NU[s], NH], F32, "ExternalInput") for s in range(2)]
        kval_d = [C.dram("kval%d" % s, [128, NU[s]], F32, "ExternalInput") for s in range(2)]
        gfin_d = C.dram("gfin", [128, NCH], F32, "ExternalInput")
        ident_d = C.dram("ident", [128, 128], F32, "ExternalInput")
        tri_d = C.dram("tri", [128, 128], F32, "ExternalInput")
        trim_d = C.dram("trim", [128, 128], F32, "ExternalInput")
        yo_d = C.dram("yo", [128, NCH, T], F32, "ExternalOutput")

    ps = [C.ps("ps%d" % i) for i in range(8)]
    psS = ps[0:2]
    psO = ps[2]
    psSm = ps[3]
    psG = ps[4:8]
    phase_consts(C)
    phase_load_x(C, xT_d)
    sg = [C.sb("sg%d" % i, [128, T], F32) for i in range(2)]
    C.xsq = sg
    C.hT = C.sb("hT", [128, NCH, T], BF16)
    C.hr = [Res("h%d" % c) for c in range(NCH)]
    C.rstd = C.sb("rstd", [128, T], F32)
    phase_norm(C, gn_d, "n1", psG[0], psG[1])

    wqb = [C.sb("wqb%d" % i, [128, NCH, 128], BF16) for i in range(2)]
    wgb = [C.sb("wgb%d" % i, [128, NCH, 128], BF16) for i in range(2)]
    wob = C.sb("wob", [128, HG, D_MODEL], BF16)
    qT = [C.sb("qT%d" % i, [128, T], BF16) for i in range(2)]
    og = C.sb("og", [128, HG, T], BF16)
    kxc = [C.sb("kxc%d" % i, [128, 8 * 128], BF16) for i in range(2)]
    vxc = [C.sb("vxc%d" % i, [128, 8, 128], BF16) for i in range(2)]
    Sb = [C.sb("Sb%d" % i, [128, 512], F32) for i in range(2)]
    PT = [C.sb("PT%d" % i, [128, 512], BF16) for i in range(2)]
    rs = C.sb("rs", [128, 512], F32)
    wgt = C.sb("wgt", [128, 512], F32)

    if not isb:
        kmask = C.sb("kmask", [128, 2, 8], F32)
        P.dma("sp", lambda e: e.dma_start(out=kmask.t[:], in_=kmask_d), writes=[kmask.r])
        cmask = C.sb("cmask", [128, 640], F32)
        P.dma("sp", lambda e: e.dma_start(out=cmask.t[:], in_=cmask_d), writes=[cmask.r])
        BT = [C.sb("BT%d" % i, [128, 640], F32) for i in range(2)]
    else:
        ident = C.sb("ident", [128, 128], F32)
        tri = C.sb("tri", [128, 128], F32)
        trim = C.sb("trim", [128, 128], F32)
        for (t_, d_) in ((ident, ident_d), (tri, tri_d), (trim, trim_d)):
            P.dma("sp", lambda e, t_=t_, d_=d_: e.dma_start(out=t_.t[:], in_=d_), writes=[t_.r])
        nlf_t = C.sb("nlf_t", [128, 32, NH], F32)
        tot_t = C.sb("tot_t", [128, 32, NH], F32)
        ND = [C.sb("ND%d" % s, [128, NU[s], NH], F32) for s in range(2)]
        SC = [C.sb("SC%d" % s, [128, NU[s], NH], F32) for s in range(2)]
        kval = [C.sb("kval%d" % s, [128, NU[s]], F32) for s in range(2)]
        NDQ = C.sb("NDQ", [128, 512], F32)
        NDQd = C.sb("NDQd", [128, 512], F32)
        dexp = [C.sb("dexp%d" % i, [128, 4, 128], BF16) for i in range(2)]
        for s in range(2):
            U = NU[s]
            P.dma("sp", lambda e, s=s, U=U: e.dma_start(out=nlf_t.t[:, 0:U, :], in_=nlf_d[s]), writes=[nlf_t.r])
            P.dma("sp", lambda e, s=s: e.dma_start(out=kval[s].t[:], in_=kval_d[s]), writes=[kval[s].r])
            nflat = nlf_t.t[:, 0:U, :].rearrange("p u h -> p (u h)")
            ndflat = ND[s].t[:].rearrange("p u h -> p (u h)")
            totflat = tot_t.t[:, 0:U, :].rearrange("p u h -> p (u h)")
            for j in range(U * NH // 512):
                sl = slice(j * 512, (j + 1) * 512)
                pa = psG[(2 * j) % 4]
                pb = psG[(2 * j + 1) % 4]
                P.op("pe", lambda e, pa=pa, sl=sl, nflat=nflat: e.matmul(pa.t[:], lhsT=tri.t[:], rhs=nflat[:, sl],
                                                                         start=True, stop=True),
                     reads=[tri.r, nlf_t.r], writes=[pa.r])
                P.op("pe", lambda e, pb=pb, sl=sl, nflat=nflat: e.matmul(pb.t[:], lhsT=C.ones_f.t[:], rhs=nflat[:, sl],
                                                                         start=True, stop=True),
                     reads=[C.ones_f.r, nlf_t.r], writes=[pb.r])
                P.op("act", lambda e, pa=pa, sl=sl, ndflat=ndflat: e.activation(out=ndflat[:, sl], in_=pa.t[:], func=AF.Copy),
                     reads=[pa.r], writes=[ND[s].r])
                P.op("dve", lambda e, pb=pb, sl=sl, totflat=totflat: e.tensor_copy(out=totflat[:, sl], in_=pb.t[:]),
                     reads=[pb.r], writes=[tot_t.r])
            for u in range(1, U - 1):
                P.op("dve", lambda e, u=u: e.tensor_tensor(out=tot_t.t[:, u, :], in0=tot_t.t[:, u, :],
                                                           in1=tot_t.t[:, u - 1, :], op=ALU.add),
                     reads=[tot_t.r], writes=[tot_t.r])
            P.op("dve", lambda e, s=s, U=U: e.tensor_tensor(out=ND[s].t[:, 1:U, :], in0=ND[s].t[:, 1:U, :],
                                                            in1=tot_t.t[:, 0:U - 1, :], op=ALU.add),
                 reads=[tot_t.r, ND[s].r], writes=[ND[s].r])
            P.op("dve", lambda e, s=s, U=U: e.tensor_tensor(
                out=SC[s].t[:], in0=kval[s].t[:].unsqueeze(2).to_broadcast([128, U, NH]), in1=ND[s].t[:],
                op=ALU.subtract), reads=[kval[s].r, ND[s].r], writes=[SC[s].r])

    nK = 0
    nUnit = 0
    for grp in range(NH // HG):
        P.dma("pool", lambda e, grp=grp: e.dma_start(out=wob.t[:], in_=wo_d[grp]), writes=[wob.r])
        for hh in range(HG):
            h = grp * HG + hh
            wq_, wg_ = wqb[h % 2], wgb[h % 2]
            P.dma("pool", lambda e, wq_=wq_, h=h: e.dma_start(out=wq_.t[:], in_=wq_d[h]), writes=[wq_.r])
            P.dma("pool", lambda e, wg_=wg_, h=h: e.dma_start(out=wg_.t[:], in_=wg_d[h]), writes=[wg_.r])
            q_ = qT[h % 2]
            g_ = sg[h % 2]
            for hf in range(2):
                pp = psG[hf]
                proj_fm(C, wq_, pp, hf)
                P.op("dve", lambda e, q_=q_, pp=pp, hf=hf: e.tensor_scalar(
                    out=q_.t[:, hf * 512:(hf + 1) * 512], in0=pp.t[:], scalar1=SCALE, scalar2=None, op0=ALU.mult),
                    reads=[pp.r], writes=[q_.r])
            for hf in range(2):
                pp = psG[2 + hf]
                proj_fm(C, wg_, pp, hf)
                P.op("act", lambda e, g_=g_, pp=pp, hf=hf: e.activation(
                    out=g_.t[:, hf * 512:(hf + 1) * 512], in_=pp.t[:], func=AF.Silu),
                    reads=[pp.r], writes=[g_.r])
            if not isb:
                bt = BT[h % 2]
                src = bass.AP(rbx_d.tensor, h * 767, [[1, 128], [1, 640]])
                P.dma("sp", lambda e, bt=bt, src=src: e.dma_start(out=bt.t[:], in_=src), writes=[bt.r])
                P.op("pool", lambda e, bt=bt: e.tensor_tensor(out=bt.t[:], in0=bt.t[:], in1=cmask.t[:], op=ALU.add),
                     reads=[bt.r, cmask.r], writes=[bt.r])
            for s in range(2):
                U = NU[s]
                if isb:
                    dx = dexp[(2 * h + s) % 2]
                    for i in range(4):
                        P.op("pool", lambda e, dx=dx, i=i, s=s, h=h: e.tensor_tensor(
                            out=dx.t[:, i, :], in0=ident.t[:], in1=ND[s].t[:, 3 - i, h:h + 1].to_broadcast([128, 128]),
                            op=ALU.mult), reads=[ident.r, ND[s].r], writes=[dx.r])
                    pq = psG[(2 * h + s) % 4]
                    P.op("pe", lambda e, dx=dx, pq=pq: e.matmul(pq.t[:], lhsT=C.ones_b.t[:],
                                                                rhs=dx.t[:].rearrange("p i q -> p (i q)"),
                                                                start=True, stop=True),
                         reads=[C.ones_b.r, dx.r], writes=[pq.r])
                    P.op("act", lambda e, pq=pq: e.activation(out=NDQ.t[:], in_=pq.t[:], func=AF.Copy),
                         reads=[pq.r], writes=[NDQ.r])
                    P.op("pool", lambda e: e.tensor_tensor(
                        out=NDQd.t[:].rearrange("p (i q) -> p i q", i=4), in0=NDQ.t[:].rearrange("p (i q) -> p i q", i=4),
                        in1=trim.t[:].unsqueeze(1).to_broadcast([128, 4, 128]), op=ALU.add),
                        reads=[NDQ.r, trim.r], writes=[NDQd.r])
                first = True
                for ch in range(U // 8):
                    kx = kxc[nK % 2]
                    vx = vxc[nK % 2]
                    nK += 1
                    P.dma("sp", lambda e, kx=kx, s=s, h=h, ch=ch: e.dma_start(
                        out=kx.t[:], in_=kT_d[s][h, :, ch * 1024:(ch + 1) * 1024]), writes=[kx.r])
                    P.dma("sp", lambda e, vx=vx, s=s, h=h, ch=ch: e.dma_start(
                        out=vx.t[:], in_=V_d[s][ch * 8:(ch + 1) * 8, :, h * 128:(h + 1) * 128].rearrange("u p d -> p u d")),
                        writes=[vx.r])
                    if ch == 0:
                        order = [3, 4, 0, 1, 2, 5, 6, 7] if not isb else [3, 2, 1, 0, 4, 5, 6, 7]
                    else:
                        order = list(range(8))
                    for ul in order:
                        u = ch * 8 + ul
                        if not isb:
                            ilo, ihi = max(0, u - 4), min(3, u)
                            rlo = ilo + 4 - u
                            ncol = (ihi - ilo + 1) * 128
                            adds = [(0, ncol, BT[h % 2].t[:, rlo * 128:rlo * 128 + ncol], BT[h % 2].r)]
                            sc_ap = kmask.t[:, s, u:u + 1]
                            ares = [kmask.r]
                        else:
                            if u < 4:
                                ilo, ihi = 3 - u, 3
                                ncol = (ihi - ilo + 1) * 128
                                adds = [(0, 128, NDQd.t[:, ilo * 128:(ilo + 1) * 128], NDQd.r)]
                                if ncol > 128:
                                    adds.append((128, ncol - 128, NDQ.t[:, (ilo + 1) * 128:512], NDQ.r))
                            else:
                                ilo, ihi = 0, 3
                                ncol = 512
                                adds = [(0, 512, NDQ.t[:, :], NDQ.r)]
                            sc_ap = SC[s].t[:, u, h:h + 1]
                            ares = [SC[s].r]
                        c0 = ilo * 128
                        attn_unit(C, kx.t[:, ul * 128:(ul + 1) * 128], vx.t[:, ul, :],
                                  q_.t[:, s * 512 + c0:s * 512 + c0 + ncol], ncol, sc_ap, adds, psO, psSm, c0, first,
                                  kx.r, vx.r, q_.r, ares, psS[nUnit % 2], Sb[nUnit % 2], PT[nUnit % 2])
                        first = False
                        nUnit += 1
                P.op("dve", lambda e: e.reciprocal(out=rs.t[:], in_=psSm.t[:]), reads=[psSm.r], writes=[rs.r])
                P.op("pool", lambda e, g_=g_, s=s: e.tensor_tensor(out=wgt.t[:], in0=rs.t[:],
                                                                   in1=g_.t[:, s * 512:(s + 1) * 512], op=ALU.mult),
                     reads=[rs.r, g_.r], writes=[wgt.r])
                P.op("dve", lambda e, hh=hh, s=s: e.tensor_tensor(out=og.t[:, hh, s * 512:(s + 1) * 512], in0=psO.t[:],
                                                                  in1=wgt.t[:], op=ALU.mult),
                     reads=[psO.r, wgt.r], writes=[og.r])
        k = 0
        for c in range(NCH):
            for hf in range(2):
                pp = psG[k % 4]
                k += 1
                for hh in range(HG):
                    P.op("pe", lambda e, pp=pp, hh=hh, c=c, hf=hf: e.matmul(
                        pp.t[:], lhsT=wob.t[:, hh, c * 128:(c + 1) * 128], rhs=og.t[:, hh, hf * 512:(hf + 1) * 512],
                        start=(hh == 0), stop=(hh == HG - 1)), reads=[wob.r, og.r], writes=[pp.r])
                P.op("dve", lambda e, pp=pp, c=c, hf=hf: e.tensor_tensor(
                    out=C.xT.t[:, c, hf * 512:(hf + 1) * 512], in0=pp.t[:], in1=C.xT.t[:, c, hf * 512:(hf + 1) * 512],
                    op=ALU.add), reads=[pp.r, C.xr[c]], writes=[C.xr[c]])
    for c0 in range(0, NCH, 4):
        C.outs.append(P.dma("sp", lambda e, c0=c0: e.dma_start(out=xo_d[:, c0:c0 + 4, :], in_=C.xT.t[:, c0:c0 + 4, :]),
                            reads=C.xr[c0:c0 + 4]))
    if isb:
        gf = C.sb("gfin", [128, NCH], F32)
        P.dma("sp", lambda e: e.dma_start(out=gf.t[:], in_=gfin_d), writes=[gf.r])
        pss = (psG[0], psG[1])
        for c in range(NCH):
            sq = sg[c % 2]
            P.op("act", lambda e, c=c, sq=sq: e.activation(out=sq.t[:], in_=C.xT.t[:, c, :], func=AF.Square),
                 reads=[C.xr[c]], writes=[sq.r])
            for hf in range(2):
                P.op("pe", lambda e, c=c, sq=sq, hf=hf: e.matmul(
                    pss[hf].t[:], lhsT=C.ones_f.t[:], rhs=sq.t[:, hf * 512:(hf + 1) * 512],
                    start=(c == 0), stop=(c == NCH - 1)), reads=[sq.r, C.ones_f.r], writes=[pss[hf].r])
        for hf in range(2):
            sl = slice(hf * 512, (hf + 1) * 512)
            P.op("act", lambda e, hf=hf, sl=sl: e.activation(
                out=C.rstd.t[:, sl], in_=pss[hf].t[:], func=AF.Sqrt, bias=EPS, scale=1.0 / D_MODEL),
                reads=[pss[hf].r], writes=[C.rstd.r])
        P.op("dve", lambda e: e.reciprocal(out=C.rstd.t[:], in_=C.rstd.t[:]), reads=[C.rstd.r], writes=[C.rstd.r])
        for c in range(NCH):
            yb = sg[c % 2]
            P.op("dve", lambda e, c=c, yb=yb: e.scalar_tensor_tensor(
                out=yb.t[:], in0=C.xT.t[:, c, :], scalar=gf.t[:, c:c + 1], in1=C.rstd.t[:],
                op0=ALU.mult, op1=ALU.mult), reads=[C.xr[c], gf.r, C.rstd.r], writes=[yb.r])
            C.outs.append(P.dma("sp", lambda e, c=c, yb=yb: e.dma_start(out=yo_d[:, c, :], in_=yb.t[:]), reads=[yb.r]))
    return C.finish()


def _tile_wo(Wo):
    return np.ascontiguousarray(Wo.reshape(NH // HG, HG, 128, D_MODEL).transpose(0, 2, 1, 3))


def _gather_seq(res, key):
    out = []
    for b in range(BATCH):
        if key == "kT":
            g = np.zeros((NH, 128, SEQ), dtype=res[0][key].dtype)
            for j in range(4 * b, 4 * b + 4):
                for s, sgm in enumerate(core_segments(j)):
                    g[:, :, sgm * SEG:(sgm + 1) * SEG] = res[j][key][:, :, s * SEG:(s + 1) * SEG]
        else:
            w = res[0][key].shape[2]
            g = np.zeros((SEQ // 128, 128, w), dtype=res[0][key].dtype)
            for j in range(4 * b, 4 * b + 4):
                for s, sgm in enumerate(core_segments(j)):
                    g[sgm * 4:(sgm + 1) * 4] = res[j][key][s * 4:(s + 1) * 4]
        out.append(g)
    return out


def run_a(xT_cores, gn, W_in, rel_bias, W_out, kT_g, V_g):
    nc = get_nc("a")
    wq = tile_w_cols(W_in, 0, D_INNER, 128)
    wg = tile_w_cols(W_in, 3 * D_INNER, D_INNER, 128)
    wo = _tile_wo(W_out)
    g = tile_vec(gn)
    idx = np.clip(np.arange(767) - 127, -256, 256) + 256
    rbx = np.ascontiguousarray(rel_bias[:, idx])
    cmask = np.zeros((128, 5, 128), np.float32)
    cmask[:64, 0, :64] = NEG
    cmask[64:, 4, 64:] = NEG
    cmask = cmask.reshape(128, 640)
    in_maps = []
    for j in range(8):
        b = j // 4
        m = {"xT": xT_cores[j], "gn": g, "wq": wq, "wg": wg, "wo": wo, "rbx": rbx, "cmask": cmask}
        kmask = np.zeros((128, 2, 8), np.float32)
        for s, sgm in enumerate(core_segments(j)):
            kT = np.zeros((NH, 128, 2 * SEG), dtype=kT_g[b].dtype)
            V = np.zeros((8, 128, D_INNER), dtype=V_g[b].dtype)
            kT[:, :, SEG:] = kT_g[b][:, :, sgm * SEG:(sgm + 1) * SEG]
            V[4:] = V_g[b][sgm * 4:(sgm + 1) * 4]
            if sgm > 0:
                kT[:, :, :SEG] = kT_g[b][:, :, (sgm - 1) * SEG:sgm * SEG]
                V[:4] = V_g[b][(sgm - 1) * 4:sgm * 4]
            else:
                kmask[:, s, :4] = NEG
            m["kTs%d" % s] = np.ascontiguousarray(kT.reshape(NH, 128, 8, 128)[:, :, :, ::-1]).reshape(NH, 128, 1024)
            m["Vs%d" % s] = np.ascontiguousarray(V[:, ::-1, :])
        m["kmask"] = kmask
        in_maps.append(m)
    res = run_bass_kernel_spmd(nc, in_maps, core_ids=list(range(8)))
    return [r["xo"] for r in res.results]


def run_b(xT_cores, gn, W_in, W_out, kT_g, V_g, nlf_g, gfin):
    nc = get_nc("b")
    wq = tile_w_cols(W_in, 0, D_INNER, 128)
    wg = tile_w_cols(W_in, D_INNER, D_INNER, 128)
    wo = _tile_wo(W_out)
    g = tile_vec(gn)
    gf = tile_vec(gfin)
    ident = np.eye(128, dtype=np.float32)
    kl = np.arange(128)
    tri = (kl[:, None] > kl[None, :]).astype(np.float32)
    trim = np.where(kl[:, None] > kl[None, :], NEG, 0.0).astype(np.float32)
    in_maps = []
    for j in range(8):
        b = j // 4
        m = {"xT": xT_cores[j], "gn": g, "wq": wq, "wg": wg, "wo": wo, "gfin": gf, "ident": ident, "tri": tri,
             "trim": trim}
        for s, sgm in enumerate(core_segments(j)):
            U = UMAX[s]
            tiles = [4 * sgm + 3 - u for u in range(4 * sgm + 4)]
            kT = np.zeros((NH, 128, U * 128), dtype=kT_g[b].dtype)
            V = np.zeros((U, 128, D_INNER), dtype=V_g[b].dtype)
            nlf = np.zeros((128, U, NH), np.float32)
            kval = np.full((128, U), NEG, np.float32)
            for u, tix in enumerate(tiles):
                kT[:, :, u * 128:(u + 1) * 128] = kT_g[b][:, :, tix * 128:(tix + 1) * 128]
                V[u] = V_g[b][tix]
                nlf[:, u, :] = nlf_g[b][tix]
                kval[:, u] = 0.0
            m["kTs%d" % s] = kT
            m["Vs%d" % s] = V
            m["nlfs%d" % s] = nlf
            m["kval%d" % s] = kval
        in_maps.append(m)
    res = run_bass_kernel_spmd(nc, in_maps, core_ids=list(range(8)))
    return [r["xo"] for r in res.results], [r["yo"] for r in res.results]


def kernel_unfused(x, a_norm, a_w_in, a_rel_bias, a_w_out, kv_norm, kv_w, f_w, f_b, b_norm, b_w_in, b_w_out, final_norm):
    x = np.asarray(x, np.float32)
    xT = [to_xT(x[j // 4][core_tokens(j)]) for j in range(8)]
    f_w = np.asarray(f_w, np.float32)
    f_b = np.asarray(f_b, np.float32)
    for l in range(2):
        W = np.asarray(a_w_in[l], np.float32)
        r = run_kv(xT, np.asarray(a_norm[l], np.float32), W, D_INNER, 2 * D_INNER, f_w, f_b)
        kT_g = _gather_seq(r, "kT")
        V_g = _gather_seq(r, "V")
        xT = run_a(xT, np.asarray(a_norm[l], np.float32), W, np.asarray(a_rel_bias[l], np.float32),
                   np.asarray(a_w_out[l], np.float32), kT_g, V_g)
    r = run_kv(xT, np.asarray(kv_norm, np.float32), np.asarray(kv_w, np.float32), 0, D_INNER, f_w, f_b)
    kT_g = _gather_seq(r, "kT")
    V_g = _gather_seq(r, "V")
    nlf_g = _gather_seq(r, "nlf")
    yT = None
    for l in range(2):
        xT, yT = run_b(xT, np.asarray(b_norm[l], np.float32), np.asarray(b_w_in[l], np.float32),
                       np.asarray(b_w_out[l], np.float32), kT_g, V_g, nlf_g, np.asarray(final_norm, np.float32))
    out = np.zeros((BATCH, SEQ, D_MODEL), np.float32)
    for j in range(8):
        out[j // 4][core_tokens(j)] = from_xT(yT[j])
    return out


NSEGP = 11
KSEGE = 1024 * 512
VSEGE = 512 * 512
NSEGE = 512 * NH
GROUPS = [[0, 1, 2, 3], [4, 5, 6, 7]]


def build_fused():
    C = Ctx()
    P = C.P
    nc = C.nc
    xT_d = C.dram("xT", [128, NCH, T], F32, "ExternalInput")
    gns_d = C.dram("gns", [6, 128, NCH], F32, "ExternalInput")
    a_wq_d = C.dram("a_wq", [2, NH, 128, NCH, 128], F32, "ExternalInput")
    a_wg_d = C.dram("a_wg", [2, NH, 128, NCH, 128], F32, "ExternalInput")
    a_wk_d = C.dram("a_wk", [2, NH, 128, NCH, 128], F32, "ExternalInput")
    a_wv_d = C.dram("a_wv", [2, 8, 128, NCH, 512], F32, "ExternalInput")
    a_wo_d = C.dram("a_wo", [2, NH // HG, 128, HG, D_MODEL], F32, "ExternalInput")
    s_wk_d = C.dram("s_wk", [NH, 128, NCH, 128], F32, "ExternalInput")
    s_wv_d = C.dram("s_wv", [8, 128, NCH, 512], F32, "ExternalInput")
    b_wq_d = C.dram("b_wq", [2, NH, 128, NCH, 128], F32, "ExternalInput")
    b_wg_d = C.dram("b_wg", [2, NH, 128, NCH, 128], F32, "ExternalInput")
    b_wo_d = C.dram("b_wo", [2, NH // HG, 128, HG, D_MODEL], F32, "ExternalInput")
    fw_d = C.dram("fw", [128, NCH, NH], F32, "ExternalInput")
    fb_d = C.dram("fb", [128, NH], F32, "ExternalInput")
    rbx_d = C.dram("rbx", [2, NH, 767], F32, "ExternalInput")
    cmask_d = C.dram("cmask", [128, 640], F32, "ExternalInput")
    kmask_d = C.dram("kmask", [128, 2, 8], F32, "ExternalInput")
    kval_d = [C.dram("kval%d" % s, [128, UMAX[s]], F32, "ExternalInput") for s in range(2)]
    cst_d = C.dram("cst", [4, 128, 128], F32, "ExternalInput")
    yo_d = C.dram("yo", [128, NCH, T], F32, "ExternalOutput")
    kTo = [nc.dram_tensor("kTo%d" % i, [4 * 2 * 1024, 512], BF16) for i in range(1)]
    Vo = [nc.dram_tensor("Vo%d" % i, [8 * 2 * 512, 512], BF16) for i in range(1)]
    nlfo = nc.dram_tensor("nlfo", [T, NH], F32)
    KL = [nc.dram_tensor("KL%d" % i, [4 * NSEGP * 1024, 512], BF16) for i in range(2)]
    VL = [nc.dram_tensor("VL%d" % i, [4 * NSEGP * 1024, 512], BF16) for i in range(2)]
    NL = nc.dram_tensor("NL", [NSEGP * 512, NH], F32)
    kTo_r = [[Res() for _ in range(NH)] for _ in range(2)]
    Vo_r = [[Res() for _ in range(8)] for _ in range(2)]
    nlfo_r = Res()
    KL_r = [[[Res() for _ in range(2)] for _ in range(4)] for _ in range(2)]
    VL_r = [[[Res() for _ in range(2)] for _ in range(4)] for _ in range(2)]
    NL_r = [Res() for _ in range(2)]
    pad_r = Res()
    Kloc = nc.dram_tensor("Kloc", [4 * 8 * 1024, 512], BF16)
    Vloc = nc.dram_tensor("Vloc", [4 * 8 * 1024, 512], BF16)
    Nloc = nc.dram_tensor("Nloc", [8 * 512, NH], F32)
    Kloc_r = [Res() for _ in range(4)]
    Vloc_r = [Res() for _ in range(4)]
    Nloc_r = Res()

    ps = [C.ps("ps%d" % i) for i in range(8)]
    psS = ps[0:3]
    psOs = [ps[3], ps[3]]
    psSms = [ps[4], ps[4]]
    psO = psOs[0]
    psSm = psSms[0]
    psG = [ps[5], ps[6], ps[7], ps[0]]
    psG2 = [ps[5], ps[6]]
    phase_consts(C)
    phase_load_x(C, xT_d)

    sg = [C.sb("sg%d" % i, [128, T], F32) for i in range(2)]
    C.xsq = sg
    C.hT = C.sb("hT", [128, NCH, T], BF16)
    C.hr = [Res("h%d" % c) for c in range(NCH)]
    C.rstd = C.sb("rstd", [128, T], F32)
    WB0 = C.sb("WB0", [128, 8192], BF16)
    WB1 = C.sb("WB1", [128, 8192], BF16)
    wq_r = [Res(), Res()]
    wg_r = [Res(), Res()]
    WB1_rs = wq_r + wg_r

    def wvb_ap(i):
        return (WB0 if i == 0 else WB1).t[:].rearrange("p (c n) -> p c n", c=NCH)

    def wvb_res(i):
        return [WB0.r] if i == 0 else WB1_rs

    wob_ap = WB0.t[:].rearrange("p (h n) -> p h n", h=HG)

    def wqb_ap(i):
        return WB1.t[:, i * 2048:(i + 1) * 2048].rearrange("p (c n) -> p c n", c=NCH)

    def wgb_ap(i):
        return WB1.t[:, 4096 + i * 2048:4096 + (i + 1) * 2048].rearrange("p (c n) -> p c n", c=NCH)

    kvbuf = [C.sb("kvbuf%d" % i, [128, 2048], BF16) for i in range(3)]
    qT = [C.sb("qT%d" % i, [128, T], BF16) for i in range(2)]
    PT = [C.sb("PT%d" % i, [128, 512], BF16) for i in range(4)]
    vo = PT
    Sb = [C.sb("Sb%d" % i, [128, 512], F32) for i in range(4)]
    og = C.sb("og", [128, HG, T], BF16)
    rs = C.sb("rs", [128, 512], F32)
    wgt = C.sb("wgt", [128, 512], F32)
    BT = [C.sb("BT%d" % i, [128, 640], F32) for i in range(2)]
    cmask = C.sb("cmask", [128, 640], F32)
    kmask = C.sb("kmask", [128, 2, 8], F32)
    cst = C.sb("cst", [128, 4, 128], F32)
    ND = [C.sb("ND%d" % s, [128, UMAX[s], NH], F32) for s in range(2)]
    SC = [C.sb("SC%d" % s, [128, UMAX[s], NH], F32) for s in range(2)]
    kval = [C.sb("kval%d" % s, [128, UMAX[s]], F32) for s in range(2)]
    NDQ = BT[0]
    NDQd = BT[1]
    dexp = [C.sb("dexp%d" % i, [128, 4, 128], BF16) for i in range(2)]
    fb = C.sb("fb", [128, NH], F32)
    ones_col = C.sb("ones_col", [128, 1], F32)
    xsqc = [C.sb("xsqc%d" % i, [128, 128], F32) for i in range(2)]
    zs = [C.sb("zs%d" % i, [128, NH], F32) for i in range(2)]
    rt = [C.sb("rt%d" % i, [128, 1], F32) for i in range(2)]
    zero = kvbuf[0]

    ident_ap = cst.t[:, 0, :]
    tri_ap = cst.t[:, 1, :]
    trim_ap = cst.t[:, 2, :]
    J_ap = cst.t[:, 3, :]
    for (t_, d_) in ((cmask, cmask_d), (kmask, kmask_d), (fb, fb_d), (kval[0], kval_d[0]), (kval[1], kval_d[1])):
        P.dma("sp", lambda e, t_=t_, d_=d_: e.dma_start(out=t_.t[:], in_=d_), writes=[t_.r])
    P.dma("sp", lambda e: e.dma_start(out=cst.t[:], in_=cst_d.rearrange("k p n -> p k n")), writes=[cst.r])
    P.op("pool", lambda e: e.memset(ones_col.t[:], 1.0), writes=[ones_col.r])
    P.op("pool", lambda e: e.memset(zero.t[:], 0.0), writes=[zero.r])
    for st in range(1):
        segs = (0, 1, 2) if st == 0 else (2,)
        for i in range(4):
            for sgp in segs:
                for half in range(2):
                    r0 = (i * NSEGP + sgp) * 1024 + half * 512
                    P.dma("sp", lambda e, st=st, r0=r0: e.dma_start(
                        out=KL[st][r0:r0 + 512, :].rearrange("(p a) n -> p (a n)", p=128), in_=zero.t[:]),
                        reads=[zero.r], writes=[pad_r])
        for b in range(4):
            for sgp in segs:
                for half in range(2):
                    r0 = (b * NSEGP + sgp) * 1024 + half * 512
                    P.dma("sp", lambda e, st=st, r0=r0: e.dma_start(
                        out=VL[st][r0:r0 + 512, :].rearrange("(p a) n -> p (a n)", p=128), in_=zero.t[:]),
                        reads=[zero.r], writes=[pad_r])
    P.dma("sp", lambda e: e.dma_start(out=NL[0:3 * 512, :].rearrange("(p a) n -> p (a n)", p=128),
                                      in_=zero.t[:, 0:384].bitcast(F32) if False else zero.t[:, 0:768].bitcast(F32)),
          reads=[zero.r], writes=[pad_r])

    dyn = {}

    def pre_sp(e):
        jj = e.snap(e.partition_id() % 4, min_val=0, max_val=3)
        dyn["k"] = e.snap(jj * KSEGE, min_val=0, max_val=3 * KSEGE)

    def pre_pool(e):
        jj = e.snap(e.partition_id() % 4, min_val=0, max_val=3)
        dyn["v"] = e.snap(jj * KSEGE, min_val=0, max_val=3 * KSEGE)
        dyn["n"] = e.snap(jj * NSEGE, min_val=0, max_val=3 * NSEGE)

    P.pre = {"sp": pre_sp, "pool": pre_pool}

    def localize_k(i):
        P.dma("sp", lambda e, i=i: e.dma_start(
            out=bass.AP(Kloc, i * 8 * KSEGE, [[32768, 128], [1, 32768]]),
            in_=bass.AP(KL[0], dyn["k"] + i * NSEGP * KSEGE, [[32768, 128], [1, 32768]])),
            reads=[KL_r[0][i][0], KL_r[0][i][1], pad_r], writes=[Kloc_r[i]])

    def localize_v(p_):
        P.dma("pool", lambda e, p_=p_: e.dma_start(
            out=bass.AP(Vloc, p_ * 8 * KSEGE, [[32768, 128], [1, 32768]]),
            in_=bass.AP(VL[0], dyn["v"] + p_ * NSEGP * KSEGE, [[32768, 128], [1, 32768]])),
            reads=[VL_r[0][p_][0], VL_r[0][p_][1], pad_r], writes=[Vloc_r[p_]])

    def localize_n():
        P.dma("pool", lambda e: e.dma_start(
            out=bass.AP(Nloc, 0, [[1024, 128], [1, 1024]]),
            in_=bass.AP(NL, dyn["n"], [[1024, 128], [1, 1024]])),
            reads=[NL_r[0], NL_r[1], pad_r], writes=[Nloc_r])

    def kv_phase(st, gidx, wk_ap, wv_ap, gates):
        gn = phase_norm(C, gns_d[gidx], "g%d" % gidx, psG[0], psG[1])
        NT = T // 128
        if gates:
            fw_ap = Sb[0].t[:].rearrange("p (c h) -> p c h", c=NCH)
            gfw_ap = Sb[1].t[:].rearrange("p (c h) -> p c h", c=NCH)
            P.dma("sp", lambda e: e.dma_start(out=fw_ap, in_=fw_d), writes=[Sb[0].r])
            for c in range(NCH):
                P.op("pool", lambda e, c=c: e.tensor_scalar(out=gfw_ap[:, c, :], in0=fw_ap[:, c, :],
                                                            scalar1=gn.t[:, c:c + 1], scalar2=None, op0=ALU.mult),
                     reads=[Sb[0].r, gn.r], writes=[Sb[1].r])
            k = 0
            for tt in range(NT):
                pz = psS[tt % 2]
                pq = (psO, psSm)[tt % 2]
                tsl = slice(tt * 128, (tt + 1) * 128)
                for c in range(NCH):
                    sq = xsqc[k % 2]
                    k += 1
                    P.op("act", lambda e, c=c, sq=sq, tsl=tsl: e.activation(out=sq.t[:], in_=C.xT.t[:, c, tsl],
                                                                            func=AF.Square),
                         reads=[C.xr[c]], writes=[sq.r])
                    P.op("pe", lambda e, c=c, sq=sq, pq=pq: e.matmul(pq.t[:, 0:1], lhsT=sq.t[:], rhs=ones_col.t[:],
                                                                     start=(c == 0), stop=(c == NCH - 1)),
                         reads=[sq.r, ones_col.r], writes=[pq.r])
                    P.op("pe", lambda e, c=c, tsl=tsl, pz=pz: e.matmul(pz.t[:, 0:NH], lhsT=C.xT.t[:, c, tsl],
                                                                       rhs=gfw_ap[:, c, :],
                                                                       start=(c == 0), stop=(c == NCH - 1)),
                         reads=[C.xr[c], Sb[1].r], writes=[pz.r])
                r1 = rt[tt % 2]
                z = zs[tt % 2]
                P.op("act", lambda e, r1=r1, pq=pq: e.activation(out=r1.t[:], in_=pq.t[:, 0:1], func=AF.Sqrt, bias=EPS,
                                                                 scale=1.0 / D_MODEL), reads=[pq.r], writes=[r1.r])
                P.op("dve", lambda e, r1=r1: e.reciprocal(out=r1.t[:], in_=r1.t[:]), reads=[r1.r], writes=[r1.r])
                P.op("dve", lambda e, r1=r1, z=z, pz=pz: e.scalar_tensor_tensor(
                    out=z.t[:], in0=pz.t[:, 0:NH], scalar=r1.t[:, 0:1], in1=fb.t[:], op0=ALU.mult, op1=ALU.add),
                    reads=[pz.r, r1.r, fb.r], writes=[z.r])
                P.op("act", lambda e, z=z: e.activation(out=z.t[:], in_=z.t[:], func=AF.Exp, scale=-1.0),
                     reads=[z.r], writes=[z.r])
                P.op("act", lambda e, z=z: e.activation(out=z.t[:], in_=z.t[:], func=AF.Ln, bias=1.0, scale=1.0),
                     reads=[z.r], writes=[z.r])
                P.dma("sp", lambda e, z=z, tt=tt: e.dma_start(out=nlfo[tt * 128:(tt + 1) * 128, :], in_=z.t[:]),
                      reads=[z.r], writes=[nlfo_r])
            for sl in range(2):
                P.coll(("n", sl), lambda e, sl=sl: e.collective_compute(
                    "AllGather", ALU.bypass, replica_groups=GROUPS, ins=[nlfo[sl * 512:(sl + 1) * 512, :]],
                    outs=[NL[(3 + sl * 4) * 512:(3 + sl * 4 + 4) * 512, :]]), reads=[nlfo_r], writes=[NL_r[sl]])
        def load_wk(h):
            kb_ = kvbuf[h % 2]
            P.dma("pool", lambda e, kb_=kb_, h=h: e.dma_start(out=kb_.t[:].rearrange("p (c n) -> p c n", c=NCH),
                                                             in_=wk_ap(h)), writes=[kb_.r])
        load_wk(0)
        for h in range(NH):
            kb = kvbuf[h % 2]
            w_ap = kb.t[:].rearrange("p (c n) -> p c n", c=NCH)
            o = qT[h % 2]
            if h + 1 < NH:
                load_wk(h + 1)
            for hf in range(2):
                pp = psG[(2 * h + hf) % 4]
                for c in range(NCH):
                    P.op("pe", lambda e, c=c, pp=pp, w_ap=w_ap, hf=hf: e.matmul(
                        pp.t[:], lhsT=w_ap[:, c, :], rhs=C.hT.t[:, c, hf * 512:(hf + 1) * 512],
                        start=(c == 0), stop=(c == NCH - 1)), reads=[kb.r, C.hr[c]], writes=[pp.r])
                if hf == 0:
                    P.op("act", lambda e, o=o, pp=pp: e.activation(out=o.t[:, 0:512], in_=pp.t[:], func=AF.Copy),
                         reads=[pp.r], writes=[o.r])
                else:
                    P.op("dve", lambda e, o=o, pp=pp: e.tensor_copy(out=o.t[:, 512:1024], in_=pp.t[:]),
                         reads=[pp.r], writes=[o.r])
            for sl in range(2):
                r0 = ((h // 8) * 2 + sl) * 1024 + (h % 8) * 128
                P.dma("sp", lambda e, o=o, r0=r0, sl=sl: e.dma_start(out=kTo[st][r0:r0 + 128, :],
                                                                     in_=o.t[:, sl * 512:(sl + 1) * 512]),
                      reads=[o.r], writes=[kTo_r[st][h]])
            if h % 8 == 7:
                i = h // 8
                for sl in range(2):
                    P.coll(("k", i, sl), lambda e, i=i, sl=sl: e.collective_compute(
                        "AllGather", ALU.bypass, replica_groups=GROUPS,
                        ins=[kTo[st][(i * 2 + sl) * 1024:(i * 2 + sl + 1) * 1024, :]],
                        outs=[KL[st][(i * NSEGP + 3 + sl * 4) * 1024:(i * NSEGP + 3 + sl * 4 + 4) * 1024, :]]),
                        reads=kTo_r[st][i * 8:(i + 1) * 8], writes=[KL_r[st][i][sl]])
                if i >= 1:
                    localize_k(i - 1)
        k = 0
        def load_wv(b):
            P.dma("pool", lambda e, b=b: e.dma_start(out=wvb_ap(b % 2), in_=wv_ap(b)), writes=wvb_res(b % 2))
        load_wv(0)
        for b in range(8):
            w_ap = wvb_ap(b % 2)
            w_rs = wvb_res(b % 2)
            if b + 1 < 8:
                load_wv(b + 1)
            for tt in range(NT):
                pp = psG[k % 4]
                o = vo[k % 4]
                for c in range(NCH):
                    P.op("pe", lambda e, c=c, tt=tt, w_ap=w_ap, pp=pp: e.matmul(
                        pp.t[:], lhsT=C.hT.t[:, c, tt * 128:(tt + 1) * 128], rhs=w_ap[:, c, :],
                        start=(c == 0), stop=(c == NCH - 1)), reads=w_rs + [C.hr[c]], writes=[pp.r])
                if k % 2 == 0:
                    P.op("act", lambda e, o=o, pp=pp: e.activation(out=o.t[:], in_=pp.t[:], func=AF.Copy),
                         reads=[pp.r], writes=[o.r])
                else:
                    P.op("dve", lambda e, o=o, pp=pp: e.tensor_copy(out=o.t[:], in_=pp.t[:]), reads=[pp.r], writes=[o.r])
                r0 = (((b // 2) * 2 + tt // 4) * 2 + b % 2) * 512 + (tt % 4) * 128
                P.dma("act", lambda e, o=o, r0=r0: e.dma_start(out=Vo[st][r0:r0 + 128, :], in_=o.t[:]),
                      reads=[o.r], writes=[Vo_r[st][b]])
                k += 1
            if b % 2 == 1:
                p_ = b // 2
                for sl in range(2):
                    P.coll(("v", p_, sl), lambda e, p_=p_, sl=sl: e.collective_compute(
                        "AllGather", ALU.bypass, replica_groups=GROUPS,
                        ins=[Vo[st][(p_ * 2 + sl) * 1024:(p_ * 2 + sl + 1) * 1024, :]],
                        outs=[VL[st][(p_ * NSEGP + 3 + sl * 4) * 1024:(p_ * NSEGP + 3 + sl * 4 + 4) * 1024, :]]),
                        reads=[Vo_r[st][b - 1], Vo_r[st][b]], writes=[VL_r[st][p_][sl]])
                if p_ == 0:
                    localize_k(3)
                if p_ >= 1:
                    localize_v(p_ - 1)
        localize_v(3)
        if gates:
            localize_n()

    def decay_prep():
        nlf_t = sg[0].t[:].rearrange("p (u h) -> p u h", u=32)
        tot_t = sg[1].t[:].rearrange("p (u h) -> p u h", u=32)
        for s in range(2):
            U = UMAX[s]
            for a in range(U // 4):
                off = (3 + 4 * s - a) * NSEGE
                P.dma("sp", lambda e, a=a, off=off: e.dma_start(
                    out=nlf_t[:, 4 * a:4 * a + 4, :],
                    in_=bass.AP(Nloc, off, [[NH, 128], [128 * NH, 4], [1, NH]])),
                    reads=[Nloc_r], writes=[sg[0].r])
            nflat = sg[0].t[:, 0:U * NH]
            ndflat = ND[s].t[:].rearrange("p u h -> p (u h)")
            totflat = sg[1].t[:, 0:U * NH]
            for j in range(U * NH // 512):
                sl_ = slice(j * 512, (j + 1) * 512)
                pa = psG[(2 * j) % 4]
                pb = psG[(2 * j + 1) % 4]
                P.op("pe", lambda e, pa=pa, sl_=sl_, nflat=nflat: e.matmul(pa.t[:], lhsT=tri_ap, rhs=nflat[:, sl_],
                                                                           start=True, stop=True),
                     reads=[cst.r, sg[0].r], writes=[pa.r])
                P.op("pe", lambda e, pb=pb, sl_=sl_, nflat=nflat: e.matmul(pb.t[:], lhsT=C.ones_f.t[:], rhs=nflat[:, sl_],
                                                                           start=True, stop=True),
                     reads=[C.ones_f.r, sg[0].r], writes=[pb.r])
                P.op("act", lambda e, pa=pa, sl_=sl_, ndflat=ndflat: e.activation(out=ndflat[:, sl_], in_=pa.t[:],
                                                                                  func=AF.Copy),
                     reads=[pa.r], writes=[ND[s].r])
                P.op("dve", lambda e, pb=pb, sl_=sl_, totflat=totflat: e.tensor_copy(out=totflat[:, sl_], in_=pb.t[:]),
                     reads=[pb.r], writes=[sg[1].r])

            def uof(v):
                return 4 * (v // 4) + 3 - (v % 4)
            for v in range(1, U):
                u1, u0 = uof(v), uof(v - 1)
                P.op("dve", lambda e, s=s, u1=u1, u0=u0: e.tensor_tensor(out=ND[s].t[:, u1, :], in0=ND[s].t[:, u1, :],
                                                                         in1=tot_t[:, u0, :], op=ALU.add),
                     reads=[sg[1].r, ND[s].r], writes=[ND[s].r])
                if v < U - 1:
                    P.op("dve", lambda e, u1=u1, u0=u0: e.tensor_tensor(out=tot_t[:, u1, :], in0=tot_t[:, u1, :],
                                                                        in1=tot_t[:, u0, :], op=ALU.add),
                         reads=[sg[1].r], writes=[sg[1].r])
            P.op("dve", lambda e, s=s, U=U: e.tensor_tensor(
                out=SC[s].t[:], in0=kval[s].t[:].unsqueeze(2).to_broadcast([128, U, NH]), in1=ND[s].t[:],
                op=ALU.subtract), reads=[kval[s].r, ND[s].r], writes=[SC[s].r])

    cnt = {"K": 0, "U": 0, "A": 0}

    def mix_phase(isb, st, gidx, wq_ap, wg_ap, wo_ap, rb_layer):
        NU = UMAX if isb else (8, 8)
        if isb:
            phase_norm(C, gns_d[gidx], "g%d" % gidx, psG[0], psG[1])
        if isb and gidx == 3:
            decay_prep()
        for grp in range(NH // HG):
            P.dma("pool", lambda e, grp=grp: e.dma_start(out=wob_ap, in_=wo_ap(grp)), writes=[WB0.r])
            for hh in range(HG):
                h = grp * HG + hh
                i8 = h // 8
                b4 = h // 4
                wq_, wg_ = wqb_ap(h % 2), wgb_ap(h % 2)

                def load_qg(h2):
                    P.dma("pool", lambda e, h2=h2: e.dma_start(out=wqb_ap(h2 % 2), in_=wq_ap(h2)), writes=[wq_r[h2 % 2]])
                    P.dma("pool", lambda e, h2=h2: e.dma_start(out=wgb_ap(h2 % 2), in_=wg_ap(h2)), writes=[wg_r[h2 % 2]])
                if h == 0:
                    load_qg(0)
                if h + 1 < NH:
                    load_qg(h + 1)
                q_ = qT[h % 2]
                g_ = sg[h % 2]
                for hf in range(2):
                    pp = psG2[hf]
                    for c in range(NCH):
                        P.op("pe", lambda e, c=c, pp=pp, wq_=wq_, hf=hf: e.matmul(
                            pp.t[:], lhsT=wq_[:, c, :], rhs=C.hT.t[:, c, hf * 512:(hf + 1) * 512],
                            start=(c == 0), stop=(c == NCH - 1)), reads=[wq_r[h % 2], C.hr[c]], writes=[pp.r])
                    P.op("dve", lambda e, q_=q_, pp=pp, hf=hf: e.tensor_scalar(
                        out=q_.t[:, hf * 512:(hf + 1) * 512], in0=pp.t[:], scalar1=SCALE, scalar2=None, op0=ALU.mult),
                        reads=[pp.r], writes=[q_.r])
                for hf in range(2):
                    pp = psG2[hf]
                    for c in range(NCH):
                        P.op("pe", lambda e, c=c, pp=pp, wg_=wg_, hf=hf: e.matmul(
                            pp.t[:], lhsT=wg_[:, c, :], rhs=C.hT.t[:, c, hf * 512:(hf + 1) * 512],
                            start=(c == 0), stop=(c == NCH - 1)), reads=[wg_r[h % 2], C.hr[c]], writes=[pp.r])
                    P.op("act", lambda e, g_=g_, pp=pp, hf=hf: e.activation(
                        out=g_.t[:, hf * 512:(hf + 1) * 512], in_=pp.t[:], func=AF.Silu),
                        reads=[pp.r], writes=[g_.r])
                def prep_bt(h2):
                    bt = BT[h2 % 2]
                    src = bass.AP(rbx_d.tensor, (rb_layer * NH + h2) * 767, [[1, 128], [1, 640]])
                    P.dma("sp", lambda e, bt=bt, src=src: e.dma_start(out=bt.t[:], in_=src), writes=[bt.r])
                    pj = (psG2[0], psG2[1])
                    P.op("pe", lambda e, bt=bt: e.matmul(pj[0].t[:], lhsT=J_ap, rhs=bt.t[:, 0:512], start=True, stop=True),
                         reads=[cst.r, bt.r], writes=[pj[0].r])
                    P.op("pe", lambda e, bt=bt: e.matmul(pj[1].t[:, 0:128], lhsT=J_ap, rhs=bt.t[:, 512:640], start=True,
                                                         stop=True), reads=[cst.r, bt.r], writes=[pj[1].r])
                    P.op("dve", lambda e, bt=bt: e.tensor_tensor(out=bt.t[:, 0:512], in0=pj[0].t[:], in1=cmask.t[:, 0:512],
                                                                 op=ALU.add), reads=[pj[0].r, cmask.r], writes=[bt.r])
                    P.op("dve", lambda e, bt=bt: e.tensor_tensor(out=bt.t[:, 512:640], in0=pj[1].t[:, 0:128],
                                                                 in1=cmask.t[:, 512:640], op=ALU.add),
                         reads=[pj[1].r, cmask.r], writes=[bt.r])

                def prep_ndq(g2):
                    h2, s2_ = g2 // 2, g2 % 2
                    dx = dexp[g2 % 2]
                    for i in range(4):
                        P.op("pool", lambda e, dx=dx, i=i, s2_=s2_, h2=h2: e.tensor_tensor(
                            out=dx.t[:, i, :], in0=ident_ap, in1=ND[s2_].t[:, i, h2:h2 + 1].to_broadcast([128, 128]),
                            op=ALU.mult), reads=[cst.r, ND[s2_].r], writes=[dx.r])
                    pq = psG2[g2 % 2]
                    nq = BT[g2 % 2]
                    P.op("pe", lambda e, dx=dx, pq=pq: e.matmul(pq.t[:], lhsT=C.ones_b.t[:],
                                                                rhs=dx.t[:].rearrange("p i q -> p (i q)"),
                                                                start=True, stop=True),
                         reads=[C.ones_b.r, dx.r], writes=[pq.r])
                    P.op("act", lambda e, pq=pq, nq=nq: e.activation(out=nq.t[:, 0:512], in_=pq.t[:], func=AF.Copy),
                         reads=[pq.r], writes=[nq.r])

                if not isb:
                    if h == 0:
                        prep_bt(0)
                    if h + 1 < NH:
                        prep_bt(h + 1)
                for s in range(2):
                    U = NU[s]
                    if isb:
                        gi_ = 2 * h + s
                        if gi_ == 0:
                            prep_ndq(0)
                        if gi_ + 1 < 2 * NH:
                            prep_ndq(gi_ + 1)
                        NDQg = BT[gi_ % 2]
                    first = True
                    units = []
                    chunk_loads = []
                    psO = psOs[cnt["A"] % 2]
                    psSm = psSms[cnt["A"] % 2]
                    cnt["A"] += 1
                    for ch in range(U // 8):
                        kb = kvbuf[cnt["K"] % 3]
                        cnt["K"] += 1
                        kx_ap = kb.t[:, 0:1024]
                        vx_ap = kb.t[:, 1024:2048].rearrange("p (u d) -> p u d", u=8)
                        kdeps = [Kloc_r[i8]]
                        vdeps = [Vloc_r[b4 // 2]]
                        def load_chunk(kb=kb, kx_ap=kx_ap, vx_ap=vx_ap, ch=ch, kdeps=kdeps, vdeps=vdeps, s=s, h=h,
                                       i8=i8, b4=b4):
                            for s2 in range(2):
                                if isb:
                                    sgp = 3 + 4 * s - (2 * ch + s2)
                                else:
                                    sgp = 3 + 4 * s - 1 + s2
                                koff = ((i8 * 8 + sgp) * 1024 + (h % 8) * 128) * 512
                                voff = ((((b4 // 2) * 8 + sgp) * 2 + b4 % 2) * 512) * 512 + (h % 4) * 128
                                P.dma("sp", lambda e, s2=s2, koff=koff: e.dma_start(
                                    out=kx_ap[:, s2 * 512:(s2 + 1) * 512],
                                    in_=bass.AP(Kloc, koff, [[512, 128], [1, 512]])),
                                    reads=kdeps, writes=[kb.r])
                                P.dma("sp", lambda e, s2=s2, voff=voff: e.dma_start(
                                    out=vx_ap[:, s2 * 4:(s2 + 1) * 4, :],
                                    in_=bass.AP(Vloc, voff, [[512, 128], [128 * 512, 4], [1, 128]])),
                                    reads=vdeps, writes=[kb.r])
                        chunk_loads.append(load_chunk)
                        if ch == 0:
                            order = [3, 4, 0, 1, 2, 5, 6, 7] if not isb else list(range(8))
                        else:
                            order = list(range(8))
                        for ul in order:
                            u = ch * 8 + ul
                            diag = False
                            if not isb:
                                ilo, ihi = max(0, u - 4), min(3, u)
                                rlo = ilo + 4 - u
                                ncol = (ihi - ilo + 1) * 128
                                adds = [(0, ncol, BT[h % 2].t[:, rlo * 128:rlo * 128 + ncol], BT[h % 2].r)]
                                sc_ap = kmask.t[:, s, u:u + 1]
                                ares = [kmask.r]
                            else:
                                if u < 4:
                                    ilo, ihi = u, 3
                                    ncol = (ihi - ilo + 1) * 128
                                    adds = [(0, ncol, NDQg.t[:, ilo * 128:512], NDQg.r)]
                                    diag = True
                                else:
                                    ilo, ihi = 0, 3
                                    ncol = 512
                                    adds = [(0, 512, NDQg.t[:, 0:512], NDQg.r)]
                                sc_ap = SC[s].t[:, u, h:h + 1]
                                ares = [SC[s].r]
                            c0 = ilo * 128
                            n_ = cnt["U"]
                            cnt["U"] += 1
                            units.append((kx_ap[:, ul * 128:(ul + 1) * 128], vx_ap[:, ul, :],
                                          q_.t[:, s * 512 + c0:s * 512 + c0 + ncol], ncol, sc_ap, adds, c0, first,
                                          kb.r, q_.r, ares, psS[n_ % 3], Sb[n_ % 4], PT[n_ % 4], diag))
                            first = False
                    LA = 2
                    for idx in range(len(units) + LA):
                        if idx < len(units) and idx % 8 == 0:
                            ch_ = idx // 8
                            if ch_ == 0:
                                chunk_loads[0]()
                            if ch_ + 1 < len(chunk_loads):
                                chunk_loads[ch_ + 1]()
                        if idx < len(units):
                            (kt_, v_, qa_, ncol, sc_ap, adds, c0, fst, kr_, qr_, ares, pS_, Sb_, PT_, dg_) = units[idx]
                            P.op("pe", lambda e, pS_=pS_, ncol=ncol, kt_=kt_, qa_=qa_: e.matmul(
                                pS_.t[:, 0:ncol], lhsT=kt_, rhs=qa_, start=True, stop=True),
                                reads=[kr_, qr_], writes=[pS_.r])
                        if idx >= LA:
                            (kt_, v_, qa_, ncol, sc_ap, adds, c0, fst, kr_, qr_, ares, pS_, Sb_, PT_, dg_) = units[idx - LA]
                            for (o_, n2, ap_, r_) in adds:
                                P.op("dve", lambda e, o_=o_, n2=n2, ap_=ap_, pS_=pS_, Sb_=Sb_, sc_ap=sc_ap: e.scalar_tensor_tensor(
                                    out=Sb_.t[:, o_:o_ + n2], in0=pS_.t[:, o_:o_ + n2], scalar=sc_ap, in1=ap_,
                                    op0=ALU.add, op1=ALU.add), reads=[pS_.r, r_] + list(ares), writes=[Sb_.r])
                            if dg_:
                                P.op("dve", lambda e, Sb_=Sb_: e.tensor_tensor(out=Sb_.t[:, 0:128], in0=Sb_.t[:, 0:128],
                                                                               in1=trim_ap, op=ALU.add),
                                     reads=[Sb_.r, cst.r], writes=[Sb_.r])
                            P.op("act", lambda e, PT_=PT_, Sb_=Sb_, ncol=ncol: e.activation(
                                out=PT_.t[:, 0:ncol], in_=Sb_.t[:, 0:ncol], func=AF.Exp), reads=[Sb_.r], writes=[PT_.r])
                            P.op("pe", lambda e, v_=v_, PT_=PT_, ncol=ncol, c0=c0, fst=fst, psO=psO: e.matmul(
                                psO.t[:, c0:c0 + ncol], lhsT=v_, rhs=PT_.t[:, 0:ncol], start=fst, stop=False,
                                skip_group_check=True), reads=[kr_, PT_.r], writes=[psO.r])
                            P.op("pe", lambda e, PT_=PT_, ncol=ncol, c0=c0, fst=fst, psSm=psSm: e.matmul(
                                psSm.t[:, c0:c0 + ncol], lhsT=C.ones_b.t[:], rhs=PT_.t[:, 0:ncol], start=fst, stop=False,
                                skip_group_check=True), reads=[C.ones_b.r, PT_.r], writes=[psSm.r])
                    P.op("dve", lambda e, psSm=psSm: e.reciprocal(out=rs.t[:], in_=psSm.t[:]), reads=[psSm.r], writes=[rs.r])
                    P.op("pool", lambda e, g_=g_, s=s: e.tensor_tensor(out=wgt.t[:], in0=rs.t[:],
                                                                       in1=g_.t[:, s * 512:(s + 1) * 512], op=ALU.mult),
                         reads=[rs.r, g_.r], writes=[wgt.r])
                    P.op("dve", lambda e, hh=hh, s=s, psO=psO: e.tensor_tensor(out=og.t[:, hh, s * 512:(s + 1) * 512],
                                                                      in0=psO.t[:], in1=wgt.t[:], op=ALU.mult),
                         reads=[psO.r, wgt.r], writes=[og.r])
            k = 0
            for c in range(NCH):
                for hf in range(2):
                    pp = psG2[k % 2]
                    k += 1
                    for hh in range(HG):
                        P.op("pe", lambda e, pp=pp, hh=hh, c=c, hf=hf: e.matmul(
                            pp.t[:], lhsT=wob_ap[:, hh, c * 128:(c + 1) * 128], rhs=og.t[:, hh, hf * 512:(hf + 1) * 512],
                            start=(hh == 0), stop=(hh == HG - 1)), reads=[WB0.r, og.r], writes=[pp.r])
                    P.op("dve", lambda e, pp=pp, c=c, hf=hf: e.tensor_tensor(
                        out=C.xT.t[:, c, hf * 512:(hf + 1) * 512], in0=pp.t[:], in1=C.xT.t[:, c, hf * 512:(hf + 1) * 512],
                        op=ALU.add), reads=[pp.r, C.xr[c]], writes=[C.xr[c]])

    for l in range(2):
        kv_phase(0, l, lambda h, l=l: a_wk_d[l, h], lambda b, l=l: a_wv_d[l, b], False)
        mix_phase(False, 0, l, lambda h, l=l: a_wq_d[l, h], lambda h, l=l: a_wg_d[l, h], lambda g, l=l: a_wo_d[l, g], l)
    kv_phase(0, 2, lambda h: s_wk_d[h], lambda b: s_wv_d[b], True)
    for l in range(2):
        mix_phase(True, 0, 3 + l, lambda h, l=l: b_wq_d[l, h], lambda h, l=l: b_wg_d[l, h], lambda g, l=l: b_wo_d[l, g], 0)

    gf = C.sb("gfin", [128, NCH], F32)
    P.dma("sp", lambda e: e.dma_start(out=gf.t[:], in_=gns_d[5]), writes=[gf.r])
    pss = (psG[0], psG[1])
    for c in range(NCH):
        sq = sg[c % 2]
        P.op("act", lambda e, c=c, sq=sq: e.activation(out=sq.t[:], in_=C.xT.t[:, c, :], func=AF.Square),
             reads=[C.xr[c]], writes=[sq.r])
        for hf in range(2):
            P.op("pe", lambda e, c=c, sq=sq, hf=hf: e.matmul(
                pss[hf].t[:], lhsT=C.ones_f.t[:], rhs=sq.t[:, hf * 512:(hf + 1) * 512],
                start=(c == 0), stop=(c == NCH - 1)), reads=[sq.r, C.ones_f.r], writes=[pss[hf].r])
    for hf in range(2):
        sl = slice(hf * 512, (hf + 1) * 512)
        P.op("act", lambda e, hf=hf, sl=sl: e.activation(
            out=C.rstd.t[:, sl], in_=pss[hf].t[:], func=AF.Sqrt, bias=EPS, scale=1.0 / D_MODEL),
            reads=[pss[hf].r], writes=[C.rstd.r])
    P.op("dve", lambda e: e.reciprocal(out=C.rstd.t[:], in_=C.rstd.t[:]), reads=[C.rstd.r], writes=[C.rstd.r])
    for c in range(NCH):
        yb = sg[c % 2]
        P.op("dve", lambda e, c=c, yb=yb: e.scalar_tensor_tensor(
            out=yb.t[:], in0=C.xT.t[:, c, :], scalar=gf.t[:, c:c + 1], in1=C.rstd.t[:],
            op0=ALU.mult, op1=ALU.mult), reads=[C.xr[c], gf.r, C.rstd.r], writes=[yb.r])
        C.outs.append(P.dma("sp", lambda e, c=c, yb=yb: e.dma_start(out=yo_d[:, c, :], in_=yb.t[:]), reads=[yb.r]))
    return C.finish()


def kernel(x, a_norm, a_w_in, a_rel_bias, a_w_out, kv_norm, kv_w, f_w, f_b, b_norm, b_w_in, b_w_out, final_norm):
    f = lambda a: np.asarray(a, np.float32)
    x = f(x)
    a_w_in, a_w_out, kv_w, b_w_in, b_w_out = f(a_w_in), f(a_w_out), f(kv_w), f(b_w_in), f(b_w_out)
    gns = np.stack([tile_vec(f(a_norm)[0]), tile_vec(f(a_norm)[1]), tile_vec(f(kv_norm)), tile_vec(f(b_norm)[0]),
                    tile_vec(f(b_norm)[1]), tile_vec(f(final_norm))])
    shared = {
        "gns": gns,
        "a_wq": np.stack([tile_w_cols(a_w_in[l], 0, D_INNER, 128) for l in range(2)]),
        "a_wk": np.stack([tile_w_cols(a_w_in[l], D_INNER, D_INNER, 128) for l in range(2)]),
        "a_wv": np.stack([tile_w_cols(a_w_in[l], 2 * D_INNER, D_INNER, 512) for l in range(2)]),
        "a_wg": np.stack([tile_w_cols(a_w_in[l], 3 * D_INNER, D_INNER, 128) for l in range(2)]),
        "a_wo": np.stack([_tile_wo(a_w_out[l]) for l in range(2)]),
        "s_wk": tile_w_cols(kv_w, 0, D_INNER, 128),
        "s_wv": tile_w_cols(kv_w, D_INNER, D_INNER, 512),
        "b_wq": np.stack([tile_w_cols(b_w_in[l], 0, D_INNER, 128) for l in range(2)]),
        "b_wg": np.stack([tile_w_cols(b_w_in[l], D_INNER, D_INNER, 128) for l in range(2)]),
        "b_wo": np.stack([_tile_wo(b_w_out[l]) for l in range(2)]),
        "fw": np.ascontiguousarray(f(f_w).reshape(NCH, 128, NH).transpose(1, 0, 2)),
        "fb": np.ascontiguousarray(np.broadcast_to(f(f_b)[None, :], (128, NH))),
    }
    idx = np.clip(np.arange(767) - 127, -256, 256) + 256
    shared["rbx"] = np.ascontiguousarray(f(a_rel_bias)[:, :, idx])
    cmask = np.zeros((128, 5, 128), np.float32)
    cmask[64:, 0, :64] = NEG
    cmask[:64, 4, 64:] = NEG
    shared["cmask"] = cmask.reshape(128, 640)
    kl = np.arange(128)
    cst = np.zeros((4, 128, 128), np.float32)
    cst[0] = np.eye(128, dtype=np.float32)
    cst[1] = (kl[:, None] > kl[None, :]).astype(np.float32)
    cst[2] = np.where(kl[:, None] > kl[None, :], NEG, 0.0)
    cst[3] = np.eye(128, dtype=np.float32)[::-1]
    shared["cst"] = cst
    in_maps = []
    for j in range(8):
        jj = j % 4
        m = dict(shared)
        m["xT"] = to_xT(x[j // 4][core_tokens(j)])
        kmask = np.zeros((128, 2, 8), np.float32)
        if jj == 0:
            kmask[:, 0, :4] = NEG
        m["kmask"] = kmask
        for s in range(2):
            kv = np.full((128, UMAX[s]), NEG, np.float32)
            for u in range(UMAX[s]):
                if jj + 4 * s - u // 4 >= 0:
                    kv[:, u] = 0.0
            m["kval%d" % s] = kv
        in_maps.append(m)
    if "fused" not in _NC_CACHE:
        _NC_CACHE["fused"] = build_fused()
    res = run_bass_kernel_spmd(_NC_CACHE["fused"], in_maps, core_ids=list(range(8)))
    out = np.zeros((BATCH, SEQ, D_MODEL), np.float32)
    for j in range(8):
        out[j // 4][core_tokens(j)] = from_xT(res.results[j]["yo"])
    return out
```

```python
import numpy as np
from contextlib import ExitStack
import ml_dtypes
import concourse.bass as bass
import concourse.mybir as mybir
from concourse.bass_utils import run_bass_kernel_spmd

F32 = mybir.dt.float32
BF16 = mybir.dt.bfloat16
AF = mybir.ActivationFunctionType
ALU = mybir.AluOpType
NPBF = ml_dtypes.bfloat16

D_MODEL = 2048
NCH = 16
D_INNER = 4096
NH = 32
DH = 128
SEQ = 4096
BATCH = 2
T = 1024
SEG = 512
NSEG = 2
EPS = 1e-6
NEG = -30000.0
SCALE = DH ** -0.5
HG = 4
UMAX = (16, 32)


class Res:
    __slots__ = ("name", "w", "rs")

    def __init__(self, name=""):
        self.name = name
        self.w = None
        self.rs = {}


class Op:
    __slots__ = ("eng", "fn", "deps", "dma", "sig", "need", "n", "cc")

    def __init__(self, eng, fn, dma, cc=None):
        self.eng = eng
        self.fn = fn
        self.dma = dma
        self.deps = []
        self.sig = None
        self.need = dma
        self.n = 0
        self.cc = cc


class Prog:
    ENGS = ("pe", "act", "dve", "pool", "sp")
    NDSEM = 24
    EPOCH = 30000

    def __init__(self, nc):
        self.nc = nc
        self.ops = {e: [] for e in self.ENGS}
        self.ndma = {e: 0 for e in self.ENGS}
        self.count = 0
        self.ccs = {}

    def _track(self, o, reads, writes):
        deps = {}
        for r in reads:
            if r.w is not None:
                deps[id(r.w)] = r.w
        for r in writes:
            if r.w is not None:
                deps[id(r.w)] = r.w
            for x in r.rs.values():
                deps[id(x)] = x
        for d in deps.values():
            if d is o:
                continue
            if (not d.dma) and (not o.dma) and d.eng == "pe" and o.eng == "pe":
                continue
            d.need = True
            o.deps.append(d)
        for r in reads:
            key = ("dma", self.count) if o.dma else o.eng
            r.rs[key] = o
        for r in writes:
            r.w = o
            r.rs = {}
        self.count += 1

    def op(self, eng, fn, reads=(), writes=()):
        o = Op(eng, fn, False)
        self._track(o, reads, writes)
        self.ops[eng].append(o)
        return o

    def dma(self, q, fn, reads=(), writes=()):
        o = Op(q, fn, True)
        self._track(o, reads, writes)
        self.ops[q].append(o)
        return o

    def coll(self, key, fn, reads=(), writes=()):
        o = Op("pool", fn, True, cc=key)
        self._track(o, reads, writes)
        self.ops["pool"].append(o)
        return o

    def emit(self, es, final_deps):
        nc = self.nc
        fin = Op("sp", None, False)
        for d in final_deps:
            d.need = True
            fin.deps.append(d)
        self.ops["sp"].append(fin)
        sems = {}
        for e in self.ENGS:
            cnt = 0
            nd = 0
            esems = []
            dsems = []
            for o in self.ops[e]:
                if o.cc is not None:
                    if o.cc not in self.ccs:
                        self.ccs[o.cc] = [es.enter_context(nc.semaphore("cc_%s" % str(o.cc))), 0]
                    self.ccs[o.cc][1] += 1
                    o.sig = (self.ccs[o.cc][0], self.ccs[o.cc][1])
                elif o.dma:
                    k = nd % self.NDSEM
                    if k >= len(dsems):
                        dsems.append(es.enter_context(nc.semaphore("d_%s_%d" % (e, k))))
                    o.sig = (dsems[k], 16 * (nd // self.NDSEM + 1))
                    o.n = nd
                    nd += 1
                elif o.need:
                    ep = cnt // self.EPOCH
                    if ep >= len(esems):
                        esems.append(es.enter_context(nc.semaphore("c_%s_%d" % (e, ep))))
                    o.sig = (esems[ep], cnt % self.EPOCH + 1)
                    cnt += 1
            sems[e] = (esems, dsems)
        engobj = {"pe": nc.tensor, "act": nc.scalar, "dve": nc.vector, "pool": nc.gpsimd, "sp": nc.sync}
        block = es.enter_context(nc.Block())

        def make(e):
            def body(eng):
                waited = {}
                pre = getattr(self, "pre", {}).get(e)
                if pre is not None:
                    pre(eng)
                for o in self.ops[e]:
                    ws = []
                    for d in o.deps:
                        ws.append(d.sig)
                    if o.dma and o.cc is None and o.n >= self.NDSEM:
                        ws.append((o.sig[0], o.sig[1] - 16))
                    for (s, v) in ws:
                        if waited.get(id(s), 0) < v:
                            waited[id(s)] = v
                            eng.wait_ge(s, v)
                    if o.fn is None:
                        continue
                    ins = o.fn(eng)
                    if o.cc is not None:
                        ins.then_inc(o.sig[0], 1)
                    elif o.dma:
                        ins.then_inc(o.sig[0], 16)
                    elif o.sig is not None:
                        ins.then_inc(o.sig[0], 1)
            return body

        block.tensor(make("pe"))
        block.scalar(make("act"))
        block.vector(make("dve"))
        block.gpsimd(make("pool"))
        block.sync(make("sp"))


class Tl:
    __slots__ = ("t", "r")

    def __init__(self, t, name):
        self.t = t
        self.r = Res(name)


class Ctx:
    def __init__(self):
        self.nc = bass.Bass("TRN2", target_bir_lowering=False)
        self.P = Prog(self.nc)
        self.es = ExitStack()
        self.outs = []
        self.nps = 0

    def dram(self, name, shape, dt, kind):
        return self.nc.dram_tensor(name, list(shape), dt, kind=kind).ap()

    def sb(self, name, shape, dt):
        return Tl(self.es.enter_context(self.nc.sbuf_tensor("s_" + name, list(shape), dt)), name)

    def ps(self, name, dt=F32):
        n = 512 if dt == F32 else 1024
        return Tl(self.es.enter_context(self.nc.psum_tensor("p_" + name, [128, n], dt)), name)

    def finish(self):
        self.P.emit(self.es, self.outs)
        self.es.close()
        return self.nc


def phase_consts(C):
    P = C.P
    C.ones_f = C.sb("ones_f", [128, 128], F32)
    C.ones_b = C.sb("ones_b", [128, 128], BF16)
    P.op("pool", lambda e: e.memset(C.ones_f.t[:], 1.0), writes=[C.ones_f.r])
    P.op("pool", lambda e: e.memset(C.ones_b.t[:], 1.0), writes=[C.ones_b.r])


def phase_load_x(C, xT_d):
    P = C.P
    C.xT = C.sb("xT", [128, NCH, T], F32)
    C.xr = [Res("x%d" % c) for c in range(NCH)]
    for c0 in range(0, NCH, 4):
        P.dma("sp", lambda e, c0=c0: e.dma_start(out=C.xT.t[:, c0:c0 + 4, :], in_=xT_d[:, c0:c0 + 4, :]),
              writes=C.xr[c0:c0 + 4])


def phase_norm(C, gn_d, tag, psA, psB, want_tok_rstd=False):
    P = C.P
    if not hasattr(C, "hT"):
        C.hT = C.sb("hT", [128, NCH, T], BF16)
        C.hr = [Res("h%d" % c) for c in range(NCH)]
        C.xsq = [C.sb("xsq%d" % i, [128, T], F32) for i in range(2)]
        C.rstd = C.sb("rstd", [128, T], F32)
    gn = C.sb("gn_" + tag, [128, NCH], F32)
    P.dma("sp", lambda e: e.dma_start(out=gn.t[:], in_=gn_d), writes=[gn.r])
    pss = (psA, psB)
    for c in range(NCH):
        sq = C.xsq[c % 2]
        P.op("act", lambda e, c=c, sq=sq: e.activation(out=sq.t[:], in_=C.xT.t[:, c, :], func=AF.Square),
             reads=[C.xr[c]], writes=[sq.r])
        for hf in range(2):
            P.op("pe", lambda e, c=c, sq=sq, hf=hf: e.matmul(
                pss[hf].t[:], lhsT=C.ones_f.t[:], rhs=sq.t[:, hf * 512:(hf + 1) * 512],
                start=(c == 0), stop=(c == NCH - 1)),
                reads=[sq.r, C.ones_f.r], writes=[pss[hf].r])
    for hf in range(2):
        sl = slice(hf * 512, (hf + 1) * 512)
        P.op("act", lambda e, hf=hf, sl=sl: e.activation(
            out=C.rstd.t[:, sl], in_=pss[hf].t[:], func=AF.Sqrt, bias=EPS, scale=1.0 / D_MODEL),
            reads=[pss[hf].r], writes=[C.rstd.r])
    P.op("dve", lambda e: e.reciprocal(out=C.rstd.t[:], in_=C.rstd.t[:]), reads=[C.rstd.r], writes=[C.rstd.r])
    for c in range(NCH):
        P.op("dve", lambda e, c=c: e.scalar_tensor_tensor(
            out=C.hT.t[:, c, :], in0=C.xT.t[:, c, :], scalar=gn.t[:, c:c + 1], in1=C.rstd.t[:],
            op0=ALU.mult, op1=ALU.mult), reads=[C.xr[c], gn.r, C.rstd.r], writes=[C.hr[c]])
    return gn


def proj_fm(C, wtile, ps, hf, extra_reads=()):
    P = C.P
    for c in range(NCH):
        P.op("pe", lambda e, c=c: e.matmul(ps.t[:], lhsT=wtile.t[:, c, :], rhs=C.hT.t[:, c, hf * 512:(hf + 1) * 512],
                                           start=(c == 0), stop=(c == NCH - 1)),
             reads=[wtile.r, C.hr[c]] + list(extra_reads), writes=[ps.r])


def build_kv():
    C = Ctx()
    P = C.P
    xT_d = C.dram("xT", [128, NCH, T], F32, "ExternalInput")
    gn_d = C.dram("gn", [128, NCH], F32, "ExternalInput")
    wk_d = C.dram("wk", [NH, 128, NCH, 128], F32, "ExternalInput")
    wv_d = C.dram("wv", [8, 128, NCH, 512], F32, "ExternalInput")
    fw_d = C.dram("fw", [128, NCH, NH], F32, "ExternalInput")
    fb_d = C.dram("fb", [128, NH], F32, "ExternalInput")
    kT_o = C.dram("kT", [NH, 128, T], BF16, "ExternalOutput")
    V_o = C.dram("V", [T // 128, 128, D_INNER], BF16, "ExternalOutput")
    nlf_o = C.dram("nlf", [T // 128, 128, NH], F32, "ExternalOutput")

    ps = [C.ps("ps%d" % i) for i in range(8)]
    phase_consts(C)
    phase_load_x(C, xT_d)
    gn = phase_norm(C, gn_d, "kv", ps[0], ps[1])

    fw = C.sb("fw", [128, NCH, NH], F32)
    fb = C.sb("fb", [128, NH], F32)
    P.dma("sp", lambda e: e.dma_start(out=fw.t[:], in_=fw_d), writes=[fw.r])
    P.dma("sp", lambda e: e.dma_start(out=fb.t[:], in_=fb_d), writes=[fb.r])
    gfw = C.sb("gfw", [128, NCH, NH], F32)
    for c in range(NCH):
        P.op("pool", lambda e, c=c: e.tensor_scalar(out=gfw.t[:, c, :], in0=fw.t[:, c, :], scalar1=gn.t[:, c:c + 1],
                                                    scalar2=None, op0=ALU.mult),
             reads=[fw.r, gn.r], writes=[gfw.r])
    ones_col = C.sb("ones_col", [128, 1], F32)
    P.op("pool", lambda e: e.memset(ones_col.t[:], 1.0), writes=[ones_col.r])
    xsqc = [C.sb("xsqc%d" % i, [128, 128], F32) for i in range(2)]
    zs = [C.sb("zs%d" % i, [128, NH], F32) for i in range(2)]
    rt = [C.sb("rt%d" % i, [128, 1], F32) for i in range(2)]
    NT = T // 128
    k = 0
    for tt in range(NT):
        pz = ps[2 + (tt % 2) * 2]
        pq = ps[3 + (tt % 2) * 2]
        tsl = slice(tt * 128, (tt + 1) * 128)
        for c in range(NCH):
            sq = xsqc[k % 2]
            k += 1
            P.op("act", lambda e, c=c, sq=sq, tsl=tsl: e.activation(out=sq.t[:], in_=C.xT.t[:, c, tsl], func=AF.Square),
                 reads=[C.xr[c]], writes=[sq.r])
            P.op("pe", lambda e, c=c, sq=sq, pq=pq: e.matmul(pq.t[:, 0:1], lhsT=sq.t[:], rhs=ones_col.t[:],
                                                             start=(c == 0), stop=(c == NCH - 1)),
                 reads=[sq.r, ones_col.r], writes=[pq.r])
            P.op("pe", lambda e, c=c, tsl=tsl, pz=pz: e.matmul(pz.t[:, 0:NH], lhsT=C.xT.t[:, c, tsl], rhs=gfw.t[:, c, :],
                                                               start=(c == 0), stop=(c == NCH - 1)),
                 reads=[C.xr[c], gfw.r], writes=[pz.r])
        r1 = rt[tt % 2]
        z = zs[tt % 2]
        P.op("act", lambda e, r1=r1, pq=pq: e.activation(out=r1.t[:], in_=pq.t[:, 0:1], func=AF.Sqrt, bias=EPS,
                                                         scale=1.0 / D_MODEL), reads=[pq.r], writes=[r1.r])
        P.op("dve", lambda e, r1=r1: e.reciprocal(out=r1.t[:], in_=r1.t[:]), reads=[r1.r], writes=[r1.r])
        P.op("dve", lambda e, r1=r1, z=z, pz=pz: e.scalar_tensor_tensor(
            out=z.t[:], in0=pz.t[:, 0:NH], scalar=r1.t[:, 0:1], in1=fb.t[:], op0=ALU.mult, op1=ALU.add),
            reads=[pz.r, r1.r, fb.r], writes=[z.r])
        P.op("act", lambda e, z=z: e.activation(out=z.t[:], in_=z.t[:], func=AF.Exp, scale=-1.0),
             reads=[z.r], writes=[z.r])
        P.op("act", lambda e, z=z: e.activation(out=z.t[:], in_=z.t[:], func=AF.Ln, bias=1.0, scale=1.0),
             reads=[z.r], writes=[z.r])
        C.outs.append(P.dma("sp", lambda e, z=z, tt=tt: e.dma_start(out=nlf_o[tt], in_=z.t[:]), reads=[z.r]))

    wkb = [C.sb("wkb%d" % i, [128, NCH, 128], BF16) for i in range(2)]
    ko = [C.sb("ko%d" % i, [128, T], BF16) for i in range(2)]
    for h in range(NH):
        w = wkb[h % 2]
        o = ko[h % 2]
        P.dma("pool", lambda e, w=w, h=h: e.dma_start(out=w.t[:], in_=wk_d[h]), writes=[w.r])
        for hf in range(2):
            pp = ps[(2 * h + hf) % 4]
            proj_fm(C, w, pp, hf)
            if hf == 0:
                P.op("act", lambda e, o=o, pp=pp: e.activation(out=o.t[:, 0:512], in_=pp.t[:], func=AF.Copy),
                     reads=[pp.r], writes=[o.r])
            else:
                P.op("dve", lambda e, o=o, pp=pp: e.tensor_copy(out=o.t[:, 512:1024], in_=pp.t[:]),
                     reads=[pp.r], writes=[o.r])
        C.outs.append(P.dma("sp", lambda e, o=o, h=h: e.dma_start(out=kT_o[h], in_=o.t[:]), reads=[o.r]))

    wvb = [C.sb("wvb%d" % i, [128, NCH, 512], BF16) for i in range(2)]
    vo = [C.sb("vo%d" % i, [128, 512], BF16) for i in range(4)]
    k = 0
    for b in range(8):
        w = wvb[b % 2]
        P.dma("pool", lambda e, w=w, b=b: e.dma_start(out=w.t[:], in_=wv_d[b]), writes=[w.r])
        for tt in range(NT):
            pp = ps[4 + k % 4]
            o = vo[k % 4]
            for c in range(NCH):
                P.op("pe", lambda e, c=c, tt=tt, w=w, pp=pp: e.matmul(
                    pp.t[:], lhsT=C.hT.t[:, c, tt * 128:(tt + 1) * 128], rhs=w.t[:, c, :],
                    start=(c == 0), stop=(c == NCH - 1)), reads=[w.r, C.hr[c]], writes=[pp.r])
            if k % 2 == 0:
                P.op("act", lambda e, o=o, pp=pp: e.activation(out=o.t[:], in_=pp.t[:], func=AF.Copy),
                     reads=[pp.r], writes=[o.r])
            else:
                P.op("dve", lambda e, o=o, pp=pp: e.tensor_copy(out=o.t[:], in_=pp.t[:]), reads=[pp.r], writes=[o.r])
            C.outs.append(P.dma("sp", lambda e, o=o, tt=tt, b=b: e.dma_start(
                out=V_o[tt, :, b * 512:(b + 1) * 512], in_=o.t[:]), reads=[o.r]))
            k += 1
    return C.finish()


def core_segments(j):
    jj = j % 4
    return (jj, jj + 4)


def core_tokens(j):
    s0, s1 = core_segments(j)
    return np.concatenate([np.arange(s0 * SEG, (s0 + 1) * SEG), np.arange(s1 * SEG, (s1 + 1) * SEG)])


def tile_w_cols(W, col0, ncols, blk):
    Wc = W[:, col0:col0 + ncols]
    nb = ncols // blk
    return np.ascontiguousarray(Wc.reshape(NCH, 128, nb, blk).transpose(2, 1, 0, 3))


def tile_vec(g):
    return np.ascontiguousarray(g.reshape(NCH, 128).T)


def to_xT(xtok):
    t = xtok.shape[0]
    return np.ascontiguousarray(xtok.T.reshape(NCH, 128, t).transpose(1, 0, 2))


def from_xT(xT):
    t = xT.shape[2]
    return np.ascontiguousarray(xT.transpose(1, 0, 2).reshape(D_MODEL, t).T)


_NC_CACHE = {}


def get_nc(kind):
    if kind not in _NC_CACHE:
        _NC_CACHE[kind] = build_kv() if kind == "kv" else build_mix(kind)
    return _NC_CACHE[kind]


def run_kv(xT_cores, gn, W, koff, voff, f_w, f_b):
    nc = get_nc("kv")
    wk = tile_w_cols(W, koff, D_INNER, 128)
    wv = tile_w_cols(W, voff, D_INNER, 512)
    fw = np.ascontiguousarray(f_w.reshape(NCH, 128, NH).transpose(1, 0, 2))
    fb = np.ascontiguousarray(np.broadcast_to(f_b[None, :], (128, NH)))
    g = tile_vec(gn)
    in_maps = [{"xT": xT_cores[j], "gn": g, "wk": wk, "wv": wv, "fw": fw, "fb": fb} for j in range(8)]
    res = run_bass_kernel_spmd(nc, in_maps, core_ids=list(range(8)))
    return res.results


def attn_unit(C, kt_ap, v_ap, q_ap, ncol, sc_ap, adds, acc_o, acc_s, c0, first, kres, vres, qres, ares, psS, Sb, PT):
    P = C.P
    P.op("pe", lambda e: e.matmul(psS.t[:, 0:ncol], lhsT=kt_ap, rhs=q_ap, start=True, stop=True),
         reads=[kres, qres], writes=[psS.r])
    for (o, n, ap, r) in adds:
        P.op("dve", lambda e, o=o, n=n, ap=ap: e.scalar_tensor_tensor(
            out=Sb.t[:, o:o + n], in0=psS.t[:, o:o + n], scalar=sc_ap, in1=ap, op0=ALU.add, op1=ALU.add),
            reads=[psS.r, r] + list(ares), writes=[Sb.r])
    P.op("act", lambda e: e.activation(out=PT.t[:, 0:ncol], in_=Sb.t[:, 0:ncol], func=AF.Exp),
         reads=[Sb.r], writes=[PT.r])
    P.op("pe", lambda e: e.matmul(acc_o.t[:, c0:c0 + ncol], lhsT=v_ap, rhs=PT.t[:, 0:ncol], start=first, stop=False,
                                  skip_group_check=True),
         reads=[vres, PT.r], writes=[acc_o.r])
    P.op("pe", lambda e: e.matmul(acc_s.t[:, c0:c0 + ncol], lhsT=C.ones_b.t[:], rhs=PT.t[:, 0:ncol], start=first,
                                  stop=False, skip_group_check=True),
         reads=[C.ones_b.r, PT.r], writes=[acc_s.r])


def build_mix(kind):
    C = Ctx()
    P = C.P
    isb = kind == "b"
    xT_d = C.dram("xT", [128, NCH, T], F32, "ExternalInput")
    gn_d = C.dram("gn", [128, NCH], F32, "ExternalInput")
    wq_d = C.dram("wq", [NH, 128, NCH, 128], F32, "ExternalInput")
    wg_d = C.dram("wg", [NH, 128, NCH, 128], F32, "ExternalInput")
    wo_d = C.dram("wo", [NH // HG, 128, HG, D_MODEL], F32, "ExternalInput")
    xo_d = C.dram("xo", [128, NCH, T], F32, "ExternalOutput")
    if not isb:
        NU = (8, 8)
        kT_d = [C.dram("kTs%d" % s, [NH, 128, 8 * 128], BF16, "ExternalInput") for s in range(2)]
        V_d = [C.dram("Vs%d" % s, [8, 128, D_INNER], BF16, "ExternalInput") for s in range(2)]
        kmask_d = C.dram("kmask", [128, 2, 8], F32, "ExternalInput")
        rbx_d = C.dram("rbx", [NH, 767], F32, "ExternalInput")
        cmask_d = C.dram("cmask", [128, 640], F32, "ExternalInput")
    else:
        NU = UMAX
        kT_d = [C.dram("kTs%d" % s, [NH, 128, NU[s] * 128], BF16, "ExternalInput") for s in range(2)]
        V_d = [C.dram("Vs%d" % s, [NU[s], 128, D_INNER], BF16, "ExternalInput") for s in range(2)]
        nlf_d = [C.dram("nlfs%d" % s, [128, NU[s], NH], F32, "ExternalInput") for s in range(2)]
        kval_d = [C.dram("kval%d" % s, [128, NU[s]], F32, "ExternalInput") for s in range(2)]
        gfin_d = C.dram("gfin", [128, NCH], F32, "ExternalInput")
        ident_d = C.dram("ident", [128, 128], F32, "ExternalInput")
        tri_d = C.dram("tri", [128, 128], F32, "ExternalInput")
        trim_d = C.dram("trim", [128, 128], F32, "ExternalInput")
        yo_d = C.dram("yo", [128, NCH, T], F32, "ExternalOutput")

    ps = [C.ps("ps%d" % i) for i in range(8)]
    psS = ps[0:2]
    psO = ps[2]
    psSm = ps[3]
    psG = ps[4:8]
    phase_consts(C)
    phase_load_x(C, xT_d)
    sg = [C.sb("sg%d" % i, [128, T], F32) for i in range(2)]
    C.xsq = sg
    C.hT = C.sb("hT", [128, NCH, T], BF16)
    C.hr = [Res("h%d" % c) for c in range(NCH)]
    C.rstd = C.sb("rstd", [128, T], F32)
    phase_norm(C, gn_d, "n1", psG[0], psG[1])

    wqb = [C.sb("wqb%d" % i, [128, NCH, 128], BF16) for i in range(2)]
    wgb = [C.sb("wgb%d" % i, [128, NCH, 128], BF16) for i in range(2)]
    wob = C.sb("wob", [128, HG, D_MODEL], BF16)
    qT = [C.sb("qT%d" % i, [128, T], BF16) for i in range(2)]
    og = C.sb("og", [128, HG, T], BF16)
    kxc = [C.sb("kxc%d" % i, [128, 8 * 128], BF16) for i in range(2)]
    vxc = [C.sb("vxc%d" % i, [128, 8, 128], BF16) for i in range(2)]
    Sb = [C.sb("Sb%d" % i, [128, 512], F32) for i in range(2)]
    PT = [C.sb("PT%d" % i, [128, 512], BF16) for i in range(2)]
    rs = C.sb("rs", [128, 512], F32)
    wgt = C.sb("wgt", [128, 512], F32)

    if not isb:
        kmask = C.sb("kmask", [128, 2, 8], F32)
        P.dma("sp", lambda e: e.dma_start(out=kmask.t[:], in_=kmask_d), writes=[kmask.r])
        cmask = C.sb("cmask", [128, 640], F32)
        P.dma("sp", lambda e: e.dma_start(out=cmask.t[:], in_=cmask_d), writes=[cmask.r])
        BT = [C.sb("BT%d" % i, [128, 640], F32) for i in range(2)]
    else:
        ident = C.sb("ident", [128, 128], F32)
        tri = C.sb("tri", [128, 128], F32)
        trim = C.sb("trim", [128, 128], F32)
        for (t_, d_) in ((ident, ident_d), (tri, tri_d), (trim, trim_d)):
            P.dma("sp", lambda e, t_=t_, d_=d_: e.dma_start(out=t_.t[:], in_=d_), writes=[t_.r])
        nlf_t = C.sb("nlf_t", [128, 32, NH], F32)
        tot_t = C.sb("tot_t", [128, 32, NH], F32)
        ND = [C.sb("ND%d" % s, [128, NU[s], NH], F32) for s in range(2)]
        SC = [C.sb("SC%d" % s, [128, NU[s], NH], F32) for s in range(2)]
        kval = [C.sb("kval%d" % s, [128, NU[s]], F32) for s in range(2)]
        NDQ = C.sb("NDQ", [128, 512], F32)
        NDQd = C.sb("NDQd", [128, 512], F32)
        dexp = [C.sb("dexp%d" % i, [128, 4, 128], BF16) for i in range(2)]
        for s in range(2):
            U = NU[s]
            P.dma("sp", lambda e, s=s, U=U: e.dma_start(out=nlf_t.t[:, 0:U, :], in_=nlf_d[s]), writes=[nlf_t.r])
            P.dma("sp", lambda e, s=s: e.dma_start(out=kval[s].t[:], in_=kval_d[s]), writes=[kval[s].r])
            nflat = nlf_t.t[:, 0:U, :].rearrange("p u h -> p (u h)")
            ndflat = ND[s].t[:].rearrange("p u h -> p (u h)")
            totflat = tot_t.t[:, 0:U, :].rearrange("p u h -> p (u h)")
            for j in range(U * NH // 512):
                sl = slice(j * 512, (j + 1) * 512)
                pa = psG[(2 * j) % 4]
                pb = psG[(2 * j + 1) % 4]
                P.op("pe", lambda e, pa=pa, sl=sl, nflat=nflat: e.matmul(pa.t[:], lhsT=tri.t[:], rhs=nflat[:, sl],
                                                                         start=True, stop=True),
                     reads=[tri.r, nlf_t.r], writes=[pa.r])
                P.op("pe", lambda e, pb=pb, sl=sl, nflat=nflat: e.matmul(pb.t[:], lhsT=C.ones_f.t[:], rhs=nflat[:, sl],
                                                                         start=True, stop=True),
                     reads=[C.ones_f.r, nlf_t.r], writes=[pb.r])
                P.op("act", lambda e, pa=pa, sl=sl, ndflat=ndflat: e.activation(out=ndflat[:, sl], in_=pa.t[:], func=AF.Copy),
                     reads=[pa.r], writes=[ND[s].r])
                P.op("dve", lambda e, pb=pb, sl=sl, totflat=totflat: e.tensor_copy(out=totflat[:, sl], in_=pb.t[:]),
                     reads=[pb.r], writes=[tot_t.r])
            for u in range(1, U - 1):
                P.op("dve", lambda e, u=u: e.tensor_tensor(out=tot_t.t[:, u, :], in0=tot_t.t[:, u, :],
                                                           in1=tot_t.t[:, u - 1, :], op=ALU.add),
                     reads=[tot_t.r], writes=[tot_t.r])
            P.op("dve", lambda e, s=s, U=U: e.tensor_tensor(out=ND[s].t[:, 1:U, :], in0=ND[s].t[:, 1:U, :],
                                                            in1=tot_t.t[:, 0:U - 1, :], op=ALU.add),
                 reads=[tot_t.r, ND[s].r], writes=[ND[s].r])
            P.op("dve", lambda e, s=s, U=U: e.tensor_tensor(
                out=SC[s].t[:], in0=kval[s].t[:].unsqueeze(2).to_broadcast([128, U, NH]), in1=ND[s].t[:],
                op=ALU.subtract), reads=[kval[s].r, ND[s].r], writes=[SC[s].r])

    nK = 0
    nUnit = 0
    for grp in range(NH // HG):
        P.dma("pool", lambda e, grp=grp: e.dma_start(out=wob.t[:], in_=wo_d[grp]), writes=[wob.r])
        for hh in range(HG):
            h = grp * HG + hh
            wq_, wg_ = wqb[h % 2], wgb[h % 2]
            P.dma("pool", lambda e, wq_=wq_, h=h: e.dma_start(out=wq_.t[:], in_=wq_d[h]), writes=[wq_.r])
            P.dma("pool", lambda e, wg_=wg_, h=h: e.dma_start(out=wg_.t[:], in_=wg_d[h]), writes=[wg_.r])
            q_ = qT[h % 2]
            g_ = sg[h % 2]
            for hf in range(2):
                pp = psG[hf]
                proj_fm(C, wq_, pp, hf)
                P.op("dve", lambda e, q_=q_, pp=pp, hf=hf: e.tensor_scalar(
                    out=q_.t[:, hf * 512:(hf + 1) * 512], in0=pp.t[:], scalar1=SCALE, scalar2=None, op0=ALU.mult),
                    reads=[pp.r], writes=[q_.r])
            for hf in range(2):
                pp = psG[2 + hf]
                proj_fm(C, wg_, pp, hf)
                P.op("act", lambda e, g_=g_, pp=pp, hf=hf: e.activation(
                    out=g_.t[:, hf * 512:(hf + 1) * 512], in_=pp.t[:], func=AF.Silu),
                    reads=[pp.r], writes=[g_.r])
            if not isb:
                bt = BT[h % 2]
                src = bass.AP(rbx_d.tensor, h * 767, [[1, 128], [1, 640]])
                P.dma("sp", lambda e, bt=bt, src=src: e.dma_start(out=bt.t[:], in_=src), writes=[bt.r])
                P.op("pool", lambda e, bt=bt: e.tensor_tensor(out=bt.t[:], in0=bt.t[:], in1=cmask.t[:], op=ALU.add),
                     reads=[bt.r, cmask.r], writes=[bt.r])
            for s in range(2):
                U = NU[s]
                if isb:
                    dx = dexp[(2 * h + s) % 2]
                    for i in range(4):
                        P.op("pool", lambda e, dx=dx, i=i, s=s, h=h: e.tensor_tensor(
                            out=dx.t[:, i, :], in0=ident.t[:], in1=ND[s].t[:, 3 - i, h:h + 1].to_broadcast([128, 128]),
                            op=ALU.mult), reads=[ident.r, ND[s].r], writes=[dx.r])
                    pq = psG[(2 * h + s) % 4]
                    P.op("pe", lambda e, dx=dx, pq=pq: e.matmul(pq.t[:], lhsT=C.ones_b.t[:],
                                                                rhs=dx.t[:].rearrange("p i q -> p (i q)"),
                                                                start=True, stop=True),
                         reads=[C.ones_b.r, dx.r], writes=[pq.r])
                    P.op("act", lambda e, pq=pq: e.activation(out=NDQ.t[:], in_=pq.t[:], func=AF.Copy),
                         reads=[pq.r], writes=[NDQ.r])
                    P.op("pool", lambda e: e.tensor_tensor(
                        out=NDQd.t[:].rearrange("p (i q) -> p i q", i=4), in0=NDQ.t[:].rearrange("p (i q) -> p i q", i=4),
                        in1=trim.t[:].unsqueeze(1).to_broadcast([128, 4, 128]), op=ALU.add),
                        reads=[NDQ.r, trim.r], writes=[NDQd.r])
                first = True
                for ch in range(U // 8):
                    kx = kxc[nK % 2]
                    vx = vxc[nK % 2]
                    nK += 1
                    P.dma("sp", lambda e, kx=kx, s=s, h=h, ch=ch: e.dma_start(
                        out=kx.t[:], in_=kT_d[s][h, :, ch * 1024:(ch + 1) * 1024]), writes=[kx.r])
                    P.dma("sp", lambda e, vx=vx, s=s, h=h, ch=ch: e.dma_start(
                        out=vx.t[:], in_=V_d[s][ch * 8:(ch + 1) * 8, :, h * 128:(h + 1) * 128].rearrange("u p d -> p u d")),
                        writes=[vx.r])
                    if ch == 0:
                        order = [3, 4, 0, 1, 2, 5, 6, 7] if not isb else [3, 2, 1, 0, 4, 5, 6, 7]
                    else:
                        order = list(range(8))
                    for ul in order:
                        u = ch * 8 + ul
                        if not isb:
                            ilo, ihi = max(0, u - 4), min(3, u)
                            rlo = ilo + 4 - u
                            ncol = (ihi - ilo + 1) * 128
                            adds = [(0, ncol, BT[h % 2].t[:, rlo * 128:rlo * 128 + ncol], BT[h % 2].r)]
                            sc_ap = kmask.t[:, s, u:u + 1]
                            ares = [kmask.r]
                        else:
                            if u < 4:
                                ilo, ihi = 3 - u, 3
                                ncol = (ihi - ilo + 1) * 128
                                adds = [(0, 128, NDQd.t[:, ilo * 128:(ilo + 1) * 128], NDQd.r)]
                                if ncol > 128:
                                    adds.append((128, ncol - 128, NDQ.t[:, (ilo + 1) * 128:512], NDQ.r))
                            else:
                                ilo, ihi = 0, 3
                                ncol = 512
                                adds = [(0, 512, NDQ.t[:, :], NDQ.r)]
                            sc_ap = SC[s].t[:, u, h:h + 1]
                            ares = [SC[s].r]
                        c0 = ilo * 128
                        attn_unit(C, kx.t[:, ul * 128:(ul + 1) * 128], vx.t[:, ul, :],
                                  q_.t[:, s * 512 + c0:s * 512 + c0 + ncol], ncol, sc_ap, adds, psO, psSm, c0, first,
                                  kx.r, vx.r, q_.r, ares, psS[nUnit % 2], Sb[nUnit % 2], PT[nUnit % 2])
                        first = False
                        nUnit += 1
                P.op("dve", lambda e: e.reciprocal(out=rs.t[:], in_=psSm.t[:]), reads=[psSm.r], writes=[rs.r])
                P.op("pool", lambda e, g_=g_, s=s: e.tensor_tensor(out=wgt.t[:], in0=rs.t[:],
                                                                   in1=g_.t[:, s * 512:(s + 1) * 512], op=ALU.mult),
                     reads=[rs.r, g_.r], writes=[wgt.r])
                P.op("dve", lambda e, hh=hh, s=s: e.tensor_tensor(out=og.t[:, hh, s * 512:(s + 1) * 512], in0=psO.t[:],
                                                                  in1=wgt.t[:], op=ALU.mult),
                     reads=[psO.r, wgt.r], writes=[og.r])
        k = 0
        for c in range(NCH):
            for hf in range(2):
                pp = psG[k % 4]
                k += 1
                for hh in range(HG):
                    P.op("pe", lambda e, pp=pp, hh=hh, c=c, hf=hf: e.matmul(
                        pp.t[:], lhsT=wob.t[:, hh, c * 128:(c + 1) * 128], rhs=og.t[:, hh, hf * 512:(hf + 1) * 512],
                        start=(hh == 0), stop=(hh == HG - 1)), reads=[wob.r, og.r], writes=[pp.r])
                P.op("dve", lambda e, pp=pp, c=c, hf=hf: e.tensor_tensor(
                    out=C.xT.t[:, c, hf * 512:(hf + 1) * 512], in0=pp.t[:], in1=C.xT.t[:, c, hf * 512:(hf + 1) * 512],
                    op=ALU.add), reads=[pp.r, C.xr[c]], writes=[C.xr[c]])
    for c0 in range(0, NCH, 4):
        C.outs.append(P.dma("sp", lambda e, c0=c0: e.dma_start(out=xo_d[:, c0:c0 + 4, :], in_=C.xT.t[:, c0:c0 + 4, :]),
                            reads=C.xr[c0:c0 + 4]))
    if isb:
        gf = C.sb("gfin", [128, NCH], F32)
        P.dma("sp", lambda e: e.dma_start(out=gf.t[:], in_=gfin_d), writes=[gf.r])
        pss = (psG[0], psG[1])
        for c in range(NCH):
            sq = sg[c % 2]
            P.op("act", lambda e, c=c, sq=sq: e.activation(out=sq.t[:], in_=C.xT.t[:, c, :], func=AF.Square),
                 reads=[C.xr[c]], writes=[sq.r])
            for hf in range(2):
                P.op("pe", lambda e, c=c, sq=sq, hf=hf: e.matmul(
                    pss[hf].t[:], lhsT=C.ones_f.t[:], rhs=sq.t[:, hf * 512:(hf + 1) * 512],
                    start=(c == 0), stop=(c == NCH - 1)), reads=[sq.r, C.ones_f.r], writes=[pss[hf].r])
        for hf in range(2):
            sl = slice(hf * 512, (hf + 1) * 512)
            P.op("act", lambda e, hf=hf, sl=sl: e.activation(
                out=C.rstd.t[:, sl], in_=pss[hf].t[:], func=AF.Sqrt, bias=EPS, scale=1.0 / D_MODEL),
                reads=[pss[hf].r], writes=[C.rstd.r])
        P.op("dve", lambda e: e.reciprocal(out=C.rstd.t[:], in_=C.rstd.t[:]), reads=[C.rstd.r], writes=[C.rstd.r])
        for c in range(NCH):
            yb = sg[c % 2]
            P.op("dve", lambda e, c=c, yb=yb: e.scalar_tensor_tensor(
                out=yb.t[:], in0=C.xT.t[:, c, :], scalar=gf.t[:, c:c + 1], in1=C.rstd.t[:],
                op0=ALU.mult, op1=ALU.mult), reads=[C.xr[c], gf.r, C.rstd.r], writes=[yb.r])
            C.outs.append(P.dma("sp", lambda e, c=c, yb=yb: e.dma_start(out=yo_d[:, c, :], in_=yb.t[:]), reads=[yb.r]))
    return C.finish()


def _tile_wo(Wo):
    return np.ascontiguousarray(Wo.reshape(NH // HG, HG, 128, D_MODEL).transpose(0, 2, 1, 3))


def _gather_seq(res, key):
    out = []
    for b in range(BATCH):
        if key == "kT":
            g = np.zeros((NH, 128, SEQ), dtype=res[0][key].dtype)
            for j in range(4 * b, 4 * b + 4):
                for s, sgm in enumerate(core_segments(j)):
                    g[:, :, sgm * SEG:(sgm + 1) * SEG] = res[j][key][:, :, s * SEG:(s + 1) * SEG]
        else:
            w = res[0][key].shape[2]
            g = np.zeros((SEQ // 128, 128, w), dtype=res[0][key].dtype)
            for j in range(4 * b, 4 * b + 4):
                for s, sgm in enumerate(core_segments(j)):
                    g[sgm * 4:(sgm + 1) * 4] = res[j][key][s * 4:(s + 1) * 4]
        out.append(g)
    return out


def run_a(xT_cores, gn, W_in, rel_bias, W_out, kT_g, V_g):
    nc = get_nc("a")
    wq = tile_w_cols(W_in, 0, D_INNER, 128)
    wg = tile_w_cols(W_in, 3 * D_INNER, D_INNER, 128)
    wo = _tile_wo(W_out)
    g = tile_vec(gn)
    idx = np.clip(np.arange(767) - 127, -256, 256) + 256
    rbx = np.ascontiguousarray(rel_bias[:, idx])
    cmask = np.zeros((128, 5, 128), np.float32)
    cmask[:64, 0, :64] = NEG
    cmask[64:, 4, 64:] = NEG
    cmask = cmask.reshape(128, 640)
    in_maps = []
    for j in range(8):
        b = j // 4
        m = {"xT": xT_cores[j], "gn": g, "wq": wq, "wg": wg, "wo": wo, "rbx": rbx, "cmask": cmask}
        kmask = np.zeros((128, 2, 8), np.float32)
        for s, sgm in enumerate(core_segments(j)):
            kT = np.zeros((NH, 128, 2 * SEG), dtype=kT_g[b].dtype)
            V = np.zeros((8, 128, D_INNER), dtype=V_g[b].dtype)
            kT[:, :, SEG:] = kT_g[b][:, :, sgm * SEG:(sgm + 1) * SEG]
            V[4:] = V_g[b][sgm * 4:(sgm + 1) * 4]
            if sgm > 0:
                kT[:, :, :SEG] = kT_g[b][:, :, (sgm - 1) * SEG:sgm * SEG]
                V[:4] = V_g[b][(sgm - 1) * 4:sgm * 4]
            else:
                kmask[:, s, :4] = NEG
            m["kTs%d" % s] = np.ascontiguousarray(kT.reshape(NH, 128, 8, 128)[:, :, :, ::-1]).reshape(NH, 128, 1024)
            m["Vs%d" % s] = np.ascontiguousarray(V[:, ::-1, :])
        m["kmask"] = kmask
        in_maps.append(m)
    res = run_bass_kernel_spmd(nc, in_maps, core_ids=list(range(8)))
    return [r["xo"] for r in res.results]


def run_b(xT_cores, gn, W_in, W_out, kT_g, V_g, nlf_g, gfin):
    nc = get_nc("b")
    wq = tile_w_cols(W_in, 0, D_INNER, 128)
    wg = tile_w_cols(W_in, D_INNER, D_INNER, 128)
    wo = _tile_wo(W_out)
    g = tile_vec(gn)
    gf = tile_vec(gfin)
    ident = np.eye(128, dtype=np.float32)
    kl = np.arange(128)
    tri = (kl[:, None] > kl[None, :]).astype(np.float32)
    trim = np.where(kl[:, None] > kl[None, :], NEG, 0.0).astype(np.float32)
    in_maps = []
    for j in range(8):
        b = j // 4
        m = {"xT": xT_cores[j], "gn": g, "wq": wq, "wg": wg, "wo": wo, "gfin": gf, "ident": ident, "tri": tri,
             "trim": trim}
        for s, sgm in enumerate(core_segments(j)):
            U = UMAX[s]
            tiles = [4 * sgm + 3 - u for u in range(4 * sgm + 4)]
            kT = np.zeros((NH, 128, U * 128), dtype=kT_g[b].dtype)
            V = np.zeros((U, 128, D_INNER), dtype=V_g[b].dtype)
            nlf = np.zeros((128, U, NH), np.float32)
            kval = np.full((128, U), NEG, np.float32)
            for u, tix in enumerate(tiles):
                kT[:, :, u * 128:(u + 1) * 128] = kT_g[b][:, :, tix * 128:(tix + 1) * 128]
                V[u] = V_g[b][tix]
                nlf[:, u, :] = nlf_g[b][tix]
                kval[:, u] = 0.0
            m["kTs%d" % s] = kT
            m["Vs%d" % s] = V
            m["nlfs%d" % s] = nlf
            m["kval%d" % s] = kval
        in_maps.append(m)
    res = run_bass_kernel_spmd(nc, in_maps, core_ids=list(range(8)))
    return [r["xo"] for r in res.results], [r["yo"] for r in res.results]


def kernel_unfused(x, a_norm, a_w_in, a_rel_bias, a_w_out, kv_norm, kv_w, f_w, f_b, b_norm, b_w_in, b_w_out, final_norm):
    x = np.asarray(x, np.float32)
    xT = [to_xT(x[j // 4][core_tokens(j)]) for j in range(8)]
    f_w = np.asarray(f_w, np.float32)
    f_b = np.asarray(f_b, np.float32)
    for l in range(2):
        W = np.asarray(a_w_in[l], np.float32)
        r = run_kv(xT, np.asarray(a_norm[l], np.float32), W, D_INNER, 2 * D_INNER, f_w, f_b)
        kT_g = _gather_seq(r, "kT")
        V_g = _gather_seq(r, "V")
        xT = run_a(xT, np.asarray(a_norm[l], np.float32), W, np.asarray(a_rel_bias[l], np.float32),
                   np.asarray(a_w_out[l], np.float32), kT_g, V_g)
    r = run_kv(xT, np.asarray(kv_norm, np.float32), np.asarray(kv_w, np.float32), 0, D_INNER, f_w, f_b)
    kT_g = _gather_seq(r, "kT")
    V_g = _gather_seq(r, "V")
    nlf_g = _gather_seq(r, "nlf")
    yT = None
    for l in range(2):
        xT, yT = run_b(xT, np.asarray(b_norm[l], np.float32), np.asarray(b_w_in[l], np.float32),
                       np.asarray(b_w_out[l], np.float32), kT_g, V_g, nlf_g, np.asarray(final_norm, np.float32))
    out = np.zeros((BATCH, SEQ, D_MODEL), np.float32)
    for j in range(8):
        out[j // 4][core_tokens(j)] = from_xT(yT[j])
    return out


NSEGP = 11
KSEGE = 1024 * 512
VSEGE = 512 * 512
NSEGE = 512 * NH
GROUPS = [[0, 1, 2, 3], [4, 5, 6, 7]]


def build_fused():
    C = Ctx()
    P = C.P
    nc = C.nc
    xT_d = C.dram("xT", [128, NCH, T], F32, "ExternalInput")
    gns_d = C.dram("gns", [6, 128, NCH], F32, "ExternalInput")
    a_wq_d = C.dram("a_wq", [2, NH, 128, NCH, 128], F32, "ExternalInput")
    a_wg_d = C.dram("a_wg", [2, NH, 128, NCH, 128], F32, "ExternalInput")
    a_wk_d = C.dram("a_wk", [2, NH, 128, NCH, 128], F32, "ExternalInput")
    a_wv_d = C.dram("a_wv", [2, 8, 128, NCH, 512], F32, "ExternalInput")
    a_wo_d = C.dram("a_wo", [2, NH // HG, 128, HG, D_MODEL], F32, "ExternalInput")
    s_wk_d = C.dram("s_wk", [NH, 128, NCH, 128], F32, "ExternalInput")
    s_wv_d = C.dram("s_wv", [8, 128, NCH, 512], F32, "ExternalInput")
    b_wq_d = C.dram("b_wq", [2, NH, 128, NCH, 128], F32, "ExternalInput")
    b_wg_d = C.dram("b_wg", [2, NH, 128, NCH, 128], F32, "ExternalInput")
    b_wo_d = C.dram("b_wo", [2, NH // HG, 128, HG, D_MODEL], F32, "ExternalInput")
    fw_d = C.dram("fw", [128, NCH, NH], F32, "ExternalInput")
    fb_d = C.dram("fb", [128, NH], F32, "ExternalInput")
    rbx_d = C.dram("rbx", [2, NH, 767], F32, "ExternalInput")
    cmask_d = C.dram("cmask", [128, 640], F32, "ExternalInput")
    kmask_d = C.dram("kmask", [128, 2, 8], F32, "ExternalInput")
    kval_d = [C.dram("kval%d" % s, [128, UMAX[s]], F32, "ExternalInput") for s in range(2)]
    cst_d = C.dram("cst", [4, 128, 128], F32, "ExternalInput")
    yo_d = C.dram("yo", [128, NCH, T], F32, "ExternalOutput")
    kTo = [nc.dram_tensor("kTo%d" % i, [4 * 2 * 1024, 512], BF16) for i in range(1)]
    Vo = [nc.dram_tensor("Vo%d" % i, [8 * 2 * 512, 512], BF16) for i in range(1)]
    nlfo = nc.dram_tensor("nlfo", [T, NH], F32)
    KL = [nc.dram_tensor("KL%d" % i, [4 * NSEGP * 1024, 512], BF16) for i in range(2)]
    VL = [nc.dram_tensor("VL%d" % i, [4 * NSEGP * 1024, 512], BF16) for i in range(2)]
    NL = nc.dram_tensor("NL", [NSEGP * 512, NH], F32)
    kTo_r = [[Res() for _ in range(NH)] for _ in range(2)]
    Vo_r = [[Res() for _ in range(8)] for _ in range(2)]
    nlfo_r = Res()
    KL_r = [[[Res() for _ in range(2)] for _ in range(4)] for _ in range(2)]
    VL_r = [[[Res() for _ in range(2)] for _ in range(4)] for _ in range(2)]
    NL_r = [Res() for _ in range(2)]
    pad_r = Res()
    Kloc = nc.dram_tensor("Kloc", [4 * 8 * 1024, 512], BF16)
    Vloc = nc.dram_tensor("Vloc", [4 * 8 * 1024, 512], BF16)
    Nloc = nc.dram_tensor("Nloc", [8 * 512, NH], F32)
    Kloc_r = [Res() for _ in range(4)]
    Vloc_r = [Res() for _ in range(4)]
    Nloc_r = Res()

    ps = [C.ps("ps%d" % i) for i in range(8)]
    psS = [ps[0], ps[1], ps[2], ps[7]]
    psOs = [ps[3], ps[3]]
    psSms = [ps[4], ps[4]]
    psO = psOs[0]
    psSm = psSms[0]
    psG = [ps[5], ps[6], ps[7], ps[0]]
    psG2 = [ps[5], ps[6]]
    phase_consts(C)
    phase_load_x(C, xT_d)

    sg = [C.sb("sg%d" % i, [128, T], F32) for i in range(2)]
    C.xsq = sg
    C.hT = C.sb("hT", [128, NCH, T], BF16)
    C.hr = [Res("h%d" % c) for c in range(NCH)]
    C.rstd = C.sb("rstd", [128, T], F32)
    WB0 = C.sb("WB0", [128, 8192], BF16)
    WB1 = C.sb("WB1", [128, 8192], BF16)
    wq_r = [Res(), Res()]
    wg_r = [Res(), Res()]
    WB1_rs = wq_r + wg_r

    def wvb_ap(i):
        return (WB0 if i == 0 else WB1).t[:].rearrange("p (c n) -> p c n", c=NCH)

    def wvb_res(i):
        return [WB0.r] if i == 0 else WB1_rs

    wob_ap = WB0.t[:].rearrange("p (h n) -> p h n", h=HG)

    def wqb_ap(i):
        return WB1.t[:, i * 2048:(i + 1) * 2048].rearrange("p (c n) -> p c n", c=NCH)

    def wgb_ap(i):
        return WB1.t[:, 4096 + i * 2048:4096 + (i + 1) * 2048].rearrange("p (c n) -> p c n", c=NCH)

    kvbuf = [C.sb("kvbuf%d" % i, [128, 2048], BF16) for i in range(3)]
    qT = [C.sb("qT%d" % i, [128, T], BF16) for i in range(2)]
    PT = [C.sb("PT%d" % i, [128, 512], BF16) for i in range(4)]
    vo = PT
    Sb = [C.sb("Sb%d" % i, [128, 512], F32) for i in range(4)]
    og = C.sb("og", [128, HG, T], BF16)
    rs = C.sb("rs", [128, 512], F32)
    wgt = C.sb("wgt", [128, 512], F32)
    BT = [C.sb("BT%d" % i, [128, 640], F32) for i in range(2)]
    cmask = C.sb("cmask", [128, 640], F32)
    kmask = C.sb("kmask", [128, 2, 8], F32)
    cst = C.sb("cst", [128, 4, 128], F32)
    ND = [C.sb("ND%d" % s, [128, UMAX[s], NH], F32) for s in range(2)]
    SC = [C.sb("SC%d" % s, [128, UMAX[s], NH], F32) for s in range(2)]
    kval = [C.sb("kval%d" % s, [128, UMAX[s]], F32) for s in range(2)]
    NDQ = BT[0]
    NDQd = BT[1]
    dexp = [C.sb("dexp%d" % i, [128, 4, 128], BF16) for i in range(2)]
    fb = C.sb("fb", [128, NH], F32)
    ones_col = C.sb("ones_col", [128, 1], F32)
    xsqc = [C.sb("xsqc%d" % i, [128, 128], F32) for i in range(2)]
    zs = [C.sb("zs%d" % i, [128, NH], F32) for i in range(2)]
    rt = [C.sb("rt%d" % i, [128, 1], F32) for i in range(2)]
    zero = kvbuf[0]

    ident_ap = cst.t[:, 0, :]
    tri_ap = cst.t[:, 1, :]
    trim_ap = cst.t[:, 2, :]
    J_ap = cst.t[:, 3, :]
    for (t_, d_) in ((cmask, cmask_d), (kmask, kmask_d), (fb, fb_d), (kval[0], kval_d[0]), (kval[1], kval_d[1])):
        P.dma("sp", lambda e, t_=t_, d_=d_: e.dma_start(out=t_.t[:], in_=d_), writes=[t_.r])
    P.dma("sp", lambda e: e.dma_start(out=cst.t[:], in_=cst_d.rearrange("k p n -> p k n")), writes=[cst.r])
    P.op("pool", lambda e: e.memset(ones_col.t[:], 1.0), writes=[ones_col.r])
    P.op("pool", lambda e: e.memset(zero.t[:], 0.0), writes=[zero.r])
    for st in range(1):
        segs = (0, 1, 2) if st == 0 else (2,)
        for i in range(4):
            for sgp in segs:
                for half in range(2):
                    r0 = (i * NSEGP + sgp) * 1024 + half * 512
                    P.dma("sp", lambda e, st=st, r0=r0: e.dma_start(
                        out=KL[st][r0:r0 + 512, :].rearrange("(p a) n -> p (a n)", p=128), in_=zero.t[:]),
                        reads=[zero.r], writes=[pad_r])
        for b in range(4):
            for sgp in segs:
                for half in range(2):
                    r0 = (b * NSEGP + sgp) * 1024 + half * 512
                    P.dma("sp", lambda e, st=st, r0=r0: e.dma_start(
                        out=VL[st][r0:r0 + 512, :].rearrange("(p a) n -> p (a n)", p=128), in_=zero.t[:]),
                        reads=[zero.r], writes=[pad_r])
    P.dma("sp", lambda e: e.dma_start(out=NL[0:3 * 512, :].rearrange("(p a) n -> p (a n)", p=128),
                                      in_=zero.t[:, 0:384].bitcast(F32) if False else zero.t[:, 0:768].bitcast(F32)),
          reads=[zero.r], writes=[pad_r])

    dyn = {}

    def pre_sp(e):
        jj = e.snap(e.partition_id() % 4, min_val=0, max_val=3)
        dyn["k"] = e.snap(jj * KSEGE, min_val=0, max_val=3 * KSEGE)

    def pre_pool(e):
        jj = e.snap(e.partition_id() % 4, min_val=0, max_val=3)
        dyn["v"] = e.snap(jj * KSEGE, min_val=0, max_val=3 * KSEGE)
        dyn["n"] = e.snap(jj * NSEGE, min_val=0, max_val=3 * NSEGE)

    P.pre = {"sp": pre_sp, "pool": pre_pool}

    def localize_k(i):
        P.dma("sp", lambda e, i=i: e.dma_start(
            out=bass.AP(Kloc, i * 8 * KSEGE, [[32768, 128], [1, 32768]]),
            in_=bass.AP(KL[0], dyn["k"] + i * NSEGP * KSEGE, [[32768, 128], [1, 32768]])),
            reads=[KL_r[0][i][0], KL_r[0][i][1], pad_r], writes=[Kloc_r[i]])

    def localize_v(p_):
        P.dma("pool", lambda e, p_=p_: e.dma_start(
            out=bass.AP(Vloc, p_ * 8 * KSEGE, [[32768, 128], [1, 32768]]),
            in_=bass.AP(VL[0], dyn["v"] + p_ * NSEGP * KSEGE, [[32768, 128], [1, 32768]])),
            reads=[VL_r[0][p_][0], VL_r[0][p_][1], pad_r], writes=[Vloc_r[p_]])

    def localize_n():
        P.dma("pool", lambda e: e.dma_start(
            out=bass.AP(Nloc, 0, [[1024, 128], [1, 1024]]),
            in_=bass.AP(NL, dyn["n"], [[1024, 128], [1, 1024]])),
            reads=[NL_r[0], NL_r[1], pad_r], writes=[Nloc_r])

    def kv_phase(st, gidx, wk_ap, wv_ap, gates):
        gn = phase_norm(C, gns_d[gidx], "g%d" % gidx, psG[0], psG[1])
        NT = T // 128
        if gates:
            fw_ap = Sb[0].t[:].rearrange("p (c h) -> p c h", c=NCH)
            gfw_ap = Sb[1].t[:].rearrange("p (c h) -> p c h", c=NCH)
            P.dma("sp", lambda e: e.dma_start(out=fw_ap, in_=fw_d), writes=[Sb[0].r])
            for c in range(NCH):
                P.op("pool", lambda e, c=c: e.tensor_scalar(out=gfw_ap[:, c, :], in0=fw_ap[:, c, :],
                                                            scalar1=gn.t[:, c:c + 1], scalar2=None, op0=ALU.mult),
                     reads=[Sb[0].r, gn.r], writes=[Sb[1].r])
            k = 0
            for tt in range(NT):
                pz = psS[tt % 2]
                pq = (psO, psSm)[tt % 2]
                tsl = slice(tt * 128, (tt + 1) * 128)
                for c in range(NCH):
                    sq = xsqc[k % 2]
                    k += 1
                    P.op("act", lambda e, c=c, sq=sq, tsl=tsl: e.activation(out=sq.t[:], in_=C.xT.t[:, c, tsl],
                                                                            func=AF.Square),
                         reads=[C.xr[c]], writes=[sq.r])
                    P.op("pe", lambda e, c=c, sq=sq, pq=pq: e.matmul(pq.t[:, 0:1], lhsT=sq.t[:], rhs=ones_col.t[:],
                                                                     start=(c == 0), stop=(c == NCH - 1)),
                         reads=[sq.r, ones_col.r], writes=[pq.r])
                    P.op("pe", lambda e, c=c, tsl=tsl, pz=pz: e.matmul(pz.t[:, 0:NH], lhsT=C.xT.t[:, c, tsl],
                                                                       rhs=gfw_ap[:, c, :],
                                                                       start=(c == 0), stop=(c == NCH - 1)),
                         reads=[C.xr[c], Sb[1].r], writes=[pz.r])
                r1 = rt[tt % 2]
                z = zs[tt % 2]
                P.op("act", lambda e, r1=r1, pq=pq: e.activation(out=r1.t[:], in_=pq.t[:, 0:1], func=AF.Sqrt, bias=EPS,
                                                                 scale=1.0 / D_MODEL), reads=[pq.r], writes=[r1.r])
                P.op("dve", lambda e, r1=r1: e.reciprocal(out=r1.t[:], in_=r1.t[:]), reads=[r1.r], writes=[r1.r])
                P.op("dve", lambda e, r1=r1, z=z, pz=pz: e.scalar_tensor_tensor(
                    out=z.t[:], in0=pz.t[:, 0:NH], scalar=r1.t[:, 0:1], in1=fb.t[:], op0=ALU.mult, op1=ALU.add),
                    reads=[pz.r, r1.r, fb.r], writes=[z.r])
                P.op("act", lambda e, z=z: e.activation(out=z.t[:], in_=z.t[:], func=AF.Exp, scale=-1.0),
                     reads=[z.r], writes=[z.r])
                P.op("act", lambda e, z=z: e.activation(out=z.t[:], in_=z.t[:], func=AF.Ln, bias=1.0, scale=1.0),
                     reads=[z.r], writes=[z.r])
                P.dma("sp", lambda e, z=z, tt=tt: e.dma_start(out=nlfo[tt * 128:(tt + 1) * 128, :], in_=z.t[:]),
                      reads=[z.r], writes=[nlfo_r])
            for sl in range(2):
                P.coll(("n", sl), lambda e, sl=sl: e.collective_compute(
                    "AllGather", ALU.bypass, replica_groups=GROUPS, ins=[nlfo[sl * 512:(sl + 1) * 512, :]],
                    outs=[NL[(3 + sl * 4) * 512:(3 + sl * 4 + 4) * 512, :]]), reads=[nlfo_r], writes=[NL_r[sl]])
        def load_wk(h):
            kb_ = kvbuf[h % 2]
            P.dma("pool", lambda e, kb_=kb_, h=h: e.dma_start(out=kb_.t[:].rearrange("p (c n) -> p c n", c=NCH),
                                                             in_=wk_ap(h)), writes=[kb_.r])
        load_wk(0)
        for h in range(NH):
            kb = kvbuf[h % 2]
            w_ap = kb.t[:].rearrange("p (c n) -> p c n", c=NCH)
            o = qT[h % 2]
            if h + 1 < NH:
                load_wk(h + 1)
            for hf in range(2):
                pp = psG[(2 * h + hf) % 4]
                for c in range(NCH):
                    P.op("pe", lambda e, c=c, pp=pp, w_ap=w_ap, hf=hf: e.matmul(
                        pp.t[:], lhsT=w_ap[:, c, :], rhs=C.hT.t[:, c, hf * 512:(hf + 1) * 512],
                        start=(c == 0), stop=(c == NCH - 1)), reads=[kb.r, C.hr[c]], writes=[pp.r])
                if hf == 0:
                    P.op("act", lambda e, o=o, pp=pp: e.activation(out=o.t[:, 0:512], in_=pp.t[:], func=AF.Copy),
                         reads=[pp.r], writes=[o.r])
                else:
                    P.op("dve", lambda e, o=o, pp=pp: e.tensor_copy(out=o.t[:, 512:1024], in_=pp.t[:]),
                         reads=[pp.r], writes=[o.r])
            for sl in range(2):
                r0 = ((h // 8) * 2 + sl) * 1024 + (h % 8) * 128
                P.dma("sp", lambda e, o=o, r0=r0, sl=sl: e.dma_start(out=kTo[st][r0:r0 + 128, :],
                                                                     in_=o.t[:, sl * 512:(sl + 1) * 512]),
                      reads=[o.r], writes=[kTo_r[st][h]])
            if h % 8 == 7:
                i = h // 8
                for sl in range(2):
                    P.coll(("k", i, sl), lambda e, i=i, sl=sl: e.collective_compute(
                        "AllGather", ALU.bypass, replica_groups=GROUPS,
                        ins=[kTo[st][(i * 2 + sl) * 1024:(i * 2 + sl + 1) * 1024, :]],
                        outs=[KL[st][(i * NSEGP + 3 + sl * 4) * 1024:(i * NSEGP + 3 + sl * 4 + 4) * 1024, :]]),
                        reads=kTo_r[st][i * 8:(i + 1) * 8], writes=[KL_r[st][i][sl]])
                if i >= 1:
                    localize_k(i - 1)
        k = 0
        def load_wv(b):
            P.dma("pool", lambda e, b=b: e.dma_start(out=wvb_ap(b % 2), in_=wv_ap(b)), writes=wvb_res(b % 2))
        load_wv(0)
        for b in range(8):
            w_ap = wvb_ap(b % 2)
            w_rs = wvb_res(b % 2)
            if b + 1 < 8:
                load_wv(b + 1)
            for tt in range(NT):
                pp = psG[k % 4]
                o = vo[k % 4]
                for c in range(NCH):
                    P.op("pe", lambda e, c=c, tt=tt, w_ap=w_ap, pp=pp: e.matmul(
                        pp.t[:], lhsT=C.hT.t[:, c, tt * 128:(tt + 1) * 128], rhs=w_ap[:, c, :],
                        start=(c == 0), stop=(c == NCH - 1)), reads=w_rs + [C.hr[c]], writes=[pp.r])
                if k % 2 == 0:
                    P.op("act", lambda e, o=o, pp=pp: e.activation(out=o.t[:], in_=pp.t[:], func=AF.Copy),
                         reads=[pp.r], writes=[o.r])
                else:
                    P.op("dve", lambda e, o=o, pp=pp: e.tensor_copy(out=o.t[:], in_=pp.t[:]), reads=[pp.r], writes=[o.r])
                r0 = (((b // 2) * 2 + tt // 4) * 2 + b % 2) * 512 + (tt % 4) * 128
                P.dma("act", lambda e, o=o, r0=r0: e.dma_start(out=Vo[st][r0:r0 + 128, :], in_=o.t[:]),
                      reads=[o.r], writes=[Vo_r[st][b]])
                k += 1
            if b % 2 == 1:
                p_ = b // 2
                for sl in range(2):
                    P.coll(("v", p_, sl), lambda e, p_=p_, sl=sl: e.collective_compute(
                        "AllGather", ALU.bypass, replica_groups=GROUPS,
                        ins=[Vo[st][(p_ * 2 + sl) * 1024:(p_ * 2 + sl + 1) * 1024, :]],
                        outs=[VL[st][(p_ * NSEGP + 3 + sl * 4) * 1024:(p_ * NSEGP + 3 + sl * 4 + 4) * 1024, :]]),
                        reads=[Vo_r[st][b - 1], Vo_r[st][b]], writes=[VL_r[st][p_][sl]])
                if p_ == 0:
                    localize_k(3)
                if p_ >= 1:
                    localize_v(p_ - 1)
        localize_v(3)
        if gates:
            localize_n()

    def decay_prep():
        nlf_t = sg[0].t[:].rearrange("p (u h) -> p u h", u=32)
        tot_t = sg[1].t[:].rearrange("p (u h) -> p u h", u=32)
        for s in range(2):
            U = UMAX[s]
            for a in range(U // 4):
                off = (3 + 4 * s - a) * NSEGE
                P.dma("sp", lambda e, a=a, off=off: e.dma_start(
                    out=nlf_t[:, 4 * a:4 * a + 4, :],
                    in_=bass.AP(Nloc, off, [[NH, 128], [128 * NH, 4], [1, NH]])),
                    reads=[Nloc_r], writes=[sg[0].r])
            nflat = sg[0].t[:, 0:U * NH]
            ndflat = ND[s].t[:].rearrange("p u h -> p (u h)")
            totflat = sg[1].t[:, 0:U * NH]
            for j in range(U * NH // 512):
                sl_ = slice(j * 512, (j + 1) * 512)
                pa = psG[(2 * j) % 4]
                pb = psG[(2 * j + 1) % 4]
                P.op("pe", lambda e, pa=pa, sl_=sl_, nflat=nflat: e.matmul(pa.t[:], lhsT=tri_ap, rhs=nflat[:, sl_],
                                                                           start=True, stop=True),
                     reads=[cst.r, sg[0].r], writes=[pa.r])
                P.op("pe", lambda e, pb=pb, sl_=sl_, nflat=nflat: e.matmul(pb.t[:], lhsT=C.ones_f.t[:], rhs=nflat[:, sl_],
                                                                           start=True, stop=True),
                     reads=[C.ones_f.r, sg[0].r], writes=[pb.r])
                P.op("act", lambda e, pa=pa, sl_=sl_, ndflat=ndflat: e.activation(out=ndflat[:, sl_], in_=pa.t[:],
                                                                                  func=AF.Copy),
                     reads=[pa.r], writes=[ND[s].r])
                P.op("dve", lambda e, pb=pb, sl_=sl_, totflat=totflat: e.tensor_copy(out=totflat[:, sl_], in_=pb.t[:]),
                     reads=[pb.r], writes=[sg[1].r])

            def uof(v):
                return 4 * (v // 4) + 3 - (v % 4)
            for v in range(1, U):
                u1, u0 = uof(v), uof(v - 1)
                P.op("dve", lambda e, s=s, u1=u1, u0=u0: e.tensor_tensor(out=ND[s].t[:, u1, :], in0=ND[s].t[:, u1, :],
                                                                         in1=tot_t[:, u0, :], op=ALU.add),
                     reads=[sg[1].r, ND[s].r], writes=[ND[s].r])
                if v < U - 1:
                    P.op("dve", lambda e, u1=u1, u0=u0: e.tensor_tensor(out=tot_t[:, u1, :], in0=tot_t[:, u1, :],
                                                                        in1=tot_t[:, u0, :], op=ALU.add),
                         reads=[sg[1].r], writes=[sg[1].r])
            P.op("dve", lambda e, s=s, U=U: e.tensor_tensor(
                out=SC[s].t[:], in0=kval[s].t[:].unsqueeze(2).to_broadcast([128, U, NH]), in1=ND[s].t[:],
                op=ALU.subtract), reads=[kval[s].r, ND[s].r], writes=[SC[s].r])

    cnt = {"K": 0, "U": 0, "A": 0}

    def mix_phase(isb, st, gidx, wq_ap, wg_ap, wo_ap, rb_layer):
        NU = UMAX if isb else (8, 8)
        if isb:
            phase_norm(C, gns_d[gidx], "g%d" % gidx, psG[0], psG[1])
        if isb and gidx == 3:
            decay_prep()
        for grp in range(NH // HG):
            P.dma("pool", lambda e, grp=grp: e.dma_start(out=wob_ap, in_=wo_ap(grp)), writes=[WB0.r])
            for hh in range(HG):
                h = grp * HG + hh
                i8 = h // 8
                b4 = h // 4
                wq_, wg_ = wqb_ap(h % 2), wgb_ap(h % 2)

                def load_qg(h2):
                    P.dma("pool", lambda e, h2=h2: e.dma_start(out=wqb_ap(h2 % 2), in_=wq_ap(h2)), writes=[wq_r[h2 % 2]])
                    P.dma("pool", lambda e, h2=h2: e.dma_start(out=wgb_ap(h2 % 2), in_=wg_ap(h2)), writes=[wg_r[h2 % 2]])
                if h == 0:
                    load_qg(0)
                if h + 1 < NH:
                    load_qg(h + 1)
                q_ = qT[h % 2]
                g_ = sg[h % 2]
                for hf in range(2):
                    pp = psG2[hf]
                    for c in range(NCH):
                        P.op("pe", lambda e, c=c, pp=pp, wq_=wq_, hf=hf: e.matmul(
                            pp.t[:], lhsT=wq_[:, c, :], rhs=C.hT.t[:, c, hf * 512:(hf + 1) * 512],
                            start=(c == 0), stop=(c == NCH - 1)), reads=[wq_r[h % 2], C.hr[c]], writes=[pp.r])
                    P.op("dve", lambda e, q_=q_, pp=pp, hf=hf: e.tensor_scalar(
                        out=q_.t[:, hf * 512:(hf + 1) * 512], in0=pp.t[:], scalar1=SCALE, scalar2=None, op0=ALU.mult),
                        reads=[pp.r], writes=[q_.r])
                for hf in range(2):
                    pp = psG2[hf]
                    for c in range(NCH):
                        P.op("pe", lambda e, c=c, pp=pp, wg_=wg_, hf=hf: e.matmul(
                            pp.t[:], lhsT=wg_[:, c, :], rhs=C.hT.t[:, c, hf * 512:(hf + 1) * 512],
                            start=(c == 0), stop=(c == NCH - 1)), reads=[wg_r[h % 2], C.hr[c]], writes=[pp.r])
                    P.op("act", lambda e, g_=g_, pp=pp, hf=hf: e.activation(
                        out=g_.t[:, hf * 512:(hf + 1) * 512], in_=pp.t[:], func=AF.Silu),
                        reads=[pp.r], writes=[g_.r])
                def prep_bt(h2):
                    bt = BT[h2 % 2]
                    src = bass.AP(rbx_d.tensor, (rb_layer * NH + h2) * 767, [[1, 128], [1, 640]])
                    P.dma("sp", lambda e, bt=bt, src=src: e.dma_start(out=bt.t[:], in_=src), writes=[bt.r])
                    pj = (psG2[0], psG2[1])
                    P.op("pe", lambda e, bt=bt: e.matmul(pj[0].t[:], lhsT=J_ap, rhs=bt.t[:, 0:512], start=True, stop=True),
                         reads=[cst.r, bt.r], writes=[pj[0].r])
                    P.op("pe", lambda e, bt=bt: e.matmul(pj[1].t[:, 0:128], lhsT=J_ap, rhs=bt.t[:, 512:640], start=True,
                                                         stop=True), reads=[cst.r, bt.r], writes=[pj[1].r])
                    P.op("dve", lambda e, bt=bt: e.tensor_tensor(out=bt.t[:, 0:512], in0=pj[0].t[:], in1=cmask.t[:, 0:512],
                                                                 op=ALU.add), reads=[pj[0].r, cmask.r], writes=[bt.r])
                    P.op("dve", lambda e, bt=bt: e.tensor_tensor(out=bt.t[:, 512:640], in0=pj[1].t[:, 0:128],
                                                                 in1=cmask.t[:, 512:640], op=ALU.add),
                         reads=[pj[1].r, cmask.r], writes=[bt.r])

                def prep_ndq(g2):
                    h2, s2_ = g2 // 2, g2 % 2
                    dx = dexp[g2 % 2]
                    for i in range(4):
                        P.op("pool", lambda e, dx=dx, i=i, s2_=s2_, h2=h2: e.tensor_tensor(
                            out=dx.t[:, i, :], in0=ident_ap, in1=ND[s2_].t[:, i, h2:h2 + 1].to_broadcast([128, 128]),
                            op=ALU.mult), reads=[cst.r, ND[s2_].r], writes=[dx.r])
                    pq = psG2[g2 % 2]
                    nq = BT[g2 % 2]
                    P.op("pe", lambda e, dx=dx, pq=pq: e.matmul(pq.t[:], lhsT=C.ones_b.t[:],
                                                                rhs=dx.t[:].rearrange("p i q -> p (i q)"),
                                                                start=True, stop=True),
                         reads=[C.ones_b.r, dx.r], writes=[pq.r])
                    P.op("act", lambda e, pq=pq, nq=nq: e.activation(out=nq.t[:, 0:512], in_=pq.t[:], func=AF.Copy),
                         reads=[pq.r], writes=[nq.r])

                if not isb:
                    if h == 0:
                        prep_bt(0)
                    if h + 1 < NH:
                        prep_bt(h + 1)
                for s in range(2):
                    U = NU[s]
                    if isb:
                        gi_ = 2 * h + s
                        if gi_ == 0:
                            prep_ndq(0)
                        if gi_ + 1 < 2 * NH:
                            prep_ndq(gi_ + 1)
                        NDQg = BT[gi_ % 2]
                    first = True
                    units = []
                    chunk_loads = []
                    psO = psOs[cnt["A"] % 2]
                    psSm = psSms[cnt["A"] % 2]
                    cnt["A"] += 1
                    for ch in range(U // 8):
                        kb = kvbuf[cnt["K"] % 3]
                        cnt["K"] += 1
                        kx_ap = kb.t[:, 0:1024]
                        vx_ap = kb.t[:, 1024:2048].rearrange("p (u d) -> p u d", u=8)
                        kdeps = [Kloc_r[i8]]
                        vdeps = [Vloc_r[b4 // 2]]
                        def load_chunk(kb=kb, kx_ap=kx_ap, vx_ap=vx_ap, ch=ch, kdeps=kdeps, vdeps=vdeps, s=s, h=h,
                                       i8=i8, b4=b4):
                            for s2 in range(2):
                                if isb:
                                    sgp = 3 + 4 * s - (2 * ch + s2)
                                else:
                                    sgp = 3 + 4 * s - 1 + s2
                                koff = ((i8 * 8 + sgp) * 1024 + (h % 8) * 128) * 512
                                voff = ((((b4 // 2) * 8 + sgp) * 2 + b4 % 2) * 512) * 512 + (h % 4) * 128
                                P.dma("sp", lambda e, s2=s2, koff=koff: e.dma_start(
                                    out=kx_ap[:, s2 * 512:(s2 + 1) * 512],
                                    in_=bass.AP(Kloc, koff, [[512, 128], [1, 512]])),
                                    reads=kdeps, writes=[kb.r])
                                P.dma("sp", lambda e, s2=s2, voff=voff: e.dma_start(
                                    out=vx_ap[:, s2 * 4:(s2 + 1) * 4, :],
                                    in_=bass.AP(Vloc, voff, [[512, 128], [128 * 512, 4], [1, 128]])),
                                    reads=vdeps, writes=[kb.r])
                        chunk_loads.append(load_chunk)
                        if ch == 0:
                            order = [3, 4, 0, 1, 2, 5, 6, 7] if not isb else list(range(8))
                        else:
                            order = list(range(8))
                        for ul in order:
                            u = ch * 8 + ul
                            diag = False
                            if not isb:
                                ilo, ihi = max(0, u - 4), min(3, u)
                                rlo = ilo + 4 - u
                                ncol = (ihi - ilo + 1) * 128
                                adds = [(0, ncol, BT[h % 2].t[:, rlo * 128:rlo * 128 + ncol], BT[h % 2].r)]
                                sc_ap = kmask.t[:, s, u:u + 1]
                                ares = [kmask.r]
                            else:
                                if u < 4:
                                    ilo, ihi = u, 3
                                    ncol = (ihi - ilo + 1) * 128
                                    adds = [(0, ncol, NDQg.t[:, ilo * 128:512], NDQg.r)]
                                    diag = True
                                else:
                                    ilo, ihi = 0, 3
                                    ncol = 512
                                    adds = [(0, 512, NDQg.t[:, 0:512], NDQg.r)]
                                sc_ap = SC[s].t[:, u, h:h + 1]
                                ares = [SC[s].r]
                            c0 = ilo * 128
                            n_ = cnt["U"]
                            cnt["U"] += 1
                            units.append((kx_ap[:, ul * 128:(ul + 1) * 128], vx_ap[:, ul, :],
                                          q_.t[:, s * 512 + c0:s * 512 + c0 + ncol], ncol, sc_ap, adds, c0, first,
                                          kb.r, q_.r, ares, psS[n_ % 4], Sb[n_ % 4], PT[n_ % 4], diag))
                            first = False
                    LA = 3
                    for idx in range(len(units) + LA):
                        if idx < len(units) and idx % 8 == 0:
                            ch_ = idx // 8
                            if ch_ == 0:
                                chunk_loads[0]()
                            if ch_ + 1 < len(chunk_loads):
                                chunk_loads[ch_ + 1]()
                        if idx < len(units):
                            (kt_, v_, qa_, ncol, sc_ap, adds, c0, fst, kr_, qr_, ares, pS_, Sb_, PT_, dg_) = units[idx]
                            P.op("pe", lambda e, pS_=pS_, ncol=ncol, kt_=kt_, qa_=qa_: e.matmul(
                                pS_.t[:, 0:ncol], lhsT=kt_, rhs=qa_, start=True, stop=True),
                                reads=[kr_, qr_], writes=[pS_.r])
                        if idx >= LA:
                            (kt_, v_, qa_, ncol, sc_ap, adds, c0, fst, kr_, qr_, ares, pS_, Sb_, PT_, dg_) = units[idx - LA]
                            for (o_, n2, ap_, r_) in adds:
                                P.op("dve", lambda e, o_=o_, n2=n2, ap_=ap_, pS_=pS_, Sb_=Sb_, sc_ap=sc_ap: e.scalar_tensor_tensor(
                                    out=Sb_.t[:, o_:o_ + n2], in0=pS_.t[:, o_:o_ + n2], scalar=sc_ap, in1=ap_,
                                    op0=ALU.add, op1=ALU.add), reads=[pS_.r, r_] + list(ares), writes=[Sb_.r])
                            if dg_:
                                P.op("dve", lambda e, Sb_=Sb_: e.tensor_tensor(out=Sb_.t[:, 0:128], in0=Sb_.t[:, 0:128],
                                                                               in1=trim_ap, op=ALU.add),
                                     reads=[Sb_.r, cst.r], writes=[Sb_.r])
                            P.op("act", lambda e, PT_=PT_, Sb_=Sb_, ncol=ncol: e.activation(
                                out=PT_.t[:, 0:ncol], in_=Sb_.t[:, 0:ncol], func=AF.Exp), reads=[Sb_.r], writes=[PT_.r])
                            P.op("pe", lambda e, v_=v_, PT_=PT_, ncol=ncol, c0=c0, fst=fst, psO=psO: e.matmul(
                                psO.t[:, c0:c0 + ncol], lhsT=v_, rhs=PT_.t[:, 0:ncol], start=fst, stop=False,
                                skip_group_check=True), reads=[kr_, PT_.r], writes=[psO.r])
                            P.op("pe", lambda e, PT_=PT_, ncol=ncol, c0=c0, fst=fst, psSm=psSm: e.matmul(
                                psSm.t[:, c0:c0 + ncol], lhsT=C.ones_b.t[:], rhs=PT_.t[:, 0:ncol], start=fst, stop=False,
                                skip_group_check=True), reads=[C.ones_b.r, PT_.r], writes=[psSm.r])
                    P.op("dve", lambda e, psSm=psSm: e.reciprocal(out=rs.t[:], in_=psSm.t[:]), reads=[psSm.r], writes=[rs.r])
                    P.op("pool", lambda e, g_=g_, s=s: e.tensor_tensor(out=wgt.t[:], in0=rs.t[:],
                                                                       in1=g_.t[:, s * 512:(s + 1) * 512], op=ALU.mult),
                         reads=[rs.r, g_.r], writes=[wgt.r])
                    P.op("dve", lambda e, hh=hh, s=s, psO=psO: e.tensor_tensor(out=og.t[:, hh, s * 512:(s + 1) * 512],
                                                                      in0=psO.t[:], in1=wgt.t[:], op=ALU.mult),
                         reads=[psO.r, wgt.r], writes=[og.r])
            k = 0
            for c in range(NCH):
                for hf in range(2):
                    pp = psG2[k % 2]
                    k += 1
                    for hh in range(HG):
                        P.op("pe", lambda e, pp=pp, hh=hh, c=c, hf=hf: e.matmul(
                            pp.t[:], lhsT=wob_ap[:, hh, c * 128:(c + 1) * 128], rhs=og.t[:, hh, hf * 512:(hf + 1) * 512],
                            start=(hh == 0), stop=(hh == HG - 1)), reads=[WB0.r, og.r], writes=[pp.r])
                    P.op("dve", lambda e, pp=pp, c=c, hf=hf: e.tensor_tensor(
                        out=C.xT.t[:, c, hf * 512:(hf + 1) * 512], in0=pp.t[:], in1=C.xT.t[:, c, hf * 512:(hf + 1) * 512],
                        op=ALU.add), reads=[pp.r, C.xr[c]], writes=[C.xr[c]])

    for l in range(2):
        kv_phase(0, l, lambda h, l=l: a_wk_d[l, h], lambda b, l=l: a_wv_d[l, b], False)
        mix_phase(False, 0, l, lambda h, l=l: a_wq_d[l, h], lambda h, l=l: a_wg_d[l, h], lambda g, l=l: a_wo_d[l, g], l)
    kv_phase(0, 2, lambda h: s_wk_d[h], lambda b: s_wv_d[b], True)
    for l in range(2):
        mix_phase(True, 0, 3 + l, lambda h, l=l: b_wq_d[l, h], lambda h, l=l: b_wg_d[l, h], lambda g, l=l: b_wo_d[l, g], 0)

    gf = C.sb("gfin", [128, NCH], F32)
    P.dma("sp", lambda e: e.dma_start(out=gf.t[:], in_=gns_d[5]), writes=[gf.r])
    pss = (psG[0], psG[1])
    for c in range(NCH):
        sq = sg[c % 2]
        P.op("act", lambda e, c=c, sq=sq: e.activation(out=sq.t[:], in_=C.xT.t[:, c, :], func=AF.Square),
             reads=[C.xr[c]], writes=[sq.r])
        for hf in range(2):
            P.op("pe", lambda e, c=c, sq=sq, hf=hf: e.matmul(
                pss[hf].t[:], lhsT=C.ones_f.t[:], rhs=sq.t[:, hf * 512:(hf + 1) * 512],
                start=(c == 0), stop=(c == NCH - 1)), reads=[sq.r, C.ones_f.r], writes=[pss[hf].r])
    for hf in range(2):
        sl = slice(hf * 512, (hf + 1) * 512)
        P.op("act", lambda e, hf=hf, sl=sl: e.activation(
            out=C.rstd.t[:, sl], in_=pss[hf].t[:], func=AF.Sqrt, bias=EPS, scale=1.0 / D_MODEL),
            reads=[pss[hf].r], writes=[C.rstd.r])
    P.op("dve", lambda e: e.reciprocal(out=C.rstd.t[:], in_=C.rstd.t[:]), reads=[C.rstd.r], writes=[C.rstd.r])
    for c in range(NCH):
        yb = sg[c % 2]
        P.op("dve", lambda e, c=c, yb=yb: e.scalar_tensor_tensor(
            out=yb.t[:], in0=C.xT.t[:, c, :], scalar=gf.t[:, c:c + 1], in1=C.rstd.t[:],
            op0=ALU.mult, op1=ALU.mult), reads=[C.xr[c], gf.r, C.rstd.r], writes=[yb.r])
        C.outs.append(P.dma("sp", lambda e, c=c, yb=yb: e.dma_start(out=yo_d[:, c, :], in_=yb.t[:]), reads=[yb.r]))
    return C.finish()


def kernel(x, a_norm, a_w_in, a_rel_bias, a_w_out, kv_norm, kv_w, f_w, f_b, b_norm, b_w_in, b_w_out, final_norm):
    f = lambda a: np.asarray(a, np.float32)
    x = f(x)
    a_w_in, a_w_out, kv_w, b_w_in, b_w_out = f(a_w_in), f(a_w_out), f(kv_w), f(b_w_in), f(b_w_out)
    gns = np.stack([tile_vec(f(a_norm)[0]), tile_vec(f(a_norm)[1]), tile_vec(f(kv_norm)), tile_vec(f(b_norm)[0]),
                    tile_vec(f(b_norm)[1]), tile_vec(f(final_norm))])
    shared = {
        "gns": gns,
        "a_wq": np.stack([tile_w_cols(a_w_in[l], 0, D_INNER, 128) for l in range(2)]),
        "a_wk": np.stack([tile_w_cols(a_w_in[l], D_INNER, D_INNER, 128) for l in range(2)]),
        "a_wv": np.stack([tile_w_cols(a_w_in[l], 2 * D_INNER, D_INNER, 512) for l in range(2)]),
        "a_wg": np.stack([tile_w_cols(a_w_in[l], 3 * D_INNER, D_INNER, 128) for l in range(2)]),
        "a_wo": np.stack([_tile_wo(a_w_out[l]) for l in range(2)]),
        "s_wk": tile_w_cols(kv_w, 0, D_INNER, 128),
        "s_wv": tile_w_cols(kv_w, D_INNER, D_INNER, 512),
        "b_wq": np.stack([tile_w_cols(b_w_in[l], 0, D_INNER, 128) for l in range(2)]),
        "b_wg": np.stack([tile_w_cols(b_w_in[l], D_INNER, D_INNER, 128) for l in range(2)]),
        "b_wo": np.stack([_tile_wo(b_w_out[l]) for l in range(2)]),
        "fw": np.ascontiguousarray(f(f_w).reshape(NCH, 128, NH).transpose(1, 0, 2)),
        "fb": np.ascontiguousarray(np.broadcast_to(f(f_b)[None, :], (128, NH))),
    }
    idx = np.clip(np.arange(767) - 127, -256, 256) + 256
    shared["rbx"] = np.ascontiguousarray(f(a_rel_bias)[:, :, idx])
    cmask = np.zeros((128, 5, 128), np.float32)
    cmask[64:, 0, :64] = NEG
    cmask[:64, 4, 64:] = NEG
    shared["cmask"] = cmask.reshape(128, 640)
    kl = np.arange(128)
    cst = np.zeros((4, 128, 128), np.float32)
    cst[0] = np.eye(128, dtype=np.float32)
    cst[1] = (kl[:, None] > kl[None, :]).astype(np.float32)
    cst[2] = np.where(kl[:, None] > kl[None, :], NEG, 0.0)
    cst[3] = np.eye(128, dtype=np.float32)[::-1]
    shared["cst"] = cst
    in_maps = []
    for j in range(8):
        jj = j % 4
        m = dict(shared)
        m["xT"] = to_xT(x[j // 4][core_tokens(j)])
        kmask = np.zeros((128, 2, 8), np.float32)
        if jj == 0:
            kmask[:, 0, :4] = NEG
        m["kmask"] = kmask
        for s in range(2):
            kv = np.full((128, UMAX[s]), NEG, np.float32)
            for u in range(UMAX[s]):
                if jj + 4 * s - u // 4 >= 0:
                    kv[:, u] = 0.0
            m["kval%d" % s] = kv
        in_maps.append(m)
    if "fused" not in _NC_CACHE:
        _NC_CACHE["fused"] = build_fused()
    res = run_bass_kernel_spmd(_NC_CACHE["fused"], in_maps, core_ids=list(range(8)))
    out = np.zeros((BATCH, SEQ, D_MODEL), np.float32)
    for j in range(8):
        out[j // 4][core_tokens(j)] = from_xT(res.results[j]["yo"])
    return out
```
